# Optimizing a Trainium2 kernel written in Bass

```python
import math
import jax, jax.numpy as jnp
from jax import lax
import numpy as np

D_MODEL = 1024
BATCH = 2
SEQ = 8192
DEPTH = 4
DEC_BATCH = 128
DEC_SEQ = 8
PAST_LEN = 2048
PAGE_SIZE = 128

N_MIXERS = 3
N_CONV_LAYERS = (DEPTH + 2) // N_MIXERS
N_GLA_LAYERS = (DEPTH + 1) // N_MIXERS
N_ATT_LAYERS = DEPTH // N_MIXERS
CONV_CHANNELS = D_MODEL
CONV_SIZE = 31
GLA_HEADS = 4
GLA_DK = D_MODEL // 2 // GLA_HEADS
GLA_DV = D_MODEL // GLA_HEADS
GLA_RANK = 16
GLA_TAU = 16.0
GLA_CHUNK = 64
ATT_GROUPS = ((128, 1), (512, 4), (2048, 16))
ATT_HEADS = 8
ATT_HEAD_DIM = 64
ATT_WIDTH = ATT_HEADS * ATT_HEAD_DIM
Q_BLOCK = 128
ROPE_THETA = 10000.0
EPS = 1e-6

kernel_name = 'hybrid_conv_gla_dilated_swa_step'


def _rms_norm(x, g):
    x32 = x.astype(jnp.float32)
    y = x32 * lax.rsqrt(jnp.mean(x32 * x32, axis=-1, keepdims=True) + EPS)
    return (y * g.astype(jnp.float32)).astype(x.dtype)


def _layer_norm(x, g, b):
    x32 = x.astype(jnp.float32)
    xc = x32 - jnp.mean(x32, axis=-1, keepdims=True)
    y = xc * lax.rsqrt(jnp.mean(xc * xc, axis=-1, keepdims=True) + EPS)
    return (y * g.astype(jnp.float32) + b.astype(jnp.float32)).astype(x.dtype)


def _conv_mixer(h, buf, w_in, w_dw, b_dw, g_ln, b_ln, w_out):
    a, a_gate, z = jnp.split(h @ w_in, 3, axis=-1)
    u = a * jax.nn.sigmoid(a_gate)
    u_ext = jnp.concatenate([buf.astype(u.dtype), u], axis=1)
    y = lax.conv_general_dilated(u_ext, w_dw[:, None, :].astype(u.dtype), window_strides=(1,), padding='VALID',
                                 dimension_numbers=('NWC', 'WIO', 'NWC'),
                                 feature_group_count=CONV_CHANNELS) + b_dw
    y = jax.nn.silu(_layer_norm(y, g_ln, b_ln)) * jax.nn.silu(z)
    return y @ w_out, u_ext[:, -(CONV_SIZE - 1):]


def _gla_chunked(q, k, v, log_a, s0):
    B, L, H, K = q.shape
    V = v.shape[-1]
    C = math.gcd(L, GLA_CHUNK)
    n = L // C

    def chunks(t):
        return t.astype(jnp.float32).reshape(B, n, C, H, t.shape[-1]).swapaxes(0, 1)

    causal = jnp.tril(jnp.ones((C, C), dtype=bool))

    def step(S, xs):
        qc, kc, vc, ac = xs
        bc = jnp.cumsum(ac, axis=1)
        diff = jnp.where(causal[None, :, :, None, None], bc[:, :, None] - bc[:, None, :], -jnp.inf)
        attn = jnp.einsum('bihk,bjhk,bijhk->bhij', qc, kc, jnp.exp(diff))
        o = (jnp.einsum('bhij,bjhv->bihv', attn, vc)
             + jnp.einsum('bihk,bhkv->bihv', qc * jnp.exp(bc), S))
        btot = bc[:, -1]
        S = (jnp.exp(btot)[..., None] * S
             + jnp.einsum('bjhk,bjhv->bhkv', kc * jnp.exp(btot[:, None] - bc), vc))
        return S, o

    S, o = lax.scan(step, s0.astype(jnp.float32), (chunks(q), chunks(k), chunks(v), chunks(log_a)))
    return o.swapaxes(0, 1).reshape(B, L, H, V), S


def _gla_mixer(h, s0, w_in, w_a1, w_a2, b_a, g_norm, w_out):
    B, L, _ = h.shape
    qk = GLA_HEADS * GLA_DK
    vw = GLA_HEADS * GLA_DV
    q, k, v, r = jnp.split(h @ w_in, [qk, 2 * qk, 2 * qk + vw], axis=-1)
    q = q.reshape(B, L, GLA_HEADS, GLA_DK) * (GLA_DK ** -0.5)
    k = k.reshape(B, L, GLA_HEADS, GLA_DK)
    v = v.reshape(B, L, GLA_HEADS, GLA_DV)
    log_a = jax.nn.log_sigmoid(((h @ w_a1) @ w_a2 + b_a).astype(jnp.float32)) / GLA_TAU
    log_a = log_a.reshape(B, L, GLA_HEADS, GLA_DK)
    o, s = _gla_chunked(q, k, v, log_a, s0)
    o = _rms_norm(o.astype(h.dtype), g_norm).reshape(B, L, vw) * jax.nn.silu(r)
    return o @ w_out, s


def _rope(x, pos):
    half = ATT_HEAD_DIM // 2
    inv = ROPE_THETA ** (-jnp.arange(half, dtype=jnp.float32) / half)
    ang = pos.astype(jnp.float32)[:, None] * inv[None, :]
    cos = jnp.cos(ang)[None, :, None, None, :]
    sin = jnp.sin(ang)[None, :, None, None, :]
    x32 = x.astype(jnp.float32)
    x1, x2 = x32[..., :half], x32[..., half:]
    return jnp.concatenate([x1 * cos - x2 * sin, x2 * cos + x1 * sin], axis=-1).astype(x.dtype)


def _dilated_attention(q, ks_ext, vs_ext, first_valid):
    B, L, G, H, Dh = q.shape
    blk = math.gcd(L, Q_BLOCK)
    nblk = L // blk
    qb = q.reshape(B, nblk, blk, G, H, Dh).swapaxes(0, 1)
    scale = Dh ** -0.5

    def block(args):
        bi, qblk = args
        start = bi * blk
        outs, lses = [], []
        for g, (W, d) in enumerate(ATT_GROUPS):
            n_keys = W // d + 1
            idx = W + jnp.arange(blk)[:, None] - d * jnp.arange(n_keys)[None, :]
            valid = (start + idx) >= first_valid[g]
            k_slab = lax.dynamic_slice_in_dim(ks_ext[g], start, W + blk, axis=1)
            v_slab = lax.dynamic_slice_in_dim(vs_ext[g], start, W + blk, axis=1)
            k_gat = jnp.take(k_slab, idx, axis=1)
            v_gat = jnp.take(v_slab, idx, axis=1)
            s = jnp.einsum('bqhd,bqkhd->bqhk', qblk[:, :, g], k_gat,
                           preferred_element_type=jnp.float32) * scale
            s = jnp.where(valid[None, :, None, :], s, -1e30)
            m = jnp.max(s, axis=-1, keepdims=True)
            p = jnp.exp(s - m)
            den = jnp.sum(p, axis=-1)
            o = jnp.einsum('bqhk,bqkhd->bqhd', p.astype(v_gat.dtype), v_gat,
                           preferred_element_type=jnp.float32) / den[..., None]
            outs.append(o)
            lses.append(m[..., 0] + jnp.log(den))
        w = jax.nn.softmax(jnp.stack(lses), axis=0)
        return jnp.sum(w[..., None] * jnp.stack(outs), axis=0).astype(q.dtype)

    out = lax.map(block, (jnp.arange(nblk), qb))
    return out.swapaxes(0, 1).reshape(B, L, H, Dh)


def _dilated_mixer(h, pos, bufs_k, bufs_v, w_in, w_out):
    B, L, _ = h.shape
    G = len(ATT_GROUPS)
    gw = G * ATT_WIDTH
    q, k, v, z = jnp.split(h @ w_in, [gw, 2 * gw, 3 * gw], axis=-1)
    shp = (B, L, G, ATT_HEADS, ATT_HEAD_DIM)
    q = _rope(q.reshape(shp), pos)
    k = _rope(k.reshape(shp), pos)
    v = v.reshape(shp)
    ks_ext, vs_ext, first_valid, new_k, new_v = [], [], [], [], []
    for g, (W, _) in enumerate(ATT_GROUPS):
        kg, vg = k[:, :, g], v[:, :, g]
        if bufs_k is None:
            prev_k = jnp.zeros((B, W, ATT_HEADS, ATT_HEAD_DIM), kg.dtype)
            prev_v = prev_k
            fv = W
            keep = min(W, L)
            new_k.append(kg[:, L - keep:])
            new_v.append(vg[:, L - keep:])
        else:
            nb = bufs_k[g].shape[1]
            pad = jnp.zeros((B, W - nb, ATT_HEADS, ATT_HEAD_DIM), kg.dtype)
            prev_k = jnp.concatenate([pad, bufs_k[g].astype(kg.dtype)], axis=1)
            prev_v = jnp.concatenate([pad, bufs_v[g].astype(vg.dtype)], axis=1)
            fv = W - nb
            new_k.append(kg)
            new_v.append(vg)
        ks_ext.append(jnp.concatenate([prev_k, kg], axis=1))
        vs_ext.append(jnp.concatenate([prev_v, vg], axis=1))
        first_valid.append(fv)
    o = _dilated_attention(q, ks_ext, vs_ext, first_valid).reshape(B, L, ATT_WIDTH) * jax.nn.silu(z)
    return o @ w_out, new_k, new_v


def setup_inputs(seed: int = 0) -> dict:
    key = jax.random.key(seed)
    ks = iter(jax.random.split(key, 40))
    f32 = jnp.float32

    def nrm(shape, scale=1.0):
        return jax.random.normal(next(ks), shape, f32) * scale

    D = D_MODEL
    C = CONV_CHANNELS
    gw = len(ATT_GROUPS) * ATT_WIDTH
    gla_cols = 2 * GLA_HEADS * GLA_DK + 2 * GLA_HEADS * GLA_DV
    inp = {}
    inp['x_prompt'] = nrm((BATCH, SEQ, D))
    inp['x_sample'] = nrm((DEC_BATCH, DEC_SEQ, D))
    inp['state_conv'] = nrm((N_CONV_LAYERS, DEC_BATCH, CONV_SIZE - 1, C), 0.5)
    inp['state_gla'] = nrm((N_GLA_LAYERS, DEC_BATCH, GLA_HEADS, GLA_DK, GLA_DV), 0.5)
    for i, (W, _) in enumerate(ATT_GROUPS):
        nb = min(W, PAST_LEN)
        inp['cache_k_g%d' % i] = nrm((N_ATT_LAYERS, DEC_BATCH, nb, ATT_HEADS, ATT_HEAD_DIM))
        inp['cache_v_g%d' % i] = nrm((N_ATT_LAYERS, DEC_BATCH, nb, ATT_HEADS, ATT_HEAD_DIM))
    inp['c_prompt'] = nrm((BATCH, D))
    inp['c_sample'] = nrm((DEC_BATCH, D))
    inp['w_ada'] = nrm((DEPTH, D, 3 * D), 0.5 * D ** -0.5)
    inp['b_ada'] = nrm((DEPTH, 3 * D), 0.02)
    inp['g_pre'] = 1.0 + nrm((DEPTH, D), 0.05)
    inp['g_post'] = 1.0 + nrm((DEPTH, D), 0.05)
    inp['w_conv_in'] = nrm((N_CONV_LAYERS, D, 3 * C), D ** -0.5)
    inp['w_dw'] = nrm((N_CONV_LAYERS, CONV_SIZE, C), CONV_SIZE ** -0.5)
    inp['b_dw'] = nrm((N_CONV_LAYERS, C), 0.02)
    inp['g_conv_ln'] = 1.0 + nrm((N_CONV_LAYERS, C), 0.05)
    inp['b_conv_ln'] = nrm((N_CONV_LAYERS, C), 0.02)
    inp['w_conv_out'] = nrm((N_CONV_LAYERS, C, D), C ** -0.5)
    inp['w_gla_in'] = nrm((N_GLA_LAYERS, D, gla_cols), D ** -0.5)
    inp['w_gla_a1'] = nrm((N_GLA_LAYERS, D, GLA_RANK), D ** -0.5)
    inp['w_gla_a2'] = nrm((N_GLA_LAYERS, GLA_RANK, GLA_HEADS * GLA_DK), GLA_RANK ** -0.5)
    inp['b_gla_a'] = nrm((N_GLA_LAYERS, GLA_HEADS * GLA_DK), 0.02)
    inp['g_gla_norm'] = 1.0 + nrm((N_GLA_LAYERS, GLA_DV), 0.05)
    inp['w_gla_out'] = nrm((N_GLA_LAYERS, GLA_HEADS * GLA_DV, D), (GLA_HEADS * GLA_DV) ** -0.5)
    inp['w_att_in'] = nrm((N_ATT_LAYERS, D, 3 * gw + ATT_WIDTH), D ** -0.5)
    inp['w_att_out'] = nrm((N_ATT_LAYERS, ATT_WIDTH, D), ATT_WIDTH ** -0.5)
    return inp


def reference(x_prompt, x_sample, state_conv, state_gla, cache_k_g0, cache_v_g0, cache_k_g1, cache_v_g1,
              cache_k_g2, cache_v_g2, c_prompt, c_sample, w_ada, b_ada, g_pre, g_post,
              w_conv_in, w_dw, b_dw, g_conv_ln, b_conv_ln, w_conv_out,
              w_gla_in, w_gla_a1, w_gla_a2, b_gla_a, g_gla_norm, w_gla_out,
              w_att_in, w_att_out):
    xs = [x_prompt, x_sample]
    cs = [c_prompt, c_sample]
    pos = [jnp.arange(x_prompt.shape[1]), PAST_LEN + jnp.arange(x_sample.shape[1])]
    caches_k = (cache_k_g0, cache_k_g1, cache_k_g2)
    caches_v = (cache_v_g0, cache_v_g1, cache_v_g2)
    n_groups = len(ATT_GROUPS)
    conv_new = ([], [])
    gla_new = ([], [])
    k_new = ([[] for _ in range(n_groups)], [[] for _ in range(n_groups)])
    v_new = ([[] for _ in range(n_groups)], [[] for _ in range(n_groups)])

    for l in range(DEPTH):
        kind, j = l % N_MIXERS, l // N_MIXERS
        for grp in range(2):
            x = xs[grp]
            B = x.shape[0]
            mod = cs[grp] @ w_ada[l] + b_ada[l]
            shift, scale, gate = [t[:, None, :] for t in jnp.split(mod, 3, axis=-1)]
            h = _rms_norm(x, g_pre[l]) * (1 + scale) + shift
            if kind == 0:
                buf = (jnp.zeros((B, CONV_SIZE - 1, CONV_CHANNELS), x.dtype) if grp == 0 else state_conv[j])
                o, st = _conv_mixer(h, buf, w_conv_in[j], w_dw[j], b_dw[j], g_conv_ln[j], b_conv_ln[j], w_conv_out[j])
                conv_new[grp].append(st.astype(state_conv.dtype))
            elif kind == 1:
                s0 = (jnp.zeros((B, GLA_HEADS, GLA_DK, GLA_DV), jnp.float32) if grp == 0 else state_gla[j])
                o, st = _gla_mixer(h, s0, w_gla_in[j], w_gla_a1[j], w_gla_a2[j], b_gla_a[j], g_gla_norm[j], w_gla_out[j])
                gla_new[grp].append(st.astype(state_gla.dtype))
            else:
                bk = None if grp == 0 else [c[j] for c in caches_k]
                bv = None if grp == 0 else [c[j] for c in caches_v]
                o, nk, nv = _dilated_mixer(h, pos[grp], bk, bv, w_att_in[j], w_att_out[j])
                for g in range(n_groups):
                    k_new[grp][g].append(nk[g])
                    v_new[grp][g].append(nv[g])
            xs[grp] = x + gate * _rms_norm(o, g_post[l])

    y_prompt, y_sample = xs
    new_state_conv_prompt = jnp.stack(conv_new[0])
    new_state_conv_sample = jnp.stack(conv_new[1])
    new_state_gla_prompt = jnp.stack(gla_new[0])
    new_state_gla_sample = jnp.stack(gla_new[1])
    new_k_g0_prompt = jnp.stack(k_new[0][0])
    new_v_g0_prompt = jnp.stack(v_new[0][0])
    new_k_g1_prompt = jnp.stack(k_new[0][1])
    new_v_g1_prompt = jnp.stack(v_new[0][1])
    new_k_g2_prompt = jnp.stack(k_new[0][2])
    new_v_g2_prompt = jnp.stack(v_new[0][2])
    new_k_g0_sample = jnp.stack(k_new[1][0])
    new_v_g0_sample = jnp.stack(v_new[1][0])
    new_k_g1_sample = jnp.stack(k_new[1][1])
    new_v_g1_sample = jnp.stack(v_new[1][1])
    new_k_g2_sample = jnp.stack(k_new[1][2])
    new_v_g2_sample = jnp.stack(v_new[1][2])
    return (y_prompt, y_sample, new_state_conv_prompt, new_state_conv_sample,
            new_state_gla_prompt, new_state_gla_sample,
            new_k_g0_prompt, new_v_g0_prompt, new_k_g1_prompt, new_v_g1_prompt, new_k_g2_prompt, new_v_g2_prompt,
            new_k_g0_sample, new_v_g0_sample, new_k_g1_sample, new_v_g1_sample, new_k_g2_sample, new_v_g2_sample)
```

```python
import numpy as np
from contextlib import ExitStack
import concourse.bass as bass
import concourse.mybir as mybir
from concourse.bass_utils import run_bass_kernel_spmd

F32 = mybir.dt.float32
BF16 = mybir.dt.bfloat16
I32 = mybir.dt.int32
AF = mybir.ActivationFunctionType
ALU = mybir.AluOpType
AX = mybir.AxisListType

NCORES = 8
D = 1024
KC = 8
LP = 2048
NS = 128
NT = LP + NS
SEQ = 8192
DEPTH = 4
EPS = 1e-6
NQ = 16
TN = 256
import os
NLAYERS = int(os.environ.get('NLAYERS', '4'))
ATT_EN = os.environ.get('ATT_EN', 'AXSHB')


def _en(x):
    return x in ATT_EN


A_PARTS = int(os.environ.get('A_PARTS', '31'))
QK_G = [int(c) for c in os.environ.get('QK_G', '012')]
QK_DMA = int(os.environ.get('QK_DMA', '7'))
QK_STEPS = int(os.environ.get('QK_STEPS', '9'))
B_STEPS = int(os.environ.get('B_STEPS', '9'))


class Tk:
    __slots__ = ("name", "w", "r", "multi", "wm", "psum")

    def __init__(self, name, multi=False, psum=False):
        self.name = name
        self.psum = psum
        self.w = None
        self.r = {}
        self.multi = multi
        self.wm = {}


class KB:
    def __init__(self, nc, es):
        self.nc = nc
        self.E = {"pe": nc.tensor, "act": nc.scalar, "dve": nc.vector, "pool": nc.gpsimd, "sp": nc.sync}
        self.sem = {k: es.enter_context(nc.semaphore("s_" + k)) for k in self.E}
        self.cnt = {k: 0 for k in self.E}
        self.seen = {k: {} for k in self.E}
        self.pend = {k: [] for k in self.E}
        self.dsem = {q: [es.enter_context(nc.semaphore("d_%s%d" % (q, i))) for i in range(NQ)]
                     for q in ("sp", "pool")}
        self.dcnt = {}
        for q in self.dsem:
            for s in self.dsem[q]:
                self.dcnt[s.name] = 0
        self.dnext = {q: 0 for q in self.dsem}
        self.ccsem = es.enter_context(nc.semaphore("cc_sem"))
        self.ccn = 0
        self.semobj = {self.ccsem.name: self.ccsem}
        for s in self.sem.values():
            self.semobj[s.name] = s
        for q in self.dsem:
            for s in self.dsem[q]:
                self.semobj[s.name] = s

    def _waits(self, e, reads, writes):
        need = {}

        def add(tok):
            if tok is None:
                return
            n, v = tok
            if need.get(n, 0) < v:
                need[n] = v
        own = self.sem[e].name if e in self.sem else None
        for t in reads:
            if t.multi:
                for n, v in t.wm.items():
                    add((n, v))
            else:
                add(t.w)
            if t.psum:
                for n, v in t.r.items():
                    if n != own:
                        add((n, v))
        for t in writes:
            if not t.multi:
                add(t.w)
            for n, v in t.r.items():
                add((n, v))
        for n, v in need.items():
            if self.seen[e].get(n, 0) >= v:
                continue
            if e == "pe" and n == self.sem["pe"].name:
                continue
            self.E[e].wait_ge(self.semobj[n], v)
            self.seen[e][n] = v

    def _record(self, tok, reads, writes):
        n, v = tok
        for t in reads:
            if t.r.get(n, 0) < v:
                t.r[n] = v
        for t in writes:
            if t.multi:
                if t.wm.get(n, 0) < v:
                    t.wm[n] = v
            else:
                t.w = tok
                t.r = {}

    def op(self, e, fn, reads=(), writes=(), inc=True):
        self._waits(e, reads, writes)
        ins = fn(self.E[e])
        if not inc:
            self.pend[e].append((tuple(reads), tuple(writes)))
            return
        self.cnt[e] += 1
        ins.then_inc(self.sem[e], 1)
        tok = (self.sem[e].name, self.cnt[e])
        for (r, w) in self.pend[e]:
            self._record(tok, r, w)
        self.pend[e] = []
        self._record(tok, reads, writes)

    def dma(self, q, out, in_, reads=(), writes=(), **kw):
        i = self.dnext[q]
        self.dnext[q] = (i + 1) % NQ
        s = self.dsem[q][i]
        prev = self.dcnt[s.name]
        if prev and self.seen[q].get(s.name, 0) < prev:
            self.E[q].wait_ge(s, prev)
            self.seen[q][s.name] = prev
        self._waits(q, reads, writes)
        self.E[q].dma_start(out=out, in_=in_, **kw).then_inc(s, 16)
        self.dcnt[s.name] = prev + 16
        self._record((s.name, prev + 16), reads, writes)

    def collective(self, fn, reads=(), writes=()):
        self._waits("pool", reads, writes)
        self.ccn += 1
        fn(self.E["pool"]).then_inc(self.ccsem, 1)
        self._record((self.ccsem.name, self.ccn), reads, writes)

    def dma_custom(self, q, fn, reads=(), writes=()):
        i = self.dnext[q]
        self.dnext[q] = (i + 1) % NQ
        s = self.dsem[q][i]
        prev = self.dcnt[s.name]
        if prev and self.seen[q].get(s.name, 0) < prev:
            self.E[q].wait_ge(s, prev)
            self.seen[q][s.name] = prev
        self._waits(q, reads, writes)
        fn(self.E[q]).then_inc(s, 16)
        self.dcnt[s.name] = prev + 16
        self._record((s.name, prev + 16), reads, writes)

    def barrier_on(self, k):
        n, v = self.sem[k].name, self.cnt[k]
        for e in self.E:
            if e != k and v and self.seen[e].get(n, 0) < v:
                self.E[e].wait_ge(self.sem[k], v)
                self.seen[e][n] = v

    def barrier(self):
        for e in self.E:
            for n, s in self.semobj.items():
                if n in self.dcnt:
                    v = self.dcnt[n]
                elif n == self.ccsem.name:
                    v = self.ccn
                else:
                    k = [kk for kk in self.sem if self.sem[kk].name == n][0]
                    if k == e:
                        continue
                    v = self.cnt[k]
                if v and self.seen[e].get(n, 0) < v:
                    self.E[e].wait_ge(s, v)
                    self.seen[e][n] = v

    def final_wait(self):
        e = "sp"
        for n, v in self.dcnt.items():
            if v and self.seen[e].get(n, 0) < v:
                self.E[e].wait_ge(self.semobj[n], v)
                self.seen[e][n] = v


def _vec_layout():
    lay = {}
    off = 0

    def add(name, n):
        nonlocal off
        lay[name] = (off, n)
        off += n
    add("g_pre", 4 * 8)
    add("g_post", 4 * 8)
    add("b_ada", 4 * 24)
    add("b_dw", 2 * 8)
    add("g_cln", 2 * 8)
    add("b_cln", 2 * 8)
    add("w_dw", 2 * 8 * 31)
    add("hflag", 1)
    add("eps", 1)
    add("one", 1)
    add("sel", 4)
    add("g_gn", 2)
    add("ident", 128)
    return lay, off


VLAY, NV = _vec_layout()


def _fm(v):
    v = np.asarray(v, np.float32)
    n = v.shape[-1] // 128
    r = v.reshape(v.shape[:-1] + (n, 128))
    return np.moveaxis(r, -1, 0)


def build_program():
    nc = bass.Bass("TRN2", target_bir_lowering=False)
    es = ExitStack()
    kb = KB(nc, es)

    def din(name, shape, dt=F32):
        return nc.dram_tensor(name, list(shape), dt, kind="ExternalInput").ap()

    def dout(name, shape, dt=F32):
        return nc.dram_tensor(name, list(shape), dt, kind="ExternalOutput").ap()

    xT_in = din("xT", [128, KC, NT])
    xh_in = din("xh", [128, KC, 32])
    cT_in = din("cT", [128, KC, 129])
    vecs_in = din("vecs", [128, NV])
    w_ada_in = din("w_ada", [DEPTH, D, 3 * D])
    w_conv_in_in = din("w_conv_in", [2, D, 3 * D])
    w_conv_out_in = din("w_conv_out", [2, D, D])
    sc_fm_in = din("sc_fm", [128, 2, KC, 16, 30])
    sc_old_in = din("sc_old", [2, 16, 22, D])
    w_gla_in_in = din("w_gla_in", [D, 3 * D])
    w_gla_out_in = din("w_gla_out", [D, D])
    w_a1_in = din("w_a1", [D, 16])
    w_a2_in = din("w_a2", [16, 512])
    b_a_in = din("b_a", [1, 512])
    gmask_in = din("gmask", [128, 2, 3, 128])
    gseg_in = din("gseg", [128, 2, 2, 16])
    sgla_in = din("sgla", [128, 16, 4, 256])
    w_att_in_in = din("w_att_in", [D, 5120])
    w_att_out_in = din("w_att_out", [512, D])
    rope_in = din("rope", [128, 2, NT])
    pm_in = din("pm", [128, 128])
    amask_in = din("amask", [128, 3, 512])
    smask_in = din("smask", [128, 13, 64])
    snew_in = din("snew", [128, 3, 512])
    idxk_in = din("idxk", [128, 1], I32)
    idxv_in = din("idxv", [128, 7], I32)
    if NLAYERS >= 3 and _en('S'):
        ck_in = [din("ck%d" % g, [16, w, 512]) for g, w in enumerate((128, 512, 2048))]
        cv_in = [din("cv%d" % g, [16, w, 512]) for g, w in enumerate((128, 512, 2048))]
    kout = dout("kout", [128, 10752])
    vout = dout("vout", [2688, 512])
    ks_out = dout("ks_out", [128, 12, NS])
    vs_out = dout("vs_out", [NS, 1536])
    gla_p_out = dout("gla_p", [128, 4, 256])
    gla_s_out = dout("gla_s", [128, 16, 4, 256])
    yT_out = dout("yT", [128, KC, NT])
    conv_tail_out = dout("conv_tail", [2, 128, KC, 32])
    conv_new_out = dout("conv_new", [2, 128, KC, NS])
    conv_old_out = dout("conv_old", [2, 16, 22, D])
    xs = nc.dram_tensor("xs", [128, KC, NT], F32)

    uid = [0]

    def S(name, shape, dt, stack=es):
        uid[0] += 1
        return stack.enter_context(nc.sbuf_tensor("sb_%s_%d" % (name, uid[0]), list(shape), dt))

    vecs = S("vecs_sb", [128, NV], F32)
    ones_bf = S("ones_bf", [128, 128], BF16)
    ident_bf = S("ident_bf", [128, 128], BF16)
    cT_bf = S("cT_bf", [128, KC, 129], BF16)
    modT = S("modT", [128, 24, 129], F32)
    modp = S("modp", [128, 3, 8], F32)
    mods = S("mods", [128, 2, 8, NS], F32)
    t_vecs, t_const, t_cT, t_modT, t_modd = Tk("vecs"), Tk("const"), Tk("cT"), Tk("modT"), Tk("modd")
    ps = [es.enter_context(nc.psum_tensor("ps%d" % i, [128, 512], F32)) for i in range(8)]
    t_ps = [Tk("ps%d" % i, psum=True) for i in range(8)]
    psn = [0]
    ps_reserved = set()

    def next_ps():
        while True:
            i = psn[0]
            psn[0] = (i + 1) % 8
            if i not in ps_reserved:
                return ps[i], t_ps[i]

    def V(name, i0=0, n=None):
        off, w = VLAY[name]
        if n is None:
            n = w - i0
        return vecs[:, off + i0: off + i0 + n]

    kb.dma("sp", vecs[:, :], vecs_in[:, :], writes=[t_vecs])
    kb.dma("pool", cT_bf[:, :, :], cT_in[:, :, :], writes=[t_cT])
    kb.op("dve", lambda e: e.memset(ones_bf[:, :], 1.0), writes=[t_const])
    kb.op("dve", lambda e: e.tensor_copy(out=ident_bf[:, :], in_=V("ident")), reads=[t_vecs], writes=[t_const])

    t_x = {}

    def xtk(key):
        if key not in t_x:
            t_x[key] = Tk("x%s" % (key,))
        return t_x[key]

    TILES = [(i * TN, TN) for i in range(LP // TN)] + [(LP, NS)]

    def compute_mod(l, ls, after_issue=None):
        wsrc = w_ada_in[l].rearrange("(kc p) n -> p kc n", p=128)
        ws = ExitStack()
        wb = [S("wada%d" % i, [128, KC, 512], BF16, ws) for i in range(6)]
        t_wb = [Tk("wada%d" % i) for i in range(6)]
        for blk in range(6):
            kb.dma("pool", wb[blk][:, :, :], wsrc[:, :, blk * 512:(blk + 1) * 512], writes=[t_wb[blk]])
        if after_issue is not None:
            after_issue()
        for blk in range(6):
            b = blk
            for mm in range(4):
                m = blk * 4 + mm
                p, tp = next_ps()
                for kc in range(KC):
                    kb.op("pe", lambda e, p=p, b=b, mm=mm, kc=kc: e.matmul(
                        p[:, 0:129], lhsT=wb[b][:, kc, mm * 128:(mm + 1) * 128], rhs=cT_bf[:, kc, :],
                        start=(kc == 0), stop=(kc == KC - 1)),
                        reads=[t_wb[b], t_cT], writes=[tp], inc=(kc == KC - 1))
                kb.op("act", lambda e, p=p, m=m: e.activation(
                    out=modT[:, m, :], in_=p[:, 0:129], func=AF.Identity,
                    bias=V("b_ada", l * 24 + m, 1), scale=1.0),
                    reads=[tp, t_vecs], writes=[t_modT])
        gpre = V("g_pre", l * 8, 8)
        gpost = V("g_post", l * 8, 8)
        kb.op("dve", lambda e: e.scalar_tensor_tensor(
            out=modp[:, 0, :], in0=modT[:, 8:16, 128], scalar=1.0, in1=gpre, op0=ALU.add, op1=ALU.mult),
            reads=[t_modT, t_vecs], writes=[t_modd])
        kb.op("dve", lambda e: e.tensor_copy(out=modp[:, 1, :], in_=modT[:, 0:8, 128]),
              reads=[t_modT], writes=[t_modd])
        kb.op("dve", lambda e: e.tensor_tensor(
            out=modp[:, 2, :], in0=modT[:, 16:24, 128], in1=gpost, op=ALU.mult),
            reads=[t_modT, t_vecs], writes=[t_modd])
        kb.op("dve", lambda e: e.scalar_tensor_tensor(
            out=mods[:, 0, :, :], in0=modT[:, 8:16, 0:NS], scalar=1.0,
            in1=gpre.unsqueeze(2).broadcast_to([128, 8, NS]), op0=ALU.add, op1=ALU.mult),
            reads=[t_modT, t_vecs], writes=[t_modd])
        kb.op("dve", lambda e: e.tensor_tensor(
            out=mods[:, 1, :, :], in0=modT[:, 16:24, 0:NS],
            in1=gpost.unsqueeze(2).broadcast_to([128, 8, NS]), op=ALU.mult),
            reads=[t_modT, t_vecs], writes=[t_modd])
        kb.barrier_on("pe")
        ws.close()

    class Bufs:
        pass

    def rstd_from_ps(p, tp, n, out, t_out, scale=1.0 / D):
        kb.op("act", lambda e: e.activation(out=out[:, 0:n], in_=p[:, 0:n], func=AF.Sqrt, bias=V("eps"), scale=scale),
              reads=[tp, t_vecs], writes=[t_out])
        kb.op("dve", lambda e: e.reciprocal(out=out[:, 0:n], in_=out[:, 0:n]), reads=[t_out], writes=[t_out])

    def prenorm(B, xt, t_xt, n, sample, hout=None, t_hout=None):
        if hout is None:
            hout, t_hout = B.hT[:, :, 0:n], B.t_hT
        kb.op("act", lambda e: e.activation(out=B.sq[:, :, 0:n], in_=xt[:, :, 0:n], func=AF.Square),
              reads=[t_xt], writes=[B.t_sq])
        p, tp = next_ps()
        for kc in range(KC):
            kb.op("pe", lambda e, kc=kc: e.matmul(p[:, 0:n], lhsT=ones_bf[:, :], rhs=B.sq[:, kc, 0:n],
                                                   start=(kc == 0), stop=(kc == KC - 1)),
                  reads=[B.t_sq, t_const], writes=[tp], inc=(kc == KC - 1))
        rstd_from_ps(p, tp, n, B.rstd, B.t_rstd)
        kb.op("dve", lambda e: e.tensor_tensor(
            out=B.t1[:, :, 0:n], in0=xt[:, :, 0:n],
            in1=B.rstd[:, 0:n].unsqueeze(1).broadcast_to([128, KC, n]), op=ALU.mult),
            reads=[t_xt, B.t_rstd], writes=[B.t_t1])
        if not sample:
            for kc in range(KC):
                kb.op("act", lambda e, kc=kc: e.activation(
                    out=hout[:, kc, :], in_=B.t1[:, kc, 0:n], func=AF.Identity,
                    bias=modp[:, 1, kc:kc + 1], scale=modp[:, 0, kc:kc + 1]),
                    reads=[B.t_t1, t_modd], writes=[t_hout])
        else:
            kb.op("pool", lambda e: e.tensor_tensor(out=B.t1[:, :, 0:n], in0=B.t1[:, :, 0:n],
                                                    in1=mods[:, 0, :, :], op=ALU.mult),
                  reads=[B.t_t1, t_modd], writes=[B.t_t1])
            kb.op("pool", lambda e: e.tensor_tensor(out=hout, in0=B.t1[:, :, 0:n],
                                                    in1=modT[:, 0:8, 0:NS], op=ALU.add),
                  reads=[B.t_t1, t_modT], writes=[t_hout])

    def postnorm_residual(B, xt, t_xt, n, sample, l):
        p, tp = next_ps()
        for kc in range(KC):
            kb.op("pe", lambda e, kc=kc: e.matmul(p[:, 0:n], lhsT=ones_bf[:, :], rhs=B.sq[:, kc, 0:n],
                                                   start=(kc == 0), stop=(kc == KC - 1)),
                  reads=[B.t_sq, t_const], writes=[tp], inc=(kc == KC - 1))
        rstd_from_ps(p, tp, n, B.rstd, B.t_rstd)
        kb.op("dve", lambda e: e.tensor_tensor(
            out=B.oT[:, :, 0:n], in0=B.oT[:, :, 0:n],
            in1=B.rstd[:, 0:n].unsqueeze(1).broadcast_to([128, KC, n]), op=ALU.mult),
            reads=[B.t_oT, B.t_rstd], writes=[B.t_oT])
        if not sample:
            for kc in range(KC):
                kb.op("dve", lambda e, kc=kc: e.scalar_tensor_tensor(
                    out=xt[:, kc, 0:n], in0=B.oT[:, kc, 0:n], scalar=modp[:, 2, kc:kc + 1],
                    in1=xt[:, kc, 0:n], op0=ALU.mult, op1=ALU.add),
                    reads=[B.t_oT, t_modd, t_xt], writes=[t_xt])
        else:
            kb.op("pool", lambda e: e.tensor_tensor(out=B.oT[:, :, 0:n], in0=B.oT[:, :, 0:n],
                                                    in1=mods[:, 1, :, :], op=ALU.mult),
                  reads=[B.t_oT, t_modd], writes=[B.t_oT])
            kb.op("pool", lambda e: e.tensor_tensor(out=xt[:, :, 0:n], in0=xt[:, :, 0:n],
                                                    in1=B.oT[:, :, 0:n], op=ALU.add),
                  reads=[B.t_oT, t_xt], writes=[t_xt])

    def common_bufs(ls, with_hT=True, nxt=1):
        B = Bufs()
        B.xt = [S("xt%d" % i, [128, KC, TN], F32, ls) for i in range(nxt)]
        B.t_xt = [Tk("xt%d" % i) for i in range(nxt)]
        B.sq = S("sq", [128, KC, TN], BF16, ls)
        B.t_sq = Tk("sq")
        B.rstd = S("rstd", [128, TN], F32, ls)
        B.t_rstd = Tk("rstd")
        B.oT = S("oT", [128, KC, TN], F32, ls)
        B.t_oT = Tk("oT")
        B.t1 = B.oT
        B.t_t1 = B.t_oT
        if with_hT:
            B.hT = S("hT", [128, KC, TN], BF16, ls)
            B.t_hT = Tk("hT")
        return B

    def load_w(ls, name, src2d, ncols, blk=512, issue=True):
        w = S(name, [128, KC, ncols], BF16, ls)
        src = src2d.rearrange("(kc p) n -> p kc n", p=128)
        tks = [Tk("%s_%d" % (name, b0)) for b0 in range(0, ncols, blk)]

        def do_issue():
            for i, b0 in enumerate(range(0, ncols, blk)):
                kb.dma("pool", w[:, :, b0:b0 + blk], src[:, :, b0:b0 + blk], writes=[tks[i]])
        if issue:
            do_issue()
            return w, tks, blk
        return w, tks, blk, do_issue

    def x_src(l):
        return xT_in if l == 0 else xs

    def x_dst(l):
        return yT_out if l == NLAYERS - 1 else xs

    def layer_conv(l, jl):
        ls = ExitStack()
        w_in, t_win, wblk, iss1 = load_w(ls, "w_in", w_conv_in_in[jl], 3 * D, issue=False)
        w_out, t_wout, _, iss2 = load_w(ls, "w_out", w_conv_out_in[jl], D, issue=False)
        compute_mod(l, ls, lambda: (iss1(), iss2()))
        B = common_bufs(ls, with_hT=False, nxt=2)
        SN = 512
        ub = [S("ub%d" % i, [128, KC, 32 + SN], BF16, ls) for i in range(2)]
        hT5 = S("hT5", [128, KC, SN], BF16, ls)
        t_hT5 = Tk("hT5")
        t_ub = [Tk("ub%d" % i) for i in range(2)]
        ues = S("ues", [128, KC, 16, 38], BF16, ls)
        t_ues = Tk("ues")
        sg = S("sg", [128, SN], F32, ls)
        t_sg = Tk("sg")
        sz = S("sz", [128, KC, SN], BF16, ls)
        t_sz = Tk("sz")
        Dg = [S("Dg%d" % i, [128, 31, 128], BF16, ls) for i in range(2)]
        t_Dg = [Tk("Dg%d" % i) for i in range(2)]
        t_DgB = [Tk("DgB%d" % i) for i in range(2)]
        yT = S("yT", [128, KC, TN], F32, ls)
        t_yT = Tk("yT")
        ybf = S("ybf", [128, KC, TN], BF16, ls)
        t_ybf = Tk("ybf")
        mean = S("mean", [128, TN], F32, ls)
        t_mean = Tk("mean")
        var = S("var", [128, TN], F32, ls)
        t_var = Tk("var")
        yg = S("yg", [128, KC, TN], BF16, ls)
        t_yg = Tk("yg")
        u32 = S("u32", [128, KC, NS], F32, ls)
        t_u32 = Tk("u32")
        xh = S("xh", [128, KC, 32], F32, ls)
        t_xh = Tk("xh")
        dcnt = [0]

        def wtk(col0):
            return t_win[col0 // wblk]

        def inproj_u(n, utarget, t_ut, u32cols=None, r3=False, hsrc=None, t_hsrc=None):
            if hsrc is None:
                hsrc, t_hsrc = B.hT, B.t_hT
            def vw(ap):
                return ap.rearrange("p (s i) -> p s i", i=8) if r3 else ap
            for c in range(KC):
                pa, tpa = next_ps()
                pg, tpg = next_ps()
                for kc in range(KC):
                    kb.op("pe", lambda e, kc=kc, c=c, pa=pa: e.matmul(
                        pa[:, 0:n], lhsT=w_in[:, kc, c * 128:(c + 1) * 128], rhs=hsrc[:, kc, 0:n],
                        start=(kc == 0), stop=(kc == KC - 1)),
                        reads=[wtk(c * 128), t_hsrc], writes=[tpa], inc=(kc == KC - 1))
                for kc in range(KC):
                    kb.op("pe", lambda e, kc=kc, c=c, pg=pg: e.matmul(
                        pg[:, 0:n], lhsT=w_in[:, kc, D + c * 128:D + (c + 1) * 128], rhs=hsrc[:, kc, 0:n],
                        start=(kc == 0), stop=(kc == KC - 1)),
                        reads=[wtk(D + c * 128), t_hsrc], writes=[tpg], inc=(kc == KC - 1))
                kb.op("act", lambda e, pg=pg: e.activation(out=sg[:, 0:n], in_=pg[:, 0:n], func=AF.Sigmoid),
                      reads=[tpg], writes=[t_sg])
                kb.op("dve", lambda e, c=c, pa=pa: e.tensor_tensor(out=utarget(c), in0=vw(pa[:, 0:n]),
                                                                   in1=vw(sg[:, 0:n]), op=ALU.mult),
                      reads=[tpa, t_sg], writes=[t_ut])
                if u32cols is not None:
                    c0, nn = u32cols
                    kb.op("dve", lambda e, c=c, pa=pa: e.tensor_tensor(
                        out=u32[:, c, 0:nn], in0=pa[:, c0:c0 + nn], in1=sg[:, c0:c0 + nn], op=ALU.mult),
                        reads=[tpa, t_sg], writes=[t_u32])

        if l == 0:
            kb.dma("sp", xh[:, :, :], xh_in[:, :, :], writes=[t_xh])
            prenorm(B, xh, t_xh, 32, False, hout=hT5[:, :, 0:32], t_hout=t_hT5)
            inproj_u(32, lambda c: ub[0][:, c, 0:32], t_ub[0], hsrc=hT5, t_hsrc=t_hT5)
            kb.op("dve", lambda e: e.tensor_tensor(
                out=ub[0][:, :, 0:32], in0=ub[0][:, :, 0:32],
                in1=V("hflag").unsqueeze(1).broadcast_to([128, KC, 32]), op=ALU.mult),
                reads=[t_ub[0], t_vecs], writes=[t_ub[0]])
        else:
            utl = S("utl", [128, KC, 32], BF16, ls)
            t_utl = Tk("utl")
            gsl = S("gsl", [128, 4, KC * 32], BF16, ls)
            t_gsl = Tk("gsl")
            gxc = nc.dram_tensor("cv_gx%d" % l, [128, KC * 32], BF16)
            ggc = nc.dram_tensor("cv_gg%d" % l, [512, KC * 32], BF16)
            t_gxc, t_ggc = Tk("gxc"), Tk("ggc")
            kb.dma("sp", xh[:, :, :], x_src(l)[:, :, LP - 32:LP], reads=[xtk(LP // TN - 1)], writes=[t_xh])
            prenorm(B, xh, t_xh, 32, False, hout=hT5[:, :, 0:32], t_hout=t_hT5)
            inproj_u(32, lambda c: utl[:, c, :], t_utl, hsrc=hT5, t_hsrc=t_hT5)
            kb.dma("sp", gxc[:, :], utl[:, :, :].rearrange("p c t -> p (c t)"), reads=[t_utl], writes=[t_gxc])
            kb.collective(lambda e: e.collective_compute(
                "AllGather", ALU.bypass, replica_groups=[[0, 1, 2, 3], [4, 5, 6, 7]],
                ins=[gxc[:, :]], outs=[ggc[:, :]]), reads=[t_gxc], writes=[t_ggc])
            kb.dma("sp", gsl[:, :, :], ggc.ap().rearrange("(r p) n -> p r n", p=128), reads=[t_ggc], writes=[t_gsl])
            ubv = ub[0][:, :, 0:32]

            def slot(i):
                return gsl[:, i, :].rearrange("p (c t) -> p c t", t=32)
            kb.op("dve", lambda e: e.tensor_scalar(out=ubv, in0=slot(0), scalar1=V("sel", 1, 1), scalar2=None,
                                                   op0=ALU.mult), reads=[t_gsl, t_vecs], writes=[t_ub[0]])
            for i in (1, 2):
                kb.op("dve", lambda e, i=i: e.scalar_tensor_tensor(
                    out=ubv, in0=slot(i), scalar=V("sel", i + 1, 1), in1=ubv, op0=ALU.mult, op1=ALU.add),
                    reads=[t_gsl, t_vecs, t_ub[0]], writes=[t_ub[0]])

        for kc in range(KC):
            kb.dma("pool", ues[:, kc, :, 0:30], sc_fm_in[:, jl, kc, :, :], writes=[t_ues])
        kb.dma("sp", conv_old_out[jl], sc_old_in[jl])

        NST = LP // SN
        for st in range(NST + 1):
            sample = (st == NST)
            ui = st % 2
            if not sample:
                halves = [(SN * st + TN * hf, TN, 2 * st + hf) for hf in range(SN // TN)]
                nn5 = SN
            else:
                halves = [(LP, NS, len(TILES) - 1)]
                nn5 = NS
            for hi, (c0, n, ti) in enumerate(halves):
                xt, t_xt = B.xt[hi], B.t_xt[hi]
                kb.dma("sp", xt[:, :, 0:n], x_src(l)[:, :, c0:c0 + n], reads=[xtk(ti)], writes=[t_xt])
                prenorm(B, xt, t_xt, n, sample, hout=hT5[:, :, hi * TN:hi * TN + n], t_hout=t_hT5)
            n = nn5
            if not sample:
                last = (st == NST - 1)
                inproj_u(n, lambda c: ub[ui][:, c, 32:32 + n], t_ub[ui],
                         u32cols=((n - 32, 32) if last else None), hsrc=hT5, t_hsrc=t_hT5)
                if last:
                    kb.dma("sp", conv_tail_out[jl], u32[:, :, 0:32], reads=[t_u32])
                if st + 1 < NST:
                    kb.op("pool", lambda e: e.tensor_copy(out=ub[1 - ui][:, :, 0:32], in_=ub[ui][:, :, n:n + 32]),
                          reads=[t_ub[ui]], writes=[t_ub[1 - ui]])
            else:
                inproj_u(n, lambda c: ues[:, c, :, 30:38],
                         t_ues, u32cols=(0, NS), r3=True, hsrc=hT5, t_hsrc=t_hT5)
                kb.dma("sp", conv_new_out[jl], u32[:, :, :], reads=[t_u32])
            for c in range(KC):
                pz, tpz = next_ps()
                for kc in range(KC):
                    kb.op("pe", lambda e, kc=kc, c=c, pz=pz: e.matmul(
                        pz[:, 0:n], lhsT=w_in[:, kc, 2 * D + c * 128:2 * D + (c + 1) * 128], rhs=hT5[:, kc, 0:n],
                        start=(kc == 0), stop=(kc == KC - 1)),
                        reads=[wtk(2 * D + c * 128), t_hT5], writes=[tpz], inc=(kc == KC - 1))
                kb.op("act", lambda e, c=c, pz=pz: e.activation(out=sz[:, c, 0:n], in_=pz[:, 0:n], func=AF.Silu),
                      reads=[tpz], writes=[t_sz])
            for hi, (c0, n, ti) in enumerate(halves):
                h0 = hi * TN
                xt, t_xt = B.xt[hi], B.t_xt[hi]
                tx = xtk(ti)
                for c in range(KC):
                    di = dcnt[0] % 2
                    dcnt[0] += 1
                    wd = V("w_dw", (jl * 8 + c) * 31, 31)
                    NDV = 20
                    kb.op("dve", lambda e, di=di, wd=wd: e.tensor_tensor(
                        out=Dg[di][:, 0:NDV, :], in0=ident_bf[:, :].unsqueeze(1).broadcast_to([128, NDV, 128]),
                        in1=wd[:, 0:NDV].unsqueeze(2).broadcast_to([128, NDV, 128]), op=ALU.mult),
                        reads=[t_const, t_vecs], writes=[t_Dg[di]])
                    kb.op("pool", lambda e, di=di, wd=wd: e.tensor_tensor(
                        out=Dg[di][:, NDV:31, :], in0=ident_bf[:, :].unsqueeze(1).broadcast_to([128, 31 - NDV, 128]),
                        in1=wd[:, NDV:31].unsqueeze(2).broadcast_to([128, 31 - NDV, 128]), op=ALU.mult),
                        reads=[t_const, t_vecs], writes=[t_DgB[di]])
                    py, tpy = next_ps()
                    for k in range(31):
                        if not sample:
                            rhs = ub[ui][:, c, h0 + 2 + k:h0 + 2 + k + n]
                            rt = t_ub[ui]
                            outp = py[:, 0:n]
                        else:
                            rhs = ues[:, c, :, k:k + 8]
                            rt = t_ues
                            outp = py[:, 0:n].rearrange("p (s i) -> p s i", i=8)
                        kb.op("pe", lambda e, k=k, di=di, rhs=rhs, outp=outp: e.matmul(
                            outp, lhsT=Dg[di][:, k, :], rhs=rhs, start=(k == 0), stop=(k == 30)),
                            reads=[t_Dg[di] if k < 20 else t_DgB[di], rt], writes=[tpy], inc=(k == 30))
                    bdw = V("b_dw", jl * 8 + c, 1)
                    kb.op("act", lambda e, c=c, py=py, bdw=bdw: e.activation(
                        out=yT[:, c, 0:n], in_=py[:, 0:n], func=AF.Identity, bias=bdw, scale=1.0),
                        reads=[tpy, t_vecs], writes=[t_yT])
                    kb.op("act", lambda e, c=c, py=py, bdw=bdw: e.activation(
                        out=B.sq[:, c, 0:n], in_=py[:, 0:n], func=AF.Square, bias=bdw, scale=1.0),
                        reads=[tpy, t_vecs], writes=[B.t_sq])
                    kb.op("pool", lambda e, c=c: e.tensor_copy(out=ybf[:, c, 0:n], in_=yT[:, c, 0:n]),
                          reads=[t_yT], writes=[t_ybf])
                p1, tp1 = next_ps()
                p2, tp2 = next_ps()
                for c in range(KC):
                    kb.op("pe", lambda e, c=c: e.matmul(p1[:, 0:n], lhsT=ones_bf[:, :], rhs=ybf[:, c, 0:n],
                                                        start=(c == 0), stop=(c == KC - 1)),
                          reads=[t_ybf, t_const], writes=[tp1], inc=(c == KC - 1))
                for c in range(KC):
                    kb.op("pe", lambda e, c=c: e.matmul(p2[:, 0:n], lhsT=ones_bf[:, :], rhs=B.sq[:, c, 0:n],
                                                        start=(c == 0), stop=(c == KC - 1)),
                          reads=[B.t_sq, t_const], writes=[tp2], inc=(c == KC - 1))
                kb.op("dve", lambda e: e.tensor_scalar(out=mean[:, 0:n], in0=p1[:, 0:n], scalar1=1.0 / D, scalar2=None,
                                                       op0=ALU.mult), reads=[tp1], writes=[t_mean])
                kb.op("dve", lambda e: e.tensor_tensor(out=var[:, 0:n], in0=mean[:, 0:n], in1=mean[:, 0:n], op=ALU.mult),
                      reads=[t_mean], writes=[t_var])
                kb.op("dve", lambda e: e.scalar_tensor_tensor(
                    out=var[:, 0:n], in0=p2[:, 0:n], scalar=1.0 / D, in1=var[:, 0:n], op0=ALU.mult, op1=ALU.subtract),
                    reads=[tp2, t_var], writes=[t_var])
                kb.op("act", lambda e: e.activation(out=var[:, 0:n], in_=var[:, 0:n], func=AF.Sqrt, bias=V("eps"), scale=1.0),
                      reads=[t_var, t_vecs], writes=[t_var])
                kb.op("dve", lambda e: e.reciprocal(out=var[:, 0:n], in_=var[:, 0:n]), reads=[t_var], writes=[t_var])
                kb.op("dve", lambda e: e.tensor_tensor(
                    out=yT[:, :, 0:n], in0=yT[:, :, 0:n],
                    in1=mean[:, 0:n].unsqueeze(1).broadcast_to([128, KC, n]), op=ALU.subtract),
                    reads=[t_yT, t_mean], writes=[t_yT])
                kb.op("dve", lambda e: e.tensor_tensor(
                    out=yT[:, :, 0:n], in0=yT[:, :, 0:n],
                    in1=var[:, 0:n].unsqueeze(1).broadcast_to([128, KC, n]), op=ALU.mult),
                    reads=[t_yT, t_var], writes=[t_yT])
                for c in range(KC):
                    kb.op("act", lambda e, c=c: e.activation(
                        out=ybf[:, c, 0:n], in_=yT[:, c, 0:n], func=AF.Silu,
                        bias=V("b_cln", jl * 8 + c, 1), scale=V("g_cln", jl * 8 + c, 1)),
                        reads=[t_yT, t_vecs], writes=[t_ybf])
                kb.op("pool", lambda e: e.tensor_tensor(out=yg[:, :, 0:n], in0=ybf[:, :, 0:n], in1=sz[:, :, h0:h0 + n],
                                                        op=ALU.mult),
                      reads=[t_ybf, t_sz], writes=[t_yg])
                for m in range(KC):
                    po, tpo = next_ps()
                    for c in range(KC):
                        kb.op("pe", lambda e, c=c, m=m, po=po: e.matmul(
                            po[:, 0:n], lhsT=w_out[:, c, m * 128:(m + 1) * 128], rhs=yg[:, c, 0:n],
                            start=(c == 0), stop=(c == KC - 1)),
                            reads=[t_wout[m * 128 // 512], t_yg], writes=[tpo], inc=(c == KC - 1))
                    kb.op("act", lambda e, m=m, po=po: e.activation(out=B.oT[:, m, 0:n], in_=po[:, 0:n], func=AF.Identity),
                          reads=[tpo], writes=[B.t_oT])
                    kb.op("act", lambda e, m=m, po=po: e.activation(out=B.sq[:, m, 0:n], in_=po[:, 0:n], func=AF.Square),
                          reads=[tpo], writes=[B.t_sq])
                postnorm_residual(B, xt, t_xt, n, sample, l)
                kb.dma("sp", x_dst(l)[:, :, c0:c0 + n], xt[:, :, 0:n], reads=[t_xt], writes=[tx])
        kb.barrier()
        ls.close()


    def layer_gla(l, jl):
        ls = ExitStack()
        w_in, t_win, wblk, iss1 = load_w(ls, "wg_in", w_gla_in_in, 3 * D, issue=False)
        w_out, t_wout, _, iss2 = load_w(ls, "wg_out", w_gla_out_in, D, issue=False)
        compute_mod(l, ls, lambda: (iss1(), iss2()))
        B = common_bufs(ls)
        w_a1 = S("w_a1", [128, KC, 16], BF16, ls)
        t_wa = Tk("w_a")
        kb.dma("pool", w_a1[:, :, :], w_a1_in.rearrange("(kc p) n -> p kc n", p=128), writes=[t_wa])
        w_a2 = S("w_a2", [16, 512], BF16, ls)
        kb.dma("pool", w_a2[:, :], w_a2_in[:, :], writes=[t_wa])
        ba_row = S("ba_row", [1, 512], BF16, ls)
        kb.dma("pool", ba_row[:, :], b_a_in[:, :], writes=[t_wa])
        ones_row = S("ones_row", [1, 128], BF16, ls)
        kb.op("dve", lambda e: e.memset(ones_row[:, :], 1.0), writes=[t_wa])
        gm = S("gm", [128, 2, 3, 128], F32, ls)
        gseg = S("gseg", [128, 2, 2, 16], F32, ls)
        t_gm = Tk("gm")
        kb.dma("sp", gm[:, :, :, :], gmask_in[:, :, :, :], writes=[t_gm])
        kb.dma("sp", gseg[:, :, :, :], gseg_in[:, :, :, :], writes=[t_gm])

        def mk(name, shape, dt):
            return S(name, shape, dt, ls), Tk(name)
        qT, t_qT = mk("qT", [128, 4, TN], F32)
        kT, t_kT = mk("kT", [128, 4, TN], F32)
        rT, t_rT = mk("rT", [128, KC, TN], BF16)
        t1T, t_t1T = mk("t1T", [16, TN], BF16)
        cc2 = [dict(vtok=mk("vtok%d" % i, [128, 1024], BF16), la=mk("la%d" % i, [128, 512], F32),
                    ed=mk("ed%d" % i, [128, 512], F32), Kes=mk("Kes%d" % i, [128, 512], BF16),
                    ebt=mk("ebt%d" % i, [128, 4, 16], F32)) for i in range(2)]
        ccn = [0]
        vtok = t_vtok = la = t_la = ed = t_ed = Kes = t_Kes = ebt = t_ebt = None

        def use_cc():
            nonlocal vtok, t_vtok, la, t_la, ed, t_ed, Kes, t_Kes, ebt, t_ebt
            d = cc2[ccn[0] % 2]
            ccn[0] += 1
            (vtok, t_vtok), (la, t_la), (ed, t_ed), (Kes, t_Kes), (ebt, t_ebt) = (
                d["vtok"], d["la"], d["ed"], d["Kes"], d["ebt"])
        e1, t_e1 = mk("e1", [128, 4, 128], F32)
        e2, t_e2 = mk("e2", [128, 4, 128], F32)
        QeT, t_QeT = mk("QeT", [128, 4, 128], BF16)
        KeT, t_KeT = mk("KeT", [128, 4, 128], BF16)
        attm, t_attm = mk("attm", [128, 4, 128], BF16)
        go, t_go = mk("go", [128, KC, TN], F32)
        gsq, t_gsq = mk("gsq", [128, KC, TN], BF16)
        rsh, t_rsh = mk("rsh", [128, 4, 128], F32)
        Sst, t_S = mk("Sst", [128, 4, 256], F32)
        Sbf, t_Sbf = mk("Sbf", [128, 4, 256], BF16)
        Atot, t_Atot = mk("Atot", [128, 4], F32)
        big, t_big = mk("big16", [128, 4112], F32)
        accb, t_accb = mk("accb", [128, 4, 256], F32)
        S0bf, t_S0bf = mk("S0bf", [128, 4, 4, 256], BF16)
        Vblk, t_Vblk = mk("Vblk", [128, 4, 256], BF16)
        gx = nc.dram_tensor("gla_gx", [128, 1028], F32)
        gg = nc.dram_tensor("gla_gg", [512, 1028], F32)
        t_gx, t_gg = Tk("gx"), Tk("gg")
        DKS = 128.0 ** -0.5

        def wtk(col0):
            return t_win[col0 // wblk]

        def proj_fm(col0, n, evac):
            p, tp = next_ps()
            for kc in range(KC):
                kb.op("pe", lambda e, kc=kc: e.matmul(p[:, 0:n], lhsT=w_in[:, kc, col0:col0 + 128], rhs=B.hT[:, kc, 0:n],
                                                       start=(kc == 0), stop=(kc == KC - 1)),
                      reads=[wtk(col0), B.t_hT], writes=[tp], inc=(kc == KC - 1))
            evac(p, tp)

        def proj_tok(col0, c0):
            p, tp = next_ps()
            for kc in range(KC):
                kb.op("pe", lambda e, kc=kc: e.matmul(p[:, :], lhsT=B.hT[:, kc, c0:c0 + 128], rhs=w_in[:, kc, col0:col0 + 512],
                                                       start=(kc == 0), stop=(kc == KC - 1)),
                      reads=[wtk(col0), B.t_hT], writes=[tp], inc=(kc == KC - 1))
            return p, tp

        def tile_logarank(n):
            p, tp = next_ps()
            for kc in range(KC):
                kb.op("pe", lambda e, kc=kc: e.matmul(p[0:16, 0:n], lhsT=w_a1[:, kc, :], rhs=B.hT[:, kc, 0:n],
                                                       start=(kc == 0), stop=(kc == KC - 1)),
                      reads=[t_wa, B.t_hT], writes=[tp], inc=(kc == KC - 1))
            kb.op("act", lambda e: e.activation(out=t1T[:, 0:n], in_=p[0:16, 0:n], func=AF.Identity),
                  reads=[tp], writes=[t_t1T])

        def chunk_common(c0, mi, nseg):
            use_cc()
            pz, tpz = next_ps()
            kb.op("pe", lambda e: e.matmul(pz[:, :], lhsT=t1T[:, c0:c0 + 128], rhs=w_a2[:, :], start=True, stop=False),
                  reads=[t_t1T, t_wa], writes=[tpz], inc=False)
            kb.op("pe", lambda e: e.matmul(pz[:, :], lhsT=ones_row[:, :], rhs=ba_row[:, :], start=False, stop=True),
                  reads=[t_wa], writes=[tpz])
            kb.op("act", lambda e: e.activation(out=la[:, :], in_=pz[:, :], func=AF.Exp, scale=-1.0),
                  reads=[tpz], writes=[t_la])
            kb.op("act", lambda e: e.activation(out=la[:, :], in_=la[:, :], func=AF.Ln, bias=V("one"), scale=1.0),
                  reads=[t_la, t_vecs], writes=[t_la])
            pd, tpd = next_ps()
            kb.op("pe", lambda e: e.matmul(pd[:, :], lhsT=gm[:, mi, 1, :], rhs=la[:, :], start=True, stop=True),
                  reads=[t_gm, t_la], writes=[tpd])
            kb.op("act", lambda e: e.activation(out=ed[:, :], in_=pd[:, :], func=AF.Exp), reads=[tpd], writes=[t_ed])
            pk, tpk = proj_tok(512, c0)
            kb.op("dve", lambda e: e.tensor_tensor(out=Kes[:, :], in0=pk[:, :], in1=ed[:, :], op=ALU.mult),
                  reads=[tpk, t_ed], writes=[t_Kes])
            for half in range(2):
                pv, tpv = proj_tok(1024 + half * 512, c0)
                kb.op("act", lambda e, half=half, pv=pv: e.activation(out=vtok[:, half * 512:(half + 1) * 512], in_=pv[:, :],
                                                                    func=AF.Identity), reads=[tpv], writes=[t_vtok])
            pb, tpb = next_ps()
            for h in range(4):
                kb.op("pe", lambda e, h=h: e.matmul(pb[:, h * nseg:(h + 1) * nseg], lhsT=la[:, h * 128:(h + 1) * 128],
                                                     rhs=gseg[:, mi, 0, 0:nseg], start=True, stop=True),
                      reads=[t_la, t_gm], writes=[tpb], inc=(h == 3))
            kb.op("act", lambda e: e.activation(out=ebt[:, :, 0:nseg],
                                                in_=pb[:, 0:4 * nseg].rearrange("p (h s) -> p h s", s=nseg), func=AF.Exp),
                  reads=[tpb], writes=[t_ebt])

        def state_update_prompt(with_atot):
            for hp in range(2):
                p, tp = next_ps()
                for hh in range(2):
                    h = hp * 2 + hh
                    kb.op("pe", lambda e, h=h, hh=hh, p=p: e.matmul(
                        p[:, hh * 256:(hh + 1) * 256], lhsT=Kes[:, h * 128:(h + 1) * 128], rhs=vtok[:, h * 256:(h + 1) * 256],
                        start=True, stop=True), reads=[t_Kes, t_vtok], writes=[tp], inc=(hh == 1))
                for hh in range(2):
                    h = hp * 2 + hh
                    kb.op("dve", lambda e, h=h, hh=hh, p=p: e.scalar_tensor_tensor(
                        out=Sst[:, h, :], in0=Sst[:, h, :], scalar=ebt[:, h, 0:1], in1=p[:, hh * 256:(hh + 1) * 256],
                        op0=ALU.mult, op1=ALU.add), reads=[t_S, t_ebt, tp], writes=[t_S])
            if with_atot:
                kb.op("dve", lambda e: e.tensor_tensor(out=Atot[:, :], in0=Atot[:, :], in1=ebt[:, :, 0], op=ALU.mult),
                      reads=[t_Atot, t_ebt], writes=[t_Atot])

        def chunk_full(c0, mi, sample):
            pbc, tpbc = next_ps()
            for h in range(4):
                kb.op("pe", lambda e, h=h: e.matmul(pbc[:, h * 128:(h + 1) * 128], lhsT=la[:, h * 128:(h + 1) * 128],
                                                     rhs=gm[:, mi, 0, :], start=True, stop=True),
                      reads=[t_la, t_gm], writes=[tpbc], inc=(h == 3))
            kb.op("act", lambda e: e.activation(out=e1[:, :, :], in_=pbc[:, :].rearrange("p (h t) -> p h t", t=128),
                                                func=AF.Exp), reads=[tpbc], writes=[t_e1])
            kb.op("act", lambda e: e.activation(out=e2[:, :, :], in_=pbc[:, :].rearrange("p (h t) -> p h t", t=128),
                                                func=AF.Exp, scale=-1.0), reads=[tpbc], writes=[t_e2])
            kb.op("pool", lambda e: e.tensor_tensor(out=QeT[:, :, :], in0=qT[:, :, c0:c0 + 128], in1=e1[:, :, :], op=ALU.mult),
                  reads=[t_qT, t_e1], writes=[t_QeT])
            kb.op("pool", lambda e: e.tensor_tensor(out=KeT[:, :, :], in0=kT[:, :, c0:c0 + 128], in1=e2[:, :, :], op=ALU.mult),
                  reads=[t_kT, t_e2], writes=[t_KeT])
            pat, tpat = next_ps()
            for h in range(4):
                kb.op("pe", lambda e, h=h: e.matmul(pat[:, h * 128:(h + 1) * 128], lhsT=KeT[:, h, :], rhs=QeT[:, h, :],
                                                     start=True, stop=True),
                      reads=[t_KeT, t_QeT], writes=[tpat], inc=(h == 3))
            kb.op("dve", lambda e: e.tensor_tensor(
                out=attm[:, :, :], in0=pat[:, :].rearrange("p (h t) -> p h t", t=128),
                in1=gm[:, mi, 2, :].unsqueeze(1).broadcast_to([128, 4, 128]), op=ALU.mult),
                reads=[tpat, t_gm], writes=[t_attm])
            pos_ = [next_ps(), next_ps()]
            if not sample:
                kb.op("act", lambda e: e.activation(out=Sbf[:, :, :], in_=Sst[:, :, :], func=AF.Identity),
                      reads=[t_S], writes=[t_Sbf])
            for h in range(4):
                po, tpo = pos_[h // 2]
                for vc in range(2):
                    reg = po[:, ((h % 2) * 2 + vc) * 128:((h % 2) * 2 + vc + 1) * 128]
                    kb.op("pe", lambda e, h=h, vc=vc, reg=reg: e.matmul(
                        reg, lhsT=vtok[:, h * 256 + vc * 128:h * 256 + (vc + 1) * 128], rhs=attm[:, h, :],
                        start=(h % 2 == 0 and vc == 0), stop=False), reads=[t_vtok, t_attm], writes=[tpo], inc=False)
                    if not sample:
                        kb.op("pe", lambda e, h=h, vc=vc, reg=reg: e.matmul(
                            reg, lhsT=Sbf[:, h, vc * 128:(vc + 1) * 128], rhs=QeT[:, h, :], start=False, stop=True),
                            reads=[t_Sbf, t_QeT], writes=[tpo], inc=(h % 2 == 1 and vc == 1))
            if sample:
                for g in range(4):
                    kb.dma("pool", S0bf[:, :, :, :], sgla_in[:, 4 * g:4 * g + 4, :, :], writes=[t_S0bf])
                    for sl in range(4):
                        s_ = 4 * g + sl
                        for h in range(4):
                            po, tpo = pos_[h // 2]
                            for vc in range(2):
                                reg = po[:, ((h % 2) * 2 + vc) * 128 + 8 * s_:((h % 2) * 2 + vc) * 128 + 8 * s_ + 8]
                                lastm = (g == 3 and sl == 3 and vc == 1 and h % 2 == 1)
                                kb.op("pe", lambda e, h=h, vc=vc, reg=reg, sl=sl, s_=s_: e.matmul(
                                    reg, lhsT=S0bf[:, sl, h, vc * 128:(vc + 1) * 128], rhs=QeT[:, h, 8 * s_:8 * s_ + 8],
                                    start=False, stop=True), reads=[t_S0bf, t_QeT], writes=[tpo],
                                    inc=(lastm or (sl == 3 and vc == 1 and h == 3)))
            for hp in range(2):
                po, tpo = pos_[hp]
                kb.op("act", lambda e, hp=hp, po=po: e.activation(
                    out=go[:, hp * 4:(hp + 1) * 4, c0:c0 + 128], in_=po[:, :].rearrange("p (c t) -> p c t", t=128),
                    func=AF.Identity), reads=[tpo], writes=[t_go])
                kb.op("act", lambda e, hp=hp, po=po: e.activation(
                    out=gsq[:, hp * 4:(hp + 1) * 4, c0:c0 + 128], in_=po[:, :].rearrange("p (c t) -> p c t", t=128),
                    func=AF.Square), reads=[tpo], writes=[t_gsq])

        def tile_finish(n, xt, t_xt, sample):
            for c0 in range(0, n, 128):
                p, tp = next_ps()
                for h in range(4):
                    for vc in range(2):
                        kb.op("pe", lambda e, h=h, vc=vc: e.matmul(
                            p[:, h * 128:(h + 1) * 128], lhsT=ones_bf[:, :], rhs=gsq[:, h * 2 + vc, c0:c0 + 128],
                            start=(vc == 0), stop=(vc == 1)), reads=[t_gsq, t_const], writes=[tp],
                            inc=(h == 3 and vc == 1))
                kb.op("act", lambda e: e.activation(out=rsh[:, :, :], in_=p[:, :].rearrange("p (h t) -> p h t", t=128),
                                                    func=AF.Sqrt, bias=V("eps"), scale=1.0 / 256), reads=[tp, t_vecs],
                      writes=[t_rsh])
                kb.op("dve", lambda e: e.reciprocal(out=rsh[:, :, :], in_=rsh[:, :, :]), reads=[t_rsh], writes=[t_rsh])
                gv = go[:, :, c0:c0 + 128].rearrange("p (h v) t -> p h v t", v=2)
                kb.op("dve", lambda e, gv=gv: e.tensor_tensor(
                    out=gv, in0=gv, in1=rsh[:, :, :].unsqueeze(2).broadcast_to([128, 4, 2, 128]), op=ALU.mult),
                    reads=[t_go, t_rsh], writes=[t_go])
                for vc in range(2):
                    gvv = go[:, :, c0:c0 + 128].rearrange("p (h v) t -> p h v t", v=2)[:, :, vc, :]
                    rv = rT[:, :, c0:c0 + 128].rearrange("p (h v) t -> p h v t", v=2)[:, :, vc, :]
                    ov = gsq[:, :, c0:c0 + 128].rearrange("p (h v) t -> p h v t", v=2)[:, :, vc, :]
                    kb.op("dve", lambda e, gvv=gvv, rv=rv, ov=ov, vc=vc: e.scalar_tensor_tensor(
                        out=ov, in0=gvv, scalar=V("g_gn", vc, 1), in1=rv, op0=ALU.mult, op1=ALU.mult),
                        reads=[t_go, t_rT, t_vecs, t_gsq], writes=[t_gsq])
            for m in range(KC):
                po, tpo = next_ps()
                for c in range(KC):
                    kb.op("pe", lambda e, c=c, m=m, po=po: e.matmul(
                        po[:, 0:n], lhsT=w_out[:, c, m * 128:(m + 1) * 128], rhs=gsq[:, c, 0:n],
                        start=(c == 0), stop=(c == KC - 1)),
                        reads=[t_wout[m * 128 // 512], t_gsq], writes=[tpo], inc=(c == KC - 1))
                kb.op("act", lambda e, m=m, po=po: e.activation(out=B.oT[:, m, 0:n], in_=po[:, 0:n], func=AF.Identity),
                      reads=[tpo], writes=[B.t_oT])
                kb.op("act", lambda e, m=m, po=po: e.activation(out=B.sq[:, m, 0:n], in_=po[:, 0:n], func=AF.Square),
                      reads=[tpo], writes=[B.t_sq])
            postnorm_residual(B, xt, t_xt, n, sample, l)

        xt, t_xt = B.xt[0], B.t_xt[0]
        kb.op("dve", lambda e: e.memset(Sst[:, :, :], 0.0), writes=[t_S])
        kb.op("dve", lambda e: e.memset(Atot[:, :], 1.0), writes=[t_Atot])
        for ti, (c0t, n) in enumerate(TILES[:-1]):
            kb.dma("sp", xt[:, :, 0:n], x_src(l)[:, :, c0t:c0t + n], reads=[xtk(ti)], writes=[t_xt])
            prenorm(B, xt, t_xt, n, False)
            tile_logarank(n)
            for c0 in range(0, n, 128):
                chunk_common(c0, 0, 1)
                state_update_prompt(True)
        kb.dma("sp", gx[:, 0:1024], Sst[:, :, :].rearrange("p h v -> p (h v)"), reads=[t_S], writes=[t_gx])
        kb.dma("sp", gx[:, 1024:1028], Atot[:, :], reads=[t_Atot], writes=[t_gx])
        kb.collective(lambda e: e.collective_compute("AllGather", ALU.bypass, replica_groups=[[0, 1, 2, 3], [4, 5, 6, 7]],
                                                     ins=[gx[:, :]], outs=[gg[:, :]]), reads=[t_gx], writes=[t_gg])
        gsb = big[:, 0:4112].rearrange("p (r n) -> p r n", r=4)
        kb.dma("sp", gsb, gg.ap().rearrange("(r p) n -> p r n", p=128), reads=[t_gg], writes=[t_big])

        def Bs(i):
            return gsb[:, i, 0:1024].rearrange("p (h v) -> p h v", h=4)
        kb.op("dve", lambda e: e.tensor_copy(out=accb[:, :, :], in_=Bs(0)), reads=[t_big], writes=[t_accb])
        kb.op("dve", lambda e: e.tensor_scalar(out=Sst[:, :, :], in0=accb[:, :, :], scalar1=V("sel", 1, 1), scalar2=None,
                                               op0=ALU.mult), reads=[t_accb, t_vecs], writes=[t_S])
        for i in (1, 2):
            for h in range(4):
                kb.op("dve", lambda e, i=i, h=h: e.scalar_tensor_tensor(
                    out=accb[:, h, :], in0=accb[:, h, :], scalar=gsb[:, i, 1024 + h:1025 + h], in1=Bs(i)[:, h, :],
                    op0=ALU.mult, op1=ALU.add), reads=[t_accb, t_big], writes=[t_accb])
            kb.op("dve", lambda e, i=i: e.scalar_tensor_tensor(
                out=Sst[:, :, :], in0=accb[:, :, :], scalar=V("sel", i + 1, 1), in1=Sst[:, :, :],
                op0=ALU.mult, op1=ALU.add), reads=[t_accb, t_vecs, t_S], writes=[t_S])
        for ti, (c0t, n) in enumerate(TILES):
            sample = (c0t >= LP)
            mi = 1 if sample else 0
            tx = xtk(ti)
            kb.dma("sp", xt[:, :, 0:n], x_src(l)[:, :, c0t:c0t + n], reads=[tx], writes=[t_xt])
            prenorm(B, xt, t_xt, n, sample)
            tile_logarank(n)
            for h in range(4):
                proj_fm(h * 128, n, lambda p, tp, h=h: kb.op("act", lambda e: e.activation(
                    out=qT[:, h, 0:n], in_=p[:, 0:n], func=AF.Identity, scale=DKS), reads=[tp], writes=[t_qT]))
                proj_fm(512 + h * 128, n, lambda p, tp, h=h: kb.op("act", lambda e: e.activation(
                    out=kT[:, h, 0:n], in_=p[:, 0:n], func=AF.Identity), reads=[tp], writes=[t_kT]))
            for c in range(KC):
                proj_fm(2048 + c * 128, n, lambda p, tp, c=c: kb.op("act", lambda e: e.activation(
                    out=rT[:, c, 0:n], in_=p[:, 0:n], func=AF.Silu), reads=[tp], writes=[t_rT]))
            for c0 in range(0, n, 128):
                chunk_common(c0, mi, 16 if sample else 1)
                chunk_full(c0, mi, sample)
                if not sample:
                    state_update_prompt(False)
                else:
                    S0g = big[:, 0:4096].rearrange("p (s h v) -> p s h v", s=4, h=4)
                    for g in range(4):
                        kb.dma("sp", S0g, sgla_in[:, 4 * g:4 * g + 4, :, :], writes=[t_big])
                        for h in range(4):
                            kb.op("pool", lambda e, h=h, g=g: e.tensor_tensor(
                                out=Vblk[:, :, :], in0=vtok[:, h * 256:(h + 1) * 256].unsqueeze(1).broadcast_to([128, 4, 256]),
                                in1=gseg[:, 1, 1, 4 * g:4 * g + 4].unsqueeze(2).broadcast_to([128, 4, 256]), op=ALU.mult),
                                reads=[t_vtok, t_gm], writes=[t_Vblk])
                            for half in range(2):
                                p, tp = next_ps()
                                kb.op("pe", lambda e, h=h, half=half, p=p: e.matmul(
                                    p[:, :], lhsT=Kes[:, h * 128:(h + 1) * 128],
                                    rhs=Vblk[:, 2 * half:2 * half + 2, :].rearrange("p s v -> p (s v)"),
                                    start=True, stop=True), reads=[t_Kes, t_Vblk], writes=[tp])
                                for sl2 in range(2):
                                    sl = 2 * half + sl2
                                    s_ = 4 * g + sl
                                    kb.op("dve", lambda e, h=h, sl=sl, sl2=sl2, s_=s_, p=p: e.scalar_tensor_tensor(
                                        out=S0g[:, sl, h, :], in0=S0g[:, sl, h, :], scalar=ebt[:, h, s_:s_ + 1],
                                        in1=p[:, sl2 * 256:(sl2 + 1) * 256], op0=ALU.mult, op1=ALU.add),
                                        reads=[t_big, t_ebt, tp], writes=[t_big])
                        kb.dma("sp", gla_s_out[:, 4 * g:4 * g + 4, :, :], S0g, reads=[t_big])
            tile_finish(n, xt, t_xt, sample)
            kb.dma("sp", x_dst(l)[:, :, c0t:c0t + n], xt[:, :, 0:n], reads=[t_xt], writes=[tx])
            if ti == len(TILES) - 2:
                kb.dma("sp", gla_p_out[:, :, :], Sst[:, :, :], reads=[t_S])
        kb.barrier()
        ls.close()


    def layer_att(l):
        ls = ExitStack()
        compute_mod(l, ls)
        B = common_bufs(ls, with_hT=False)
        L = LP
        GD = [(128, 1), (512, 4), (2048, 16)]
        EXT = [dd * 128 + L for (_, dd) in GD]

        def sublen(g):
            return 128 + L // GD[g][1]
        kTd = [nc.dram_tensor("kTd%d" % g, [128, 4, EXT[g]], BF16) for g in range(3)]
        qTd = [nc.dram_tensor("qTd%d" % g, [128, 4, L], BF16) for g in range(3)]
        vd = [nc.dram_tensor("vd%d" % g, [EXT[g], 512], BF16) for g in range(3)]
        ktp = [nc.dram_tensor("ktp%d" % i, [128, 3584], BF16) for i in range(3)]
        vtp = [nc.dram_tensor("vtp%d" % i, [896, 512], BF16) for i in range(3)]
        ggK = [nc.dram_tensor("ggK%d" % i, [512, 3584], BF16) for i in range(3)]
        ggV = [nc.dram_tensor("ggV%d" % i, [3584, 512], BF16) for i in range(3)]
        t_kTd = [Tk("kTd%d" % g, multi=True) for g in range(3)]
        t_qTd = [Tk("qTd%d" % g, multi=True) for g in range(3)]
        t_vd = [Tk("vd%d" % g, multi=True) for g in range(3)]
        t_ktp = [Tk("ktp%d" % i, multi=True) for i in range(3)]
        t_vtp = [Tk("vtp%d" % i, multi=True) for i in range(3)]
        t_ggK = [Tk("ggK%d" % i) for i in range(3)]
        t_ggV = [Tk("ggV%d" % i) for i in range(3)]
        t_outs = Tk("att_outs", multi=True)

        def mk(name, shape, dt, st=None):
            return S(name, shape, dt, st if st is not None else ls), Tk(name)
        zT, t_zT = mk("zT", [128, 4, NT], BF16)
        kTs, t_kTs = mk("kTs", [128, 12, NS], BF16)
        qTs, t_qTs = mk("qTs", [128, 12, NS], BF16)
        vS, t_vS = mk("vS", [128, 1536], BF16)
        sA = ExitStack()
        hTa, t_hTa = mk("hTa", [128, KC, NT], BF16, sA)
        rope, t_rope = mk("rope", [128, 2, NT], F32, sA)
        pmf, t_pmf = mk("pmf", [128, 128], F32, sA)
        pmb, t_pmb = mk("pmb", [128, 128], BF16, sA)
        for ci in range(2):
            for hb_ in range(2):
                kb.dma("sp", rope[:, ci, hb_ * 1088:(hb_ + 1) * 1088], rope_in[:, ci, hb_ * 1088:(hb_ + 1) * 1088],
                       writes=[t_rope])
        kb.dma("sp", pmf[:, :], pm_in[:, :], writes=[t_pmf])
        kb.op("dve", lambda e: e.tensor_copy(out=pmb[:, :], in_=pmf[:, :]), reads=[t_pmf], writes=[t_pmb])

        sa = ExitStack()
        w_in, t_win, wblk = load_w(sa, "wa_qk", w_att_in_in[:, 0:3072], 3072)
        xb, t_xb = mk("xb", [128, 512], BF16, sa)
        t1, t_t1 = mk("ra1", [128, 512], F32, sa)
        t2, t_t2 = mk("ra2", [128, 512], F32, sa)
        kf, t_kf = mk("kf", [128, 512], F32, sa)
        kbf, t_kbf = mk("kbf", [128, 512], BF16, sa)
        xt, t_xt = B.xt[0], B.t_xt[0]

        def wtk(col0):
            return t_win[col0 // wblk]

        for ti, (c0t, n) in enumerate(TILES):
            sample = (c0t >= LP)
            kb.dma("sp", xt[:, :, 0:n], x_src(l)[:, :, c0t:c0t + n], reads=[xtk(ti)], writes=[t_xt])
            prenorm(B, xt, t_xt, n, sample, hout=hTa[:, :, c0t:c0t + n], t_hout=t_hTa)

        def dec2(ap2d, g, u):
            if g == 0:
                return ap2d[:, 512 * u:512 * u + 512]
            if g == 1:
                return ap2d.rearrange("p (n r) -> p r n", r=4)[:, u, :]
            return ap2d.rearrange("p (n r) -> p r n", r=16)[:, 4 * u:4 * u + 4, :]

        def qk_core(col0, rhs_fn, n, cos_ap, sin_ap, vwf):
            p, tp = next_ps()
            for kc in range(KC):
                kb.op("pe", lambda e, kc=kc: e.matmul(vwf(p[:, 0:n]), lhsT=w_in[:, kc, col0:col0 + 128], rhs=rhs_fn(kc),
                                                       start=(kc == 0), stop=(kc == KC - 1)),
                      reads=[wtk(col0), t_hTa], writes=[tp], inc=(kc == KC - 1))
            if QK_STEPS < 2:
                return
            kb.op("act", lambda e: e.activation(out=xb[:, 0:n], in_=p[:, 0:n], func=AF.Identity), reads=[tp], writes=[t_xb])
            if QK_STEPS < 3:
                return
            pr, tpr = next_ps()
            kb.op("pe", lambda e: e.matmul(pr[:, 0:n], lhsT=pmb[:, :], rhs=xb[:, 0:n], start=True, stop=True),
                  reads=[t_pmb, t_xb], writes=[tpr])
            if QK_STEPS < 4:
                return
            kb.op("dve", lambda e: e.tensor_tensor(out=vwf(t1[:, 0:n]), in0=vwf(p[:, 0:n]), in1=cos_ap, op=ALU.mult),
                  reads=[tp, t_rope], writes=[t_t1])
            if QK_STEPS < 5:
                return
            kb.op("dve", lambda e: e.tensor_tensor(out=vwf(t2[:, 0:n]), in0=vwf(pr[:, 0:n]), in1=sin_ap, op=ALU.mult),
                  reads=[tpr, t_rope], writes=[t_t2])
            if QK_STEPS < 6:
                return
            kb.op("pool", lambda e: e.tensor_tensor(out=kf[:, 0:n], in0=t1[:, 0:n], in1=t2[:, 0:n], op=ALU.add),
                  reads=[t_t1, t_t2], writes=[t_kf])
            if QK_STEPS < 7:
                return
            kb.op("act", lambda e: e.activation(out=kbf[:, 0:n], in_=kf[:, 0:n], func=AF.Identity),
                  reads=[t_kf], writes=[t_kbf])

        def tail_col(g, hc, r):
            if g == 0:
                return hc * 128
            if g == 1:
                return 512 + (hc * 4 + r) * 128
            return 2560 + (hc * 16 + r) * 128

        for g in (QK_G if (A_PARTS & 1) else []):
            W, dd = GD[g]
            vwf = (lambda a: a.rearrange("p (r n) -> p r n", r=4)) if g == 2 else (lambda a: a)
            for kind in range(2):
                for u in range(4):
                    for hc in range(4):
                        col0 = kind * 1536 + g * 512 + hc * 128
                        qk_core(col0, lambda kc, g=g, u=u: dec2(hTa[:, kc, 0:L], g, u), 512,
                                dec2(rope[:, 0, 0:L], g, u), dec2(rope[:, 1, 0:L], g, u), vwf)
                        if kind == 0:
                            if QK_DMA & 1:
                                kb.dma("sp", qTd[g][:, hc, 512 * u:512 * u + 512], kbf[:, :], reads=[t_kbf], writes=[t_qTd[g]])
                            continue
                        if not (QK_DMA & 2):
                            continue
                        if g == 0:
                            dst = kTd[g][:, hc, 128 + 512 * u:128 + 512 * u + 512]
                            src = kbf[:, :]
                        elif g == 1:
                            dst = kTd[g][:, hc, u * 640 + 128:u * 640 + 640]
                            src = kbf[:, :]
                        else:
                            dst = kTd[g][:, hc, :].rearrange("p (r e) -> p r e", e=256)[:, 4 * u:4 * u + 4, 128:256]
                            src = kbf[:, :].rearrange("p (r n) -> p r n", r=4)
                        kb.dma("sp", dst, src, reads=[t_kbf], writes=[t_kTd[g]])
                        if not (QK_DMA & 4):
                            continue
                        if g == 0 and u != 3:
                            continue
                        if g == 2:
                            col = tail_col(2, hc, 4 * u)
                            pc, off = col // 3584, col % 3584
                            kb.dma("sp", ktp[pc][:, off:off + 512], kbf[:, :], reads=[t_kbf], writes=[t_ktp[pc]])
                            kb.dma("sp", kout[:, col:col + 512], kf[:, :], reads=[t_kf], writes=[t_outs])
                        else:
                            col = tail_col(g, hc, u if g == 1 else 0)
                            pc, off = col // 3584, col % 3584
                            kb.dma("sp", ktp[pc][:, off:off + 128], kbf[:, 384:512], reads=[t_kbf], writes=[t_ktp[pc]])
                            kb.dma("sp", kout[:, col:col + 128], kf[:, 384:512], reads=[t_kf], writes=[t_outs])
        for g in range(3 if (A_PARTS & 2) else 0):
            for kind in range(2):
                for hc in range(4):
                    col0 = kind * 1536 + g * 512 + hc * 128
                    qk_core(col0, lambda kc: hTa[:, kc, LP:LP + NS], NS, rope[:, 0, LP:LP + NS], rope[:, 1, LP:LP + NS],
                            lambda a: a)
                    if kind == 0:
                        kb.op("pool", lambda e, g=g, hc=hc: e.tensor_copy(out=qTs[:, g * 4 + hc, :], in_=kbf[:, 0:NS]),
                              reads=[t_kbf], writes=[t_qTs])
                    else:
                        kb.op("pool", lambda e, g=g, hc=hc: e.tensor_copy(out=kTs[:, g * 4 + hc, :], in_=kbf[:, 0:NS]),
                              reads=[t_kbf], writes=[t_kTs])
                        kb.dma("sp", ks_out[:, g * 4 + hc, :], kf[:, 0:NS], reads=[t_kf], writes=[t_outs])
        kb.barrier()
        sa.close()
        sa = ExitStack()
        w_in, t_win, wblk = load_w(sa, "wa_vz", w_att_in_in[:, 3072:5120], 2048)
        vb, t_vb = mk("vb", [128, 512], BF16, sa)
        vf, t_vf = mk("vf", [128, 512], F32, sa)
        for ti, (c0t, n) in enumerate(TILES if (A_PARTS & 4) else []):
            for c in range(4):
                p, tp = next_ps()
                for kc in range(KC):
                    kb.op("pe", lambda e, kc=kc, c=c, p=p: e.matmul(
                        p[:, 0:n], lhsT=w_in[:, kc, 1536 + c * 128:1536 + (c + 1) * 128], rhs=hTa[:, kc, c0t:c0t + n],
                        start=(kc == 0), stop=(kc == KC - 1)), reads=[wtk(1536 + c * 128), t_hTa], writes=[tp],
                        inc=(kc == KC - 1))
                kb.op("act", lambda e, c=c, p=p: e.activation(out=zT[:, c, c0t:c0t + n], in_=p[:, 0:n], func=AF.Silu),
                      reads=[tp], writes=[t_zT])
        for g in range(3 if (A_PARTS & 8) else 0):
            W, dd = GD[g]
            nb = L // dd // 128
            for r in range(dd):
                for b in range(nb):
                    def lhs(kc, g=g, r=r, b=b):
                        a = hTa[:, kc, 0:L]
                        if g == 0:
                            return a[:, 128 * b:128 * b + 128]
                        return a.rearrange("p (n r) -> p r n", r=GD[g][1])[:, r, 128 * b:128 * b + 128]
                    p, tp = next_ps()
                    for kc in range(KC):
                        kb.op("pe", lambda e, kc=kc, p=p: e.matmul(
                            p[:, :], lhsT=lhs(kc), rhs=w_in[:, kc, g * 512:(g + 1) * 512],
                            start=(kc == 0), stop=(kc == KC - 1)), reads=[wtk(g * 512), t_hTa], writes=[tp],
                            inc=(kc == KC - 1))
                    kb.op("act", lambda e, p=p: e.activation(out=vb[:, :], in_=p[:, :], func=AF.Identity),
                          reads=[tp], writes=[t_vb])
                    e0 = r * sublen(g) + 128 + 128 * b
                    kb.dma("sp", vd[g][e0:e0 + 128, :], vb[:, :], reads=[t_vb], writes=[t_vd[g]])
                    if b == nb - 1:
                        row = (0, 128, 640)[g] + r * 128
                        pc, off = row // 896, row % 896
                        kb.dma("sp", vtp[pc][off:off + 128, :], vb[:, :], reads=[t_vb], writes=[t_vtp[pc]])
                        kb.op("dve", lambda e, p=p: e.tensor_copy(out=vf[:, :], in_=p[:, :]), reads=[tp], writes=[t_vf])
                        kb.dma("sp", vout[row:row + 128, :], vf[:, :], reads=[t_vf], writes=[t_outs])
            p, tp = next_ps()
            for kc in range(KC):
                kb.op("pe", lambda e, kc=kc, p=p: e.matmul(
                    p[:, :], lhsT=hTa[:, kc, LP:LP + NS], rhs=w_in[:, kc, g * 512:(g + 1) * 512],
                    start=(kc == 0), stop=(kc == KC - 1)), reads=[wtk(g * 512), t_hTa], writes=[tp],
                    inc=(kc == KC - 1))
            kb.op("act", lambda e, p=p, g=g: e.activation(out=vS[:, g * 512:(g + 1) * 512], in_=p[:, :], func=AF.Identity),
                  reads=[tp], writes=[t_vS])
            kb.op("dve", lambda e, p=p: e.tensor_copy(out=vf[:, :], in_=p[:, :]), reads=[tp], writes=[t_vf])
            kb.dma("sp", vs_out[:, g * 512:(g + 1) * 512], vf[:, :], reads=[t_vf], writes=[t_outs])
        kb.barrier()
        sa.close()
        sA.close()

        sb2 = ExitStack()
        w_out = S("wa_out", [128, 4, D], BF16, sb2)
        t_wo = Tk("wa_out")
        kb.dma("pool", w_out[:, :, :], w_att_out_in.rearrange("(c p) n -> p c n", p=128), writes=[t_wo])
        nacc, t_nacc = mk("nacc", [128, 4, L], F32, sb2)
        dacc, t_dacc = mk("dacc", [128, 4, L], F32, sb2)
        naccS, t_naccS = mk("naccS", [128, 4, NS], F32, sb2)
        daccS, t_daccS = mk("daccS", [128, 4, NS], F32, sb2)
        amask, t_am = mk("amask", [128, 3, 512], BF16, sb2)
        smask, t_sm = mk("smask", [128, 13, 64], BF16, sb2)
        snew, t_sn = mk("snew", [128, 3, 512], BF16, sb2)
        kb.dma("pool", amask[:, :, :], amask_in[:, :, :], writes=[t_am])
        kb.dma("pool", smask[:, :, :], smask_in[:, :, :], writes=[t_sm])
        kb.dma("pool", snew[:, :, :], snew_in[:, :, :], writes=[t_sn])
        PT, t_PT = mk("PT", [128, 2, 8, 128], BF16, sb2)
        kt, t_kt = mk("kt", [128, 4, 256], BF16, sb2)
        vt, t_vt = mk("vt", [128, 2, 512], BF16, sb2)
        quA, t_qu = mk("quA", [128, 4, 512], BF16, sb2)
        quB, _ = mk("quB", [128, 4, 512], BF16, sb2)
        qsA, t_qs = mk("qsA", [128, 12, NS], BF16, sb2)
        qsB, _ = mk("qsB", [128, 12, NS], BF16, sb2)
        for zt in (quA, quB, qsA, qsB):
            kb.op("pool", lambda e, zt=zt: e.memset(zt[:, :, :], 0.0), writes=[t_qu, t_qs])
        kb.op("pool", lambda e: e.tensor_copy(out=qsA[0:64, :, :], in_=qTs[0:64, :, :]), reads=[t_qTs], writes=[t_qs])
        kb.op("pool", lambda e: e.tensor_copy(out=qsB[64:128, :, :], in_=qTs[64:128, :, :]), reads=[t_qTs], writes=[t_qs])
        Kt2 = [mk("Kt%d" % i, [128, 512], BF16, sb2) for i in range(2)]
        Vt2 = [mk("Vt%d" % i, [128, 512], BF16, sb2) for i in range(2)]
        kTt2 = [mk("kTt%d" % i, [128, 4, 128], BF16, sb2) for i in range(2)]
        PTs2 = [mk("PTs%d" % i, [128, 64], BF16, sb2) for i in range(2)]
        Kf2 = [mk("Kf%d" % i, [128, 512], F32, sb2) for i in range(2)]
        Vf2 = [mk("Vf%d" % i, [128, 512], F32, sb2) for i in range(2)]
        stile = [0]
        hk, t_hk = mk("hk", [128, 3584], BF16, sb2)
        hv, _ = mk("hv", [128, 2, 512], BF16, sb2)
        t_hvb = [Tk("hv0"), Tk("hv1")]
        idxk, t_idx = mk("idxk", [128, 1], I32, sb2)
        idxv, _ = mk("idxv", [128, 7], I32, sb2)
        og, t_og = mk("og", [128, 4, TN], BF16, sb2)
        ps_reserved.add(7)
        ptb = ps[7][:, :].bitcast(BF16)[:, 0:512]
        t_ptb = t_ps[7]
        kb.dma("sp", idxk[:, :], idxk_in[:, :], writes=[t_idx])
        kb.dma("sp", idxv[:, :], idxv_in[:, :], writes=[t_idx])
        kb.op("dve", lambda e: e.memset(nacc[:, :, :], 0.0), writes=[t_nacc])
        kb.op("dve", lambda e: e.memset(dacc[:, :, :], 0.0), writes=[t_dacc])
        SC = 0.125

        for i in range(3 if _en('X') else 0):
            kb.collective(lambda e, i=i: e.collective_compute(
                "AllGather", ALU.bypass, replica_groups=[[0, 1, 2, 3], [4, 5, 6, 7]],
                ins=[ktp[i][:, :]], outs=[ggK[i][:, :]]), reads=[t_ktp[i]], writes=[t_ggK[i]])
            kb.collective(lambda e, i=i: e.collective_compute(
                "AllGather", ALU.bypass, replica_groups=[[0, 1, 2, 3], [4, 5, 6, 7]],
                ins=[vtp[i][:, :]], outs=[ggV[i][:, :]]), reads=[t_vtp[i]], writes=[t_ggV[i]])

        res = []
        for _ in range(4):
            i = psn[0]
            while i in ps_reserved:
                i = (i + 1) % 8
            ps_reserved.add(i)
            res.append(i)
        pnum = [(ps[res[0]], t_ps[res[0]]), (ps[res[1]], t_ps[res[1]])]
        pden = [(ps[res[2]], t_ps[res[2]]), (ps[res[3]], t_ps[res[3]])]
        started = set()

        def first(i):
            if i in started:
                return False
            started.add(i)
            return True

        def score_bank(maskrhs, t_mask, nn, per_head):
            p, tp = next_ps()
            kb.op("pe", lambda e: e.matmul(p[:, 0:nn], lhsT=ident_bf[:, :], rhs=maskrhs, start=True, stop=False),
                  reads=[t_const, t_mask], writes=[tp], inc=False)
            per_head(p, tp)
            return p, tp

        for g in range(3 if _en('S') else 0):
            for hb in range(2):
                def heads(p, tp, g=g, hb=hb):
                    for hh in range(4):
                        h = 4 * hb + hh
                        hc, pb = h // 2, (h % 2) * 64
                        kb.op("pe", lambda e, hh=hh, hc=hc, pb=pb: e.matmul(
                            p[:, hh * 128:(hh + 1) * 128], lhsT=kTs[:, g * 4 + hc, :],
                            rhs=(qsA if pb == 0 else qsB)[:, g * 4 + hc, :], start=False, stop=True),
                            reads=[t_kTs, t_qs], writes=[tp], inc=(hh == 3))
                p, tp = score_bank(snew[:, g, :], t_sn, 512, heads)
                kb.op("act", lambda e, p=p, hb=hb: e.activation(
                    out=PT[:, 0, 4 * hb:4 * hb + 4, :], in_=p[:, :].rearrange("p (h q) -> p h q", q=128),
                    func=AF.Exp, scale=SC), reads=[tp], writes=[t_PT])
            for hb in range(2):
                pd, tpd = pden[hb]
                kb.op("pe", lambda e, hb=hb, pd=pd: e.matmul(
                    pd[:, :], lhsT=ones_bf[:, :], rhs=PT[:, 0, 4 * hb:4 * hb + 4, :].rearrange("p h q -> p (h q)"),
                    start=first(res[2 + hb]), stop=False), reads=[t_const, t_PT], writes=[tpd])
            for hc in range(4):
                pn, tpn = pnum[hc // 2]
                for ab in range(2):
                    c0 = ((hc % 2) * 2 + ab) * 128
                    kb.op("pe", lambda e, hc=hc, ab=ab, c0=c0, pn=pn, g=g: e.matmul(
                        pn[:, c0:c0 + 128], lhsT=vS[:, g * 512 + hc * 128:g * 512 + (hc + 1) * 128],
                        rhs=PT[:, 0, 2 * hc + ab, :], start=first(res[hc // 2]), stop=False),
                        reads=[t_vS, t_PT], writes=[tpn])
        for s_ in range(16 if _en('S') else 0):
            for g in range(3):
                W, dd = GD[g]
                ntile = (1, 4, 8)[g]
                for r in range(ntile):
                    mt = (0, 1, 5)[g] + r
                    bi = stile[0] % 2
                    stile[0] += 1
                    (Kt, t_Kt), (Vt, t_Vt), (kTt, t_kTt), (PTs, t_PTs) = Kt2[bi], Vt2[bi], kTt2[bi], PTs2[bi]
                    (Kf, t_Kf), (Vf, t_Vf) = Kf2[bi], Vf2[bi]
                    kb.dma("sp", Kf[:, :], ck_in[g][s_, r::dd, :], writes=[t_Kf])
                    kb.dma("sp", Vf[:, :], cv_in[g][s_, r::dd, :], writes=[t_Vf])
                    kb.op("act", lambda e: e.activation(out=Kt[:, :], in_=Kf[:, :], func=AF.Identity),
                          reads=[t_Kf], writes=[t_Kt])
                    kb.op("act", lambda e: e.activation(out=Vt[:, :], in_=Vf[:, :], func=AF.Identity),
                          reads=[t_Vf], writes=[t_Vt])
                    for hc in range(4):
                        kb.op("pe", lambda e, hc=hc: e.transpose(ptb[:, hc * 128:(hc + 1) * 128],
                                                                  Kt[:, hc * 128:(hc + 1) * 128], ident_bf[:, :]),
                              reads=[t_Kt, t_const], writes=[t_ptb], inc=(hc == 3))
                    kb.op("dve", lambda e: e.tensor_copy(out=kTt[:, :, :], in_=ptb[:, :].rearrange("p (c k) -> p c k", c=4)),
                          reads=[t_ptb], writes=[t_kTt])

                    def heads(p, tp, g=g, s_=s_):
                        for h in range(8):
                            hc, pb = h // 2, (h % 2) * 64
                            kb.op("pe", lambda e, h=h, hc=hc, pb=pb: e.matmul(
                                p[:, h * 8:(h + 1) * 8], lhsT=kTt[:, hc, :],
                                rhs=(qsA if pb == 0 else qsB)[:, g * 4 + hc, 8 * s_:8 * s_ + 8], start=False, stop=True),
                                reads=[t_kTt, t_qs], writes=[tp], inc=(h == 7))
                    p, tp = score_bank(smask[:, mt, :], t_sm, 64, heads)
                    kb.op("act", lambda e, p=p: e.activation(out=PTs[:, :], in_=p[:, 0:64], func=AF.Exp, scale=SC),
                          reads=[tp], writes=[t_PTs])
                    for hb in range(2):
                        pd, tpd = pden[hb]
                        kb.op("pe", lambda e, hb=hb, pd=pd: e.matmul(
                            pd[:, :].rearrange("p (h q) -> p h q", q=128)[:, :, 8 * s_:8 * s_ + 8],
                            lhsT=ones_bf[:, :], rhs=PTs[:, 32 * hb:32 * hb + 32].rearrange("p (h i) -> p h i", i=8),
                            start=False, stop=False), reads=[t_const, t_PTs], writes=[tpd], inc=(hb == 1))
                    for hc in range(4):
                        pn, tpn = pnum[hc // 2]
                        for ab in range(2):
                            c0 = ((hc % 2) * 2 + ab) * 128 + 8 * s_
                            kb.op("pe", lambda e, hc=hc, ab=ab, c0=c0, pn=pn: e.matmul(
                                pn[:, c0:c0 + 8], lhsT=Vt[:, hc * 128:(hc + 1) * 128],
                                rhs=PTs[:, (2 * hc + ab) * 8:(2 * hc + ab) * 8 + 8], start=False, stop=False),
                                reads=[t_Vt, t_PTs], writes=[tpn], inc=(hc == 3 and ab == 1))
        for bk in range(2 if _en('S') else 0):
            pn, tpn = pnum[bk]
            pd, tpd = pden[bk]
            pnv = pn[:, :].rearrange("p (c a q) -> p c a q", c=2, a=2)
            pdv = pd[:, :].rearrange("p (c a q) -> p c a q", c=2, a=2)
            for ab in range(2):
                rows = slice(64 * ab, 64 * ab + 64)
                kb.op("dve", lambda e, rows=rows, ab=ab, pnv=pnv, bk=bk: e.tensor_copy(
                    out=naccS[rows, 2 * bk:2 * bk + 2, :], in_=pnv[rows, :, ab, :]), reads=[tpn], writes=[t_naccS])
                kb.op("dve", lambda e, rows=rows, ab=ab, pdv=pdv, bk=bk: e.tensor_copy(
                    out=daccS[rows, 2 * bk:2 * bk + 2, :], in_=pdv[rows, :, ab, :]), reads=[tpd], writes=[t_daccS])
        for i in res:
            ps_reserved.discard(i)

        for i in range(3 if _en('H') else 0):
            kb.dma_custom("pool", lambda e, i=i: e.indirect_dma_start(
                out=hk[:, :], out_offset=None, in_=ggK[i][:, :],
                in_offset=bass.IndirectOffsetOnAxis(ap=idxk[:, 0:1], axis=0)), reads=[t_ggK[i], t_idx], writes=[t_hk])
            for c128 in range(28):
                col = i * 3584 + c128 * 128
                cb = col // 128
                if cb < 4:
                    g, hc, r = 0, cb, 0
                elif cb < 20:
                    g, hc, r = 1, (cb - 4) // 4, (cb - 4) % 4
                else:
                    g, hc, r = 2, (cb - 20) // 16, (cb - 20) % 16
                e0 = r * sublen(g)
                kb.dma("sp", kTd[g][:, hc, e0:e0 + 128], hk[:, c128 * 128:(c128 + 1) * 128], reads=[t_hk],
                       writes=[t_kTd[g]])
            for t in range(7):
                hb_ = t % 2
                kb.dma_custom("pool", lambda e, i=i, t=t, hb_=hb_: e.indirect_dma_start(
                    out=hv[:, hb_, :], out_offset=None, in_=ggV[i][:, :],
                    in_offset=bass.IndirectOffsetOnAxis(ap=idxv[:, t:t + 1], axis=0)), reads=[t_ggV[i], t_idx],
                    writes=[t_hvb[hb_]])
                T = i * 7 + t
                if T == 0:
                    g, r = 0, 0
                elif T < 5:
                    g, r = 1, T - 1
                else:
                    g, r = 2, T - 5
                e0 = r * sublen(g)
                kb.dma("sp", vd[g][e0:e0 + 128, :], hv[:, hb_, :], reads=[t_hvb[hb_]], writes=[t_vd[g]])

        for g in range(3 if _en('B') else 0):
            W, dd = GD[g]
            nb = L // dd // 128
            for u in range(4):
                kb.dma("sp", quA[0:64, :, :], qTd[g][0:64, :, 512 * u:512 * u + 512], reads=[t_qTd[g]], writes=[t_qu])
                kb.dma("sp", quB[64:128, :, :], qTd[g][64:128, :, 512 * u:512 * u + 512], reads=[t_qTd[g]], writes=[t_qu])
                for k in range(4):
                    if g == 0:
                        r, b = 0, 4 * u + k
                    elif g == 1:
                        r, b = u, k
                    else:
                        r, b = 4 * u + k, 0
                    e0 = r * sublen(g) + 128 * b
                    kb.dma("sp", kt[:, :, :], kTd[g][:, :, e0:e0 + 256], reads=[t_kTd[g]], writes=[t_kt])
                    kb.dma("sp", vt[:, :, :], vd[g][e0:e0 + 256, :].rearrange("(t p) c -> p t c", p=128),
                           reads=[t_vd[g]], writes=[t_vt])
                    if B_STEPS < 2:
                        continue
                    for half in range(2):
                        mi = 2 if half == 1 else (1 if b == 0 else 0)
                        for hb in range(2):
                            def heads(p, tp, hb=hb, half=half, k=k):
                                for hh in range(4):
                                    h = 4 * hb + hh
                                    hc, pb = h // 2, (h % 2) * 64
                                    kb.op("pe", lambda e, hh=hh, hc=hc, pb=pb: e.matmul(
                                        p[:, hh * 128:(hh + 1) * 128], lhsT=kt[:, hc, half * 128:(half + 1) * 128],
                                        rhs=(quA if pb == 0 else quB)[:, hc, k * 128:(k + 1) * 128], start=False, stop=True),
                                        reads=[t_kt, t_qu], writes=[tp], inc=(hh == 3))
                            p, tp = score_bank(amask[:, mi, :], t_am, 512, heads)
                            kb.op("act", lambda e, p=p, hb=hb, half=half: e.activation(
                                out=PT[:, half, 4 * hb:4 * hb + 4, :], in_=p[:, :].rearrange("p (h q) -> p h q", q=128),
                                func=AF.Exp, scale=SC), reads=[tp], writes=[t_PT])
                    if B_STEPS < 3:
                        continue
                    pdl = [next_ps(), next_ps()]
                    for hb in range(2):
                        pd, tpd = pdl[hb]
                        for half in range(2):
                            kb.op("pe", lambda e, hb=hb, half=half, pd=pd: e.matmul(
                                pd[:, :], lhsT=ones_bf[:, :],
                                rhs=PT[:, half, 4 * hb:4 * hb + 4, :].rearrange("p h q -> p (h q)"),
                                start=(half == 0), stop=(half == 1)), reads=[t_const, t_PT], writes=[tpd],
                                inc=(half == 1))
                    if B_STEPS < 4:
                        continue
                    pnl = [next_ps(), next_ps()]
                    for bk in range(2):
                        pn, tpn = pnl[bk]
                        fst = True
                        for hcl in range(2):
                            hc = 2 * bk + hcl
                            for ab in range(2):
                                c0 = (hcl * 2 + ab) * 128
                                for half in range(2):
                                    kb.op("pe", lambda e, hc=hc, ab=ab, c0=c0, half=half, pn=pn, fst=fst: e.matmul(
                                        pn[:, c0:c0 + 128], lhsT=vt[:, half, hc * 128:(hc + 1) * 128],
                                        rhs=PT[:, half, 2 * hc + ab, :], start=fst, stop=(half == 1)),
                                        reads=[t_vt, t_PT], writes=[tpn], inc=(hcl == 1 and ab == 1 and half == 1))
                                    fst = False
                    if B_STEPS < 5:
                        continue
                    def tokv(acc, rows, c2):
                        a = acc[:, c2:c2 + 2, 0:L]
                        if dd > 1:
                            a = a.rearrange("p c (n r) -> p c r n", r=dd)[:, :, r, :]
                        return a[rows, :, 128 * b:128 * b + 128]
                    for bk in range(2):
                        pn, tpn = pnl[bk]
                        pd, tpd = pdl[bk]
                        pnv = pn[:, :].rearrange("p (c a q) -> p c a q", c=2, a=2)
                        pdv = pd[:, :].rearrange("p (c a q) -> p c a q", c=2, a=2)
                        for ab in range(2):
                            rows = slice(64 * ab, 64 * ab + 64)
                            kb.op("dve", lambda e, rows=rows, ab=ab, pnv=pnv, bk=bk: e.tensor_tensor(
                                out=tokv(nacc, rows, 2 * bk), in0=tokv(nacc, rows, 2 * bk), in1=pnv[rows, :, ab, :],
                                op=ALU.add), reads=[tpn, t_nacc], writes=[t_nacc])
                            kb.op("dve", lambda e, rows=rows, ab=ab, pdv=pdv, bk=bk: e.tensor_tensor(
                                out=tokv(dacc, rows, 2 * bk), in0=tokv(dacc, rows, 2 * bk), in1=pdv[rows, :, ab, :],
                                op=ALU.add), reads=[tpd, t_dacc], writes=[t_dacc])

        for ti, (c0t, n) in enumerate(TILES):
            sample = (c0t >= LP)
            tx = xtk(ti)
            kb.dma("sp", xt[:, :, 0:n], x_src(l)[:, :, c0t:c0t + n], reads=[tx], writes=[t_xt])
            if not sample:
                na, da, tna, tda = nacc[:, :, c0t:c0t + n], dacc[:, :, c0t:c0t + n], t_nacc, t_dacc
            else:
                na, da, tna, tda = naccS[:, :, :], daccS[:, :, :], t_naccS, t_daccS
            kb.op("dve", lambda e, da=da: e.reciprocal(out=da, in_=da), reads=[tda], writes=[tda])
            kb.op("dve", lambda e, da=da, na=na: e.tensor_tensor(out=na, in0=na, in1=da, op=ALU.mult),
                  reads=[tda, tna], writes=[tna])
            kb.op("pool", lambda e, na=na: e.tensor_tensor(out=og[:, :, 0:n], in0=na, in1=zT[:, :, c0t:c0t + n], op=ALU.mult),
                  reads=[tna, t_zT], writes=[t_og])
            for m in range(KC):
                po, tpo = next_ps()
                for c in range(4):
                    kb.op("pe", lambda e, c=c, m=m, po=po: e.matmul(
                        po[:, 0:n], lhsT=w_out[:, c, m * 128:(m + 1) * 128], rhs=og[:, c, 0:n],
                        start=(c == 0), stop=(c == 3)), reads=[t_wo, t_og], writes=[tpo], inc=(c == 3))
                kb.op("act", lambda e, m=m, po=po: e.activation(out=B.oT[:, m, 0:n], in_=po[:, 0:n], func=AF.Identity),
                      reads=[tpo], writes=[B.t_oT])
                kb.op("act", lambda e, m=m, po=po: e.activation(out=B.sq[:, m, 0:n], in_=po[:, 0:n], func=AF.Square),
                      reads=[tpo], writes=[B.t_sq])
            postnorm_residual(B, xt, t_xt, n, sample, l)
            kb.dma("sp", x_dst(l)[:, :, c0t:c0t + n], xt[:, :, 0:n], reads=[t_xt], writes=[tx])
        kb.barrier()
        ps_reserved.discard(7)
        sb2.close()
        ls.close()

    layer_conv(0, 0)
    if NLAYERS >= 2:
        layer_gla(1, 0)
    if NLAYERS >= 3:
        layer_att(2)
    if NLAYERS >= 4:
        layer_conv(3, 1)
    kb.final_wait()
    global _KB_DEBUG
    _KB_DEBUG = (dict(kb.cnt), dict(kb.dcnt), kb.ccn)
    es.close()
    return nc


def _prep_inputs(inp):
    f = lambda a: np.ascontiguousarray(np.asarray(a, dtype=np.float32))
    x_prompt, x_sample = f(inp["x_prompt"]), f(inp["x_sample"])
    c_prompt, c_sample = f(inp["c_prompt"]), f(inp["c_sample"])
    state_conv = f(inp["state_conv"])
    shared = {
        "w_ada": f(inp["w_ada"]),
        "w_conv_in": f(inp["w_conv_in"]),
        "w_conv_out": f(inp["w_conv_out"]),
        "w_gla_in": f(inp["w_gla_in"][0]),
        "w_gla_out": f(inp["w_gla_out"][0]),
        "w_a1": f(inp["w_gla_a1"][0]),
        "w_a2": f(inp["w_gla_a2"][0]),
        "b_a": f(inp["b_gla_a"][0]).reshape(1, 512),
        "w_att_in": f(inp["w_att_in"][0]),
        "w_att_out": f(inp["w_att_out"][0]),
        "pm": _att_consts()["pm"],
        "smask": _att_consts()["smask"],
        "snew": _att_consts()["snew"],
        "gmask": _gla_masks()[0],
        "gseg": _gla_masks()[1],
    }
    state_gla = f(inp["state_gla"])
    maps = []
    for c in range(NCORES):
        b, j = c // 4, c % 4
        s0 = 16 * c
        m = dict(shared)
        xT = np.empty((128, KC, NT), np.float32)
        xT[:, :, :LP] = _fm(x_prompt[b, LP * j:LP * (j + 1)]).transpose(0, 2, 1)
        xT[:, :, LP:] = _fm(x_sample[s0:s0 + 16].reshape(NS, D)).transpose(0, 2, 1)
        m["xT"] = xT
        xh = np.zeros((128, KC, 32), np.float32)
        if j > 0:
            xh[:] = _fm(x_prompt[b, LP * j - 32:LP * j]).transpose(0, 2, 1)
        m["xh"] = xh
        cT = np.empty((128, KC, 129), np.float32)
        cT[:, :, :NS] = _fm(np.repeat(c_sample[s0:s0 + 16], 8, axis=0)).transpose(0, 2, 1)
        cT[:, :, NS] = _fm(c_prompt[b])
        m["cT"] = cT
        vecs = np.zeros((128, NV), np.float32)

        def put(name, arr):
            off, w = VLAY[name]
            vecs[:, off:off + w] = np.asarray(arr, np.float32).reshape(128, w)
        put("g_pre", _fm(inp["g_pre"]))
        put("g_post", _fm(inp["g_post"]))
        put("b_ada", _fm(inp["b_ada"]))
        put("b_dw", _fm(inp["b_dw"]))
        put("g_cln", _fm(inp["g_conv_ln"]))
        put("b_cln", _fm(inp["b_conv_ln"]))
        put("w_dw", _fm(inp["w_dw"]).transpose(0, 1, 3, 2))
        put("hflag", np.full((128, 1), 0.0 if j == 0 else 1.0))
        put("eps", np.full((128, 1), EPS))
        put("one", np.full((128, 1), 1.0))
        selv = np.zeros((128, 4), np.float32)
        selv[:, j] = 1.0
        put("sel", selv)
        put("g_gn", _fm(inp["g_gla_norm"][0]))
        put("ident", np.eye(128, dtype=np.float32))
        m["vecs"] = vecs
        sc = state_conv[:, s0:s0 + 16]
        m["sc_fm"] = np.ascontiguousarray(_fm(sc).transpose(0, 1, 4, 2, 3))
        m["sc_old"] = np.ascontiguousarray(sc[:, :, 8:, :])
        ac = _att_consts()
        pos = np.concatenate([LP * j + np.arange(LP), 2048 + (np.arange(NS) % 8)]).astype(np.float64)
        inv = 10000.0 ** (-np.arange(32, dtype=np.float64) / 32)
        dd_ = np.arange(128) % 64
        ang = pos[None, :] * inv[dd_ % 32][:, None]
        sgn = np.where(dd_ < 32, -1.0, 1.0)[:, None]
        m["rope"] = np.stack([np.cos(ang), sgn * np.sin(ang)], axis=1).astype(np.float32)
        am = np.stack([ac["mprev"], ac["mprev"] if j > 0 else np.full((128, 128), NEG, np.float32), ac["mcur"]], axis=1)
        m["amask"] = np.ascontiguousarray(np.tile(am, (1, 1, 4)))
        pred = max(j - 1, 0)
        m["idxk"] = (pred * 128 + np.arange(128, dtype=np.int32)).reshape(128, 1).astype(np.int32)
        m["idxv"] = (pred * 896 + np.arange(7, dtype=np.int32)[None, :] * 128
                     + np.arange(128, dtype=np.int32)[:, None]).astype(np.int32)
        caches_k = (inp["cache_k_g0"], inp["cache_k_g1"], inp["cache_k_g2"])
        caches_v = (inp["cache_v_g0"], inp["cache_v_g1"], inp["cache_v_g2"])
        for g in range(3 if (NLAYERS >= 3 and _en('S')) else 0):
            m["ck%d" % g] = np.ascontiguousarray(f(caches_k[g])[0, s0:s0 + 16].reshape(16, -1, 512))
            m["cv%d" % g] = np.ascontiguousarray(f(caches_v[g])[0, s0:s0 + 16].reshape(16, -1, 512))
        m["sgla"] = np.ascontiguousarray(state_gla[0, s0:s0 + 16].transpose(2, 0, 1, 3))
        maps.append(m)
    return maps


NEG = -30000.0
_AC = {}


def _att_consts():
    if _AC:
        return _AC
    p = np.arange(128)
    _AC["mprev"] = np.where(p[:, None] >= p[None, :], 0.0, NEG).astype(np.float32)
    _AC["mcur"] = np.where(p[:, None] <= p[None, :], 0.0, NEG).astype(np.float32)
    m = np.arange(128)
    partner = np.where((m % 64) < 32, m + 32, m - 32)
    pm = np.zeros((128, 128), np.float32)
    pm[partner, m] = 1.0
    _AC["pm"] = pm
    n = np.arange(128)[:, None]
    i = np.arange(8)[None, :]
    sm = np.zeros((128, 13, 8), bool)
    sm[:, 0] = n >= i
    for r in range(4):
        sm[:, 1 + r] = ((i % 4) == r) & ~((i >= 4) & (n == 0))
    for r in range(8):
        sm[:, 5 + r] = (i == r) & (n >= 0)
    smf = np.where(sm, 0.0, NEG).astype(np.float32)
    _AC["smask"] = np.ascontiguousarray(np.tile(smf, (1, 1, 8)))
    kk = np.arange(128)[:, None]
    qq = np.arange(128)[None, :]
    same = (kk // 8) == (qq // 8)
    ki, qi = kk % 8, qq % 8
    sn = np.stack([same & (ki <= qi), same & ((ki == qi) | (ki == qi - 4)), same & (ki == qi)], axis=1)
    _AC["snew"] = np.ascontiguousarray(np.tile(np.where(sn, 0.0, NEG).astype(np.float32), (1, 1, 4)))
    return _AC


def _gla_masks():
    j = np.arange(128)
    same = (j[:, None] // 8) == (j[None, :] // 8)
    le = j[:, None] <= j[None, :]
    gt = j[:, None] > j[None, :]
    gm = np.zeros((128, 2, 3, 128), np.float32)
    gm[:, 0, 0] = np.where(le, -1.0 / 16, 0.0)
    gm[:, 0, 1] = np.where(gt, -1.0 / 16, 0.0)
    gm[:, 0, 2] = np.where(le, 1.0, 0.0)
    gm[:, 1, 0] = np.where(le & same, -1.0 / 16, 0.0)
    gm[:, 1, 1] = np.where(gt & same, -1.0 / 16, 0.0)
    gm[:, 1, 2] = np.where(le & same, 1.0, 0.0)
    seg = np.zeros((128, 2, 2, 16), np.float32)
    seg[:, 0, 0, 0] = -1.0 / 16
    seg[:, 0, 1, 0] = 1.0
    inseg = (j[:, None] // 8) == np.arange(16)[None, :]
    seg[:, 1, 0] = np.where(inseg, -1.0 / 16, 0.0)
    seg[:, 1, 1] = np.where(inseg, 1.0, 0.0)
    return gm, seg


def _tm(a):
    return np.ascontiguousarray(a.transpose(2, 1, 0).reshape(a.shape[2], -1))


_NC_CACHE = {}


def kernel(**inputs):
    if "nc" not in _NC_CACHE:
        _NC_CACHE["nc"] = build_program()
    nc = _NC_CACHE["nc"]
    maps = _prep_inputs(inputs)
    res = run_bass_kernel_spmd(nc, maps, core_ids=list(range(NCORES)))
    R = res.results
    B, DB, DS = 2, 128, 8
    y_prompt = np.zeros((B, SEQ, D), np.float32)
    y_sample = np.zeros((DB, DS, D), np.float32)
    conv_p = np.zeros((2, B, 30, D), np.float32)
    conv_s = np.zeros((2, DB, 30, D), np.float32)
    gla_p = np.zeros((1, B, 4, 128, 256), np.float32)
    gla_s = np.zeros((1, DB, 4, 128, 256), np.float32)
    kv_p = [np.zeros((1, B, w, 8, 64), np.float32) for w in (128, 128, 512, 512, 2048, 2048)]
    kv_s = [np.zeros((1, DB, DS, 8, 64), np.float32) for _ in range(6)]
    for c in range(NCORES):
        b, j = c // 4, c % 4
        s0 = 16 * c
        r = R[c]
        yt = _tm(r["yT"])
        y_prompt[b, LP * j:LP * (j + 1)] = yt[:LP]
        y_sample[s0:s0 + 16] = yt[LP:].reshape(16, 8, D)
        for jl in range(2):
            if j == 3:
                conv_p[jl, b] = _tm(r["conv_tail"][jl])[2:]
            conv_s[jl, s0:s0 + 16, :22] = r["conv_old"][jl]
            conv_s[jl, s0:s0 + 16, 22:] = _tm(r["conv_new"][jl]).reshape(16, 8, D)
        if j == 3:
            gla_p[0, b] = r["gla_p"].transpose(1, 0, 2)
        gla_s[0, s0:s0 + 16] = r["gla_s"].transpose(1, 2, 0, 3)
        if "kout" in r:
            for g, dd in enumerate((1, 4, 16)):
                kbase = (0, 512, 2560)[g]
                vbase = (0, 128, 640)[g]
                if j == 3:
                    blk = r["kout"][:, kbase:kbase + 4 * dd * 128].reshape(2, 64, 4, dd, 128)
                    kv_p[2 * g][0, b] = blk.transpose(4, 3, 2, 0, 1).reshape(128 * dd, 8, 64)
                    vblk = r["vout"][vbase:vbase + dd * 128].reshape(dd, 128, 512)
                    kv_p[2 * g + 1][0, b] = vblk.transpose(1, 0, 2).reshape(128 * dd, 8, 64)
                ks = r["ks_out"].reshape(2, 64, 3, 4, 16, 8)[:, :, g]
                kv_s[2 * g][0, s0:s0 + 16] = ks.transpose(3, 4, 2, 0, 1).reshape(16, 8, 8, 64)
                kv_s[2 * g + 1][0, s0:s0 + 16] = r["vs_out"].reshape(16, 8, 3, 8, 64)[:, :, g]
    return (y_prompt, y_sample, conv_p, conv_s, gla_p, gla_s, *kv_p, *kv_s)
```

```python
import numpy as np
from contextlib import ExitStack
import concourse.bass as bass
import concourse.mybir as mybir
from concourse.bass_utils import run_bass_kernel_spmd

F32 = mybir.dt.float32
BF16 = mybir.dt.bfloat16
I32 = mybir.dt.int32
AF = mybir.ActivationFunctionType
ALU = mybir.AluOpType
AX = mybir.AxisListType

NCORES = 8
D = 1024
KC = 8
LP = 2048
NS = 128
NT = LP + NS
SEQ = 8192
DEPTH = 4
EPS = 1e-6
NQ = 16
TN = 256
import os
NLAYERS = int(os.environ.get('NLAYERS', '4'))
ATT_EN = os.environ.get('ATT_EN', 'AXSHB')


def _en(x):
    return x in ATT_EN


A_PARTS = int(os.environ.get('A_PARTS', '31'))
QK_G = [int(c) for c in os.environ.get('QK_G', '012')]
QK_DMA = int(os.environ.get('QK_DMA', '7'))
QK_STEPS = int(os.environ.get('QK_STEPS', '9'))
B_STEPS = int(os.environ.get('B_STEPS', '9'))


class Tk:
    __slots__ = ("name", "w", "r", "multi", "wm", "psum")

    def __init__(self, name, multi=False, psum=False):
        self.name = name
        self.psum = psum
        self.w = None
        self.r = {}
        self.multi = multi
        self.wm = {}


class KB:
    def __init__(self, nc, es):
        self.nc = nc
        self.E = {"pe": nc.tensor, "act": nc.scalar, "dve": nc.vector, "pool": nc.gpsimd, "sp": nc.sync}
        self.sem = {k: es.enter_context(nc.semaphore("s_" + k)) for k in self.E}
        self.cnt = {k: 0 for k in self.E}
        self.seen = {k: {} for k in self.E}
        self.pend = {k: [] for k in self.E}
        self.dsem = {q: [es.enter_context(nc.semaphore("d_%s%d" % (q, i))) for i in range(NQ)]
                     for q in ("sp", "pool")}
        self.dcnt = {}
        for q in self.dsem:
            for s in self.dsem[q]:
                self.dcnt[s.name] = 0
        self.dnext = {q: 0 for q in self.dsem}
        self.ccsem = es.enter_context(nc.semaphore("cc_sem"))
        self.ccn = 0
        self.semobj = {self.ccsem.name: self.ccsem}
        for s in self.sem.values():
            self.semobj[s.name] = s
        for q in self.dsem:
            for s in self.dsem[q]:
                self.semobj[s.name] = s

    def _waits(self, e, reads, writes):
        need = {}

        def add(tok):
            if tok is None:
                return
            n, v = tok
            if need.get(n, 0) < v:
                need[n] = v
        own = self.sem[e].name if e in self.sem else None
        for t in reads:
            if t.multi:
                for n, v in t.wm.items():
                    add((n, v))
            else:
                add(t.w)
            if t.psum:
                for n, v in t.r.items():
                    if n != own:
                        add((n, v))
        for t in writes:
            if not t.multi:
                add(t.w)
            for n, v in t.r.items():
                add((n, v))
        for n, v in need.items():
            if self.seen[e].get(n, 0) >= v:
                continue
            if e == "pe" and n == self.sem["pe"].name:
                continue
            self.E[e].wait_ge(self.semobj[n], v)
            self.seen[e][n] = v

    def _record(self, tok, reads, writes):
        n, v = tok
        for t in reads:
            if t.r.get(n, 0) < v:
                t.r[n] = v
        for t in writes:
            if t.multi:
                if t.wm.get(n, 0) < v:
                    t.wm[n] = v
            else:
                t.w = tok
                t.r = {}

    def op(self, e, fn, reads=(), writes=(), inc=True):
        self._waits(e, reads, writes)
        ins = fn(self.E[e])
        if not inc:
            self.pend[e].append((tuple(reads), tuple(writes)))
            return
        self.cnt[e] += 1
        ins.then_inc(self.sem[e], 1)
        tok = (self.sem[e].name, self.cnt[e])
        for (r, w) in self.pend[e]:
            self._record(tok, r, w)
        self.pend[e] = []
        self._record(tok, reads, writes)

    def dma(self, q, out, in_, reads=(), writes=(), **kw):
        i = self.dnext[q]
        self.dnext[q] = (i + 1) % NQ
        s = self.dsem[q][i]
        prev = self.dcnt[s.name]
        if prev and self.seen[q].get(s.name, 0) < prev:
            self.E[q].wait_ge(s, prev)
            self.seen[q][s.name] = prev
        self._waits(q, reads, writes)
        self.E[q].dma_start(out=out, in_=in_, **kw).then_inc(s, 16)
        self.dcnt[s.name] = prev + 16
        self._record((s.name, prev + 16), reads, writes)

    def collective(self, fn, reads=(), writes=()):
        self._waits("pool", reads, writes)
        self.ccn += 1
        fn(self.E["pool"]).then_inc(self.ccsem, 1)
        self._record((self.ccsem.name, self.ccn), reads, writes)

    def dma_custom(self, q, fn, reads=(), writes=()):
        i = self.dnext[q]
        self.dnext[q] = (i + 1) % NQ
        s = self.dsem[q][i]
        prev = self.dcnt[s.name]
        if prev and self.seen[q].get(s.name, 0) < prev:
            self.E[q].wait_ge(s, prev)
            self.seen[q][s.name] = prev
        self._waits(q, reads, writes)
        fn(self.E[q]).then_inc(s, 16)
        self.dcnt[s.name] = prev + 16
        self._record((s.name, prev + 16), reads, writes)

    def barrier_on(self, k):
        n, v = self.sem[k].name, self.cnt[k]
        for e in self.E:
            if e != k and v and self.seen[e].get(n, 0) < v:
                self.E[e].wait_ge(self.sem[k], v)
                self.seen[e][n] = v

    def barrier(self):
        for e in self.E:
            for n, s in self.semobj.items():
                if n in self.dcnt:
                    v = self.dcnt[n]
                elif n == self.ccsem.name:
                    v = self.ccn
                else:
                    k = [kk for kk in self.sem if self.sem[kk].name == n][0]
                    if k == e:
                        continue
                    v = self.cnt[k]
                if v and self.seen[e].get(n, 0) < v:
                    self.E[e].wait_ge(s, v)
                    self.seen[e][n] = v

    def final_wait(self):
        e = "sp"
        for n, v in self.dcnt.items():
            if v and self.seen[e].get(n, 0) < v:
                self.E[e].wait_ge(self.semobj[n], v)
                self.seen[e][n] = v


def _vec_layout():
    lay = {}
    off = 0

    def add(name, n):
        nonlocal off
        lay[name] = (off, n)
        off += n
    add("g_pre", 4 * 8)
    add("g_post", 4 * 8)
    add("b_ada", 4 * 24)
    add("b_dw", 2 * 8)
    add("g_cln", 2 * 8)
    add("b_cln", 2 * 8)
    add("w_dw", 2 * 8 * 31)
    add("hflag", 1)
    add("eps", 1)
    add("one", 1)
    add("sel", 4)
    add("g_gn", 2)
    add("ident", 128)
    return lay, off


VLAY, NV = _vec_layout()


def _fm(v):
    v = np.asarray(v, np.float32)
    n = v.shape[-1] // 128
    r = v.reshape(v.shape[:-1] + (n, 128))
    return np.moveaxis(r, -1, 0)


def build_program():
    nc = bass.Bass("TRN2", target_bir_lowering=False)
    es = ExitStack()
    kb = KB(nc, es)

    def din(name, shape, dt=F32):
        return nc.dram_tensor(name, list(shape), dt, kind="ExternalInput").ap()

    def dout(name, shape, dt=F32):
        return nc.dram_tensor(name, list(shape), dt, kind="ExternalOutput").ap()

    xT_in = din("xT", [128, KC, NT])
    xh_in = din("xh", [128, KC, 32])
    cT_in = din("cT", [128, KC, 129])
    vecs_in = din("vecs", [128, NV])
    w_ada_in = din("w_ada", [DEPTH, D, 3 * D])
    w_conv_in_in = din("w_conv_in", [2, D, 3 * D])
    w_conv_out_in = din("w_conv_out", [2, D, D])
    sc_fm_in = din("sc_fm", [128, 2, KC, 16, 30])
    sc_old_in = din("sc_old", [2, 16, 22, D])
    w_gla_in_in = din("w_gla_in", [D, 3 * D])
    w_gla_out_in = din("w_gla_out", [D, D])
    w_a1_in = din("w_a1", [D, 16])
    w_a2_in = din("w_a2", [16, 512])
    b_a_in = din("b_a", [1, 512])
    gmask_in = din("gmask", [128, 2, 3, 128])
    gseg_in = din("gseg", [128, 2, 2, 16])
    sgla_in = din("sgla", [128, 16, 4, 256])
    w_att_in_in = din("w_att_in", [D, 5120])
    w_att_out_in = din("w_att_out", [512, D])
    rope_in = din("rope", [128, 2, NT])
    pm_in = din("pm", [128, 128])
    amask_in = din("amask", [128, 3, 512])
    smask_in = din("smask", [128, 13, 64])
    snew_in = din("snew", [128, 3, 512])
    idxk_in = din("idxk", [128, 1], I32)
    idxv_in = din("idxv", [128, 7], I32)
    if NLAYERS >= 3 and _en('S'):
        ck_in = [din("ck%d" % g, [16, w, 512]) for g, w in enumerate((128, 512, 2048))]
        cv_in = [din("cv%d" % g, [16, w, 512]) for g, w in enumerate((128, 512, 2048))]
    kout = dout("kout", [128, 10752])
    vout = dout("vout", [2688, 512])
    ks_out = dout("ks_out", [128, 12, NS])
    vs_out = dout("vs_out", [NS, 1536])
    gla_p_out = dout("gla_p", [128, 4, 256])
    gla_s_out = dout("gla_s", [128, 16, 4, 256])
    yT_out = dout("yT", [128, KC, NT])
    conv_tail_out = dout("conv_tail", [2, 128, KC, 32])
    conv_new_out = dout("conv_new", [2, 128, KC, NS])
    conv_old_out = dout("conv_old", [2, 16, 22, D])
    xs = nc.dram_tensor("xs", [128, KC, NT], F32)

    uid = [0]

    def S(name, shape, dt, stack=es):
        uid[0] += 1
        return stack.enter_context(nc.sbuf_tensor("sb_%s_%d" % (name, uid[0]), list(shape), dt))

    vecs = S("vecs_sb", [128, NV], F32)
    ones_bf = S("ones_bf", [128, 128], BF16)
    ident_bf = S("ident_bf", [128, 128], BF16)
    cT_bf = S("cT_bf", [128, KC, 129], BF16)
    modT = S("modT", [128, 24, 129], F32)
    modp = S("modp", [128, 3, 8], F32)
    mods = S("mods", [128, 2, 8, NS], F32)
    t_vecs, t_const, t_cT, t_modT, t_modd = Tk("vecs"), Tk("const"), Tk("cT"), Tk("modT"), Tk("modd")
    ps = [es.enter_context(nc.psum_tensor("ps%d" % i, [128, 512], F32)) for i in range(8)]
    t_ps = [Tk("ps%d" % i, psum=True) for i in range(8)]
    psn = [0]
    ps_reserved = set()

    def next_ps():
        while True:
            i = psn[0]
            psn[0] = (i + 1) % 8
            if i not in ps_reserved:
                return ps[i], t_ps[i]

    def V(name, i0=0, n=None):
        off, w = VLAY[name]
        if n is None:
            n = w - i0
        return vecs[:, off + i0: off + i0 + n]

    kb.dma("sp", vecs[:, :], vecs_in[:, :], writes=[t_vecs])
    kb.dma("pool", cT_bf[:, :, :], cT_in[:, :, :], writes=[t_cT])
    kb.op("dve", lambda e: e.memset(ones_bf[:, :], 1.0), writes=[t_const])
    kb.op("dve", lambda e: e.tensor_copy(out=ident_bf[:, :], in_=V("ident")), reads=[t_vecs], writes=[t_const])

    t_x = {}

    def xtk(key):
        if key not in t_x:
            t_x[key] = Tk("x%s" % (key,))
        return t_x[key]

    TILES = [(i * TN, TN) for i in range(LP // TN)] + [(LP, NS)]

    def compute_mod(l, ls, after_issue=None):
        wsrc = w_ada_in[l].rearrange("(kc p) n -> p kc n", p=128)
        ws = ExitStack()
        wb = [S("wada%d" % i, [128, KC, 512], BF16, ws) for i in range(6)]
        t_wb = [Tk("wada%d" % i) for i in range(6)]
        for blk in range(6):
            kb.dma("pool", wb[blk][:, :, :], wsrc[:, :, blk * 512:(blk + 1) * 512], writes=[t_wb[blk]])
        if after_issue is not None:
            after_issue()
        for blk in range(6):
            b = blk
            for mm in range(4):
                m = blk * 4 + mm
                p, tp = next_ps()
                for kc in range(KC):
                    kb.op("pe", lambda e, p=p, b=b, mm=mm, kc=kc: e.matmul(
                        p[:, 0:129], lhsT=wb[b][:, kc, mm * 128:(mm + 1) * 128], rhs=cT_bf[:, kc, :],
                        start=(kc == 0), stop=(kc == KC - 1)),
                        reads=[t_wb[b], t_cT], writes=[tp], inc=(kc == KC - 1))
                kb.op("act", lambda e, p=p, m=m: e.activation(
                    out=modT[:, m, :], in_=p[:, 0:129], func=AF.Identity,
                    bias=V("b_ada", l * 24 + m, 1), scale=1.0),
                    reads=[tp, t_vecs], writes=[t_modT])
        gpre = V("g_pre", l * 8, 8)
        gpost = V("g_post", l * 8, 8)
        kb.op("dve", lambda e: e.scalar_tensor_tensor(
            out=modp[:, 0, :], in0=modT[:, 8:16, 128], scalar=1.0, in1=gpre, op0=ALU.add, op1=ALU.mult),
            reads=[t_modT, t_vecs], writes=[t_modd])
        kb.op("dve", lambda e: e.tensor_copy(out=modp[:, 1, :], in_=modT[:, 0:8, 128]),
              reads=[t_modT], writes=[t_modd])
        kb.op("dve", lambda e: e.tensor_tensor(
            out=modp[:, 2, :], in0=modT[:, 16:24, 128], in1=gpost, op=ALU.mult),
            reads=[t_modT, t_vecs], writes=[t_modd])
        kb.op("dve", lambda e: e.scalar_tensor_tensor(
            out=mods[:, 0, :, :], in0=modT[:, 8:16, 0:NS], scalar=1.0,
            in1=gpre.unsqueeze(2).broadcast_to([128, 8, NS]), op0=ALU.add, op1=ALU.mult),
            reads=[t_modT, t_vecs], writes=[t_modd])
        kb.op("dve", lambda e: e.tensor_tensor(
            out=mods[:, 1, :, :], in0=modT[:, 16:24, 0:NS],
            in1=gpost.unsqueeze(2).broadcast_to([128, 8, NS]), op=ALU.mult),
            reads=[t_modT, t_vecs], writes=[t_modd])
        kb.barrier_on("pe")
        ws.close()

    class Bufs:
        pass

    def rstd_from_ps(p, tp, n, out, t_out, scale=1.0 / D):
        kb.op("act", lambda e: e.activation(out=out[:, 0:n], in_=p[:, 0:n], func=AF.Sqrt, bias=V("eps"), scale=scale),
              reads=[tp, t_vecs], writes=[t_out])
        kb.op("dve", lambda e: e.reciprocal(out=out[:, 0:n], in_=out[:, 0:n]), reads=[t_out], writes=[t_out])

    def prenorm(B, xt, t_xt, n, sample, hout=None, t_hout=None):
        if hout is None:
            hout, t_hout = B.hT[:, :, 0:n], B.t_hT
        kb.op("act", lambda e: e.activation(out=B.sq[:, :, 0:n], in_=xt[:, :, 0:n], func=AF.Square),
              reads=[t_xt], writes=[B.t_sq])
        p, tp = next_ps()
        for kc in range(KC):
            kb.op("pe", lambda e, kc=kc: e.matmul(p[:, 0:n], lhsT=ones_bf[:, :], rhs=B.sq[:, kc, 0:n],
                                                   start=(kc == 0), stop=(kc == KC - 1)),
                  reads=[B.t_sq, t_const], writes=[tp], inc=(kc == KC - 1))
        rstd_from_ps(p, tp, n, B.rstd, B.t_rstd)
        kb.op("dve", lambda e: e.tensor_tensor(
            out=B.t1[:, :, 0:n], in0=xt[:, :, 0:n],
            in1=B.rstd[:, 0:n].unsqueeze(1).broadcast_to([128, KC, n]), op=ALU.mult),
            reads=[t_xt, B.t_rstd], writes=[B.t_t1])
        if not sample:
            for kc in range(KC):
                kb.op("act", lambda e, kc=kc: e.activation(
                    out=hout[:, kc, :], in_=B.t1[:, kc, 0:n], func=AF.Identity,
                    bias=modp[:, 1, kc:kc + 1], scale=modp[:, 0, kc:kc + 1]),
                    reads=[B.t_t1, t_modd], writes=[t_hout])
        else:
            kb.op("pool", lambda e: e.tensor_tensor(out=B.t1[:, :, 0:n], in0=B.t1[:, :, 0:n],
                                                    in1=mods[:, 0, :, :], op=ALU.mult),
                  reads=[B.t_t1, t_modd], writes=[B.t_t1])
            kb.op("pool", lambda e: e.tensor_tensor(out=hout, in0=B.t1[:, :, 0:n],
                                                    in1=modT[:, 0:8, 0:NS], op=ALU.add),
                  reads=[B.t_t1, t_modT], writes=[t_hout])

    def postnorm_residual(B, xt, t_xt, n, sample, l):
        p, tp = next_ps()
        for kc in range(KC):
            kb.op("pe", lambda e, kc=kc: e.matmul(p[:, 0:n], lhsT=ones_bf[:, :], rhs=B.sq[:, kc, 0:n],
                                                   start=(kc == 0), stop=(kc == KC - 1)),
                  reads=[B.t_sq, t_const], writes=[tp], inc=(kc == KC - 1))
        rstd_from_ps(p, tp, n, B.rstd, B.t_rstd)
        kb.op("dve", lambda e: e.tensor_tensor(
            out=B.oT[:, :, 0:n], in0=B.oT[:, :, 0:n],
            in1=B.rstd[:, 0:n].unsqueeze(1).broadcast_to([128, KC, n]), op=ALU.mult),
            reads=[B.t_oT, B.t_rstd], writes=[B.t_oT])
        if not sample:
            for kc in range(KC):
                kb.op("dve", lambda e, kc=kc: e.scalar_tensor_tensor(
                    out=xt[:, kc, 0:n], in0=B.oT[:, kc, 0:n], scalar=modp[:, 2, kc:kc + 1],
                    in1=xt[:, kc, 0:n], op0=ALU.mult, op1=ALU.add),
                    reads=[B.t_oT, t_modd, t_xt], writes=[t_xt])
        else:
            kb.op("pool", lambda e: e.tensor_tensor(out=B.oT[:, :, 0:n], in0=B.oT[:, :, 0:n],
                                                    in1=mods[:, 1, :, :], op=ALU.mult),
                  reads=[B.t_oT, t_modd], writes=[B.t_oT])
            kb.op("pool", lambda e: e.tensor_tensor(out=xt[:, :, 0:n], in0=xt[:, :, 0:n],
                                                    in1=B.oT[:, :, 0:n], op=ALU.add),
                  reads=[B.t_oT, t_xt], writes=[t_xt])

    def common_bufs(ls, with_hT=True, nxt=1):
        B = Bufs()
        B.xt = [S("xt%d" % i, [128, KC, TN], F32, ls) for i in range(nxt)]
        B.t_xt = [Tk("xt%d" % i) for i in range(nxt)]
        B.sq = S("sq", [128, KC, TN], BF16, ls)
        B.t_sq = Tk("sq")
        B.rstd = S("rstd", [128, TN], F32, ls)
        B.t_rstd = Tk("rstd")
        B.oT = S("oT", [128, KC, TN], F32, ls)
        B.t_oT = Tk("oT")
        B.t1 = B.oT
        B.t_t1 = B.t_oT
        if with_hT:
            B.hT = S("hT", [128, KC, TN], BF16, ls)
            B.t_hT = Tk("hT")
        return B

    def load_w(ls, name, src2d, ncols, blk=512, issue=True):
        w = S(name, [128, KC, ncols], BF16, ls)
        src = src2d.rearrange("(kc p) n -> p kc n", p=128)
        tks = [Tk("%s_%d" % (name, b0)) for b0 in range(0, ncols, blk)]

        def do_issue():
            for i, b0 in enumerate(range(0, ncols, blk)):
                kb.dma("pool", w[:, :, b0:b0 + blk], src[:, :, b0:b0 + blk], writes=[tks[i]])
        if issue:
            do_issue()
            return w, tks, blk
        return w, tks, blk, do_issue

    def x_src(l):
        return xT_in if l == 0 else xs

    def x_dst(l):
        return yT_out if l == NLAYERS - 1 else xs

    def layer_conv(l, jl):
        ls = ExitStack()
        w_in, t_win, wblk, iss1 = load_w(ls, "w_in", w_conv_in_in[jl], 3 * D, issue=False)
        w_out, t_wout, _, iss2 = load_w(ls, "w_out", w_conv_out_in[jl], D, issue=False)
        compute_mod(l, ls, lambda: (iss1(), iss2()))
        B = common_bufs(ls, with_hT=False, nxt=2)
        SN = 512
        ub = [S("ub%d" % i, [128, KC, 32 + SN], BF16, ls) for i in range(2)]
        hT5 = S("hT5", [128, KC, SN], BF16, ls)
        t_hT5 = Tk("hT5")
        t_ub = [Tk("ub%d" % i) for i in range(2)]
        ues = S("ues", [128, KC, 16, 38], BF16, ls)
        t_ues = Tk("ues")
        sg = S("sg", [128, SN], F32, ls)
        t_sg = Tk("sg")
        sz = S("sz", [128, KC, SN], BF16, ls)
        t_sz = Tk("sz")
        NDG = 3
        Dg = [S("Dg%d" % i, [128, 31, 128], BF16, ls) for i in range(NDG)]
        t_Dg = [Tk("Dg%d" % i) for i in range(NDG)]
        t_DgB = [Tk("DgB%d" % i) for i in range(NDG)]
        yT = S("yT", [128, KC, TN], F32, ls)
        t_yT = Tk("yT")
        ybf = S("ybf", [128, KC, TN], BF16, ls)
        t_ybf = Tk("ybf")
        mean = S("mean", [128, TN], F32, ls)
        t_mean = Tk("mean")
        var, t_var = B.rstd, B.t_rstd
        yg, t_yg = ybf, t_ybf
        u32 = S("u32", [128, KC, NS], F32, ls)
        t_u32 = Tk("u32")
        xh = S("xh", [128, KC, 32], F32, ls)
        t_xh = Tk("xh")
        dcnt = [0]

        def wtk(col0):
            return t_win[col0 // wblk]

        def inproj_u(n, utarget, t_ut, u32cols=None, r3=False, hsrc=None, t_hsrc=None):
            if hsrc is None:
                hsrc, t_hsrc = B.hT, B.t_hT
            def vw(ap):
                return ap.rearrange("p (s i) -> p s i", i=8) if r3 else ap
            for c in range(KC):
                pa, tpa = next_ps()
                pg, tpg = next_ps()
                for kc in range(KC):
                    kb.op("pe", lambda e, kc=kc, c=c, pa=pa: e.matmul(
                        pa[:, 0:n], lhsT=w_in[:, kc, c * 128:(c + 1) * 128], rhs=hsrc[:, kc, 0:n],
                        start=(kc == 0), stop=(kc == KC - 1)),
                        reads=[wtk(c * 128), t_hsrc], writes=[tpa], inc=(kc == KC - 1))
                for kc in range(KC):
                    kb.op("pe", lambda e, kc=kc, c=c, pg=pg: e.matmul(
                        pg[:, 0:n], lhsT=w_in[:, kc, D + c * 128:D + (c + 1) * 128], rhs=hsrc[:, kc, 0:n],
                        start=(kc == 0), stop=(kc == KC - 1)),
                        reads=[wtk(D + c * 128), t_hsrc], writes=[tpg], inc=(kc == KC - 1))
                kb.op("act", lambda e, pg=pg: e.activation(out=sg[:, 0:n], in_=pg[:, 0:n], func=AF.Sigmoid),
                      reads=[tpg], writes=[t_sg])
                kb.op("dve", lambda e, c=c, pa=pa: e.tensor_tensor(out=utarget(c), in0=vw(pa[:, 0:n]),
                                                                   in1=vw(sg[:, 0:n]), op=ALU.mult),
                      reads=[tpa, t_sg], writes=[t_ut])
                if u32cols is not None:
                    c0, nn = u32cols
                    kb.op("dve", lambda e, c=c, pa=pa: e.tensor_tensor(
                        out=u32[:, c, 0:nn], in0=pa[:, c0:c0 + nn], in1=sg[:, c0:c0 + nn], op=ALU.mult),
                        reads=[tpa, t_sg], writes=[t_u32])

        if l == 0:
            kb.dma("sp", xh[:, :, :], xh_in[:, :, :], writes=[t_xh])
            prenorm(B, xh, t_xh, 32, False, hout=hT5[:, :, 0:32], t_hout=t_hT5)
            inproj_u(32, lambda c: ub[0][:, c, 0:32], t_ub[0], hsrc=hT5, t_hsrc=t_hT5)
            kb.op("dve", lambda e: e.tensor_tensor(
                out=ub[0][:, :, 0:32], in0=ub[0][:, :, 0:32],
                in1=V("hflag").unsqueeze(1).broadcast_to([128, KC, 32]), op=ALU.mult),
                reads=[t_ub[0], t_vecs], writes=[t_ub[0]])
        else:
            utl = S("utl", [128, KC, 32], BF16, ls)
            t_utl = Tk("utl")
            gsl = S("gsl", [128, 4, KC * 32], BF16, ls)
            t_gsl = Tk("gsl")
            gxc = nc.dram_tensor("cv_gx%d" % l, [128, KC * 32], BF16)
            ggc = nc.dram_tensor("cv_gg%d" % l, [512, KC * 32], BF16)
            t_gxc, t_ggc = Tk("gxc"), Tk("ggc")
            kb.dma("sp", xh[:, :, :], x_src(l)[:, :, LP - 32:LP], reads=[xtk(LP // TN - 1)], writes=[t_xh])
            prenorm(B, xh, t_xh, 32, False, hout=hT5[:, :, 0:32], t_hout=t_hT5)
            inproj_u(32, lambda c: utl[:, c, :], t_utl, hsrc=hT5, t_hsrc=t_hT5)
            kb.dma("sp", gxc[:, :], utl[:, :, :].rearrange("p c t -> p (c t)"), reads=[t_utl], writes=[t_gxc])
            kb.collective(lambda e: e.collective_compute(
                "AllGather", ALU.bypass, replica_groups=[[0, 1, 2, 3], [4, 5, 6, 7]],
                ins=[gxc[:, :]], outs=[ggc[:, :]]), reads=[t_gxc], writes=[t_ggc])
            kb.dma("sp", gsl[:, :, :], ggc.ap().rearrange("(r p) n -> p r n", p=128), reads=[t_ggc], writes=[t_gsl])
            ubv = ub[0][:, :, 0:32]

            def slot(i):
                return gsl[:, i, :].rearrange("p (c t) -> p c t", t=32)
            kb.op("dve", lambda e: e.tensor_scalar(out=ubv, in0=slot(0), scalar1=V("sel", 1, 1), scalar2=None,
                                                   op0=ALU.mult), reads=[t_gsl, t_vecs], writes=[t_ub[0]])
            for i in (1, 2):
                kb.op("dve", lambda e, i=i: e.scalar_tensor_tensor(
                    out=ubv, in0=slot(i), scalar=V("sel", i + 1, 1), in1=ubv, op0=ALU.mult, op1=ALU.add),
                    reads=[t_gsl, t_vecs, t_ub[0]], writes=[t_ub[0]])

        for kc in range(KC):
            kb.dma("pool", ues[:, kc, :, 0:30], sc_fm_in[:, jl, kc, :, :], writes=[t_ues])
        kb.dma("sp", conv_old_out[jl], sc_old_in[jl])

        NST = LP // SN
        for st in range(NST + 1):
            sample = (st == NST)
            ui = st % 2
            if not sample:
                halves = [(SN * st + TN * hf, TN, 2 * st + hf) for hf in range(SN // TN)]
                nn5 = SN
            else:
                halves = [(LP, NS, len(TILES) - 1)]
                nn5 = NS
            for hi, (c0, n, ti) in enumerate(halves):
                xt, t_xt = B.xt[hi], B.t_xt[hi]
                kb.dma("sp", xt[:, :, 0:n], x_src(l)[:, :, c0:c0 + n], reads=[xtk(ti)], writes=[t_xt])
                prenorm(B, xt, t_xt, n, sample, hout=hT5[:, :, hi * TN:hi * TN + n], t_hout=t_hT5)
            n = nn5
            if not sample:
                last = (st == NST - 1)
                inproj_u(n, lambda c: ub[ui][:, c, 32:32 + n], t_ub[ui],
                         u32cols=((n - 32, 32) if last else None), hsrc=hT5, t_hsrc=t_hT5)
                if last:
                    kb.dma("sp", conv_tail_out[jl], u32[:, :, 0:32], reads=[t_u32])
                if st + 1 < NST:
                    kb.op("pool", lambda e: e.tensor_copy(out=ub[1 - ui][:, :, 0:32], in_=ub[ui][:, :, n:n + 32]),
                          reads=[t_ub[ui]], writes=[t_ub[1 - ui]])
            else:
                inproj_u(n, lambda c: ues[:, c, :, 30:38],
                         t_ues, u32cols=(0, NS), r3=True, hsrc=hT5, t_hsrc=t_hT5)
                kb.dma("sp", conv_new_out[jl], u32[:, :, :], reads=[t_u32])
            for c in range(KC):
                pz, tpz = next_ps()
                for kc in range(KC):
                    kb.op("pe", lambda e, kc=kc, c=c, pz=pz: e.matmul(
                        pz[:, 0:n], lhsT=w_in[:, kc, 2 * D + c * 128:2 * D + (c + 1) * 128], rhs=hT5[:, kc, 0:n],
                        start=(kc == 0), stop=(kc == KC - 1)),
                        reads=[wtk(2 * D + c * 128), t_hT5], writes=[tpz], inc=(kc == KC - 1))
                kb.op("act", lambda e, c=c, pz=pz: e.activation(out=sz[:, c, 0:n], in_=pz[:, 0:n], func=AF.Silu),
                      reads=[tpz], writes=[t_sz])
            for hi, (c0, n, ti) in enumerate(halves):
                h0 = hi * TN
                xt, t_xt = B.xt[hi], B.t_xt[hi]
                tx = xtk(ti)
                for c in range(KC):
                    di = dcnt[0] % NDG
                    dcnt[0] += 1
                    wd = V("w_dw", (jl * 8 + c) * 31, 31)
                    NDV = 20
                    kb.op("dve", lambda e, di=di, wd=wd: e.tensor_tensor(
                        out=Dg[di][:, 0:NDV, :], in0=ident_bf[:, :].unsqueeze(1).broadcast_to([128, NDV, 128]),
                        in1=wd[:, 0:NDV].unsqueeze(2).broadcast_to([128, NDV, 128]), op=ALU.mult),
                        reads=[t_const, t_vecs], writes=[t_Dg[di]])
                    kb.op("pool", lambda e, di=di, wd=wd: e.tensor_tensor(
                        out=Dg[di][:, NDV:31, :], in0=ident_bf[:, :].unsqueeze(1).broadcast_to([128, 31 - NDV, 128]),
                        in1=wd[:, NDV:31].unsqueeze(2).broadcast_to([128, 31 - NDV, 128]), op=ALU.mult),
                        reads=[t_const, t_vecs], writes=[t_DgB[di]])
                    py, tpy = next_ps()
                    for k in range(31):
                        if not sample:
                            rhs = ub[ui][:, c, h0 + 2 + k:h0 + 2 + k + n]
                            rt = t_ub[ui]
                            outp = py[:, 0:n]
                        else:
                            rhs = ues[:, c, :, k:k + 8]
                            rt = t_ues
                            outp = py[:, 0:n].rearrange("p (s i) -> p s i", i=8)
                        kb.op("pe", lambda e, k=k, di=di, rhs=rhs, outp=outp: e.matmul(
                            outp, lhsT=Dg[di][:, k, :], rhs=rhs, start=(k == 0), stop=(k == 30)),
                            reads=[t_Dg[di] if k < 20 else t_DgB[di], rt], writes=[tpy], inc=(k == 30))
                    bdw = V("b_dw", jl * 8 + c, 1)
                    kb.op("act", lambda e, c=c, py=py, bdw=bdw: e.activation(
                        out=yT[:, c, 0:n], in_=py[:, 0:n], func=AF.Identity, bias=bdw, scale=1.0),
                        reads=[tpy, t_vecs], writes=[t_yT])
                    kb.op("act", lambda e, c=c, py=py, bdw=bdw: e.activation(
                        out=B.sq[:, c, 0:n], in_=py[:, 0:n], func=AF.Square, bias=bdw, scale=1.0),
                        reads=[tpy, t_vecs], writes=[B.t_sq])
                    kb.op("pool", lambda e, c=c: e.tensor_copy(out=ybf[:, c, 0:n], in_=yT[:, c, 0:n]),
                          reads=[t_yT], writes=[t_ybf])
                p1, tp1 = next_ps()
                p2, tp2 = next_ps()
                for c in range(KC):
                    kb.op("pe", lambda e, c=c: e.matmul(p1[:, 0:n], lhsT=ones_bf[:, :], rhs=ybf[:, c, 0:n],
                                                        start=(c == 0), stop=(c == KC - 1)),
                          reads=[t_ybf, t_const], writes=[tp1], inc=(c == KC - 1))
                for c in range(KC):
                    kb.op("pe", lambda e, c=c: e.matmul(p2[:, 0:n], lhsT=ones_bf[:, :], rhs=B.sq[:, c, 0:n],
                                                        start=(c == 0), stop=(c == KC - 1)),
                          reads=[B.t_sq, t_const], writes=[tp2], inc=(c == KC - 1))
                kb.op("dve", lambda e: e.tensor_scalar(out=mean[:, 0:n], in0=p1[:, 0:n], scalar1=1.0 / D, scalar2=None,
                                                       op0=ALU.mult), reads=[tp1], writes=[t_mean])
                kb.op("dve", lambda e: e.tensor_tensor(out=var[:, 0:n], in0=mean[:, 0:n], in1=mean[:, 0:n], op=ALU.mult),
                      reads=[t_mean], writes=[t_var])
                kb.op("dve", lambda e: e.scalar_tensor_tensor(
                    out=var[:, 0:n], in0=p2[:, 0:n], scalar=1.0 / D, in1=var[:, 0:n], op0=ALU.mult, op1=ALU.subtract),
                    reads=[tp2, t_var], writes=[t_var])
                kb.op("act", lambda e: e.activation(out=var[:, 0:n], in_=var[:, 0:n], func=AF.Sqrt, bias=V("eps"), scale=1.0),
                      reads=[t_var, t_vecs], writes=[t_var])
                kb.op("dve", lambda e: e.reciprocal(out=var[:, 0:n], in_=var[:, 0:n]), reads=[t_var], writes=[t_var])
                kb.op("dve", lambda e: e.tensor_tensor(
                    out=yT[:, :, 0:n], in0=yT[:, :, 0:n],
                    in1=mean[:, 0:n].unsqueeze(1).broadcast_to([128, KC, n]), op=ALU.subtract),
                    reads=[t_yT, t_mean], writes=[t_yT])
                kb.op("dve", lambda e: e.tensor_tensor(
                    out=yT[:, :, 0:n], in0=yT[:, :, 0:n],
                    in1=var[:, 0:n].unsqueeze(1).broadcast_to([128, KC, n]), op=ALU.mult),
                    reads=[t_yT, t_var], writes=[t_yT])
                for c in range(KC):
                    kb.op("act", lambda e, c=c: e.activation(
                        out=ybf[:, c, 0:n], in_=yT[:, c, 0:n], func=AF.Silu,
                        bias=V("b_cln", jl * 8 + c, 1), scale=V("g_cln", jl * 8 + c, 1)),
                        reads=[t_yT, t_vecs], writes=[t_ybf])
                kb.op("pool", lambda e: e.tensor_tensor(out=yg[:, :, 0:n], in0=ybf[:, :, 0:n], in1=sz[:, :, h0:h0 + n],
                                                        op=ALU.mult),
                      reads=[t_ybf, t_sz], writes=[t_yg])
                for m in range(KC):
                    po, tpo = next_ps()
                    for c in range(KC):
                        kb.op("pe", lambda e, c=c, m=m, po=po: e.matmul(
                            po[:, 0:n], lhsT=w_out[:, c, m * 128:(m + 1) * 128], rhs=yg[:, c, 0:n],
                            start=(c == 0), stop=(c == KC - 1)),
                            reads=[t_wout[m * 128 // 512], t_yg], writes=[tpo], inc=(c == KC - 1))
                    kb.op("act", lambda e, m=m, po=po: e.activation(out=B.oT[:, m, 0:n], in_=po[:, 0:n], func=AF.Identity),
                          reads=[tpo], writes=[B.t_oT])
                    kb.op("act", lambda e, m=m, po=po: e.activation(out=B.sq[:, m, 0:n], in_=po[:, 0:n], func=AF.Square),
                          reads=[tpo], writes=[B.t_sq])
                postnorm_residual(B, xt, t_xt, n, sample, l)
                kb.dma("sp", x_dst(l)[:, :, c0:c0 + n], xt[:, :, 0:n], reads=[t_xt], writes=[tx])
        kb.barrier()
        ls.close()


    def layer_gla(l, jl):
        ls = ExitStack()
        w_in, t_win, wblk, iss1 = load_w(ls, "wg_in", w_gla_in_in, 3 * D, issue=False)
        w_out, t_wout, _, iss2 = load_w(ls, "wg_out", w_gla_out_in, D, issue=False)
        compute_mod(l, ls, lambda: (iss1(), iss2()))
        B = common_bufs(ls)
        w_a1 = S("w_a1", [128, KC, 16], BF16, ls)
        t_wa = Tk("w_a")
        kb.dma("pool", w_a1[:, :, :], w_a1_in.rearrange("(kc p) n -> p kc n", p=128), writes=[t_wa])
        w_a2 = S("w_a2", [16, 512], BF16, ls)
        kb.dma("pool", w_a2[:, :], w_a2_in[:, :], writes=[t_wa])
        ba_row = S("ba_row", [1, 512], BF16, ls)
        kb.dma("pool", ba_row[:, :], b_a_in[:, :], writes=[t_wa])
        ones_row = S("ones_row", [1, 128], BF16, ls)
        kb.op("dve", lambda e: e.memset(ones_row[:, :], 1.0), writes=[t_wa])
        gm = S("gm", [128, 2, 3, 128], F32, ls)
        gseg = S("gseg", [128, 2, 2, 16], F32, ls)
        t_gm = Tk("gm")
        kb.dma("sp", gm[:, :, :, :], gmask_in[:, :, :, :], writes=[t_gm])
        kb.dma("sp", gseg[:, :, :, :], gseg_in[:, :, :, :], writes=[t_gm])

        def mk(name, shape, dt):
            return S(name, shape, dt, ls), Tk(name)
        qT, t_qT = mk("qT", [128, 4, TN], F32)
        kT, t_kT = mk("kT", [128, 4, TN], F32)
        rT, t_rT = mk("rT", [128, KC, TN], BF16)
        t1T, t_t1T = mk("t1T", [16, TN], BF16)
        cc2 = [dict(vtok=mk("vtok%d" % i, [128, 1024], BF16), la=mk("la%d" % i, [128, 512], F32),
                    ed=mk("ed%d" % i, [128, 512], F32), Kes=mk("Kes%d" % i, [128, 512], BF16),
                    ebt=mk("ebt%d" % i, [128, 4, 16], F32)) for i in range(2)]
        ccn = [0]
        vtok = t_vtok = la = t_la = ed = t_ed = Kes = t_Kes = ebt = t_ebt = None

        def use_cc():
            nonlocal vtok, t_vtok, la, t_la, ed, t_ed, Kes, t_Kes, ebt, t_ebt
            d = cc2[ccn[0] % 2]
            ccn[0] += 1
            (vtok, t_vtok), (la, t_la), (ed, t_ed), (Kes, t_Kes), (ebt, t_ebt) = (
                d["vtok"], d["la"], d["ed"], d["Kes"], d["ebt"])
        e1, t_e1 = mk("e1", [128, 4, 128], F32)
        e2, t_e2 = mk("e2", [128, 4, 128], F32)
        QeT, t_QeT = mk("QeT", [128, 4, 128], BF16)
        KeT, t_KeT = mk("KeT", [128, 4, 128], BF16)
        attm, t_attm = mk("attm", [128, 4, 128], BF16)
        go, t_go = mk("go", [128, KC, TN], F32)
        gsq, t_gsq = mk("gsq", [128, KC, TN], BF16)
        rsh, t_rsh = mk("rsh", [128, 4, 128], F32)
        Sst, t_S = mk("Sst", [128, 4, 256], F32)
        Sbf, t_Sbf = mk("Sbf", [128, 4, 256], BF16)
        Atot, t_Atot = mk("Atot", [128, 4], F32)
        big, t_big = mk("big16", [128, 4112], F32)
        accb, t_accb = mk("accb", [128, 4, 256], F32)
        S0bf, t_S0bf = mk("S0bf", [128, 4, 4, 256], BF16)
        Vblk, t_Vblk = mk("Vblk", [128, 4, 256], BF16)
        gx = nc.dram_tensor("gla_gx", [128, 1028], F32)
        gg = nc.dram_tensor("gla_gg", [512, 1028], F32)
        t_gx, t_gg = Tk("gx"), Tk("gg")
        DKS = 128.0 ** -0.5

        def wtk(col0):
            return t_win[col0 // wblk]

        def proj_fm(col0, n, evac):
            p, tp = next_ps()
            for kc in range(KC):
                kb.op("pe", lambda e, kc=kc: e.matmul(p[:, 0:n], lhsT=w_in[:, kc, col0:col0 + 128], rhs=B.hT[:, kc, 0:n],
                                                       start=(kc == 0), stop=(kc == KC - 1)),
                      reads=[wtk(col0), B.t_hT], writes=[tp], inc=(kc == KC - 1))
            evac(p, tp)

        def proj_tok(col0, c0):
            p, tp = next_ps()
            for kc in range(KC):
                kb.op("pe", lambda e, kc=kc: e.matmul(p[:, :], lhsT=B.hT[:, kc, c0:c0 + 128], rhs=w_in[:, kc, col0:col0 + 512],
                                                       start=(kc == 0), stop=(kc == KC - 1)),
                      reads=[wtk(col0), B.t_hT], writes=[tp], inc=(kc == KC - 1))
            return p, tp

        def tile_logarank(n):
            p, tp = next_ps()
            for kc in range(KC):
                kb.op("pe", lambda e, kc=kc: e.matmul(p[0:16, 0:n], lhsT=w_a1[:, kc, :], rhs=B.hT[:, kc, 0:n],
                                                       start=(kc == 0), stop=(kc == KC - 1)),
                      reads=[t_wa, B.t_hT], writes=[tp], inc=(kc == KC - 1))
            kb.op("act", lambda e: e.activation(out=t1T[:, 0:n], in_=p[0:16, 0:n], func=AF.Identity),
                  reads=[tp], writes=[t_t1T])

        def chunk_common(c0, mi, nseg):
            use_cc()
            pz, tpz = next_ps()
            kb.op("pe", lambda e: e.matmul(pz[:, :], lhsT=t1T[:, c0:c0 + 128], rhs=w_a2[:, :], start=True, stop=False),
                  reads=[t_t1T, t_wa], writes=[tpz], inc=False)
            kb.op("pe", lambda e: e.matmul(pz[:, :], lhsT=ones_row[:, :], rhs=ba_row[:, :], start=False, stop=True),
                  reads=[t_wa], writes=[tpz])
            kb.op("act", lambda e: e.activation(out=la[:, :], in_=pz[:, :], func=AF.Exp, scale=-1.0),
                  reads=[tpz], writes=[t_la])
            kb.op("act", lambda e: e.activation(out=la[:, :], in_=la[:, :], func=AF.Ln, bias=V("one"), scale=1.0),
                  reads=[t_la, t_vecs], writes=[t_la])
            pd, tpd = next_ps()
            kb.op("pe", lambda e: e.matmul(pd[:, :], lhsT=gm[:, mi, 1, :], rhs=la[:, :], start=True, stop=True),
                  reads=[t_gm, t_la], writes=[tpd])
            kb.op("act", lambda e: e.activation(out=ed[:, :], in_=pd[:, :], func=AF.Exp), reads=[tpd], writes=[t_ed])
            pk, tpk = proj_tok(512, c0)
            kb.op("dve", lambda e: e.tensor_tensor(out=Kes[:, :], in0=pk[:, :], in1=ed[:, :], op=ALU.mult),
                  reads=[tpk, t_ed], writes=[t_Kes])
            for half in range(2):
                pv, tpv = proj_tok(1024 + half * 512, c0)
                kb.op("act", lambda e, half=half, pv=pv: e.activation(out=vtok[:, half * 512:(half + 1) * 512], in_=pv[:, :],
                                                                    func=AF.Identity), reads=[tpv], writes=[t_vtok])
            pb, tpb = next_ps()
            for h in range(4):
                kb.op("pe", lambda e, h=h: e.matmul(pb[:, h * nseg:(h + 1) * nseg], lhsT=la[:, h * 128:(h + 1) * 128],
                                                     rhs=gseg[:, mi, 0, 0:nseg], start=True, stop=True),
                      reads=[t_la, t_gm], writes=[tpb], inc=(h == 3))
            kb.op("act", lambda e: e.activation(out=ebt[:, :, 0:nseg],
                                                in_=pb[:, 0:4 * nseg].rearrange("p (h s) -> p h s", s=nseg), func=AF.Exp),
                  reads=[tpb], writes=[t_ebt])

        def state_update_prompt(with_atot):
            for hp in range(2):
                p, tp = next_ps()
                for hh in range(2):
                    h = hp * 2 + hh
                    kb.op("pe", lambda e, h=h, hh=hh, p=p: e.matmul(
                        p[:, hh * 256:(hh + 1) * 256], lhsT=Kes[:, h * 128:(h + 1) * 128], rhs=vtok[:, h * 256:(h + 1) * 256],
                        start=True, stop=True), reads=[t_Kes, t_vtok], writes=[tp], inc=(hh == 1))
                for hh in range(2):
                    h = hp * 2 + hh
                    kb.op("dve", lambda e, h=h, hh=hh, p=p: e.scalar_tensor_tensor(
                        out=Sst[:, h, :], in0=Sst[:, h, :], scalar=ebt[:, h, 0:1], in1=p[:, hh * 256:(hh + 1) * 256],
                        op0=ALU.mult, op1=ALU.add), reads=[t_S, t_ebt, tp], writes=[t_S])
            if with_atot:
                kb.op("dve", lambda e: e.tensor_tensor(out=Atot[:, :], in0=Atot[:, :], in1=ebt[:, :, 0], op=ALU.mult),
                      reads=[t_Atot, t_ebt], writes=[t_Atot])

        def chunk_full(c0, mi, sample):
            pbc, tpbc = next_ps()
            for h in range(4):
                kb.op("pe", lambda e, h=h: e.matmul(pbc[:, h * 128:(h + 1) * 128], lhsT=la[:, h * 128:(h + 1) * 128],
                                                     rhs=gm[:, mi, 0, :], start=True, stop=True),
                      reads=[t_la, t_gm], writes=[tpbc], inc=(h == 3))
            kb.op("act", lambda e: e.activation(out=e1[:, :, :], in_=pbc[:, :].rearrange("p (h t) -> p h t", t=128),
                                                func=AF.Exp), reads=[tpbc], writes=[t_e1])
            kb.op("act", lambda e: e.activation(out=e2[:, :, :], in_=pbc[:, :].rearrange("p (h t) -> p h t", t=128),
                                                func=AF.Exp, scale=-1.0), reads=[tpbc], writes=[t_e2])
            kb.op("pool", lambda e: e.tensor_tensor(out=QeT[:, :, :], in0=qT[:, :, c0:c0 + 128], in1=e1[:, :, :], op=ALU.mult),
                  reads=[t_qT, t_e1], writes=[t_QeT])
            kb.op("pool", lambda e: e.tensor_tensor(out=KeT[:, :, :], in0=kT[:, :, c0:c0 + 128], in1=e2[:, :, :], op=ALU.mult),
                  reads=[t_kT, t_e2], writes=[t_KeT])
            pat, tpat = next_ps()
            for h in range(4):
                kb.op("pe", lambda e, h=h: e.matmul(pat[:, h * 128:(h + 1) * 128], lhsT=KeT[:, h, :], rhs=QeT[:, h, :],
                                                     start=True, stop=True),
                      reads=[t_KeT, t_QeT], writes=[tpat], inc=(h == 3))
            kb.op("dve", lambda e: e.tensor_tensor(
                out=attm[:, :, :], in0=pat[:, :].rearrange("p (h t) -> p h t", t=128),
                in1=gm[:, mi, 2, :].unsqueeze(1).broadcast_to([128, 4, 128]), op=ALU.mult),
                reads=[tpat, t_gm], writes=[t_attm])
            pos_ = [next_ps(), next_ps()]
            if not sample:
                kb.op("act", lambda e: e.activation(out=Sbf[:, :, :], in_=Sst[:, :, :], func=AF.Identity),
                      reads=[t_S], writes=[t_Sbf])
            for h in range(4):
                po, tpo = pos_[h // 2]
                for vc in range(2):
                    reg = po[:, ((h % 2) * 2 + vc) * 128:((h % 2) * 2 + vc + 1) * 128]
                    kb.op("pe", lambda e, h=h, vc=vc, reg=reg: e.matmul(
                        reg, lhsT=vtok[:, h * 256 + vc * 128:h * 256 + (vc + 1) * 128], rhs=attm[:, h, :],
                        start=(h % 2 == 0 and vc == 0), stop=False), reads=[t_vtok, t_attm], writes=[tpo], inc=False)
                    if not sample:
                        kb.op("pe", lambda e, h=h, vc=vc, reg=reg: e.matmul(
                            reg, lhsT=Sbf[:, h, vc * 128:(vc + 1) * 128], rhs=QeT[:, h, :], start=False, stop=True),
                            reads=[t_Sbf, t_QeT], writes=[tpo], inc=(h % 2 == 1 and vc == 1))
            if sample:
                for g in range(4):
                    kb.dma("pool", S0bf[:, :, :, :], sgla_in[:, 4 * g:4 * g + 4, :, :], writes=[t_S0bf])
                    for sl in range(4):
                        s_ = 4 * g + sl
                        for h in range(4):
                            po, tpo = pos_[h // 2]
                            for vc in range(2):
                                reg = po[:, ((h % 2) * 2 + vc) * 128 + 8 * s_:((h % 2) * 2 + vc) * 128 + 8 * s_ + 8]
                                lastm = (g == 3 and sl == 3 and vc == 1 and h % 2 == 1)
                                kb.op("pe", lambda e, h=h, vc=vc, reg=reg, sl=sl, s_=s_: e.matmul(
                                    reg, lhsT=S0bf[:, sl, h, vc * 128:(vc + 1) * 128], rhs=QeT[:, h, 8 * s_:8 * s_ + 8],
                                    start=False, stop=True), reads=[t_S0bf, t_QeT], writes=[tpo],
                                    inc=(lastm or (sl == 3 and vc == 1 and h == 3)))
            for hp in range(2):
                po, tpo = pos_[hp]
                kb.op("act", lambda e, hp=hp, po=po: e.activation(
                    out=go[:, hp * 4:(hp + 1) * 4, c0:c0 + 128], in_=po[:, :].rearrange("p (c t) -> p c t", t=128),
                    func=AF.Identity), reads=[tpo], writes=[t_go])
                kb.op("act", lambda e, hp=hp, po=po: e.activation(
                    out=gsq[:, hp * 4:(hp + 1) * 4, c0:c0 + 128], in_=po[:, :].rearrange("p (c t) -> p c t", t=128),
                    func=AF.Square), reads=[tpo], writes=[t_gsq])

        def tile_finish(n, xt, t_xt, sample):
            for c0 in range(0, n, 128):
                p, tp = next_ps()
                for h in range(4):
                    for vc in range(2):
                        kb.op("pe", lambda e, h=h, vc=vc: e.matmul(
                            p[:, h * 128:(h + 1) * 128], lhsT=ones_bf[:, :], rhs=gsq[:, h * 2 + vc, c0:c0 + 128],
                            start=(vc == 0), stop=(vc == 1)), reads=[t_gsq, t_const], writes=[tp],
                            inc=(h == 3 and vc == 1))
                kb.op("act", lambda e: e.activation(out=rsh[:, :, :], in_=p[:, :].rearrange("p (h t) -> p h t", t=128),
                                                    func=AF.Sqrt, bias=V("eps"), scale=1.0 / 256), reads=[tp, t_vecs],
                      writes=[t_rsh])
                kb.op("dve", lambda e: e.reciprocal(out=rsh[:, :, :], in_=rsh[:, :, :]), reads=[t_rsh], writes=[t_rsh])
                gv = go[:, :, c0:c0 + 128].rearrange("p (h v) t -> p h v t", v=2)
                kb.op("dve", lambda e, gv=gv: e.tensor_tensor(
                    out=gv, in0=gv, in1=rsh[:, :, :].unsqueeze(2).broadcast_to([128, 4, 2, 128]), op=ALU.mult),
                    reads=[t_go, t_rsh], writes=[t_go])
                for vc in range(2):
                    gvv = go[:, :, c0:c0 + 128].rearrange("p (h v) t -> p h v t", v=2)[:, :, vc, :]
                    rv = rT[:, :, c0:c0 + 128].rearrange("p (h v) t -> p h v t", v=2)[:, :, vc, :]
                    ov = gsq[:, :, c0:c0 + 128].rearrange("p (h v) t -> p h v t", v=2)[:, :, vc, :]
                    kb.op("dve", lambda e, gvv=gvv, rv=rv, ov=ov, vc=vc: e.scalar_tensor_tensor(
                        out=ov, in0=gvv, scalar=V("g_gn", vc, 1), in1=rv, op0=ALU.mult, op1=ALU.mult),
                        reads=[t_go, t_rT, t_vecs, t_gsq], writes=[t_gsq])
            for m in range(KC):
                po, tpo = next_ps()
                for c in range(KC):
                    kb.op("pe", lambda e, c=c, m=m, po=po: e.matmul(
                        po[:, 0:n], lhsT=w_out[:, c, m * 128:(m + 1) * 128], rhs=gsq[:, c, 0:n],
                        start=(c == 0), stop=(c == KC - 1)),
                        reads=[t_wout[m * 128 // 512], t_gsq], writes=[tpo], inc=(c == KC - 1))
                kb.op("act", lambda e, m=m, po=po: e.activation(out=B.oT[:, m, 0:n], in_=po[:, 0:n], func=AF.Identity),
                      reads=[tpo], writes=[B.t_oT])
                kb.op("act", lambda e, m=m, po=po: e.activation(out=B.sq[:, m, 0:n], in_=po[:, 0:n], func=AF.Square),
                      reads=[tpo], writes=[B.t_sq])
            postnorm_residual(B, xt, t_xt, n, sample, l)

        xt, t_xt = B.xt[0], B.t_xt[0]
        kb.op("dve", lambda e: e.memset(Sst[:, :, :], 0.0), writes=[t_S])
        kb.op("dve", lambda e: e.memset(Atot[:, :], 1.0), writes=[t_Atot])
        for ti, (c0t, n) in enumerate(TILES[:-1]):
            kb.dma("sp", xt[:, :, 0:n], x_src(l)[:, :, c0t:c0t + n], reads=[xtk(ti)], writes=[t_xt])
            prenorm(B, xt, t_xt, n, False)
            tile_logarank(n)
            for c0 in range(0, n, 128):
                chunk_common(c0, 0, 1)
                state_update_prompt(True)
        kb.dma("sp", gx[:, 0:1024], Sst[:, :, :].rearrange("p h v -> p (h v)"), reads=[t_S], writes=[t_gx])
        kb.dma("sp", gx[:, 1024:1028], Atot[:, :], reads=[t_Atot], writes=[t_gx])
        kb.collective(lambda e: e.collective_compute("AllGather", ALU.bypass, replica_groups=[[0, 1, 2, 3], [4, 5, 6, 7]],
                                                     ins=[gx[:, :]], outs=[gg[:, :]]), reads=[t_gx], writes=[t_gg])
        gsb = big[:, 0:4112].rearrange("p (r n) -> p r n", r=4)
        kb.dma("sp", gsb, gg.ap().rearrange("(r p) n -> p r n", p=128), reads=[t_gg], writes=[t_big])

        def Bs(i):
            return gsb[:, i, 0:1024].rearrange("p (h v) -> p h v", h=4)
        kb.op("dve", lambda e: e.tensor_copy(out=accb[:, :, :], in_=Bs(0)), reads=[t_big], writes=[t_accb])
        kb.op("dve", lambda e: e.tensor_scalar(out=Sst[:, :, :], in0=accb[:, :, :], scalar1=V("sel", 1, 1), scalar2=None,
                                               op0=ALU.mult), reads=[t_accb, t_vecs], writes=[t_S])
        for i in (1, 2):
            for h in range(4):
                kb.op("dve", lambda e, i=i, h=h: e.scalar_tensor_tensor(
                    out=accb[:, h, :], in0=accb[:, h, :], scalar=gsb[:, i, 1024 + h:1025 + h], in1=Bs(i)[:, h, :],
                    op0=ALU.mult, op1=ALU.add), reads=[t_accb, t_big], writes=[t_accb])
            kb.op("dve", lambda e, i=i: e.scalar_tensor_tensor(
                out=Sst[:, :, :], in0=accb[:, :, :], scalar=V("sel", i + 1, 1), in1=Sst[:, :, :],
                op0=ALU.mult, op1=ALU.add), reads=[t_accb, t_vecs, t_S], writes=[t_S])
        for ti, (c0t, n) in enumerate(TILES):
            sample = (c0t >= LP)
            mi = 1 if sample else 0
            tx = xtk(ti)
            kb.dma("sp", xt[:, :, 0:n], x_src(l)[:, :, c0t:c0t + n], reads=[tx], writes=[t_xt])
            prenorm(B, xt, t_xt, n, sample)
            tile_logarank(n)
            for h in range(4):
                proj_fm(h * 128, n, lambda p, tp, h=h: kb.op("act", lambda e: e.activation(
                    out=qT[:, h, 0:n], in_=p[:, 0:n], func=AF.Identity, scale=DKS), reads=[tp], writes=[t_qT]))
                proj_fm(512 + h * 128, n, lambda p, tp, h=h: kb.op("act", lambda e: e.activation(
                    out=kT[:, h, 0:n], in_=p[:, 0:n], func=AF.Identity), reads=[tp], writes=[t_kT]))
            for c in range(KC):
                proj_fm(2048 + c * 128, n, lambda p, tp, c=c: kb.op("act", lambda e: e.activation(
                    out=rT[:, c, 0:n], in_=p[:, 0:n], func=AF.Silu), reads=[tp], writes=[t_rT]))
            for c0 in range(0, n, 128):
                chunk_common(c0, mi, 16 if sample else 1)
                chunk_full(c0, mi, sample)
                if not sample:
                    state_update_prompt(False)
                else:
                    S0g = big[:, 0:4096].rearrange("p (s h v) -> p s h v", s=4, h=4)
                    for g in range(4):
                        kb.dma("sp", S0g, sgla_in[:, 4 * g:4 * g + 4, :, :], writes=[t_big])
                        for h in range(4):
                            kb.op("pool", lambda e, h=h, g=g: e.tensor_tensor(
                                out=Vblk[:, :, :], in0=vtok[:, h * 256:(h + 1) * 256].unsqueeze(1).broadcast_to([128, 4, 256]),
                                in1=gseg[:, 1, 1, 4 * g:4 * g + 4].unsqueeze(2).broadcast_to([128, 4, 256]), op=ALU.mult),
                                reads=[t_vtok, t_gm], writes=[t_Vblk])
                            for half in range(2):
                                p, tp = next_ps()
                                kb.op("pe", lambda e, h=h, half=half, p=p: e.matmul(
                                    p[:, :], lhsT=Kes[:, h * 128:(h + 1) * 128],
                                    rhs=Vblk[:, 2 * half:2 * half + 2, :].rearrange("p s v -> p (s v)"),
                                    start=True, stop=True), reads=[t_Kes, t_Vblk], writes=[tp])
                                for sl2 in range(2):
                                    sl = 2 * half + sl2
                                    s_ = 4 * g + sl
                                    kb.op("dve", lambda e, h=h, sl=sl, sl2=sl2, s_=s_, p=p: e.scalar_tensor_tensor(
                                        out=S0g[:, sl, h, :], in0=S0g[:, sl, h, :], scalar=ebt[:, h, s_:s_ + 1],
                                        in1=p[:, sl2 * 256:(sl2 + 1) * 256], op0=ALU.mult, op1=ALU.add),
                                        reads=[t_big, t_ebt, tp], writes=[t_big])
                        kb.dma("sp", gla_s_out[:, 4 * g:4 * g + 4, :, :], S0g, reads=[t_big])
            tile_finish(n, xt, t_xt, sample)
            kb.dma("sp", x_dst(l)[:, :, c0t:c0t + n], xt[:, :, 0:n], reads=[t_xt], writes=[tx])
            if ti == len(TILES) - 2:
                kb.dma("sp", gla_p_out[:, :, :], Sst[:, :, :], reads=[t_S])
        kb.barrier()
        ls.close()


    def layer_att(l):
        ls = ExitStack()
        compute_mod(l, ls)
        B = common_bufs(ls, with_hT=False)
        L = LP
        GD = [(128, 1), (512, 4), (2048, 16)]
        EXT = [dd * 128 + L for (_, dd) in GD]

        def sublen(g):
            return 128 + L // GD[g][1]
        kTd = [nc.dram_tensor("kTd%d" % g, [128, 4, EXT[g]], BF16) for g in range(3)]
        qTd = [nc.dram_tensor("qTd%d" % g, [128, 4, L], BF16) for g in range(3)]
        vd = [nc.dram_tensor("vd%d" % g, [EXT[g], 512], BF16) for g in range(3)]
        ktp = [nc.dram_tensor("ktp%d" % i, [128, 3584], BF16) for i in range(3)]
        vtp = [nc.dram_tensor("vtp%d" % i, [896, 512], BF16) for i in range(3)]
        ggK = [nc.dram_tensor("ggK%d" % i, [512, 3584], BF16) for i in range(3)]
        ggV = [nc.dram_tensor("ggV%d" % i, [3584, 512], BF16) for i in range(3)]
        t_kTd = [Tk("kTd%d" % g, multi=True) for g in range(3)]
        t_qTd = [Tk("qTd%d" % g, multi=True) for g in range(3)]
        t_vd = [Tk("vd%d" % g, multi=True) for g in range(3)]
        t_ktp = [Tk("ktp%d" % i, multi=True) for i in range(3)]
        t_vtp = [Tk("vtp%d" % i, multi=True) for i in range(3)]
        t_ggK = [Tk("ggK%d" % i) for i in range(3)]
        t_ggV = [Tk("ggV%d" % i) for i in range(3)]
        t_outs = Tk("att_outs", multi=True)

        def mk(name, shape, dt, st=None):
            return S(name, shape, dt, st if st is not None else ls), Tk(name)
        zT, t_zT = mk("zT", [128, 4, NT], BF16)
        kTs, t_kTs = mk("kTs", [128, 12, NS], BF16)
        qTs, t_qTs = mk("qTs", [128, 12, NS], BF16)
        vS, t_vS = mk("vS", [128, 1536], BF16)
        sA = ExitStack()
        hTa, t_hTa = mk("hTa", [128, KC, NT], BF16, sA)
        rope, t_rope = mk("rope", [128, 2, NT], F32, sA)
        pmf, t_pmf = mk("pmf", [128, 128], F32, sA)
        pmb, t_pmb = mk("pmb", [128, 128], BF16, sA)
        for ci in range(2):
            for hb_ in range(2):
                kb.dma("sp", rope[:, ci, hb_ * 1088:(hb_ + 1) * 1088], rope_in[:, ci, hb_ * 1088:(hb_ + 1) * 1088],
                       writes=[t_rope])
        kb.dma("sp", pmf[:, :], pm_in[:, :], writes=[t_pmf])
        kb.op("dve", lambda e: e.tensor_copy(out=pmb[:, :], in_=pmf[:, :]), reads=[t_pmf], writes=[t_pmb])

        sa = ExitStack()
        w_in, t_win, wblk = load_w(sa, "wa_qk", w_att_in_in[:, 0:3072], 3072)
        xb, t_xb = mk("xb", [128, 512], BF16, sa)
        t1, t_t1 = mk("ra1", [128, 512], F32, sa)
        t2, t_t2 = mk("ra2", [128, 512], F32, sa)
        kf, t_kf = mk("kf", [128, 512], F32, sa)
        kbf, t_kbf = mk("kbf", [128, 512], BF16, sa)
        xt, t_xt = B.xt[0], B.t_xt[0]

        def wtk(col0):
            return t_win[col0 // wblk]

        for ti, (c0t, n) in enumerate(TILES):
            sample = (c0t >= LP)
            kb.dma("sp", xt[:, :, 0:n], x_src(l)[:, :, c0t:c0t + n], reads=[xtk(ti)], writes=[t_xt])
            prenorm(B, xt, t_xt, n, sample, hout=hTa[:, :, c0t:c0t + n], t_hout=t_hTa)

        def dec2(ap2d, g, u):
            if g == 0:
                return ap2d[:, 512 * u:512 * u + 512]
            if g == 1:
                return ap2d.rearrange("p (n r) -> p r n", r=4)[:, u, :]
            return ap2d.rearrange("p (n r) -> p r n", r=16)[:, 4 * u:4 * u + 4, :]

        def qk_core(col0, rhs_fn, n, cos_ap, sin_ap, vwf):
            p, tp = next_ps()
            for kc in range(KC):
                kb.op("pe", lambda e, kc=kc: e.matmul(vwf(p[:, 0:n]), lhsT=w_in[:, kc, col0:col0 + 128], rhs=rhs_fn(kc),
                                                       start=(kc == 0), stop=(kc == KC - 1)),
                      reads=[wtk(col0), t_hTa], writes=[tp], inc=(kc == KC - 1))
            if QK_STEPS < 2:
                return
            kb.op("act", lambda e: e.activation(out=xb[:, 0:n], in_=p[:, 0:n], func=AF.Identity), reads=[tp], writes=[t_xb])
            if QK_STEPS < 3:
                return
            pr, tpr = next_ps()
            kb.op("pe", lambda e: e.matmul(pr[:, 0:n], lhsT=pmb[:, :], rhs=xb[:, 0:n], start=True, stop=True),
                  reads=[t_pmb, t_xb], writes=[tpr])
            if QK_STEPS < 4:
                return
            kb.op("dve", lambda e: e.tensor_tensor(out=vwf(t1[:, 0:n]), in0=vwf(p[:, 0:n]), in1=cos_ap, op=ALU.mult),
                  reads=[tp, t_rope], writes=[t_t1])
            if QK_STEPS < 5:
                return
            kb.op("dve", lambda e: e.tensor_tensor(out=vwf(t2[:, 0:n]), in0=vwf(pr[:, 0:n]), in1=sin_ap, op=ALU.mult),
                  reads=[tpr, t_rope], writes=[t_t2])
            if QK_STEPS < 6:
                return
            kb.op("pool", lambda e: e.tensor_tensor(out=kf[:, 0:n], in0=t1[:, 0:n], in1=t2[:, 0:n], op=ALU.add),
                  reads=[t_t1, t_t2], writes=[t_kf])
            if QK_STEPS < 7:
                return
            kb.op("act", lambda e: e.activation(out=kbf[:, 0:n], in_=kf[:, 0:n], func=AF.Identity),
                  reads=[t_kf], writes=[t_kbf])

        def tail_col(g, hc, r):
            if g == 0:
                return hc * 128
            if g == 1:
                return 512 + (hc * 4 + r) * 128
            return 2560 + (hc * 16 + r) * 128

        for g in (QK_G if (A_PARTS & 1) else []):
            W, dd = GD[g]
            vwf = (lambda a: a.rearrange("p (r n) -> p r n", r=4)) if g == 2 else (lambda a: a)
            for kind in range(2):
                for u in range(4):
                    for hc in range(4):
                        col0 = kind * 1536 + g * 512 + hc * 128
                        qk_core(col0, lambda kc, g=g, u=u: dec2(hTa[:, kc, 0:L], g, u), 512,
                                dec2(rope[:, 0, 0:L], g, u), dec2(rope[:, 1, 0:L], g, u), vwf)
                        if kind == 0:
                            if QK_DMA & 1:
                                kb.dma("sp", qTd[g][:, hc, 512 * u:512 * u + 512], kbf[:, :], reads=[t_kbf], writes=[t_qTd[g]])
                            continue
                        if not (QK_DMA & 2):
                            continue
                        if g == 0:
                            dst = kTd[g][:, hc, 128 + 512 * u:128 + 512 * u + 512]
                            src = kbf[:, :]
                        elif g == 1:
                            dst = kTd[g][:, hc, u * 640 + 128:u * 640 + 640]
                            src = kbf[:, :]
                        else:
                            dst = kTd[g][:, hc, :].rearrange("p (r e) -> p r e", e=256)[:, 4 * u:4 * u + 4, 128:256]
                            src = kbf[:, :].rearrange("p (r n) -> p r n", r=4)
                        kb.dma("sp", dst, src, reads=[t_kbf], writes=[t_kTd[g]])
                        if not (QK_DMA & 4):
                            continue
                        if g == 0 and u != 3:
                            continue
                        if g == 2:
                            col = tail_col(2, hc, 4 * u)
                            pc, off = col // 3584, col % 3584
                            kb.dma("sp", ktp[pc][:, off:off + 512], kbf[:, :], reads=[t_kbf], writes=[t_ktp[pc]])
                            kb.dma("sp", kout[:, col:col + 512], kf[:, :], reads=[t_kf], writes=[t_outs])
                        else:
                            col = tail_col(g, hc, u if g == 1 else 0)
                            pc, off = col // 3584, col % 3584
                            kb.dma("sp", ktp[pc][:, off:off + 128], kbf[:, 384:512], reads=[t_kbf], writes=[t_ktp[pc]])
                            kb.dma("sp", kout[:, col:col + 128], kf[:, 384:512], reads=[t_kf], writes=[t_outs])
        for g in range(3 if (A_PARTS & 2) else 0):
            for kind in range(2):
                for hc in range(4):
                    col0 = kind * 1536 + g * 512 + hc * 128
                    qk_core(col0, lambda kc: hTa[:, kc, LP:LP + NS], NS, rope[:, 0, LP:LP + NS], rope[:, 1, LP:LP + NS],
                            lambda a: a)
                    if kind == 0:
                        kb.op("pool", lambda e, g=g, hc=hc: e.tensor_copy(out=qTs[:, g * 4 + hc, :], in_=kbf[:, 0:NS]),
                              reads=[t_kbf], writes=[t_qTs])
                    else:
                        kb.op("pool", lambda e, g=g, hc=hc: e.tensor_copy(out=kTs[:, g * 4 + hc, :], in_=kbf[:, 0:NS]),
                              reads=[t_kbf], writes=[t_kTs])
                        kb.dma("sp", ks_out[:, g * 4 + hc, :], kf[:, 0:NS], reads=[t_kf], writes=[t_outs])
        kb.barrier()
        sa.close()
        sa = ExitStack()
        w_in, t_win, wblk = load_w(sa, "wa_vz", w_att_in_in[:, 3072:5120], 2048)
        vb, t_vb = mk("vb", [128, 512], BF16, sa)
        vf, t_vf = mk("vf", [128, 512], F32, sa)
        for ti, (c0t, n) in enumerate(TILES if (A_PARTS & 4) else []):
            for c in range(4):
                p, tp = next_ps()
                for kc in range(KC):
                    kb.op("pe", lambda e, kc=kc, c=c, p=p: e.matmul(
                        p[:, 0:n], lhsT=w_in[:, kc, 1536 + c * 128:1536 + (c + 1) * 128], rhs=hTa[:, kc, c0t:c0t + n],
                        start=(kc == 0), stop=(kc == KC - 1)), reads=[wtk(1536 + c * 128), t_hTa], writes=[tp],
                        inc=(kc == KC - 1))
                kb.op("act", lambda e, c=c, p=p: e.activation(out=zT[:, c, c0t:c0t + n], in_=p[:, 0:n], func=AF.Silu),
                      reads=[tp], writes=[t_zT])
        for g in range(3 if (A_PARTS & 8) else 0):
            W, dd = GD[g]
            nb = L // dd // 128
            for r in range(dd):
                for b in range(nb):
                    def lhs(kc, g=g, r=r, b=b):
                        a = hTa[:, kc, 0:L]
                        if g == 0:
                            return a[:, 128 * b:128 * b + 128]
                        return a.rearrange("p (n r) -> p r n", r=GD[g][1])[:, r, 128 * b:128 * b + 128]
                    p, tp = next_ps()
                    for kc in range(KC):
                        kb.op("pe", lambda e, kc=kc, p=p: e.matmul(
                            p[:, :], lhsT=lhs(kc), rhs=w_in[:, kc, g * 512:(g + 1) * 512],
                            start=(kc == 0), stop=(kc == KC - 1)), reads=[wtk(g * 512), t_hTa], writes=[tp],
                            inc=(kc == KC - 1))
                    kb.op("act", lambda e, p=p: e.activation(out=vb[:, :], in_=p[:, :], func=AF.Identity),
                          reads=[tp], writes=[t_vb])
                    e0 = r * sublen(g) + 128 + 128 * b
                    kb.dma("sp", vd[g][e0:e0 + 128, :], vb[:, :], reads=[t_vb], writes=[t_vd[g]])
                    if b == nb - 1:
                        row = (0, 128, 640)[g] + r * 128
                        pc, off = row // 896, row % 896
                        kb.dma("sp", vtp[pc][off:off + 128, :], vb[:, :], reads=[t_vb], writes=[t_vtp[pc]])
                        kb.op("dve", lambda e, p=p: e.tensor_copy(out=vf[:, :], in_=p[:, :]), reads=[tp], writes=[t_vf])
                        kb.dma("sp", vout[row:row + 128, :], vf[:, :], reads=[t_vf], writes=[t_outs])
            p, tp = next_ps()
            for kc in range(KC):
                kb.op("pe", lambda e, kc=kc, p=p: e.matmul(
                    p[:, :], lhsT=hTa[:, kc, LP:LP + NS], rhs=w_in[:, kc, g * 512:(g + 1) * 512],
                    start=(kc == 0), stop=(kc == KC - 1)), reads=[wtk(g * 512), t_hTa], writes=[tp],
                    inc=(kc == KC - 1))
            kb.op("act", lambda e, p=p, g=g: e.activation(out=vS[:, g * 512:(g + 1) * 512], in_=p[:, :], func=AF.Identity),
                  reads=[tp], writes=[t_vS])
            kb.op("dve", lambda e, p=p: e.tensor_copy(out=vf[:, :], in_=p[:, :]), reads=[tp], writes=[t_vf])
            kb.dma("sp", vs_out[:, g * 512:(g + 1) * 512], vf[:, :], reads=[t_vf], writes=[t_outs])
        kb.barrier()
        sa.close()
        sA.close()

        sb2 = ExitStack()
        w_out = S("wa_out", [128, 4, D], BF16, sb2)
        t_wo = Tk("wa_out")
        kb.dma("pool", w_out[:, :, :], w_att_out_in.rearrange("(c p) n -> p c n", p=128), writes=[t_wo])
        nacc, t_nacc = mk("nacc", [128, 4, L], F32, sb2)
        dacc, t_dacc = mk("dacc", [128, 4, L], F32, sb2)
        naccS, t_naccS = mk("naccS", [128, 4, NS], F32, sb2)
        daccS, t_daccS = mk("daccS", [128, 4, NS], F32, sb2)
        amask, t_am = mk("amask", [128, 3, 512], BF16, sb2)
        smask, t_sm = mk("smask", [128, 13, 64], BF16, sb2)
        snew, t_sn = mk("snew", [128, 3, 512], BF16, sb2)
        kb.dma("pool", amask[:, :, :], amask_in[:, :, :], writes=[t_am])
        kb.dma("pool", smask[:, :, :], smask_in[:, :, :], writes=[t_sm])
        kb.dma("pool", snew[:, :, :], snew_in[:, :, :], writes=[t_sn])
        PT, t_PT = mk("PT", [128, 2, 8, 128], BF16, sb2)
        kt, t_kt = mk("kt", [128, 4, 256], BF16, sb2)
        vt, t_vt = mk("vt", [128, 2, 512], BF16, sb2)
        quA, t_qu = mk("quA", [128, 4, 512], BF16, sb2)
        quB, _ = mk("quB", [128, 4, 512], BF16, sb2)
        qsA, t_qs = mk("qsA", [128, 12, NS], BF16, sb2)
        qsB, _ = mk("qsB", [128, 12, NS], BF16, sb2)
        for zt in (quA, quB, qsA, qsB):
            kb.op("pool", lambda e, zt=zt: e.memset(zt[:, :, :], 0.0), writes=[t_qu, t_qs])
        kb.op("pool", lambda e: e.tensor_copy(out=qsA[0:64, :, :], in_=qTs[0:64, :, :]), reads=[t_qTs], writes=[t_qs])
        kb.op("pool", lambda e: e.tensor_copy(out=qsB[64:128, :, :], in_=qTs[64:128, :, :]), reads=[t_qTs], writes=[t_qs])
        Kt2 = [mk("Kt%d" % i, [128, 512], BF16, sb2) for i in range(2)]
        Vt2 = [mk("Vt%d" % i, [128, 512], BF16, sb2) for i in range(2)]
        kTt2 = [mk("kTt%d" % i, [128, 4, 128], BF16, sb2) for i in range(2)]
        PTs2 = [mk("PTs%d" % i, [128, 64], BF16, sb2) for i in range(2)]
        Kf2 = [mk("Kf%d" % i, [128, 512], F32, sb2) for i in range(2)]
        Vf2 = [mk("Vf%d" % i, [128, 512], F32, sb2) for i in range(2)]
        stile = [0]
        hk, t_hk = mk("hk", [128, 3584], BF16, sb2)
        hv, _ = mk("hv", [128, 2, 512], BF16, sb2)
        t_hvb = [Tk("hv0"), Tk("hv1")]
        idxk, t_idx = mk("idxk", [128, 1], I32, sb2)
        idxv, _ = mk("idxv", [128, 7], I32, sb2)
        og, t_og = mk("og", [128, 4, TN], BF16, sb2)
        ps_reserved.add(7)
        ptb = ps[7][:, :].bitcast(BF16)[:, 0:512]
        t_ptb = t_ps[7]
        kb.dma("sp", idxk[:, :], idxk_in[:, :], writes=[t_idx])
        kb.dma("sp", idxv[:, :], idxv_in[:, :], writes=[t_idx])
        kb.op("dve", lambda e: e.memset(nacc[:, :, :], 0.0), writes=[t_nacc])
        kb.op("dve", lambda e: e.memset(dacc[:, :, :], 0.0), writes=[t_dacc])
        SC = 0.125

        for i in range(3 if _en('X') else 0):
            kb.collective(lambda e, i=i: e.collective_compute(
                "AllGather", ALU.bypass, replica_groups=[[0, 1, 2, 3], [4, 5, 6, 7]],
                ins=[ktp[i][:, :]], outs=[ggK[i][:, :]]), reads=[t_ktp[i]], writes=[t_ggK[i]])
            kb.collective(lambda e, i=i: e.collective_compute(
                "AllGather", ALU.bypass, replica_groups=[[0, 1, 2, 3], [4, 5, 6, 7]],
                ins=[vtp[i][:, :]], outs=[ggV[i][:, :]]), reads=[t_vtp[i]], writes=[t_ggV[i]])

        res = []
        for _ in range(4):
            i = psn[0]
            while i in ps_reserved:
                i = (i + 1) % 8
            ps_reserved.add(i)
            res.append(i)
        pnum = [(ps[res[0]], t_ps[res[0]]), (ps[res[1]], t_ps[res[1]])]
        pden = [(ps[res[2]], t_ps[res[2]]), (ps[res[3]], t_ps[res[3]])]
        started = set()

        def first(i):
            if i in started:
                return False
            started.add(i)
            return True

        def score_bank(maskrhs, t_mask, nn, per_head):
            p, tp = next_ps()
            kb.op("pe", lambda e: e.matmul(p[:, 0:nn], lhsT=ident_bf[:, :], rhs=maskrhs, start=True, stop=False),
                  reads=[t_const, t_mask], writes=[tp], inc=False)
            per_head(p, tp)
            return p, tp

        for g in range(3 if _en('S') else 0):
            for hb in range(2):
                def heads(p, tp, g=g, hb=hb):
                    for hh in range(4):
                        h = 4 * hb + hh
                        hc, pb = h // 2, (h % 2) * 64
                        kb.op("pe", lambda e, hh=hh, hc=hc, pb=pb: e.matmul(
                            p[:, hh * 128:(hh + 1) * 128], lhsT=kTs[:, g * 4 + hc, :],
                            rhs=(qsA if pb == 0 else qsB)[:, g * 4 + hc, :], start=False, stop=True),
                            reads=[t_kTs, t_qs], writes=[tp], inc=(hh == 3))
                p, tp = score_bank(snew[:, g, :], t_sn, 512, heads)
                kb.op("act", lambda e, p=p, hb=hb: e.activation(
                    out=PT[:, 0, 4 * hb:4 * hb + 4, :], in_=p[:, :].rearrange("p (h q) -> p h q", q=128),
                    func=AF.Exp, scale=SC), reads=[tp], writes=[t_PT])
            for hb in range(2):
                pd, tpd = pden[hb]
                kb.op("pe", lambda e, hb=hb, pd=pd: e.matmul(
                    pd[:, :], lhsT=ones_bf[:, :], rhs=PT[:, 0, 4 * hb:4 * hb + 4, :].rearrange("p h q -> p (h q)"),
                    start=first(res[2 + hb]), stop=False), reads=[t_const, t_PT], writes=[tpd])
            for hc in range(4):
                pn, tpn = pnum[hc // 2]
                for ab in range(2):
                    c0 = ((hc % 2) * 2 + ab) * 128
                    kb.op("pe", lambda e, hc=hc, ab=ab, c0=c0, pn=pn, g=g: e.matmul(
                        pn[:, c0:c0 + 128], lhsT=vS[:, g * 512 + hc * 128:g * 512 + (hc + 1) * 128],
                        rhs=PT[:, 0, 2 * hc + ab, :], start=first(res[hc // 2]), stop=False),
                        reads=[t_vS, t_PT], writes=[tpn])
        for s_ in range(16 if _en('S') else 0):
            for g in range(3):
                W, dd = GD[g]
                ntile = (1, 4, 8)[g]
                for r in range(ntile):
                    mt = (0, 1, 5)[g] + r
                    bi = stile[0] % 2
                    stile[0] += 1
                    (Kt, t_Kt), (Vt, t_Vt), (kTt, t_kTt), (PTs, t_PTs) = Kt2[bi], Vt2[bi], kTt2[bi], PTs2[bi]
                    (Kf, t_Kf), (Vf, t_Vf) = Kf2[bi], Vf2[bi]
                    kb.dma("sp", Kf[:, :], ck_in[g][s_, r::dd, :], writes=[t_Kf])
                    kb.dma("sp", Vf[:, :], cv_in[g][s_, r::dd, :], writes=[t_Vf])
                    kb.op("act", lambda e: e.activation(out=Kt[:, :], in_=Kf[:, :], func=AF.Identity),
                          reads=[t_Kf], writes=[t_Kt])
                    kb.op("act", lambda e: e.activation(out=Vt[:, :], in_=Vf[:, :], func=AF.Identity),
                          reads=[t_Vf], writes=[t_Vt])
                    for hc in range(4):
                        kb.op("pe", lambda e, hc=hc: e.transpose(ptb[:, hc * 128:(hc + 1) * 128],
                                                                  Kt[:, hc * 128:(hc + 1) * 128], ident_bf[:, :]),
                              reads=[t_Kt, t_const], writes=[t_ptb], inc=(hc == 3))
                    kb.op("dve", lambda e: e.tensor_copy(out=kTt[:, :, :], in_=ptb[:, :].rearrange("p (c k) -> p c k", c=4)),
                          reads=[t_ptb], writes=[t_kTt])

                    def heads(p, tp, g=g, s_=s_):
                        for h in range(8):
                            hc, pb = h // 2, (h % 2) * 64
                            kb.op("pe", lambda e, h=h, hc=hc, pb=pb: e.matmul(
                                p[:, h * 8:(h + 1) * 8], lhsT=kTt[:, hc, :],
                                rhs=(qsA if pb == 0 else qsB)[:, g * 4 + hc, 8 * s_:8 * s_ + 8], start=False, stop=True),
                                reads=[t_kTt, t_qs], writes=[tp], inc=(h == 7))
                    p, tp = score_bank(smask[:, mt, :], t_sm, 64, heads)
                    kb.op("act", lambda e, p=p: e.activation(out=PTs[:, :], in_=p[:, 0:64], func=AF.Exp, scale=SC),
                          reads=[tp], writes=[t_PTs])
                    for hb in range(2):
                        pd, tpd = pden[hb]
                        kb.op("pe", lambda e, hb=hb, pd=pd: e.matmul(
                            pd[:, :].rearrange("p (h q) -> p h q", q=128)[:, :, 8 * s_:8 * s_ + 8],
                            lhsT=ones_bf[:, :], rhs=PTs[:, 32 * hb:32 * hb + 32].rearrange("p (h i) -> p h i", i=8),
                            start=False, stop=False), reads=[t_const, t_PTs], writes=[tpd], inc=(hb == 1))
                    for hc in range(4):
                        pn, tpn = pnum[hc // 2]
                        for ab in range(2):
                            c0 = ((hc % 2) * 2 + ab) * 128 + 8 * s_
                            kb.op("pe", lambda e, hc=hc, ab=ab, c0=c0, pn=pn: e.matmul(
                                pn[:, c0:c0 + 8], lhsT=Vt[:, hc * 128:(hc + 1) * 128],
                                rhs=PTs[:, (2 * hc + ab) * 8:(2 * hc + ab) * 8 + 8], start=False, stop=False),
                                reads=[t_Vt, t_PTs], writes=[tpn], inc=(hc == 3 and ab == 1))
        for bk in range(2 if _en('S') else 0):
            pn, tpn = pnum[bk]
            pd, tpd = pden[bk]
            pnv = pn[:, :].rearrange("p (c a q) -> p c a q", c=2, a=2)
            pdv = pd[:, :].rearrange("p (c a q) -> p c a q", c=2, a=2)
            for ab in range(2):
                rows = slice(64 * ab, 64 * ab + 64)
                kb.op("dve", lambda e, rows=rows, ab=ab, pnv=pnv, bk=bk: e.tensor_copy(
                    out=naccS[rows, 2 * bk:2 * bk + 2, :], in_=pnv[rows, :, ab, :]), reads=[tpn], writes=[t_naccS])
                kb.op("dve", lambda e, rows=rows, ab=ab, pdv=pdv, bk=bk: e.tensor_copy(
                    out=daccS[rows, 2 * bk:2 * bk + 2, :], in_=pdv[rows, :, ab, :]), reads=[tpd], writes=[t_daccS])
        for i in res:
            ps_reserved.discard(i)

        for i in range(3 if _en('H') else 0):
            kb.dma_custom("pool", lambda e, i=i: e.indirect_dma_start(
                out=hk[:, :], out_offset=None, in_=ggK[i][:, :],
                in_offset=bass.IndirectOffsetOnAxis(ap=idxk[:, 0:1], axis=0)), reads=[t_ggK[i], t_idx], writes=[t_hk])
            for c128 in range(28):
                col = i * 3584 + c128 * 128
                cb = col // 128
                if cb < 4:
                    g, hc, r = 0, cb, 0
                elif cb < 20:
                    g, hc, r = 1, (cb - 4) // 4, (cb - 4) % 4
                else:
                    g, hc, r = 2, (cb - 20) // 16, (cb - 20) % 16
                e0 = r * sublen(g)
                kb.dma("sp", kTd[g][:, hc, e0:e0 + 128], hk[:, c128 * 128:(c128 + 1) * 128], reads=[t_hk],
                       writes=[t_kTd[g]])
            for t in range(7):
                hb_ = t % 2
                kb.dma_custom("pool", lambda e, i=i, t=t, hb_=hb_: e.indirect_dma_start(
                    out=hv[:, hb_, :], out_offset=None, in_=ggV[i][:, :],
                    in_offset=bass.IndirectOffsetOnAxis(ap=idxv[:, t:t + 1], axis=0)), reads=[t_ggV[i], t_idx],
                    writes=[t_hvb[hb_]])
                T = i * 7 + t
                if T == 0:
                    g, r = 0, 0
                elif T < 5:
                    g, r = 1, T - 1
                else:
                    g, r = 2, T - 5
                e0 = r * sublen(g)
                kb.dma("sp", vd[g][e0:e0 + 128, :], hv[:, hb_, :], reads=[t_hvb[hb_]], writes=[t_vd[g]])

        for g in range(3 if _en('B') else 0):
            W, dd = GD[g]
            nb = L // dd // 128
            for u in range(4):
                kb.dma("sp", quA[0:64, :, :], qTd[g][0:64, :, 512 * u:512 * u + 512], reads=[t_qTd[g]], writes=[t_qu])
                kb.dma("sp", quB[64:128, :, :], qTd[g][64:128, :, 512 * u:512 * u + 512], reads=[t_qTd[g]], writes=[t_qu])
                for k in range(4):
                    if g == 0:
                        r, b = 0, 4 * u + k
                    elif g == 1:
                        r, b = u, k
                    else:
                        r, b = 4 * u + k, 0
                    e0 = r * sublen(g) + 128 * b
                    kb.dma("sp", kt[:, :, :], kTd[g][:, :, e0:e0 + 256], reads=[t_kTd[g]], writes=[t_kt])
                    kb.dma("sp", vt[:, :, :], vd[g][e0:e0 + 256, :].rearrange("(t p) c -> p t c", p=128),
                           reads=[t_vd[g]], writes=[t_vt])
                    if B_STEPS < 2:
                        continue
                    for half in range(2):
                        mi = 2 if half == 1 else (1 if b == 0 else 0)
                        for hb in range(2):
                            def heads(p, tp, hb=hb, half=half, k=k):
                                for hh in range(4):
                                    h = 4 * hb + hh
                                    hc, pb = h // 2, (h % 2) * 64
                                    kb.op("pe", lambda e, hh=hh, hc=hc, pb=pb: e.matmul(
                                        p[:, hh * 128:(hh + 1) * 128], lhsT=kt[:, hc, half * 128:(half + 1) * 128],
                                        rhs=(quA if pb == 0 else quB)[:, hc, k * 128:(k + 1) * 128], start=False, stop=True),
                                        reads=[t_kt, t_qu], writes=[tp], inc=(hh == 3))
                            p, tp = score_bank(amask[:, mi, :], t_am, 512, heads)
                            kb.op("act", lambda e, p=p, hb=hb, half=half: e.activation(
                                out=PT[:, half, 4 * hb:4 * hb + 4, :], in_=p[:, :].rearrange("p (h q) -> p h q", q=128),
                                func=AF.Exp, scale=SC), reads=[tp], writes=[t_PT])
                    if B_STEPS < 3:
                        continue
                    pdl = [next_ps(), next_ps()]
                    for hb in range(2):
                        pd, tpd = pdl[hb]
                        for half in range(2):
                            kb.op("pe", lambda e, hb=hb, half=half, pd=pd: e.matmul(
                                pd[:, :], lhsT=ones_bf[:, :],
                                rhs=PT[:, half, 4 * hb:4 * hb + 4, :].rearrange("p h q -> p (h q)"),
                                start=(half == 0), stop=(half == 1)), reads=[t_const, t_PT], writes=[tpd],
                                inc=(half == 1))
                    if B_STEPS < 4:
                        continue
                    pnl = [next_ps(), next_ps()]
                    for bk in range(2):
                        pn, tpn = pnl[bk]
                        fst = True
                        for hcl in range(2):
                            hc = 2 * bk + hcl
                            for ab in range(2):
                                c0 = (hcl * 2 + ab) * 128
                                for half in range(2):
                                    kb.op("pe", lambda e, hc=hc, ab=ab, c0=c0, half=half, pn=pn, fst=fst: e.matmul(
                                        pn[:, c0:c0 + 128], lhsT=vt[:, half, hc * 128:(hc + 1) * 128],
                                        rhs=PT[:, half, 2 * hc + ab, :], start=fst, stop=(half == 1)),
                                        reads=[t_vt, t_PT], writes=[tpn], inc=(hcl == 1 and ab == 1 and half == 1))
                                    fst = False
                    if B_STEPS < 5:
                        continue
                    def tokv(acc, rows, c2):
                        a = acc[:, c2:c2 + 2, 0:L]
                        if dd > 1:
                            a = a.rearrange("p c (n r) -> p c r n", r=dd)[:, :, r, :]
                        return a[rows, :, 128 * b:128 * b + 128]
                    for bk in range(2):
                        pn, tpn = pnl[bk]
                        pd, tpd = pdl[bk]
                        pnv = pn[:, :].rearrange("p (c a q) -> p c a q", c=2, a=2)
                        pdv = pd[:, :].rearrange("p (c a q) -> p c a q", c=2, a=2)
                        for ab in range(2):
                            rows = slice(64 * ab, 64 * ab + 64)
                            kb.op("dve", lambda e, rows=rows, ab=ab, pnv=pnv, bk=bk: e.tensor_tensor(
                                out=tokv(nacc, rows, 2 * bk), in0=tokv(nacc, rows, 2 * bk), in1=pnv[rows, :, ab, :],
                                op=ALU.add), reads=[tpn, t_nacc], writes=[t_nacc])
                            kb.op("dve", lambda e, rows=rows, ab=ab, pdv=pdv, bk=bk: e.tensor_tensor(
                                out=tokv(dacc, rows, 2 * bk), in0=tokv(dacc, rows, 2 * bk), in1=pdv[rows, :, ab, :],
                                op=ALU.add), reads=[tpd, t_dacc], writes=[t_dacc])

        for ti, (c0t, n) in enumerate(TILES):
            sample = (c0t >= LP)
            tx = xtk(ti)
            kb.dma("sp", xt[:, :, 0:n], x_src(l)[:, :, c0t:c0t + n], reads=[tx], writes=[t_xt])
            if not sample:
                na, da, tna, tda = nacc[:, :, c0t:c0t + n], dacc[:, :, c0t:c0t + n], t_nacc, t_dacc
            else:
                na, da, tna, tda = naccS[:, :, :], daccS[:, :, :], t_naccS, t_daccS
            kb.op("dve", lambda e, da=da: e.reciprocal(out=da, in_=da), reads=[tda], writes=[tda])
            kb.op("dve", lambda e, da=da, na=na: e.tensor_tensor(out=na, in0=na, in1=da, op=ALU.mult),
                  reads=[tda, tna], writes=[tna])
            kb.op("pool", lambda e, na=na: e.tensor_tensor(out=og[:, :, 0:n], in0=na, in1=zT[:, :, c0t:c0t + n], op=ALU.mult),
                  reads=[tna, t_zT], writes=[t_og])
            for m in range(KC):
                po, tpo = next_ps()
                for c in range(4):
                    kb.op("pe", lambda e, c=c, m=m, po=po: e.matmul(
                        po[:, 0:n], lhsT=w_out[:, c, m * 128:(m + 1) * 128], rhs=og[:, c, 0:n],
                        start=(c == 0), stop=(c == 3)), reads=[t_wo, t_og], writes=[tpo], inc=(c == 3))
                kb.op("act", lambda e, m=m, po=po: e.activation(out=B.oT[:, m, 0:n], in_=po[:, 0:n], func=AF.Identity),
                      reads=[tpo], writes=[B.t_oT])
                kb.op("act", lambda e, m=m, po=po: e.activation(out=B.sq[:, m, 0:n], in_=po[:, 0:n], func=AF.Square),
                      reads=[tpo], writes=[B.t_sq])
            postnorm_residual(B, xt, t_xt, n, sample, l)
            kb.dma("sp", x_dst(l)[:, :, c0t:c0t + n], xt[:, :, 0:n], reads=[t_xt], writes=[tx])
        kb.barrier()
        ps_reserved.discard(7)
        sb2.close()
        ls.close()

    layer_conv(0, 0)
    if NLAYERS >= 2:
        layer_gla(1, 0)
    if NLAYERS >= 3:
        layer_att(2)
    if NLAYERS >= 4:
        layer_conv(3, 1)
    kb.final_wait()
    global _KB_DEBUG
    _KB_DEBUG = (dict(kb.cnt), dict(kb.dcnt), kb.ccn)
    es.close()
    return nc


def _prep_inputs(inp):
    f = lambda a: np.ascontiguousarray(np.asarray(a, dtype=np.float32))
    x_prompt, x_sample = f(inp["x_prompt"]), f(inp["x_sample"])
    c_prompt, c_sample = f(inp["c_prompt"]), f(inp["c_sample"])
    state_conv = f(inp["state_conv"])
    shared = {
        "w_ada": f(inp["w_ada"]),
        "w_conv_in": f(inp["w_conv_in"]),
        "w_conv_out": f(inp["w_conv_out"]),
        "w_gla_in": f(inp["w_gla_in"][0]),
        "w_gla_out": f(inp["w_gla_out"][0]),
        "w_a1": f(inp["w_gla_a1"][0]),
        "w_a2": f(inp["w_gla_a2"][0]),
        "b_a": f(inp["b_gla_a"][0]).reshape(1, 512),
        "w_att_in": f(inp["w_att_in"][0]),
        "w_att_out": f(inp["w_att_out"][0]),
        "pm": _att_consts()["pm"],
        "smask": _att_consts()["smask"],
        "snew": _att_consts()["snew"],
        "gmask": _gla_masks()[0],
        "gseg": _gla_masks()[1],
    }
    state_gla = f(inp["state_gla"])
    maps = []
    for c in range(NCORES):
        b, j = c // 4, c % 4
        s0 = 16 * c
        m = dict(shared)
        xT = np.empty((128, KC, NT), np.float32)
        xT[:, :, :LP] = _fm(x_prompt[b, LP * j:LP * (j + 1)]).transpose(0, 2, 1)
        xT[:, :, LP:] = _fm(x_sample[s0:s0 + 16].reshape(NS, D)).transpose(0, 2, 1)
        m["xT"] = xT
        xh = np.zeros((128, KC, 32), np.float32)
        if j > 0:
            xh[:] = _fm(x_prompt[b, LP * j - 32:LP * j]).transpose(0, 2, 1)
        m["xh"] = xh
        cT = np.empty((128, KC, 129), np.float32)
        cT[:, :, :NS] = _fm(np.repeat(c_sample[s0:s0 + 16], 8, axis=0)).transpose(0, 2, 1)
        cT[:, :, NS] = _fm(c_prompt[b])
        m["cT"] = cT
        vecs = np.zeros((128, NV), np.float32)

        def put(name, arr):
            off, w = VLAY[name]
            vecs[:, off:off + w] = np.asarray(arr, np.float32).reshape(128, w)
        put("g_pre", _fm(inp["g_pre"]))
        put("g_post", _fm(inp["g_post"]))
        put("b_ada", _fm(inp["b_ada"]))
        put("b_dw", _fm(inp["b_dw"]))
        put("g_cln", _fm(inp["g_conv_ln"]))
        put("b_cln", _fm(inp["b_conv_ln"]))
        put("w_dw", _fm(inp["w_dw"]).transpose(0, 1, 3, 2))
        put("hflag", np.full((128, 1), 0.0 if j == 0 else 1.0))
        put("eps", np.full((128, 1), EPS))
        put("one", np.full((128, 1), 1.0))
        selv = np.zeros((128, 4), np.float32)
        selv[:, j] = 1.0
        put("sel", selv)
        put("g_gn", _fm(inp["g_gla_norm"][0]))
        put("ident", np.eye(128, dtype=np.float32))
        m["vecs"] = vecs
        sc = state_conv[:, s0:s0 + 16]
        m["sc_fm"] = np.ascontiguousarray(_fm(sc).transpose(0, 1, 4, 2, 3))
        m["sc_old"] = np.ascontiguousarray(sc[:, :, 8:, :])
        ac = _att_consts()
        pos = np.concatenate([LP * j + np.arange(LP), 2048 + (np.arange(NS) % 8)]).astype(np.float64)
        inv = 10000.0 ** (-np.arange(32, dtype=np.float64) / 32)
        dd_ = np.arange(128) % 64
        ang = pos[None, :] * inv[dd_ % 32][:, None]
        sgn = np.where(dd_ < 32, -1.0, 1.0)[:, None]
        m["rope"] = np.stack([np.cos(ang), sgn * np.sin(ang)], axis=1).astype(np.float32)
        am = np.stack([ac["mprev"], ac["mprev"] if j > 0 else np.full((128, 128), NEG, np.float32), ac["mcur"]], axis=1)
        m["amask"] = np.ascontiguousarray(np.tile(am, (1, 1, 4)))
        pred = max(j - 1, 0)
        m["idxk"] = (pred * 128 + np.arange(128, dtype=np.int32)).reshape(128, 1).astype(np.int32)
        m["idxv"] = (pred * 896 + np.arange(7, dtype=np.int32)[None, :] * 128
                     + np.arange(128, dtype=np.int32)[:, None]).astype(np.int32)
        caches_k = (inp["cache_k_g0"], inp["cache_k_g1"], inp["cache_k_g2"])
        caches_v = (inp["cache_v_g0"], inp["cache_v_g1"], inp["cache_v_g2"])
        for g in range(3 if (NLAYERS >= 3 and _en('S')) else 0):
            m["ck%d" % g] = np.ascontiguousarray(f(caches_k[g])[0, s0:s0 + 16].reshape(16, -1, 512))
            m["cv%d" % g] = np.ascontiguousarray(f(caches_v[g])[0, s0:s0 + 16].reshape(16, -1, 512))
        m["sgla"] = np.ascontiguousarray(state_gla[0, s0:s0 + 16].transpose(2, 0, 1, 3))
        maps.append(m)
    return maps


NEG = -30000.0
_AC = {}


def _att_consts():
    if _AC:
        return _AC
    p = np.arange(128)
    _AC["mprev"] = np.where(p[:, None] >= p[None, :], 0.0, NEG).astype(np.float32)
    _AC["mcur"] = np.where(p[:, None] <= p[None, :], 0.0, NEG).astype(np.float32)
    m = np.arange(128)
    partner = np.where((m % 64) < 32, m + 32, m - 32)
    pm = np.zeros((128, 128), np.float32)
    pm[partner, m] = 1.0
    _AC["pm"] = pm
    n = np.arange(128)[:, None]
    i = np.arange(8)[None, :]
    sm = np.zeros((128, 13, 8), bool)
    sm[:, 0] = n >= i
    for r in range(4):
        sm[:, 1 + r] = ((i % 4) == r) & ~((i >= 4) & (n == 0))
    for r in range(8):
        sm[:, 5 + r] = (i == r) & (n >= 0)
    smf = np.where(sm, 0.0, NEG).astype(np.float32)
    _AC["smask"] = np.ascontiguousarray(np.tile(smf, (1, 1, 8)))
    kk = np.arange(128)[:, None]
    qq = np.arange(128)[None, :]
    same = (kk // 8) == (qq // 8)
    ki, qi = kk % 8, qq % 8
    sn = np.stack([same & (ki <= qi), same & ((ki == qi) | (ki == qi - 4)), same & (ki == qi)], axis=1)
    _AC["snew"] = np.ascontiguousarray(np.tile(np.where(sn, 0.0, NEG).astype(np.float32), (1, 1, 4)))
    return _AC


def _gla_masks():
    j = np.arange(128)
    same = (j[:, None] // 8) == (j[None, :] // 8)
    le = j[:, None] <= j[None, :]
    gt = j[:, None] > j[None, :]
    gm = np.zeros((128, 2, 3, 128), np.float32)
    gm[:, 0, 0] = np.where(le, -1.0 / 16, 0.0)
    gm[:, 0, 1] = np.where(gt, -1.0 / 16, 0.0)
    gm[:, 0, 2] = np.where(le, 1.0, 0.0)
    gm[:, 1, 0] = np.where(le & same, -1.0 / 16, 0.0)
    gm[:, 1, 1] = np.where(gt & same, -1.0 / 16, 0.0)
    gm[:, 1, 2] = np.where(le & same, 1.0, 0.0)
    seg = np.zeros((128, 2, 2, 16), np.float32)
    seg[:, 0, 0, 0] = -1.0 / 16
    seg[:, 0, 1, 0] = 1.0
    inseg = (j[:, None] // 8) == np.arange(16)[None, :]
    seg[:, 1, 0] = np.where(inseg, -1.0 / 16, 0.0)
    seg[:, 1, 1] = np.where(inseg, 1.0, 0.0)
    return gm, seg


def _tm(a):
    return np.ascontiguousarray(a.transpose(2, 1, 0).reshape(a.shape[2], -1))


_NC_CACHE = {}


def kernel(**inputs):
    if "nc" not in _NC_CACHE:
        _NC_CACHE["nc"] = build_program()
    nc = _NC_CACHE["nc"]
    maps = _prep_inputs(inputs)
    res = run_bass_kernel_spmd(nc, maps, core_ids=list(range(NCORES)))
    R = res.results
    B, DB, DS = 2, 128, 8
    y_prompt = np.zeros((B, SEQ, D), np.float32)
    y_sample = np.zeros((DB, DS, D), np.float32)
    conv_p = np.zeros((2, B, 30, D), np.float32)
    conv_s = np.zeros((2, DB, 30, D), np.float32)
    gla_p = np.zeros((1, B, 4, 128, 256), np.float32)
    gla_s = np.zeros((1, DB, 4, 128, 256), np.float32)
    kv_p = [np.zeros((1, B, w, 8, 64), np.float32) for w in (128, 128, 512, 512, 2048, 2048)]
    kv_s = [np.zeros((1, DB, DS, 8, 64), np.float32) for _ in range(6)]
    for c in range(NCORES):
        b, j = c // 4, c % 4
        s0 = 16 * c
        r = R[c]
        yt = _tm(r["yT"])
        y_prompt[b, LP * j:LP * (j + 1)] = yt[:LP]
        y_sample[s0:s0 + 16] = yt[LP:].reshape(16, 8, D)
        for jl in range(2):
            if j == 3:
                conv_p[jl, b] = _tm(r["conv_tail"][jl])[2:]
            conv_s[jl, s0:s0 + 16, :22] = r["conv_old"][jl]
            conv_s[jl, s0:s0 + 16, 22:] = _tm(r["conv_new"][jl]).reshape(16, 8, D)
        if j == 3:
            gla_p[0, b] = r["gla_p"].transpose(1, 0, 2)
        gla_s[0, s0:s0 + 16] = r["gla_s"].transpose(1, 2, 0, 3)
        if "kout" in r:
            for g, dd in enumerate((1, 4, 16)):
                kbase = (0, 512, 2560)[g]
                vbase = (0, 128, 640)[g]
                if j == 3:
                    blk = r["kout"][:, kbase:kbase + 4 * dd * 128].reshape(2, 64, 4, dd, 128)
                    kv_p[2 * g][0, b] = blk.transpose(4, 3, 2, 0, 1).reshape(128 * dd, 8, 64)
                    vblk = r["vout"][vbase:vbase + dd * 128].reshape(dd, 128, 512)
                    kv_p[2 * g + 1][0, b] = vblk.transpose(1, 0, 2).reshape(128 * dd, 8, 64)
                ks = r["ks_out"].reshape(2, 64, 3, 4, 16, 8)[:, :, g]
                kv_s[2 * g][0, s0:s0 + 16] = ks.transpose(3, 4, 2, 0, 1).reshape(16, 8, 8, 64)
                kv_s[2 * g + 1][0, s0:s0 + 16] = r["vs_out"].reshape(16, 8, 3, 8, 64)[:, :, g]
    return (y_prompt, y_sample, conv_p, conv_s, gla_p, gla_s, *kv_p, *kv_s)
```

```python
import numpy as np
from contextlib import ExitStack
import concourse.bass as bass
import concourse.mybir as mybir
from concourse.bass_utils import run_bass_kernel_spmd

F32 = mybir.dt.float32
BF16 = mybir.dt.bfloat16
I32 = mybir.dt.int32
AF = mybir.ActivationFunctionType
ALU = mybir.AluOpType
AX = mybir.AxisListType

NCORES = 8
D = 1024
KC = 8
LP = 2048
NS = 128
NT = LP + NS
SEQ = 8192
DEPTH = 4
EPS = 1e-6
NQ = 16
TN = 256
import os
NLAYERS = int(os.environ.get('NLAYERS', '4'))
ATT_EN = os.environ.get('ATT_EN', 'AXSHB')


def _en(x):
    return x in ATT_EN


A_PARTS = int(os.environ.get('A_PARTS', '31'))
QK_G = [int(c) for c in os.environ.get('QK_G', '012')]
QK_DMA = int(os.environ.get('QK_DMA', '7'))
QK_STEPS = int(os.environ.get('QK_STEPS', '9'))
B_STEPS = int(os.environ.get('B_STEPS', '9'))


class Tk:
    __slots__ = ("name", "w", "r", "multi", "wm", "psum")

    def __init__(self, name, multi=False, psum=False):
        self.name = name
        self.psum = psum
        self.w = None
        self.r = {}
        self.multi = multi
        self.wm = {}


class KB:
    def __init__(self, nc, es):
        self.nc = nc
        self.E = {"pe": nc.tensor, "act": nc.scalar, "dve": nc.vector, "pool": nc.gpsimd, "sp": nc.sync}
        self.sem = {k: es.enter_context(nc.semaphore("s_" + k)) for k in self.E}
        self.cnt = {k: 0 for k in self.E}
        self.seen = {k: {} for k in self.E}
        self.pend = {k: [] for k in self.E}
        self.dsem = {q: [es.enter_context(nc.semaphore("d_%s%d" % (q, i))) for i in range(NQ)]
                     for q in ("sp", "pool")}
        self.dcnt = {}
        for q in self.dsem:
            for s in self.dsem[q]:
                self.dcnt[s.name] = 0
        self.dnext = {q: 0 for q in self.dsem}
        self.ccsem = es.enter_context(nc.semaphore("cc_sem"))
        self.ccn = 0
        self.semobj = {self.ccsem.name: self.ccsem}
        for s in self.sem.values():
            self.semobj[s.name] = s
        for q in self.dsem:
            for s in self.dsem[q]:
                self.semobj[s.name] = s

    def _waits(self, e, reads, writes):
        need = {}

        def add(tok):
            if tok is None:
                return
            n, v = tok
            if need.get(n, 0) < v:
                need[n] = v
        own = self.sem[e].name if e in self.sem else None
        for t in reads:
            if t.multi:
                for n, v in t.wm.items():
                    add((n, v))
            else:
                add(t.w)
            if t.psum:
                for n, v in t.r.items():
                    if n != own:
                        add((n, v))
        for t in writes:
            if not t.multi:
                add(t.w)
            for n, v in t.r.items():
                add((n, v))
        for n, v in need.items():
            if self.seen[e].get(n, 0) >= v:
                continue
            if e == "pe" and n == self.sem["pe"].name:
                continue
            self.E[e].wait_ge(self.semobj[n], v)
            self.seen[e][n] = v

    def _record(self, tok, reads, writes):
        n, v = tok
        for t in reads:
            if t.r.get(n, 0) < v:
                t.r[n] = v
        for t in writes:
            if t.multi:
                if t.wm.get(n, 0) < v:
                    t.wm[n] = v
            else:
                t.w = tok
                t.r = {}

    def op(self, e, fn, reads=(), writes=(), inc=True):
        self._waits(e, reads, writes)
        ins = fn(self.E[e])
        if not inc:
            self.pend[e].append((tuple(reads), tuple(writes)))
            return
        self.cnt[e] += 1
        ins.then_inc(self.sem[e], 1)
        tok = (self.sem[e].name, self.cnt[e])
        for (r, w) in self.pend[e]:
            self._record(tok, r, w)
        self.pend[e] = []
        self._record(tok, reads, writes)

    def dma(self, q, out, in_, reads=(), writes=(), **kw):
        i = self.dnext[q]
        self.dnext[q] = (i + 1) % NQ
        s = self.dsem[q][i]
        prev = self.dcnt[s.name]
        if prev and self.seen[q].get(s.name, 0) < prev:
            self.E[q].wait_ge(s, prev)
            self.seen[q][s.name] = prev
        self._waits(q, reads, writes)
        self.E[q].dma_start(out=out, in_=in_, **kw).then_inc(s, 16)
        self.dcnt[s.name] = prev + 16
        self._record((s.name, prev + 16), reads, writes)

    def collective(self, fn, reads=(), writes=()):
        self._waits("pool", reads, writes)
        self.ccn += 1
        fn(self.E["pool"]).then_inc(self.ccsem, 1)
        self._record((self.ccsem.name, self.ccn), reads, writes)

    def dma_custom(self, q, fn, reads=(), writes=()):
        i = self.dnext[q]
        self.dnext[q] = (i + 1) % NQ
        s = self.dsem[q][i]
        prev = self.dcnt[s.name]
        if prev and self.seen[q].get(s.name, 0) < prev:
            self.E[q].wait_ge(s, prev)
            self.seen[q][s.name] = prev
        self._waits(q, reads, writes)
        fn(self.E[q]).then_inc(s, 16)
        self.dcnt[s.name] = prev + 16
        self._record((s.name, prev + 16), reads, writes)

    def barrier_on(self, k):
        n, v = self.sem[k].name, self.cnt[k]
        for e in self.E:
            if e != k and v and self.seen[e].get(n, 0) < v:
                self.E[e].wait_ge(self.sem[k], v)
                self.seen[e][n] = v

    def barrier(self):
        for e in self.E:
            for n, s in self.semobj.items():
                if n in self.dcnt:
                    v = self.dcnt[n]
                elif n == self.ccsem.name:
                    v = self.ccn
                else:
                    k = [kk for kk in self.sem if self.sem[kk].name == n][0]
                    if k == e:
                        continue
                    v = self.cnt[k]
                if v and self.seen[e].get(n, 0) < v:
                    self.E[e].wait_ge(s, v)
                    self.seen[e][n] = v

    def final_wait(self):
        e = "sp"
        for n, v in self.dcnt.items():
            if v and self.seen[e].get(n, 0) < v:
                self.E[e].wait_ge(self.semobj[n], v)
                self.seen[e][n] = v


def _vec_layout():
    lay = {}
    off = 0

    def add(name, n):
        nonlocal off
        lay[name] = (off, n)
        off += n
    add("g_pre", 4 * 8)
    add("g_post", 4 * 8)
    add("b_ada", 4 * 24)
    add("b_dw", 2 * 8)
    add("g_cln", 2 * 8)
    add("b_cln", 2 * 8)
    add("w_dw", 2 * 8 * 31)
    add("hflag", 1)
    add("eps", 1)
    add("one", 1)
    add("sel", 4)
    add("g_gn", 2)
    add("ident", 128)
    return lay, off


VLAY, NV = _vec_layout()


def _fm(v):
    v = np.asarray(v, np.float32)
    n = v.shape[-1] // 128
    r = v.reshape(v.shape[:-1] + (n, 128))
    return np.moveaxis(r, -1, 0)


def build_program():
    nc = bass.Bass("TRN2", target_bir_lowering=False)
    es = ExitStack()
    kb = KB(nc, es)

    def din(name, shape, dt=F32):
        return nc.dram_tensor(name, list(shape), dt, kind="ExternalInput").ap()

    def dout(name, shape, dt=F32):
        return nc.dram_tensor(name, list(shape), dt, kind="ExternalOutput").ap()

    xT_in = din("xT", [128, KC, NT])
    xh_in = din("xh", [128, KC, 32])
    cT_in = din("cT", [128, KC, 129])
    vecs_in = din("vecs", [128, NV])
    w_ada_in = din("w_ada", [DEPTH, D, 3 * D])
    w_conv_in_in = din("w_conv_in", [2, D, 3 * D])
    w_conv_out_in = din("w_conv_out", [2, D, D])
    sc_fm_in = din("sc_fm", [128, 2, KC, 16, 30])
    sc_old_in = din("sc_old", [2, 16, 22, D])
    w_gla_in_in = din("w_gla_in", [D, 3 * D])
    w_gla_out_in = din("w_gla_out", [D, D])
    w_a1_in = din("w_a1", [D, 16])
    w_a2_in = din("w_a2", [16, 512])
    b_a_in = din("b_a", [1, 512])
    gmask_in = din("gmask", [128, 2, 3, 128])
    gseg_in = din("gseg", [128, 2, 2, 16])
    sgla_in = din("sgla", [128, 16, 4, 256])
    w_att_in_in = din("w_att_in", [D, 5120])
    w_att_out_in = din("w_att_out", [512, D])
    rope_in = din("rope", [128, 2, NT])
    pm_in = din("pm", [128, 128])
    amask_in = din("amask", [128, 3, 512])
    smask_in = din("smask", [128, 13, 64])
    snew_in = din("snew", [128, 3, 512])
    idxk_in = din("idxk", [128, 1], I32)
    idxv_in = din("idxv", [128, 7], I32)
    if NLAYERS >= 3 and _en('S'):
        ck_in = [din("ck%d" % g, [16, w, 512]) for g, w in enumerate((128, 512, 2048))]
        cv_in = [din("cv%d" % g, [16, w, 512]) for g, w in enumerate((128, 512, 2048))]
    kout = dout("kout", [128, 10752])
    vout = dout("vout", [2688, 512])
    ks_out = dout("ks_out", [128, 12, NS])
    vs_out = dout("vs_out", [NS, 1536])
    gla_p_out = dout("gla_p", [128, 4, 256])
    gla_s_out = dout("gla_s", [128, 16, 4, 256])
    yT_out = dout("yT", [128, KC, NT])
    conv_tail_out = dout("conv_tail", [2, 128, KC, 32])
    conv_new_out = dout("conv_new", [2, 128, KC, NS])
    conv_old_out = dout("conv_old", [2, 16, 22, D])
    xs = nc.dram_tensor("xs", [128, KC, NT], F32)

    uid = [0]

    def S(name, shape, dt, stack=es):
        uid[0] += 1
        return stack.enter_context(nc.sbuf_tensor("sb_%s_%d" % (name, uid[0]), list(shape), dt))

    vecs = S("vecs_sb", [128, NV], F32)
    ones_bf = S("ones_bf", [128, 128], BF16)
    ident_bf = S("ident_bf", [128, 128], BF16)
    cT_bf = S("cT_bf", [128, KC, 129], BF16)
    modT = S("modT", [128, 24, 129], F32)
    modp = S("modp", [128, 3, 8], F32)
    mods = S("mods", [128, 2, 8, NS], F32)
    t_vecs, t_const, t_cT, t_modT, t_modd = Tk("vecs"), Tk("const"), Tk("cT"), Tk("modT"), Tk("modd")
    ps = [es.enter_context(nc.psum_tensor("ps%d" % i, [128, 512], F32)) for i in range(8)]
    t_ps = [Tk("ps%d" % i, psum=True) for i in range(8)]
    psn = [0]
    ps_reserved = set()

    def next_ps():
        while True:
            i = psn[0]
            psn[0] = (i + 1) % 8
            if i not in ps_reserved:
                return ps[i], t_ps[i]

    def V(name, i0=0, n=None):
        off, w = VLAY[name]
        if n is None:
            n = w - i0
        return vecs[:, off + i0: off + i0 + n]

    kb.dma("sp", vecs[:, :], vecs_in[:, :], writes=[t_vecs])
    kb.dma("pool", cT_bf[:, :, :], cT_in[:, :, :], writes=[t_cT])
    kb.op("dve", lambda e: e.memset(ones_bf[:, :], 1.0), writes=[t_const])
    kb.op("dve", lambda e: e.tensor_copy(out=ident_bf[:, :], in_=V("ident")), reads=[t_vecs], writes=[t_const])

    t_x = {}

    def xtk(key):
        if key not in t_x:
            t_x[key] = Tk("x%s" % (key,))
        return t_x[key]

    TILES = [(i * TN, TN) for i in range(LP // TN)] + [(LP, NS)]

    def compute_mod(l, ls, after_issue=None):
        wsrc = w_ada_in[l].rearrange("(kc p) n -> p kc n", p=128)
        ws = ExitStack()
        wb = [S("wada%d" % i, [128, KC, 512], BF16, ws) for i in range(6)]
        t_wb = [Tk("wada%d" % i) for i in range(6)]
        for blk in range(6):
            kb.dma("pool", wb[blk][:, :, :], wsrc[:, :, blk * 512:(blk + 1) * 512], writes=[t_wb[blk]])
        if after_issue is not None:
            after_issue()
        for blk in range(6):
            b = blk
            for mm in range(4):
                m = blk * 4 + mm
                p, tp = next_ps()
                for kc in range(KC):
                    kb.op("pe", lambda e, p=p, b=b, mm=mm, kc=kc: e.matmul(
                        p[:, 0:129], lhsT=wb[b][:, kc, mm * 128:(mm + 1) * 128], rhs=cT_bf[:, kc, :],
                        start=(kc == 0), stop=(kc == KC - 1)),
                        reads=[t_wb[b], t_cT], writes=[tp], inc=(kc == KC - 1))
                kb.op("act", lambda e, p=p, m=m: e.activation(
                    out=modT[:, m, :], in_=p[:, 0:129], func=AF.Identity,
                    bias=V("b_ada", l * 24 + m, 1), scale=1.0),
                    reads=[tp, t_vecs], writes=[t_modT])
        gpre = V("g_pre", l * 8, 8)
        gpost = V("g_post", l * 8, 8)
        kb.op("dve", lambda e: e.scalar_tensor_tensor(
            out=modp[:, 0, :], in0=modT[:, 8:16, 128], scalar=1.0, in1=gpre, op0=ALU.add, op1=ALU.mult),
            reads=[t_modT, t_vecs], writes=[t_modd])
        kb.op("dve", lambda e: e.tensor_copy(out=modp[:, 1, :], in_=modT[:, 0:8, 128]),
              reads=[t_modT], writes=[t_modd])
        kb.op("dve", lambda e: e.tensor_tensor(
            out=modp[:, 2, :], in0=modT[:, 16:24, 128], in1=gpost, op=ALU.mult),
            reads=[t_modT, t_vecs], writes=[t_modd])
        kb.op("dve", lambda e: e.scalar_tensor_tensor(
            out=mods[:, 0, :, :], in0=modT[:, 8:16, 0:NS], scalar=1.0,
            in1=gpre.unsqueeze(2).broadcast_to([128, 8, NS]), op0=ALU.add, op1=ALU.mult),
            reads=[t_modT, t_vecs], writes=[t_modd])
        kb.op("dve", lambda e: e.tensor_tensor(
            out=mods[:, 1, :, :], in0=modT[:, 16:24, 0:NS],
            in1=gpost.unsqueeze(2).broadcast_to([128, 8, NS]), op=ALU.mult),
            reads=[t_modT, t_vecs], writes=[t_modd])
        kb.barrier_on("pe")
        ws.close()

    class Bufs:
        pass

    def rstd_from_ps(p, tp, n, out, t_out, scale=1.0 / D):
        kb.op("act", lambda e: e.activation(out=out[:, 0:n], in_=p[:, 0:n], func=AF.Sqrt, bias=V("eps"), scale=scale),
              reads=[tp, t_vecs], writes=[t_out])
        kb.op("dve", lambda e: e.reciprocal(out=out[:, 0:n], in_=out[:, 0:n]), reads=[t_out], writes=[t_out])

    def prenorm(B, xt, t_xt, n, sample, hout=None, t_hout=None):
        if hout is None:
            hout, t_hout = B.hT[:, :, 0:n], B.t_hT
        kb.op("act", lambda e: e.activation(out=B.sq[:, :, 0:n], in_=xt[:, :, 0:n], func=AF.Square),
              reads=[t_xt], writes=[B.t_sq])
        p, tp = next_ps()
        for kc in range(KC):
            kb.op("pe", lambda e, kc=kc: e.matmul(p[:, 0:n], lhsT=ones_bf[:, :], rhs=B.sq[:, kc, 0:n],
                                                   start=(kc == 0), stop=(kc == KC - 1)),
                  reads=[B.t_sq, t_const], writes=[tp], inc=(kc == KC - 1))
        rstd_from_ps(p, tp, n, B.rstd, B.t_rstd)
        kb.op("dve", lambda e: e.tensor_tensor(
            out=B.t1[:, :, 0:n], in0=xt[:, :, 0:n],
            in1=B.rstd[:, 0:n].unsqueeze(1).broadcast_to([128, KC, n]), op=ALU.mult),
            reads=[t_xt, B.t_rstd], writes=[B.t_t1])
        if not sample:
            for kc in range(KC):
                kb.op("act", lambda e, kc=kc: e.activation(
                    out=hout[:, kc, :], in_=B.t1[:, kc, 0:n], func=AF.Identity,
                    bias=modp[:, 1, kc:kc + 1], scale=modp[:, 0, kc:kc + 1]),
                    reads=[B.t_t1, t_modd], writes=[t_hout])
        else:
            kb.op("pool", lambda e: e.tensor_tensor(out=B.t1[:, :, 0:n], in0=B.t1[:, :, 0:n],
                                                    in1=mods[:, 0, :, :], op=ALU.mult),
                  reads=[B.t_t1, t_modd], writes=[B.t_t1])
            kb.op("pool", lambda e: e.tensor_tensor(out=hout, in0=B.t1[:, :, 0:n],
                                                    in1=modT[:, 0:8, 0:NS], op=ALU.add),
                  reads=[B.t_t1, t_modT], writes=[t_hout])

    def postnorm_residual(B, xt, t_xt, n, sample, l):
        p, tp = next_ps()
        for kc in range(KC):
            kb.op("pe", lambda e, kc=kc: e.matmul(p[:, 0:n], lhsT=ones_bf[:, :], rhs=B.sq[:, kc, 0:n],
                                                   start=(kc == 0), stop=(kc == KC - 1)),
                  reads=[B.t_sq, t_const], writes=[tp], inc=(kc == KC - 1))
        rstd_from_ps(p, tp, n, B.rstd, B.t_rstd)
        kb.op("dve", lambda e: e.tensor_tensor(
            out=B.oT[:, :, 0:n], in0=B.oT[:, :, 0:n],
            in1=B.rstd[:, 0:n].unsqueeze(1).broadcast_to([128, KC, n]), op=ALU.mult),
            reads=[B.t_oT, B.t_rstd], writes=[B.t_oT])
        if not sample:
            for kc in range(KC):
                kb.op("dve", lambda e, kc=kc: e.scalar_tensor_tensor(
                    out=xt[:, kc, 0:n], in0=B.oT[:, kc, 0:n], scalar=modp[:, 2, kc:kc + 1],
                    in1=xt[:, kc, 0:n], op0=ALU.mult, op1=ALU.add),
                    reads=[B.t_oT, t_modd, t_xt], writes=[t_xt])
        else:
            kb.op("pool", lambda e: e.tensor_tensor(out=B.oT[:, :, 0:n], in0=B.oT[:, :, 0:n],
                                                    in1=mods[:, 1, :, :], op=ALU.mult),
                  reads=[B.t_oT, t_modd], writes=[B.t_oT])
            kb.op("pool", lambda e: e.tensor_tensor(out=xt[:, :, 0:n], in0=xt[:, :, 0:n],
                                                    in1=B.oT[:, :, 0:n], op=ALU.add),
                  reads=[B.t_oT, t_xt], writes=[t_xt])

    def common_bufs(ls, with_hT=True, nxt=1):
        B = Bufs()
        B.xt = [S("xt%d" % i, [128, KC, TN], F32, ls) for i in range(nxt)]
        B.t_xt = [Tk("xt%d" % i) for i in range(nxt)]
        B.sq = S("sq", [128, KC, TN], BF16, ls)
        B.t_sq = Tk("sq")
        B.rstd = S("rstd", [128, TN], F32, ls)
        B.t_rstd = Tk("rstd")
        B.oT = S("oT", [128, KC, TN], F32, ls)
        B.t_oT = Tk("oT")
        B.t1 = B.oT
        B.t_t1 = B.t_oT
        if with_hT:
            B.hT = S("hT", [128, KC, TN], BF16, ls)
            B.t_hT = Tk("hT")
        return B

    def load_w(ls, name, src2d, ncols, blk=512, issue=True):
        w = S(name, [128, KC, ncols], BF16, ls)
        src = src2d.rearrange("(kc p) n -> p kc n", p=128)
        tks = [Tk("%s_%d" % (name, b0)) for b0 in range(0, ncols, blk)]

        def do_issue():
            for i, b0 in enumerate(range(0, ncols, blk)):
                kb.dma("pool", w[:, :, b0:b0 + blk], src[:, :, b0:b0 + blk], writes=[tks[i]])
        if issue:
            do_issue()
            return w, tks, blk
        return w, tks, blk, do_issue

    def x_src(l):
        return xT_in if l == 0 else xs

    def x_dst(l):
        return yT_out if l == NLAYERS - 1 else xs

    def layer_conv(l, jl):
        ls = ExitStack()
        w_in, t_win, wblk, iss1 = load_w(ls, "w_in", w_conv_in_in[jl], 3 * D, issue=False)
        w_out, t_wout, _, iss2 = load_w(ls, "w_out", w_conv_out_in[jl], D, issue=False)
        compute_mod(l, ls, lambda: (iss1(), iss2()))
        B = common_bufs(ls, with_hT=False, nxt=2)
        SN = 512
        ub = [S("ub%d" % i, [128, KC, 32 + SN], BF16, ls) for i in range(2)]
        hT5 = S("hT5", [128, KC, SN], BF16, ls)
        t_hT5 = Tk("hT5")
        t_ub = [Tk("ub%d" % i) for i in range(2)]
        ues = S("ues", [128, KC, 16, 38], BF16, ls)
        t_ues = Tk("ues")
        sg = S("sg", [128, SN], F32, ls)
        t_sg = Tk("sg")
        sz = S("sz", [128, KC, SN], BF16, ls)
        t_sz = Tk("sz")
        NDG = 3
        Dg = [S("Dg%d" % i, [128, 31, 128], BF16, ls) for i in range(NDG)]
        t_Dg = [Tk("Dg%d" % i) for i in range(NDG)]
        t_DgB = [Tk("DgB%d" % i) for i in range(NDG)]
        yT = S("yT", [128, KC, TN], F32, ls)
        t_yT = Tk("yT")
        ybf = S("ybf", [128, KC, TN], BF16, ls)
        t_ybf = Tk("ybf")
        mean = S("mean", [128, TN], F32, ls)
        t_mean = Tk("mean")
        var, t_var = B.rstd, B.t_rstd
        yg, t_yg = ybf, t_ybf
        u32 = S("u32", [128, KC, NS], F32, ls)
        t_u32 = Tk("u32")
        xh = S("xh", [128, KC, 32], F32, ls)
        t_xh = Tk("xh")
        dcnt = [0]

        def wtk(col0):
            return t_win[col0 // wblk]

        def inproj_u(n, utarget, t_ut, u32cols=None, r3=False, hsrc=None, t_hsrc=None):
            if hsrc is None:
                hsrc, t_hsrc = B.hT, B.t_hT
            def vw(ap):
                return ap.rearrange("p (s i) -> p s i", i=8) if r3 else ap
            for c in range(KC):
                pa, tpa = next_ps()
                pg, tpg = next_ps()
                for kc in range(KC):
                    kb.op("pe", lambda e, kc=kc, c=c, pa=pa: e.matmul(
                        pa[:, 0:n], lhsT=w_in[:, kc, c * 128:(c + 1) * 128], rhs=hsrc[:, kc, 0:n],
                        start=(kc == 0), stop=(kc == KC - 1)),
                        reads=[wtk(c * 128), t_hsrc], writes=[tpa], inc=(kc == KC - 1))
                for kc in range(KC):
                    kb.op("pe", lambda e, kc=kc, c=c, pg=pg: e.matmul(
                        pg[:, 0:n], lhsT=w_in[:, kc, D + c * 128:D + (c + 1) * 128], rhs=hsrc[:, kc, 0:n],
                        start=(kc == 0), stop=(kc == KC - 1)),
                        reads=[wtk(D + c * 128), t_hsrc], writes=[tpg], inc=(kc == KC - 1))
                kb.op("act", lambda e, pg=pg: e.activation(out=sg[:, 0:n], in_=pg[:, 0:n], func=AF.Sigmoid),
                      reads=[tpg], writes=[t_sg])
                kb.op("dve", lambda e, c=c, pa=pa: e.tensor_tensor(out=utarget(c), in0=vw(pa[:, 0:n]),
                                                                   in1=vw(sg[:, 0:n]), op=ALU.mult),
                      reads=[tpa, t_sg], writes=[t_ut])
                if u32cols is not None:
                    c0, nn = u32cols
                    kb.op("dve", lambda e, c=c, pa=pa: e.tensor_tensor(
                        out=u32[:, c, 0:nn], in0=pa[:, c0:c0 + nn], in1=sg[:, c0:c0 + nn], op=ALU.mult),
                        reads=[tpa, t_sg], writes=[t_u32])

        if l == 0:
            kb.dma("sp", xh[:, :, :], xh_in[:, :, :], writes=[t_xh])
            prenorm(B, xh, t_xh, 32, False, hout=hT5[:, :, 0:32], t_hout=t_hT5)
            inproj_u(32, lambda c: ub[0][:, c, 0:32], t_ub[0], hsrc=hT5, t_hsrc=t_hT5)
            kb.op("dve", lambda e: e.tensor_tensor(
                out=ub[0][:, :, 0:32], in0=ub[0][:, :, 0:32],
                in1=V("hflag").unsqueeze(1).broadcast_to([128, KC, 32]), op=ALU.mult),
                reads=[t_ub[0], t_vecs], writes=[t_ub[0]])
        else:
            utl = S("utl", [128, KC, 32], BF16, ls)
            t_utl = Tk("utl")
            gsl = S("gsl", [128, 4, KC * 32], BF16, ls)
            t_gsl = Tk("gsl")
            gxc = nc.dram_tensor("cv_gx%d" % l, [128, KC * 32], BF16)
            ggc = nc.dram_tensor("cv_gg%d" % l, [512, KC * 32], BF16)
            t_gxc, t_ggc = Tk("gxc"), Tk("ggc")
            kb.dma("sp", xh[:, :, :], x_src(l)[:, :, LP - 32:LP], reads=[xtk(LP // TN - 1)], writes=[t_xh])
            prenorm(B, xh, t_xh, 32, False, hout=hT5[:, :, 0:32], t_hout=t_hT5)
            inproj_u(32, lambda c: utl[:, c, :], t_utl, hsrc=hT5, t_hsrc=t_hT5)
            kb.dma("sp", gxc[:, :], utl[:, :, :].rearrange("p c t -> p (c t)"), reads=[t_utl], writes=[t_gxc])
            kb.collective(lambda e: e.collective_compute(
                "AllGather", ALU.bypass, replica_groups=[[0, 1, 2, 3], [4, 5, 6, 7]],
                ins=[gxc[:, :]], outs=[ggc[:, :]]), reads=[t_gxc], writes=[t_ggc])
            kb.dma("sp", gsl[:, :, :], ggc.ap().rearrange("(r p) n -> p r n", p=128), reads=[t_ggc], writes=[t_gsl])
            ubv = ub[0][:, :, 0:32]

            def slot(i):
                return gsl[:, i, :].rearrange("p (c t) -> p c t", t=32)
            kb.op("dve", lambda e: e.tensor_scalar(out=ubv, in0=slot(0), scalar1=V("sel", 1, 1), scalar2=None,
                                                   op0=ALU.mult), reads=[t_gsl, t_vecs], writes=[t_ub[0]])
            for i in (1, 2):
                kb.op("dve", lambda e, i=i: e.scalar_tensor_tensor(
                    out=ubv, in0=slot(i), scalar=V("sel", i + 1, 1), in1=ubv, op0=ALU.mult, op1=ALU.add),
                    reads=[t_gsl, t_vecs, t_ub[0]], writes=[t_ub[0]])

        for kc in range(KC):
            kb.dma("pool", ues[:, kc, :, 0:30], sc_fm_in[:, jl, kc, :, :], writes=[t_ues])
        kb.dma("sp", conv_old_out[jl], sc_old_in[jl])

        NST = LP // SN
        for st in range(NST + 1):
            sample = (st == NST)
            ui = st % 2
            if not sample:
                halves = [(SN * st + TN * hf, TN, 2 * st + hf) for hf in range(SN // TN)]
                nn5 = SN
            else:
                halves = [(LP, NS, len(TILES) - 1)]
                nn5 = NS
            for hi, (c0, n, ti) in enumerate(halves):
                xt, t_xt = B.xt[hi], B.t_xt[hi]
                kb.dma("sp", xt[:, :, 0:n], x_src(l)[:, :, c0:c0 + n], reads=[xtk(ti)], writes=[t_xt])
                prenorm(B, xt, t_xt, n, sample, hout=hT5[:, :, hi * TN:hi * TN + n], t_hout=t_hT5)
            n = nn5
            if not sample:
                last = (st == NST - 1)
                inproj_u(n, lambda c: ub[ui][:, c, 32:32 + n], t_ub[ui],
                         u32cols=((n - 32, 32) if last else None), hsrc=hT5, t_hsrc=t_hT5)
                if last:
                    kb.dma("sp", conv_tail_out[jl], u32[:, :, 0:32], reads=[t_u32])
                if st + 1 < NST:
                    kb.op("pool", lambda e: e.tensor_copy(out=ub[1 - ui][:, :, 0:32], in_=ub[ui][:, :, n:n + 32]),
                          reads=[t_ub[ui]], writes=[t_ub[1 - ui]])
            else:
                inproj_u(n, lambda c: ues[:, c, :, 30:38],
                         t_ues, u32cols=(0, NS), r3=True, hsrc=hT5, t_hsrc=t_hT5)
                kb.dma("sp", conv_new_out[jl], u32[:, :, :], reads=[t_u32])
            for c in range(KC):
                pz, tpz = next_ps()
                for kc in range(KC):
                    kb.op("pe", lambda e, kc=kc, c=c, pz=pz: e.matmul(
                        pz[:, 0:n], lhsT=w_in[:, kc, 2 * D + c * 128:2 * D + (c + 1) * 128], rhs=hT5[:, kc, 0:n],
                        start=(kc == 0), stop=(kc == KC - 1)),
                        reads=[wtk(2 * D + c * 128), t_hT5], writes=[tpz], inc=(kc == KC - 1))
                kb.op("act", lambda e, c=c, pz=pz: e.activation(out=sz[:, c, 0:n], in_=pz[:, 0:n], func=AF.Silu),
                      reads=[tpz], writes=[t_sz])
            for hi, (c0, n, ti) in enumerate(halves):
                h0 = hi * TN
                xt, t_xt = B.xt[hi], B.t_xt[hi]
                tx = xtk(ti)
                NDV = 20

                def build_dg(c_):
                    di_ = dcnt[0] % NDG
                    dcnt[0] += 1
                    wd_ = V("w_dw", (jl * 8 + c_) * 31, 31)
                    kb.op("dve", lambda e: e.tensor_tensor(
                        out=Dg[di_][:, 0:NDV, :], in0=ident_bf[:, :].unsqueeze(1).broadcast_to([128, NDV, 128]),
                        in1=wd_[:, 0:NDV].unsqueeze(2).broadcast_to([128, NDV, 128]), op=ALU.mult),
                        reads=[t_const, t_vecs], writes=[t_Dg[di_]])
                    kb.op("pool", lambda e: e.tensor_tensor(
                        out=Dg[di_][:, NDV:31, :], in0=ident_bf[:, :].unsqueeze(1).broadcast_to([128, 31 - NDV, 128]),
                        in1=wd_[:, NDV:31].unsqueeze(2).broadcast_to([128, 31 - NDV, 128]), op=ALU.mult),
                        reads=[t_const, t_vecs], writes=[t_DgB[di_]])
                    return di_
                di_next = build_dg(0)
                for c in range(KC):
                    di = di_next
                    if c + 1 < KC:
                        di_next = build_dg(c + 1)
                    py, tpy = next_ps()
                    for k in range(31):
                        if not sample:
                            rhs = ub[ui][:, c, h0 + 2 + k:h0 + 2 + k + n]
                            rt = t_ub[ui]
                            outp = py[:, 0:n]
                        else:
                            rhs = ues[:, c, :, k:k + 8]
                            rt = t_ues
                            outp = py[:, 0:n].rearrange("p (s i) -> p s i", i=8)
                        kb.op("pe", lambda e, k=k, di=di, rhs=rhs, outp=outp: e.matmul(
                            outp, lhsT=Dg[di][:, k, :], rhs=rhs, start=(k == 0), stop=(k == 30)),
                            reads=[t_Dg[di] if k < 20 else t_DgB[di], rt], writes=[tpy], inc=(k == 30))
                    bdw = V("b_dw", jl * 8 + c, 1)
                    kb.op("act", lambda e, c=c, py=py, bdw=bdw: e.activation(
                        out=yT[:, c, 0:n], in_=py[:, 0:n], func=AF.Identity, bias=bdw, scale=1.0),
                        reads=[tpy, t_vecs], writes=[t_yT])
                    kb.op("act", lambda e, c=c, py=py, bdw=bdw: e.activation(
                        out=B.sq[:, c, 0:n], in_=py[:, 0:n], func=AF.Square, bias=bdw, scale=1.0),
                        reads=[tpy, t_vecs], writes=[B.t_sq])
                    kb.op("pool", lambda e, c=c: e.tensor_copy(out=ybf[:, c, 0:n], in_=yT[:, c, 0:n]),
                          reads=[t_yT], writes=[t_ybf])
                p1, tp1 = next_ps()
                p2, tp2 = next_ps()
                for c in range(KC):
                    kb.op("pe", lambda e, c=c: e.matmul(p1[:, 0:n], lhsT=ones_bf[:, :], rhs=ybf[:, c, 0:n],
                                                        start=(c == 0), stop=(c == KC - 1)),
                          reads=[t_ybf, t_const], writes=[tp1], inc=(c == KC - 1))
                for c in range(KC):
                    kb.op("pe", lambda e, c=c: e.matmul(p2[:, 0:n], lhsT=ones_bf[:, :], rhs=B.sq[:, c, 0:n],
                                                        start=(c == 0), stop=(c == KC - 1)),
                          reads=[B.t_sq, t_const], writes=[tp2], inc=(c == KC - 1))
                kb.op("dve", lambda e: e.tensor_scalar(out=mean[:, 0:n], in0=p1[:, 0:n], scalar1=1.0 / D, scalar2=None,
                                                       op0=ALU.mult), reads=[tp1], writes=[t_mean])
                kb.op("dve", lambda e: e.tensor_tensor(out=var[:, 0:n], in0=mean[:, 0:n], in1=mean[:, 0:n], op=ALU.mult),
                      reads=[t_mean], writes=[t_var])
                kb.op("dve", lambda e: e.scalar_tensor_tensor(
                    out=var[:, 0:n], in0=p2[:, 0:n], scalar=1.0 / D, in1=var[:, 0:n], op0=ALU.mult, op1=ALU.subtract),
                    reads=[tp2, t_var], writes=[t_var])
                kb.op("act", lambda e: e.activation(out=var[:, 0:n], in_=var[:, 0:n], func=AF.Sqrt, bias=V("eps"), scale=1.0),
                      reads=[t_var, t_vecs], writes=[t_var])
                kb.op("dve", lambda e: e.reciprocal(out=var[:, 0:n], in_=var[:, 0:n]), reads=[t_var], writes=[t_var])
                kb.op("dve", lambda e: e.tensor_tensor(
                    out=yT[:, :, 0:n], in0=yT[:, :, 0:n],
                    in1=mean[:, 0:n].unsqueeze(1).broadcast_to([128, KC, n]), op=ALU.subtract),
                    reads=[t_yT, t_mean], writes=[t_yT])
                kb.op("dve", lambda e: e.tensor_tensor(
                    out=yT[:, :, 0:n], in0=yT[:, :, 0:n],
                    in1=var[:, 0:n].unsqueeze(1).broadcast_to([128, KC, n]), op=ALU.mult),
                    reads=[t_yT, t_var], writes=[t_yT])
                for c in range(KC):
                    kb.op("act", lambda e, c=c: e.activation(
                        out=ybf[:, c, 0:n], in_=yT[:, c, 0:n], func=AF.Silu,
                        bias=V("b_cln", jl * 8 + c, 1), scale=V("g_cln", jl * 8 + c, 1)),
                        reads=[t_yT, t_vecs], writes=[t_ybf])
                kb.op("pool", lambda e: e.tensor_tensor(out=yg[:, :, 0:n], in0=ybf[:, :, 0:n], in1=sz[:, :, h0:h0 + n],
                                                        op=ALU.mult),
                      reads=[t_ybf, t_sz], writes=[t_yg])
                for m in range(KC):
                    po, tpo = next_ps()
                    for c in range(KC):
                        kb.op("pe", lambda e, c=c, m=m, po=po: e.matmul(
                            po[:, 0:n], lhsT=w_out[:, c, m * 128:(m + 1) * 128], rhs=yg[:, c, 0:n],
                            start=(c == 0), stop=(c == KC - 1)),
                            reads=[t_wout[m * 128 // 512], t_yg], writes=[tpo], inc=(c == KC - 1))
                    kb.op("act", lambda e, m=m, po=po: e.activation(out=B.oT[:, m, 0:n], in_=po[:, 0:n], func=AF.Identity),
                          reads=[tpo], writes=[B.t_oT])
                    kb.op("act", lambda e, m=m, po=po: e.activation(out=B.sq[:, m, 0:n], in_=po[:, 0:n], func=AF.Square),
                          reads=[tpo], writes=[B.t_sq])
                postnorm_residual(B, xt, t_xt, n, sample, l)
                kb.dma("sp", x_dst(l)[:, :, c0:c0 + n], xt[:, :, 0:n], reads=[t_xt], writes=[tx])
        kb.barrier()
        ls.close()


    def layer_gla(l, jl):
        ls = ExitStack()
        w_in, t_win, wblk, iss1 = load_w(ls, "wg_in", w_gla_in_in, 3 * D, issue=False)
        w_out, t_wout, _, iss2 = load_w(ls, "wg_out", w_gla_out_in, D, issue=False)
        compute_mod(l, ls, lambda: (iss1(), iss2()))
        B = common_bufs(ls)
        w_a1 = S("w_a1", [128, KC, 16], BF16, ls)
        t_wa = Tk("w_a")
        kb.dma("pool", w_a1[:, :, :], w_a1_in.rearrange("(kc p) n -> p kc n", p=128), writes=[t_wa])
        w_a2 = S("w_a2", [16, 512], BF16, ls)
        kb.dma("pool", w_a2[:, :], w_a2_in[:, :], writes=[t_wa])
        ba_row = S("ba_row", [1, 512], BF16, ls)
        kb.dma("pool", ba_row[:, :], b_a_in[:, :], writes=[t_wa])
        ones_row = S("ones_row", [1, 128], BF16, ls)
        kb.op("dve", lambda e: e.memset(ones_row[:, :], 1.0), writes=[t_wa])
        gm = S("gm", [128, 2, 3, 128], F32, ls)
        gseg = S("gseg", [128, 2, 2, 16], F32, ls)
        t_gm = Tk("gm")
        kb.dma("sp", gm[:, :, :, :], gmask_in[:, :, :, :], writes=[t_gm])
        kb.dma("sp", gseg[:, :, :, :], gseg_in[:, :, :, :], writes=[t_gm])

        def mk(name, shape, dt):
            return S(name, shape, dt, ls), Tk(name)
        qT, t_qT = mk("qT", [128, 4, TN], F32)
        kT, t_kT = mk("kT", [128, 4, TN], F32)
        rT, t_rT = mk("rT", [128, KC, TN], BF16)
        t1T, t_t1T = mk("t1T", [16, TN], BF16)
        cc2 = [dict(vtok=mk("vtok%d" % i, [128, 1024], BF16), la=mk("la%d" % i, [128, 512], F32),
                    ed=mk("ed%d" % i, [128, 512], F32), Kes=mk("Kes%d" % i, [128, 512], BF16),
                    ebt=mk("ebt%d" % i, [128, 4, 16], F32)) for i in range(2)]
        ccn = [0]
        vtok = t_vtok = la = t_la = ed = t_ed = Kes = t_Kes = ebt = t_ebt = None

        def use_cc():
            nonlocal vtok, t_vtok, la, t_la, ed, t_ed, Kes, t_Kes, ebt, t_ebt
            d = cc2[ccn[0] % 2]
            ccn[0] += 1
            (vtok, t_vtok), (la, t_la), (ed, t_ed), (Kes, t_Kes), (ebt, t_ebt) = (
                d["vtok"], d["la"], d["ed"], d["Kes"], d["ebt"])
        e1, t_e1 = mk("e1", [128, 4, 128], F32)
        e2, t_e2 = mk("e2", [128, 4, 128], F32)
        QeT, t_QeT = mk("QeT", [128, 4, 128], BF16)
        KeT, t_KeT = mk("KeT", [128, 4, 128], BF16)
        attm, t_attm = mk("attm", [128, 4, 128], BF16)
        go, t_go = mk("go", [128, KC, TN], F32)
        gsq, t_gsq = mk("gsq", [128, KC, TN], BF16)
        rsh, t_rsh = mk("rsh", [128, 4, 128], F32)
        Sst, t_S = mk("Sst", [128, 4, 256], F32)
        Sbf, t_Sbf = mk("Sbf", [128, 4, 256], BF16)
        Atot, t_Atot = mk("Atot", [128, 4], F32)
        big, t_big = mk("big16", [128, 4112], F32)
        accb, t_accb = mk("accb", [128, 4, 256], F32)
        S0bf, t_S0bf = mk("S0bf", [128, 4, 4, 256], BF16)
        Vblk, t_Vblk = mk("Vblk", [128, 4, 256], BF16)
        gx = nc.dram_tensor("gla_gx", [128, 1028], F32)
        gg = nc.dram_tensor("gla_gg", [512, 1028], F32)
        t_gx, t_gg = Tk("gx"), Tk("gg")
        DKS = 128.0 ** -0.5

        def wtk(col0):
            return t_win[col0 // wblk]

        def proj_fm(col0, n, evac):
            p, tp = next_ps()
            for kc in range(KC):
                kb.op("pe", lambda e, kc=kc: e.matmul(p[:, 0:n], lhsT=w_in[:, kc, col0:col0 + 128], rhs=B.hT[:, kc, 0:n],
                                                       start=(kc == 0), stop=(kc == KC - 1)),
                      reads=[wtk(col0), B.t_hT], writes=[tp], inc=(kc == KC - 1))
            evac(p, tp)

        def proj_tok(col0, c0):
            p, tp = next_ps()
            for kc in range(KC):
                kb.op("pe", lambda e, kc=kc: e.matmul(p[:, :], lhsT=B.hT[:, kc, c0:c0 + 128], rhs=w_in[:, kc, col0:col0 + 512],
                                                       start=(kc == 0), stop=(kc == KC - 1)),
                      reads=[wtk(col0), B.t_hT], writes=[tp], inc=(kc == KC - 1))
            return p, tp

        def tile_logarank(n):
            p, tp = next_ps()
            for kc in range(KC):
                kb.op("pe", lambda e, kc=kc: e.matmul(p[0:16, 0:n], lhsT=w_a1[:, kc, :], rhs=B.hT[:, kc, 0:n],
                                                       start=(kc == 0), stop=(kc == KC - 1)),
                      reads=[t_wa, B.t_hT], writes=[tp], inc=(kc == KC - 1))
            kb.op("act", lambda e: e.activation(out=t1T[:, 0:n], in_=p[0:16, 0:n], func=AF.Identity),
                  reads=[tp], writes=[t_t1T])

        def chunk_common(c0, mi, nseg):
            use_cc()
            pz, tpz = next_ps()
            kb.op("pe", lambda e: e.matmul(pz[:, :], lhsT=t1T[:, c0:c0 + 128], rhs=w_a2[:, :], start=True, stop=False),
                  reads=[t_t1T, t_wa], writes=[tpz], inc=False)
            kb.op("pe", lambda e: e.matmul(pz[:, :], lhsT=ones_row[:, :], rhs=ba_row[:, :], start=False, stop=True),
                  reads=[t_wa], writes=[tpz])
            kb.op("act", lambda e: e.activation(out=la[:, :], in_=pz[:, :], func=AF.Exp, scale=-1.0),
                  reads=[tpz], writes=[t_la])
            kb.op("act", lambda e: e.activation(out=la[:, :], in_=la[:, :], func=AF.Ln, bias=V("one"), scale=1.0),
                  reads=[t_la, t_vecs], writes=[t_la])
            pd, tpd = next_ps()
            kb.op("pe", lambda e: e.matmul(pd[:, :], lhsT=gm[:, mi, 1, :], rhs=la[:, :], start=True, stop=True),
                  reads=[t_gm, t_la], writes=[tpd])
            kb.op("act", lambda e: e.activation(out=ed[:, :], in_=pd[:, :], func=AF.Exp), reads=[tpd], writes=[t_ed])
            pk, tpk = proj_tok(512, c0)
            kb.op("dve", lambda e: e.tensor_tensor(out=Kes[:, :], in0=pk[:, :], in1=ed[:, :], op=ALU.mult),
                  reads=[tpk, t_ed], writes=[t_Kes])
            for half in range(2):
                pv, tpv = proj_tok(1024 + half * 512, c0)
                kb.op("act", lambda e, half=half, pv=pv: e.activation(out=vtok[:, half * 512:(half + 1) * 512], in_=pv[:, :],
                                                                    func=AF.Identity), reads=[tpv], writes=[t_vtok])
            pb, tpb = next_ps()
            for h in range(4):
                kb.op("pe", lambda e, h=h: e.matmul(pb[:, h * nseg:(h + 1) * nseg], lhsT=la[:, h * 128:(h + 1) * 128],
                                                     rhs=gseg[:, mi, 0, 0:nseg], start=True, stop=True),
                      reads=[t_la, t_gm], writes=[tpb], inc=(h == 3))
            kb.op("act", lambda e: e.activation(out=ebt[:, :, 0:nseg],
                                                in_=pb[:, 0:4 * nseg].rearrange("p (h s) -> p h s", s=nseg), func=AF.Exp),
                  reads=[tpb], writes=[t_ebt])

        def state_update_prompt(with_atot):
            for hp in range(2):
                p, tp = next_ps()
                for hh in range(2):
                    h = hp * 2 + hh
                    kb.op("pe", lambda e, h=h, hh=hh, p=p: e.matmul(
                        p[:, hh * 256:(hh + 1) * 256], lhsT=Kes[:, h * 128:(h + 1) * 128], rhs=vtok[:, h * 256:(h + 1) * 256],
                        start=True, stop=True), reads=[t_Kes, t_vtok], writes=[tp], inc=(hh == 1))
                for hh in range(2):
                    h = hp * 2 + hh
                    kb.op("dve", lambda e, h=h, hh=hh, p=p: e.scalar_tensor_tensor(
                        out=Sst[:, h, :], in0=Sst[:, h, :], scalar=ebt[:, h, 0:1], in1=p[:, hh * 256:(hh + 1) * 256],
                        op0=ALU.mult, op1=ALU.add), reads=[t_S, t_ebt, tp], writes=[t_S])
            if with_atot:
                kb.op("dve", lambda e: e.tensor_tensor(out=Atot[:, :], in0=Atot[:, :], in1=ebt[:, :, 0], op=ALU.mult),
                      reads=[t_Atot, t_ebt], writes=[t_Atot])

        def chunk_full(c0, mi, sample):
            pbc, tpbc = next_ps()
            for h in range(4):
                kb.op("pe", lambda e, h=h: e.matmul(pbc[:, h * 128:(h + 1) * 128], lhsT=la[:, h * 128:(h + 1) * 128],
                                                     rhs=gm[:, mi, 0, :], start=True, stop=True),
                      reads=[t_la, t_gm], writes=[tpbc], inc=(h == 3))
            kb.op("act", lambda e: e.activation(out=e1[:, :, :], in_=pbc[:, :].rearrange("p (h t) -> p h t", t=128),
                                                func=AF.Exp), reads=[tpbc], writes=[t_e1])
            kb.op("act", lambda e: e.activation(out=e2[:, :, :], in_=pbc[:, :].rearrange("p (h t) -> p h t", t=128),
                                                func=AF.Exp, scale=-1.0), reads=[tpbc], writes=[t_e2])
            kb.op("pool", lambda e: e.tensor_tensor(out=QeT[:, :, :], in0=qT[:, :, c0:c0 + 128], in1=e1[:, :, :], op=ALU.mult),
                  reads=[t_qT, t_e1], writes=[t_QeT])
            kb.op("pool", lambda e: e.tensor_tensor(out=KeT[:, :, :], in0=kT[:, :, c0:c0 + 128], in1=e2[:, :, :], op=ALU.mult),
                  reads=[t_kT, t_e2], writes=[t_KeT])
            pat, tpat = next_ps()
            for h in range(4):
                kb.op("pe", lambda e, h=h: e.matmul(pat[:, h * 128:(h + 1) * 128], lhsT=KeT[:, h, :], rhs=QeT[:, h, :],
                                                     start=True, stop=True),
                      reads=[t_KeT, t_QeT], writes=[tpat], inc=(h == 3))
            kb.op("dve", lambda e: e.tensor_tensor(
                out=attm[:, :, :], in0=pat[:, :].rearrange("p (h t) -> p h t", t=128),
                in1=gm[:, mi, 2, :].unsqueeze(1).broadcast_to([128, 4, 128]), op=ALU.mult),
                reads=[tpat, t_gm], writes=[t_attm])
            pos_ = [next_ps(), next_ps()]
            if not sample:
                kb.op("act", lambda e: e.activation(out=Sbf[:, :, :], in_=Sst[:, :, :], func=AF.Identity),
                      reads=[t_S], writes=[t_Sbf])
            for h in range(4):
                po, tpo = pos_[h // 2]
                for vc in range(2):
                    reg = po[:, ((h % 2) * 2 + vc) * 128:((h % 2) * 2 + vc + 1) * 128]
                    kb.op("pe", lambda e, h=h, vc=vc, reg=reg: e.matmul(
                        reg, lhsT=vtok[:, h * 256 + vc * 128:h * 256 + (vc + 1) * 128], rhs=attm[:, h, :],
                        start=(h % 2 == 0 and vc == 0), stop=False), reads=[t_vtok, t_attm], writes=[tpo], inc=False)
                    if not sample:
                        kb.op("pe", lambda e, h=h, vc=vc, reg=reg: e.matmul(
                            reg, lhsT=Sbf[:, h, vc * 128:(vc + 1) * 128], rhs=QeT[:, h, :], start=False, stop=True),
                            reads=[t_Sbf, t_QeT], writes=[tpo], inc=(h % 2 == 1 and vc == 1))
            if sample:
                for g in range(4):
                    kb.dma("pool", S0bf[:, :, :, :], sgla_in[:, 4 * g:4 * g + 4, :, :], writes=[t_S0bf])
                    for sl in range(4):
                        s_ = 4 * g + sl
                        for h in range(4):
                            po, tpo = pos_[h // 2]
                            for vc in range(2):
                                reg = po[:, ((h % 2) * 2 + vc) * 128 + 8 * s_:((h % 2) * 2 + vc) * 128 + 8 * s_ + 8]
                                lastm = (g == 3 and sl == 3 and vc == 1 and h % 2 == 1)
                                kb.op("pe", lambda e, h=h, vc=vc, reg=reg, sl=sl, s_=s_: e.matmul(
                                    reg, lhsT=S0bf[:, sl, h, vc * 128:(vc + 1) * 128], rhs=QeT[:, h, 8 * s_:8 * s_ + 8],
                                    start=False, stop=True), reads=[t_S0bf, t_QeT], writes=[tpo],
                                    inc=(lastm or (sl == 3 and vc == 1 and h == 3)))
            for hp in range(2):
                po, tpo = pos_[hp]
                kb.op("act", lambda e, hp=hp, po=po: e.activation(
                    out=go[:, hp * 4:(hp + 1) * 4, c0:c0 + 128], in_=po[:, :].rearrange("p (c t) -> p c t", t=128),
                    func=AF.Identity), reads=[tpo], writes=[t_go])
                kb.op("act", lambda e, hp=hp, po=po: e.activation(
                    out=gsq[:, hp * 4:(hp + 1) * 4, c0:c0 + 128], in_=po[:, :].rearrange("p (c t) -> p c t", t=128),
                    func=AF.Square), reads=[tpo], writes=[t_gsq])

        def tile_finish(n, xt, t_xt, sample):
            for c0 in range(0, n, 128):
                p, tp = next_ps()
                for h in range(4):
                    for vc in range(2):
                        kb.op("pe", lambda e, h=h, vc=vc: e.matmul(
                            p[:, h * 128:(h + 1) * 128], lhsT=ones_bf[:, :], rhs=gsq[:, h * 2 + vc, c0:c0 + 128],
                            start=(vc == 0), stop=(vc == 1)), reads=[t_gsq, t_const], writes=[tp],
                            inc=(h == 3 and vc == 1))
                kb.op("act", lambda e: e.activation(out=rsh[:, :, :], in_=p[:, :].rearrange("p (h t) -> p h t", t=128),
                                                    func=AF.Sqrt, bias=V("eps"), scale=1.0 / 256), reads=[tp, t_vecs],
                      writes=[t_rsh])
                kb.op("dve", lambda e: e.reciprocal(out=rsh[:, :, :], in_=rsh[:, :, :]), reads=[t_rsh], writes=[t_rsh])
                gv = go[:, :, c0:c0 + 128].rearrange("p (h v) t -> p h v t", v=2)
                kb.op("dve", lambda e, gv=gv: e.tensor_tensor(
                    out=gv, in0=gv, in1=rsh[:, :, :].unsqueeze(2).broadcast_to([128, 4, 2, 128]), op=ALU.mult),
                    reads=[t_go, t_rsh], writes=[t_go])
                for vc in range(2):
                    gvv = go[:, :, c0:c0 + 128].rearrange("p (h v) t -> p h v t", v=2)[:, :, vc, :]
                    rv = rT[:, :, c0:c0 + 128].rearrange("p (h v) t -> p h v t", v=2)[:, :, vc, :]
                    ov = gsq[:, :, c0:c0 + 128].rearrange("p (h v) t -> p h v t", v=2)[:, :, vc, :]
                    kb.op("dve", lambda e, gvv=gvv, rv=rv, ov=ov, vc=vc: e.scalar_tensor_tensor(
                        out=ov, in0=gvv, scalar=V("g_gn", vc, 1), in1=rv, op0=ALU.mult, op1=ALU.mult),
                        reads=[t_go, t_rT, t_vecs, t_gsq], writes=[t_gsq])
            for m in range(KC):
                po, tpo = next_ps()
                for c in range(KC):
                    kb.op("pe", lambda e, c=c, m=m, po=po: e.matmul(
                        po[:, 0:n], lhsT=w_out[:, c, m * 128:(m + 1) * 128], rhs=gsq[:, c, 0:n],
                        start=(c == 0), stop=(c == KC - 1)),
                        reads=[t_wout[m * 128 // 512], t_gsq], writes=[tpo], inc=(c == KC - 1))
                kb.op("act", lambda e, m=m, po=po: e.activation(out=B.oT[:, m, 0:n], in_=po[:, 0:n], func=AF.Identity),
                      reads=[tpo], writes=[B.t_oT])
                kb.op("act", lambda e, m=m, po=po: e.activation(out=B.sq[:, m, 0:n], in_=po[:, 0:n], func=AF.Square),
                      reads=[tpo], writes=[B.t_sq])
            postnorm_residual(B, xt, t_xt, n, sample, l)

        xt, t_xt = B.xt[0], B.t_xt[0]
        kb.op("dve", lambda e: e.memset(Sst[:, :, :], 0.0), writes=[t_S])
        kb.op("dve", lambda e: e.memset(Atot[:, :], 1.0), writes=[t_Atot])
        for ti, (c0t, n) in enumerate(TILES[:-1]):
            kb.dma("sp", xt[:, :, 0:n], x_src(l)[:, :, c0t:c0t + n], reads=[xtk(ti)], writes=[t_xt])
            prenorm(B, xt, t_xt, n, False)
            tile_logarank(n)
            for c0 in range(0, n, 128):
                chunk_common(c0, 0, 1)
                state_update_prompt(True)
        kb.dma("sp", gx[:, 0:1024], Sst[:, :, :].rearrange("p h v -> p (h v)"), reads=[t_S], writes=[t_gx])
        kb.dma("sp", gx[:, 1024:1028], Atot[:, :], reads=[t_Atot], writes=[t_gx])
        kb.collective(lambda e: e.collective_compute("AllGather", ALU.bypass, replica_groups=[[0, 1, 2, 3], [4, 5, 6, 7]],
                                                     ins=[gx[:, :]], outs=[gg[:, :]]), reads=[t_gx], writes=[t_gg])
        gsb = big[:, 0:4112].rearrange("p (r n) -> p r n", r=4)
        kb.dma("sp", gsb, gg.ap().rearrange("(r p) n -> p r n", p=128), reads=[t_gg], writes=[t_big])

        def Bs(i):
            return gsb[:, i, 0:1024].rearrange("p (h v) -> p h v", h=4)
        kb.op("dve", lambda e: e.tensor_copy(out=accb[:, :, :], in_=Bs(0)), reads=[t_big], writes=[t_accb])
        kb.op("dve", lambda e: e.tensor_scalar(out=Sst[:, :, :], in0=accb[:, :, :], scalar1=V("sel", 1, 1), scalar2=None,
                                               op0=ALU.mult), reads=[t_accb, t_vecs], writes=[t_S])
        for i in (1, 2):
            for h in range(4):
                kb.op("dve", lambda e, i=i, h=h: e.scalar_tensor_tensor(
                    out=accb[:, h, :], in0=accb[:, h, :], scalar=gsb[:, i, 1024 + h:1025 + h], in1=Bs(i)[:, h, :],
                    op0=ALU.mult, op1=ALU.add), reads=[t_accb, t_big], writes=[t_accb])
            kb.op("dve", lambda e, i=i: e.scalar_tensor_tensor(
                out=Sst[:, :, :], in0=accb[:, :, :], scalar=V("sel", i + 1, 1), in1=Sst[:, :, :],
                op0=ALU.mult, op1=ALU.add), reads=[t_accb, t_vecs, t_S], writes=[t_S])
        for ti, (c0t, n) in enumerate(TILES):
            sample = (c0t >= LP)
            mi = 1 if sample else 0
            tx = xtk(ti)
            kb.dma("sp", xt[:, :, 0:n], x_src(l)[:, :, c0t:c0t + n], reads=[tx], writes=[t_xt])
            prenorm(B, xt, t_xt, n, sample)
            tile_logarank(n)
            for h in range(4):
                proj_fm(h * 128, n, lambda p, tp, h=h: kb.op("act", lambda e: e.activation(
                    out=qT[:, h, 0:n], in_=p[:, 0:n], func=AF.Identity, scale=DKS), reads=[tp], writes=[t_qT]))
                proj_fm(512 + h * 128, n, lambda p, tp, h=h: kb.op("act", lambda e: e.activation(
                    out=kT[:, h, 0:n], in_=p[:, 0:n], func=AF.Identity), reads=[tp], writes=[t_kT]))
            for c in range(KC):
                proj_fm(2048 + c * 128, n, lambda p, tp, c=c: kb.op("act", lambda e: e.activation(
                    out=rT[:, c, 0:n], in_=p[:, 0:n], func=AF.Silu), reads=[tp], writes=[t_rT]))
            for c0 in range(0, n, 128):
                chunk_common(c0, mi, 16 if sample else 1)
                chunk_full(c0, mi, sample)
                if not sample:
                    state_update_prompt(False)
                else:
                    S0g = big[:, 0:4096].rearrange("p (s h v) -> p s h v", s=4, h=4)
                    for g in range(4):
                        kb.dma("sp", S0g, sgla_in[:, 4 * g:4 * g + 4, :, :], writes=[t_big])
                        for h in range(4):
                            kb.op("pool", lambda e, h=h, g=g: e.tensor_tensor(
                                out=Vblk[:, :, :], in0=vtok[:, h * 256:(h + 1) * 256].unsqueeze(1).broadcast_to([128, 4, 256]),
                                in1=gseg[:, 1, 1, 4 * g:4 * g + 4].unsqueeze(2).broadcast_to([128, 4, 256]), op=ALU.mult),
                                reads=[t_vtok, t_gm], writes=[t_Vblk])
                            for half in range(2):
                                p, tp = next_ps()
                                kb.op("pe", lambda e, h=h, half=half, p=p: e.matmul(
                                    p[:, :], lhsT=Kes[:, h * 128:(h + 1) * 128],
                                    rhs=Vblk[:, 2 * half:2 * half + 2, :].rearrange("p s v -> p (s v)"),
                                    start=True, stop=True), reads=[t_Kes, t_Vblk], writes=[tp])
                                for sl2 in range(2):
                                    sl = 2 * half + sl2
                                    s_ = 4 * g + sl
                                    kb.op("dve", lambda e, h=h, sl=sl, sl2=sl2, s_=s_, p=p: e.scalar_tensor_tensor(
                                        out=S0g[:, sl, h, :], in0=S0g[:, sl, h, :], scalar=ebt[:, h, s_:s_ + 1],
                                        in1=p[:, sl2 * 256:(sl2 + 1) * 256], op0=ALU.mult, op1=ALU.add),
                                        reads=[t_big, t_ebt, tp], writes=[t_big])
                        kb.dma("sp", gla_s_out[:, 4 * g:4 * g + 4, :, :], S0g, reads=[t_big])
            tile_finish(n, xt, t_xt, sample)
            kb.dma("sp", x_dst(l)[:, :, c0t:c0t + n], xt[:, :, 0:n], reads=[t_xt], writes=[tx])
            if ti == len(TILES) - 2:
                kb.dma("sp", gla_p_out[:, :, :], Sst[:, :, :], reads=[t_S])
        kb.barrier()
        ls.close()


    def layer_att(l):
        ls = ExitStack()
        compute_mod(l, ls)
        B = common_bufs(ls, with_hT=False)
        L = LP
        GD = [(128, 1), (512, 4), (2048, 16)]
        EXT = [dd * 128 + L for (_, dd) in GD]

        def sublen(g):
            return 128 + L // GD[g][1]
        kTd = [nc.dram_tensor("kTd%d" % g, [128, 4, EXT[g]], BF16) for g in range(3)]
        qTd = [nc.dram_tensor("qTd%d" % g, [128, 4, L], BF16) for g in range(3)]
        vd = [nc.dram_tensor("vd%d" % g, [EXT[g], 512], BF16) for g in range(3)]
        ktp = [nc.dram_tensor("ktp%d" % i, [128, 3584], BF16) for i in range(3)]
        vtp = [nc.dram_tensor("vtp%d" % i, [896, 512], BF16) for i in range(3)]
        ggK = [nc.dram_tensor("ggK%d" % i, [512, 3584], BF16) for i in range(3)]
        ggV = [nc.dram_tensor("ggV%d" % i, [3584, 512], BF16) for i in range(3)]
        t_kTd = [Tk("kTd%d" % g, multi=True) for g in range(3)]
        t_qTd = [Tk("qTd%d" % g, multi=True) for g in range(3)]
        t_vd = [Tk("vd%d" % g, multi=True) for g in range(3)]
        t_ktp = [Tk("ktp%d" % i, multi=True) for i in range(3)]
        t_vtp = [Tk("vtp%d" % i, multi=True) for i in range(3)]
        t_ggK = [Tk("ggK%d" % i) for i in range(3)]
        t_ggV = [Tk("ggV%d" % i) for i in range(3)]
        t_outs = Tk("att_outs", multi=True)

        def mk(name, shape, dt, st=None):
            return S(name, shape, dt, st if st is not None else ls), Tk(name)
        zT, t_zT = mk("zT", [128, 4, NT], BF16)
        kTs, t_kTs = mk("kTs", [128, 12, NS], BF16)
        qTs, t_qTs = mk("qTs", [128, 12, NS], BF16)
        vS, t_vS = mk("vS", [128, 1536], BF16)
        sA = ExitStack()
        hTa, t_hTa = mk("hTa", [128, KC, NT], BF16, sA)
        rope, t_rope = mk("rope", [128, 2, NT], F32, sA)
        pmf, t_pmf = mk("pmf", [128, 128], F32, sA)
        pmb, t_pmb = mk("pmb", [128, 128], BF16, sA)
        for ci in range(2):
            for hb_ in range(2):
                kb.dma("sp", rope[:, ci, hb_ * 1088:(hb_ + 1) * 1088], rope_in[:, ci, hb_ * 1088:(hb_ + 1) * 1088],
                       writes=[t_rope])
        kb.dma("sp", pmf[:, :], pm_in[:, :], writes=[t_pmf])
        kb.op("dve", lambda e: e.tensor_copy(out=pmb[:, :], in_=pmf[:, :]), reads=[t_pmf], writes=[t_pmb])

        sa = ExitStack()
        w_in, t_win, wblk = load_w(sa, "wa_qk", w_att_in_in[:, 0:3072], 3072)
        xb, t_xb = mk("xb", [128, 512], BF16, sa)
        t1, t_t1 = mk("ra1", [128, 512], F32, sa)
        t2, t_t2 = mk("ra2", [128, 512], F32, sa)
        kf, t_kf = mk("kf", [128, 512], F32, sa)
        kbf, t_kbf = mk("kbf", [128, 512], BF16, sa)
        xt, t_xt = B.xt[0], B.t_xt[0]

        def wtk(col0):
            return t_win[col0 // wblk]

        for ti, (c0t, n) in enumerate(TILES):
            sample = (c0t >= LP)
            kb.dma("sp", xt[:, :, 0:n], x_src(l)[:, :, c0t:c0t + n], reads=[xtk(ti)], writes=[t_xt])
            prenorm(B, xt, t_xt, n, sample, hout=hTa[:, :, c0t:c0t + n], t_hout=t_hTa)

        def dec2(ap2d, g, u):
            if g == 0:
                return ap2d[:, 512 * u:512 * u + 512]
            if g == 1:
                return ap2d.rearrange("p (n r) -> p r n", r=4)[:, u, :]
            return ap2d.rearrange("p (n r) -> p r n", r=16)[:, 4 * u:4 * u + 4, :]

        def qk_core(col0, rhs_fn, n, cos_ap, sin_ap, vwf):
            p, tp = next_ps()
            for kc in range(KC):
                kb.op("pe", lambda e, kc=kc: e.matmul(vwf(p[:, 0:n]), lhsT=w_in[:, kc, col0:col0 + 128], rhs=rhs_fn(kc),
                                                       start=(kc == 0), stop=(kc == KC - 1)),
                      reads=[wtk(col0), t_hTa], writes=[tp], inc=(kc == KC - 1))
            if QK_STEPS < 2:
                return
            kb.op("act", lambda e: e.activation(out=xb[:, 0:n], in_=p[:, 0:n], func=AF.Identity), reads=[tp], writes=[t_xb])
            if QK_STEPS < 3:
                return
            pr, tpr = next_ps()
            kb.op("pe", lambda e: e.matmul(pr[:, 0:n], lhsT=pmb[:, :], rhs=xb[:, 0:n], start=True, stop=True),
                  reads=[t_pmb, t_xb], writes=[tpr])
            if QK_STEPS < 4:
                return
            kb.op("dve", lambda e: e.tensor_tensor(out=vwf(t1[:, 0:n]), in0=vwf(p[:, 0:n]), in1=cos_ap, op=ALU.mult),
                  reads=[tp, t_rope], writes=[t_t1])
            if QK_STEPS < 5:
                return
            kb.op("dve", lambda e: e.tensor_tensor(out=vwf(t2[:, 0:n]), in0=vwf(pr[:, 0:n]), in1=sin_ap, op=ALU.mult),
                  reads=[tpr, t_rope], writes=[t_t2])
            if QK_STEPS < 6:
                return
            kb.op("pool", lambda e: e.tensor_tensor(out=kf[:, 0:n], in0=t1[:, 0:n], in1=t2[:, 0:n], op=ALU.add),
                  reads=[t_t1, t_t2], writes=[t_kf])
            if QK_STEPS < 7:
                return
            kb.op("act", lambda e: e.activation(out=kbf[:, 0:n], in_=kf[:, 0:n], func=AF.Identity),
                  reads=[t_kf], writes=[t_kbf])

        def tail_col(g, hc, r):
            if g == 0:
                return hc * 128
            if g == 1:
                return 512 + (hc * 4 + r) * 128
            return 2560 + (hc * 16 + r) * 128

        for g in (QK_G if (A_PARTS & 1) else []):
            W, dd = GD[g]
            vwf = (lambda a: a.rearrange("p (r n) -> p r n", r=4)) if g == 2 else (lambda a: a)
            for kind in range(2):
                for u in range(4):
                    for hc in range(4):
                        col0 = kind * 1536 + g * 512 + hc * 128
                        qk_core(col0, lambda kc, g=g, u=u: dec2(hTa[:, kc, 0:L], g, u), 512,
                                dec2(rope[:, 0, 0:L], g, u), dec2(rope[:, 1, 0:L], g, u), vwf)
                        if kind == 0:
                            if QK_DMA & 1:
                                kb.dma("sp", qTd[g][:, hc, 512 * u:512 * u + 512], kbf[:, :], reads=[t_kbf], writes=[t_qTd[g]])
                            continue
                        if not (QK_DMA & 2):
                            continue
                        if g == 0:
                            dst = kTd[g][:, hc, 128 + 512 * u:128 + 512 * u + 512]
                            src = kbf[:, :]
                        elif g == 1:
                            dst = kTd[g][:, hc, u * 640 + 128:u * 640 + 640]
                            src = kbf[:, :]
                        else:
                            dst = kTd[g][:, hc, :].rearrange("p (r e) -> p r e", e=256)[:, 4 * u:4 * u + 4, 128:256]
                            src = kbf[:, :].rearrange("p (r n) -> p r n", r=4)
                        kb.dma("sp", dst, src, reads=[t_kbf], writes=[t_kTd[g]])
                        if not (QK_DMA & 4):
                            continue
                        if g == 0 and u != 3:
                            continue
                        if g == 2:
                            col = tail_col(2, hc, 4 * u)
                            pc, off = col // 3584, col % 3584
                            kb.dma("sp", ktp[pc][:, off:off + 512], kbf[:, :], reads=[t_kbf], writes=[t_ktp[pc]])
                            kb.dma("sp", kout[:, col:col + 512], kf[:, :], reads=[t_kf], writes=[t_outs])
                        else:
                            col = tail_col(g, hc, u if g == 1 else 0)
                            pc, off = col // 3584, col % 3584
                            kb.dma("sp", ktp[pc][:, off:off + 128], kbf[:, 384:512], reads=[t_kbf], writes=[t_ktp[pc]])
                            kb.dma("sp", kout[:, col:col + 128], kf[:, 384:512], reads=[t_kf], writes=[t_outs])
        for g in range(3 if (A_PARTS & 2) else 0):
            for kind in range(2):
                for hc in range(4):
                    col0 = kind * 1536 + g * 512 + hc * 128
                    qk_core(col0, lambda kc: hTa[:, kc, LP:LP + NS], NS, rope[:, 0, LP:LP + NS], rope[:, 1, LP:LP + NS],
                            lambda a: a)
                    if kind == 0:
                        kb.op("pool", lambda e, g=g, hc=hc: e.tensor_copy(out=qTs[:, g * 4 + hc, :], in_=kbf[:, 0:NS]),
                              reads=[t_kbf], writes=[t_qTs])
                    else:
                        kb.op("pool", lambda e, g=g, hc=hc: e.tensor_copy(out=kTs[:, g * 4 + hc, :], in_=kbf[:, 0:NS]),
                              reads=[t_kbf], writes=[t_kTs])
                        kb.dma("sp", ks_out[:, g * 4 + hc, :], kf[:, 0:NS], reads=[t_kf], writes=[t_outs])
        kb.barrier()
        sa.close()
        sa = ExitStack()
        w_in, t_win, wblk = load_w(sa, "wa_vz", w_att_in_in[:, 3072:5120], 2048)
        vb, t_vb = mk("vb", [128, 512], BF16, sa)
        vf, t_vf = mk("vf", [128, 512], F32, sa)
        for ti, (c0t, n) in enumerate(TILES if (A_PARTS & 4) else []):
            for c in range(4):
                p, tp = next_ps()
                for kc in range(KC):
                    kb.op("pe", lambda e, kc=kc, c=c, p=p: e.matmul(
                        p[:, 0:n], lhsT=w_in[:, kc, 1536 + c * 128:1536 + (c + 1) * 128], rhs=hTa[:, kc, c0t:c0t + n],
                        start=(kc == 0), stop=(kc == KC - 1)), reads=[wtk(1536 + c * 128), t_hTa], writes=[tp],
                        inc=(kc == KC - 1))
                kb.op("act", lambda e, c=c, p=p: e.activation(out=zT[:, c, c0t:c0t + n], in_=p[:, 0:n], func=AF.Silu),
                      reads=[tp], writes=[t_zT])
        for g in range(3 if (A_PARTS & 8) else 0):
            W, dd = GD[g]
            nb = L // dd // 128
            for r in range(dd):
                for b in range(nb):
                    def lhs(kc, g=g, r=r, b=b):
                        a = hTa[:, kc, 0:L]
                        if g == 0:
                            return a[:, 128 * b:128 * b + 128]
                        return a.rearrange("p (n r) -> p r n", r=GD[g][1])[:, r, 128 * b:128 * b + 128]
                    p, tp = next_ps()
                    for kc in range(KC):
                        kb.op("pe", lambda e, kc=kc, p=p: e.matmul(
                            p[:, :], lhsT=lhs(kc), rhs=w_in[:, kc, g * 512:(g + 1) * 512],
                            start=(kc == 0), stop=(kc == KC - 1)), reads=[wtk(g * 512), t_hTa], writes=[tp],
                            inc=(kc == KC - 1))
                    kb.op("act", lambda e, p=p: e.activation(out=vb[:, :], in_=p[:, :], func=AF.Identity),
                          reads=[tp], writes=[t_vb])
                    e0 = r * sublen(g) + 128 + 128 * b
                    kb.dma("sp", vd[g][e0:e0 + 128, :], vb[:, :], reads=[t_vb], writes=[t_vd[g]])
                    if b == nb - 1:
                        row = (0, 128, 640)[g] + r * 128
                        pc, off = row // 896, row % 896
                        kb.dma("sp", vtp[pc][off:off + 128, :], vb[:, :], reads=[t_vb], writes=[t_vtp[pc]])
                        kb.op("dve", lambda e, p=p: e.tensor_copy(out=vf[:, :], in_=p[:, :]), reads=[tp], writes=[t_vf])
                        kb.dma("sp", vout[row:row + 128, :], vf[:, :], reads=[t_vf], writes=[t_outs])
            p, tp = next_ps()
            for kc in range(KC):
                kb.op("pe", lambda e, kc=kc, p=p: e.matmul(
                    p[:, :], lhsT=hTa[:, kc, LP:LP + NS], rhs=w_in[:, kc, g * 512:(g + 1) * 512],
                    start=(kc == 0), stop=(kc == KC - 1)), reads=[wtk(g * 512), t_hTa], writes=[tp],
                    inc=(kc == KC - 1))
            kb.op("act", lambda e, p=p, g=g: e.activation(out=vS[:, g * 512:(g + 1) * 512], in_=p[:, :], func=AF.Identity),
                  reads=[tp], writes=[t_vS])
            kb.op("dve", lambda e, p=p: e.tensor_copy(out=vf[:, :], in_=p[:, :]), reads=[tp], writes=[t_vf])
            kb.dma("sp", vs_out[:, g * 512:(g + 1) * 512], vf[:, :], reads=[t_vf], writes=[t_outs])
        kb.barrier()
        sa.close()
        sA.close()

        sb2 = ExitStack()
        w_out = S("wa_out", [128, 4, D], BF16, sb2)
        t_wo = Tk("wa_out")
        kb.dma("pool", w_out[:, :, :], w_att_out_in.rearrange("(c p) n -> p c n", p=128), writes=[t_wo])
        nacc, t_nacc = mk("nacc", [128, 4, L], F32, sb2)
        dacc, t_dacc = mk("dacc", [128, 4, L], F32, sb2)
        naccS, t_naccS = mk("naccS", [128, 4, NS], F32, sb2)
        daccS, t_daccS = mk("daccS", [128, 4, NS], F32, sb2)
        amask, t_am = mk("amask", [128, 3, 512], BF16, sb2)
        smask, t_sm = mk("smask", [128, 13, 64], BF16, sb2)
        snew, t_sn = mk("snew", [128, 3, 512], BF16, sb2)
        kb.dma("pool", amask[:, :, :], amask_in[:, :, :], writes=[t_am])
        kb.dma("pool", smask[:, :, :], smask_in[:, :, :], writes=[t_sm])
        kb.dma("pool", snew[:, :, :], snew_in[:, :, :], writes=[t_sn])
        PT, t_PT = mk("PT", [128, 2, 8, 128], BF16, sb2)
        kt, t_kt = mk("kt", [128, 4, 256], BF16, sb2)
        vt, t_vt = mk("vt", [128, 2, 512], BF16, sb2)
        quA, t_qu = mk("quA", [128, 4, 512], BF16, sb2)
        quB, _ = mk("quB", [128, 4, 512], BF16, sb2)
        qsA, t_qs = mk("qsA", [128, 12, NS], BF16, sb2)
        qsB, _ = mk("qsB", [128, 12, NS], BF16, sb2)
        for zt in (quA, quB, qsA, qsB):
            kb.op("pool", lambda e, zt=zt: e.memset(zt[:, :, :], 0.0), writes=[t_qu, t_qs])
        kb.op("pool", lambda e: e.tensor_copy(out=qsA[0:64, :, :], in_=qTs[0:64, :, :]), reads=[t_qTs], writes=[t_qs])
        kb.op("pool", lambda e: e.tensor_copy(out=qsB[64:128, :, :], in_=qTs[64:128, :, :]), reads=[t_qTs], writes=[t_qs])
        Kt2 = [mk("Kt%d" % i, [128, 512], BF16, sb2) for i in range(2)]
        Vt2 = [mk("Vt%d" % i, [128, 512], BF16, sb2) for i in range(2)]
        kTt2 = [mk("kTt%d" % i, [128, 4, 128], BF16, sb2) for i in range(2)]
        PTs2 = [mk("PTs%d" % i, [128, 64], BF16, sb2) for i in range(2)]
        Kf2 = [mk("Kf%d" % i, [128, 512], F32, sb2) for i in range(2)]
        Vf2 = [mk("Vf%d" % i, [128, 512], F32, sb2) for i in range(2)]
        stile = [0]
        hk, t_hk = mk("hk", [128, 3584], BF16, sb2)
        hv, _ = mk("hv", [128, 2, 512], BF16, sb2)
        t_hvb = [Tk("hv0"), Tk("hv1")]
        idxk, t_idx = mk("idxk", [128, 1], I32, sb2)
        idxv, _ = mk("idxv", [128, 7], I32, sb2)
        og, t_og = mk("og", [128, 4, TN], BF16, sb2)
        ps_reserved.add(7)
        ptb = ps[7][:, :].bitcast(BF16)[:, 0:512]
        t_ptb = t_ps[7]
        kb.dma("sp", idxk[:, :], idxk_in[:, :], writes=[t_idx])
        kb.dma("sp", idxv[:, :], idxv_in[:, :], writes=[t_idx])
        kb.op("dve", lambda e: e.memset(nacc[:, :, :], 0.0), writes=[t_nacc])
        kb.op("dve", lambda e: e.memset(dacc[:, :, :], 0.0), writes=[t_dacc])
        SC = 0.125

        for i in range(3 if _en('X') else 0):
            kb.collective(lambda e, i=i: e.collective_compute(
                "AllGather", ALU.bypass, replica_groups=[[0, 1, 2, 3], [4, 5, 6, 7]],
                ins=[ktp[i][:, :]], outs=[ggK[i][:, :]]), reads=[t_ktp[i]], writes=[t_ggK[i]])
            kb.collective(lambda e, i=i: e.collective_compute(
                "AllGather", ALU.bypass, replica_groups=[[0, 1, 2, 3], [4, 5, 6, 7]],
                ins=[vtp[i][:, :]], outs=[ggV[i][:, :]]), reads=[t_vtp[i]], writes=[t_ggV[i]])

        res = []
        for _ in range(4):
            i = psn[0]
            while i in ps_reserved:
                i = (i + 1) % 8
            ps_reserved.add(i)
            res.append(i)
        pnum = [(ps[res[0]], t_ps[res[0]]), (ps[res[1]], t_ps[res[1]])]
        pden = [(ps[res[2]], t_ps[res[2]]), (ps[res[3]], t_ps[res[3]])]
        started = set()

        def first(i):
            if i in started:
                return False
            started.add(i)
            return True

        def score_bank(maskrhs, t_mask, nn, per_head):
            p, tp = next_ps()
            kb.op("pe", lambda e: e.matmul(p[:, 0:nn], lhsT=ident_bf[:, :], rhs=maskrhs, start=True, stop=False),
                  reads=[t_const, t_mask], writes=[tp], inc=False)
            per_head(p, tp)
            return p, tp

        for g in range(3 if _en('S') else 0):
            for hb in range(2):
                def heads(p, tp, g=g, hb=hb):
                    for hh in range(4):
                        h = 4 * hb + hh
                        hc, pb = h // 2, (h % 2) * 64
                        kb.op("pe", lambda e, hh=hh, hc=hc, pb=pb: e.matmul(
                            p[:, hh * 128:(hh + 1) * 128], lhsT=kTs[:, g * 4 + hc, :],
                            rhs=(qsA if pb == 0 else qsB)[:, g * 4 + hc, :], start=False, stop=True),
                            reads=[t_kTs, t_qs], writes=[tp], inc=(hh == 3))
                p, tp = score_bank(snew[:, g, :], t_sn, 512, heads)
                kb.op("act", lambda e, p=p, hb=hb: e.activation(
                    out=PT[:, 0, 4 * hb:4 * hb + 4, :], in_=p[:, :].rearrange("p (h q) -> p h q", q=128),
                    func=AF.Exp, scale=SC), reads=[tp], writes=[t_PT])
            for hb in range(2):
                pd, tpd = pden[hb]
                kb.op("pe", lambda e, hb=hb, pd=pd: e.matmul(
                    pd[:, :], lhsT=ones_bf[:, :], rhs=PT[:, 0, 4 * hb:4 * hb + 4, :].rearrange("p h q -> p (h q)"),
                    start=first(res[2 + hb]), stop=False), reads=[t_const, t_PT], writes=[tpd])
            for hc in range(4):
                pn, tpn = pnum[hc // 2]
                for ab in range(2):
                    c0 = ((hc % 2) * 2 + ab) * 128
                    kb.op("pe", lambda e, hc=hc, ab=ab, c0=c0, pn=pn, g=g: e.matmul(
                        pn[:, c0:c0 + 128], lhsT=vS[:, g * 512 + hc * 128:g * 512 + (hc + 1) * 128],
                        rhs=PT[:, 0, 2 * hc + ab, :], start=first(res[hc // 2]), stop=False),
                        reads=[t_vS, t_PT], writes=[tpn])
        stiles = [(s_, g, r) for s_ in range(16 if _en('S') else 0) for g in range(3) for r in range((1, 4, 8)[g])]

        def s_load(i):
            s_, g, r = stiles[i]
            dd = GD[g][1]
            bi = i % 2
            (Kt, t_Kt), (Vt, t_Vt) = Kt2[bi], Vt2[bi]
            (Kf, t_Kf), (Vf, t_Vf) = Kf2[bi], Vf2[bi]
            kb.dma("sp", Kf[:, :], ck_in[g][s_, r::dd, :], writes=[t_Kf])
            kb.dma("sp", Vf[:, :], cv_in[g][s_, r::dd, :], writes=[t_Vf])
            kb.op("act", lambda e: e.activation(out=Kt[:, :], in_=Kf[:, :], func=AF.Identity),
                  reads=[t_Kf], writes=[t_Kt])
            kb.op("act", lambda e: e.activation(out=Vt[:, :], in_=Vf[:, :], func=AF.Identity),
                  reads=[t_Vf], writes=[t_Vt])

        def s_compute(i):
            s_, g, r = stiles[i]
            mt = (0, 1, 5)[g] + r
            bi = i % 2
            (Kt, t_Kt), (Vt, t_Vt), (kTt, t_kTt), (PTs, t_PTs) = Kt2[bi], Vt2[bi], kTt2[bi], PTs2[bi]
            for hc in range(4):
                kb.op("pe", lambda e, hc=hc: e.transpose(ptb[:, hc * 128:(hc + 1) * 128],
                                                          Kt[:, hc * 128:(hc + 1) * 128], ident_bf[:, :]),
                      reads=[t_Kt, t_const], writes=[t_ptb], inc=(hc == 3))
            kb.op("dve", lambda e: e.tensor_copy(out=kTt[:, :, :], in_=ptb[:, :].rearrange("p (c k) -> p c k", c=4)),
                  reads=[t_ptb], writes=[t_kTt])

            def heads(p, tp, g=g, s_=s_):
                for h in range(8):
                    hc, pb = h // 2, (h % 2) * 64
                    kb.op("pe", lambda e, h=h, hc=hc, pb=pb: e.matmul(
                        p[:, h * 8:(h + 1) * 8], lhsT=kTt[:, hc, :],
                        rhs=(qsA if pb == 0 else qsB)[:, g * 4 + hc, 8 * s_:8 * s_ + 8], start=False, stop=True),
                        reads=[t_kTt, t_qs], writes=[tp], inc=(h == 7))
            p, tp = score_bank(smask[:, mt, :], t_sm, 64, heads)
            kb.op("act", lambda e, p=p: e.activation(out=PTs[:, :], in_=p[:, 0:64], func=AF.Exp, scale=SC),
                  reads=[tp], writes=[t_PTs])
            for hb in range(2):
                pd, tpd = pden[hb]
                kb.op("pe", lambda e, hb=hb, pd=pd: e.matmul(
                    pd[:, :].rearrange("p (h q) -> p h q", q=128)[:, :, 8 * s_:8 * s_ + 8],
                    lhsT=ones_bf[:, :], rhs=PTs[:, 32 * hb:32 * hb + 32].rearrange("p (h i) -> p h i", i=8),
                    start=False, stop=False), reads=[t_const, t_PTs], writes=[tpd], inc=(hb == 1))
            for hc in range(4):
                pn, tpn = pnum[hc // 2]
                for ab in range(2):
                    c0 = ((hc % 2) * 2 + ab) * 128 + 8 * s_
                    kb.op("pe", lambda e, hc=hc, ab=ab, c0=c0, pn=pn: e.matmul(
                        pn[:, c0:c0 + 8], lhsT=Vt[:, hc * 128:(hc + 1) * 128],
                        rhs=PTs[:, (2 * hc + ab) * 8:(2 * hc + ab) * 8 + 8], start=False, stop=False),
                        reads=[t_Vt, t_PTs], writes=[tpn], inc=(hc == 3 and ab == 1))

        if stiles:
            s_load(0)
        for i in range(len(stiles)):
            if i + 1 < len(stiles):
                s_load(i + 1)
            s_compute(i)
        for bk in range(2 if _en('S') else 0):
            pn, tpn = pnum[bk]
            pd, tpd = pden[bk]
            pnv = pn[:, :].rearrange("p (c a q) -> p c a q", c=2, a=2)
            pdv = pd[:, :].rearrange("p (c a q) -> p c a q", c=2, a=2)
            for ab in range(2):
                rows = slice(64 * ab, 64 * ab + 64)
                kb.op("dve", lambda e, rows=rows, ab=ab, pnv=pnv, bk=bk: e.tensor_copy(
                    out=naccS[rows, 2 * bk:2 * bk + 2, :], in_=pnv[rows, :, ab, :]), reads=[tpn], writes=[t_naccS])
                kb.op("dve", lambda e, rows=rows, ab=ab, pdv=pdv, bk=bk: e.tensor_copy(
                    out=daccS[rows, 2 * bk:2 * bk + 2, :], in_=pdv[rows, :, ab, :]), reads=[tpd], writes=[t_daccS])
        for i in res:
            ps_reserved.discard(i)

        for i in range(3 if _en('H') else 0):
            kb.dma_custom("pool", lambda e, i=i: e.indirect_dma_start(
                out=hk[:, :], out_offset=None, in_=ggK[i][:, :],
                in_offset=bass.IndirectOffsetOnAxis(ap=idxk[:, 0:1], axis=0)), reads=[t_ggK[i], t_idx], writes=[t_hk])
            for c128 in range(28):
                col = i * 3584 + c128 * 128
                cb = col // 128
                if cb < 4:
                    g, hc, r = 0, cb, 0
                elif cb < 20:
                    g, hc, r = 1, (cb - 4) // 4, (cb - 4) % 4
                else:
                    g, hc, r = 2, (cb - 20) // 16, (cb - 20) % 16
                e0 = r * sublen(g)
                kb.dma("sp", kTd[g][:, hc, e0:e0 + 128], hk[:, c128 * 128:(c128 + 1) * 128], reads=[t_hk],
                       writes=[t_kTd[g]])
            for t in range(7):
                hb_ = t % 2
                kb.dma_custom("pool", lambda e, i=i, t=t, hb_=hb_: e.indirect_dma_start(
                    out=hv[:, hb_, :], out_offset=None, in_=ggV[i][:, :],
                    in_offset=bass.IndirectOffsetOnAxis(ap=idxv[:, t:t + 1], axis=0)), reads=[t_ggV[i], t_idx],
                    writes=[t_hvb[hb_]])
                T = i * 7 + t
                if T == 0:
                    g, r = 0, 0
                elif T < 5:
                    g, r = 1, T - 1
                else:
                    g, r = 2, T - 5
                e0 = r * sublen(g)
                kb.dma("sp", vd[g][e0:e0 + 128, :], hv[:, hb_, :], reads=[t_hvb[hb_]], writes=[t_vd[g]])

        for g in range(3 if _en('B') else 0):
            W, dd = GD[g]
            nb = L // dd // 128
            for u in range(4):
                kb.dma("sp", quA[0:64, :, :], qTd[g][0:64, :, 512 * u:512 * u + 512], reads=[t_qTd[g]], writes=[t_qu])
                kb.dma("sp", quB[64:128, :, :], qTd[g][64:128, :, 512 * u:512 * u + 512], reads=[t_qTd[g]], writes=[t_qu])
                for k in range(4):
                    if g == 0:
                        r, b = 0, 4 * u + k
                    elif g == 1:
                        r, b = u, k
                    else:
                        r, b = 4 * u + k, 0
                    e0 = r * sublen(g) + 128 * b
                    kb.dma("sp", kt[:, :, :], kTd[g][:, :, e0:e0 + 256], reads=[t_kTd[g]], writes=[t_kt])
                    kb.dma("sp", vt[:, :, :], vd[g][e0:e0 + 256, :].rearrange("(t p) c -> p t c", p=128),
                           reads=[t_vd[g]], writes=[t_vt])
                    if B_STEPS < 2:
                        continue
                    for half in range(2):
                        mi = 2 if half == 1 else (1 if b == 0 else 0)
                        for hb in range(2):
                            def heads(p, tp, hb=hb, half=half, k=k):
                                for hh in range(4):
                                    h = 4 * hb + hh
                                    hc, pb = h // 2, (h % 2) * 64
                                    kb.op("pe", lambda e, hh=hh, hc=hc, pb=pb: e.matmul(
                                        p[:, hh * 128:(hh + 1) * 128], lhsT=kt[:, hc, half * 128:(half + 1) * 128],
                                        rhs=(quA if pb == 0 else quB)[:, hc, k * 128:(k + 1) * 128], start=False, stop=True),
                                        reads=[t_kt, t_qu], writes=[tp], inc=(hh == 3))
                            p, tp = score_bank(amask[:, mi, :], t_am, 512, heads)
                            kb.op("act", lambda e, p=p, hb=hb, half=half: e.activation(
                                out=PT[:, half, 4 * hb:4 * hb + 4, :], in_=p[:, :].rearrange("p (h q) -> p h q", q=128),
                                func=AF.Exp, scale=SC), reads=[tp], writes=[t_PT])
                    if B_STEPS < 3:
                        continue
                    pdl = [next_ps(), next_ps()]
                    for hb in range(2):
                        pd, tpd = pdl[hb]
                        for half in range(2):
                            kb.op("pe", lambda e, hb=hb, half=half, pd=pd: e.matmul(
                                pd[:, :], lhsT=ones_bf[:, :],
                                rhs=PT[:, half, 4 * hb:4 * hb + 4, :].rearrange("p h q -> p (h q)"),
                                start=(half == 0), stop=(half == 1)), reads=[t_const, t_PT], writes=[tpd],
                                inc=(half == 1))
                    if B_STEPS < 4:
                        continue
                    pnl = [next_ps(), next_ps()]
                    for bk in range(2):
                        pn, tpn = pnl[bk]
                        fst = True
                        for hcl in range(2):
                            hc = 2 * bk + hcl
                            for ab in range(2):
                                c0 = (hcl * 2 + ab) * 128
                                for half in range(2):
                                    kb.op("pe", lambda e, hc=hc, ab=ab, c0=c0, half=half, pn=pn, fst=fst: e.matmul(
                                        pn[:, c0:c0 + 128], lhsT=vt[:, half, hc * 128:(hc + 1) * 128],
                                        rhs=PT[:, half, 2 * hc + ab, :], start=fst, stop=(half == 1)),
                                        reads=[t_vt, t_PT], writes=[tpn], inc=(hcl == 1 and ab == 1 and half == 1))
                                    fst = False
                    if B_STEPS < 5:
                        continue
                    def tokv(acc, rows, c2):
                        a = acc[:, c2:c2 + 2, 0:L]
                        if dd > 1:
                            a = a.rearrange("p c (n r) -> p c r n", r=dd)[:, :, r, :]
                        return a[rows, :, 128 * b:128 * b + 128]
                    for bk in range(2):
                        pn, tpn = pnl[bk]
                        pd, tpd = pdl[bk]
                        pnv = pn[:, :].rearrange("p (c a q) -> p c a q", c=2, a=2)
                        pdv = pd[:, :].rearrange("p (c a q) -> p c a q", c=2, a=2)
                        for ab in range(2):
                            rows = slice(64 * ab, 64 * ab + 64)
                            kb.op("dve", lambda e, rows=rows, ab=ab, pnv=pnv, bk=bk: e.tensor_tensor(
                                out=tokv(nacc, rows, 2 * bk), in0=tokv(nacc, rows, 2 * bk), in1=pnv[rows, :, ab, :],
                                op=ALU.add), reads=[tpn, t_nacc], writes=[t_nacc])
                            kb.op("dve", lambda e, rows=rows, ab=ab, pdv=pdv, bk=bk: e.tensor_tensor(
                                out=tokv(dacc, rows, 2 * bk), in0=tokv(dacc, rows, 2 * bk), in1=pdv[rows, :, ab, :],
                                op=ALU.add), reads=[tpd, t_dacc], writes=[t_dacc])

        for ti, (c0t, n) in enumerate(TILES):
            sample = (c0t >= LP)
            tx = xtk(ti)
            kb.dma("sp", xt[:, :, 0:n], x_src(l)[:, :, c0t:c0t + n], reads=[tx], writes=[t_xt])
            if not sample:
                na, da, tna, tda = nacc[:, :, c0t:c0t + n], dacc[:, :, c0t:c0t + n], t_nacc, t_dacc
            else:
                na, da, tna, tda = naccS[:, :, :], daccS[:, :, :], t_naccS, t_daccS
            kb.op("dve", lambda e, da=da: e.reciprocal(out=da, in_=da), reads=[tda], writes=[tda])
            kb.op("dve", lambda e, da=da, na=na: e.tensor_tensor(out=na, in0=na, in1=da, op=ALU.mult),
                  reads=[tda, tna], writes=[tna])
            kb.op("pool", lambda e, na=na: e.tensor_tensor(out=og[:, :, 0:n], in0=na, in1=zT[:, :, c0t:c0t + n], op=ALU.mult),
                  reads=[tna, t_zT], writes=[t_og])
            for m in range(KC):
                po, tpo = next_ps()
                for c in range(4):
                    kb.op("pe", lambda e, c=c, m=m, po=po: e.matmul(
                        po[:, 0:n], lhsT=w_out[:, c, m * 128:(m + 1) * 128], rhs=og[:, c, 0:n],
                        start=(c == 0), stop=(c == 3)), reads=[t_wo, t_og], writes=[tpo], inc=(c == 3))
                kb.op("act", lambda e, m=m, po=po: e.activation(out=B.oT[:, m, 0:n], in_=po[:, 0:n], func=AF.Identity),
                      reads=[tpo], writes=[B.t_oT])
                kb.op("act", lambda e, m=m, po=po: e.activation(out=B.sq[:, m, 0:n], in_=po[:, 0:n], func=AF.Square),
                      reads=[tpo], writes=[B.t_sq])
            postnorm_residual(B, xt, t_xt, n, sample, l)
            kb.dma("sp", x_dst(l)[:, :, c0t:c0t + n], xt[:, :, 0:n], reads=[t_xt], writes=[tx])
        kb.barrier()
        ps_reserved.discard(7)
        sb2.close()
        ls.close()

    layer_conv(0, 0)
    if NLAYERS >= 2:
        layer_gla(1, 0)
    if NLAYERS >= 3:
        layer_att(2)
    if NLAYERS >= 4:
        layer_conv(3, 1)
    kb.final_wait()
    global _KB_DEBUG
    _KB_DEBUG = (dict(kb.cnt), dict(kb.dcnt), kb.ccn)
    es.close()
    return nc


def _prep_inputs(inp):
    f = lambda a: np.ascontiguousarray(np.asarray(a, dtype=np.float32))
    x_prompt, x_sample = f(inp["x_prompt"]), f(inp["x_sample"])
    c_prompt, c_sample = f(inp["c_prompt"]), f(inp["c_sample"])
    state_conv = f(inp["state_conv"])
    shared = {
        "w_ada": f(inp["w_ada"]),
        "w_conv_in": f(inp["w_conv_in"]),
        "w_conv_out": f(inp["w_conv_out"]),
        "w_gla_in": f(inp["w_gla_in"][0]),
        "w_gla_out": f(inp["w_gla_out"][0]),
        "w_a1": f(inp["w_gla_a1"][0]),
        "w_a2": f(inp["w_gla_a2"][0]),
        "b_a": f(inp["b_gla_a"][0]).reshape(1, 512),
        "w_att_in": f(inp["w_att_in"][0]),
        "w_att_out": f(inp["w_att_out"][0]),
        "pm": _att_consts()["pm"],
        "smask": _att_consts()["smask"],
        "snew": _att_consts()["snew"],
        "gmask": _gla_masks()[0],
        "gseg": _gla_masks()[1],
    }
    state_gla = f(inp["state_gla"])
    maps = []
    for c in range(NCORES):
        b, j = c // 4, c % 4
        s0 = 16 * c
        m = dict(shared)
        xT = np.empty((128, KC, NT), np.float32)
        xT[:, :, :LP] = _fm(x_prompt[b, LP * j:LP * (j + 1)]).transpose(0, 2, 1)
        xT[:, :, LP:] = _fm(x_sample[s0:s0 + 16].reshape(NS, D)).transpose(0, 2, 1)
        m["xT"] = xT
        xh = np.zeros((128, KC, 32), np.float32)
        if j > 0:
            xh[:] = _fm(x_prompt[b, LP * j - 32:LP * j]).transpose(0, 2, 1)
        m["xh"] = xh
        cT = np.empty((128, KC, 129), np.float32)
        cT[:, :, :NS] = _fm(np.repeat(c_sample[s0:s0 + 16], 8, axis=0)).transpose(0, 2, 1)
        cT[:, :, NS] = _fm(c_prompt[b])
        m["cT"] = cT
        vecs = np.zeros((128, NV), np.float32)

        def put(name, arr):
            off, w = VLAY[name]
            vecs[:, off:off + w] = np.asarray(arr, np.float32).reshape(128, w)
        put("g_pre", _fm(inp["g_pre"]))
        put("g_post", _fm(inp["g_post"]))
        put("b_ada", _fm(inp["b_ada"]))
        put("b_dw", _fm(inp["b_dw"]))
        put("g_cln", _fm(inp["g_conv_ln"]))
        put("b_cln", _fm(inp["b_conv_ln"]))
        put("w_dw", _fm(inp["w_dw"]).transpose(0, 1, 3, 2))
        put("hflag", np.full((128, 1), 0.0 if j == 0 else 1.0))
        put("eps", np.full((128, 1), EPS))
        put("one", np.full((128, 1), 1.0))
        selv = np.zeros((128, 4), np.float32)
        selv[:, j] = 1.0
        put("sel", selv)
        put("g_gn", _fm(inp["g_gla_norm"][0]))
        put("ident", np.eye(128, dtype=np.float32))
        m["vecs"] = vecs
        sc = state_conv[:, s0:s0 + 16]
        m["sc_fm"] = np.ascontiguousarray(_fm(sc).transpose(0, 1, 4, 2, 3))
        m["sc_old"] = np.ascontiguousarray(sc[:, :, 8:, :])
        ac = _att_consts()
        pos = np.concatenate([LP * j + np.arange(LP), 2048 + (np.arange(NS) % 8)]).astype(np.float64)
        inv = 10000.0 ** (-np.arange(32, dtype=np.float64) / 32)
        dd_ = np.arange(128) % 64
        ang = pos[None, :] * inv[dd_ % 32][:, None]
        sgn = np.where(dd_ < 32, -1.0, 1.0)[:, None]
        m["rope"] = np.stack([np.cos(ang), sgn * np.sin(ang)], axis=1).astype(np.float32)
        am = np.stack([ac["mprev"], ac["mprev"] if j > 0 else np.full((128, 128), NEG, np.float32), ac["mcur"]], axis=1)
        m["amask"] = np.ascontiguousarray(np.tile(am, (1, 1, 4)))
        pred = max(j - 1, 0)
        m["idxk"] = (pred * 128 + np.arange(128, dtype=np.int32)).reshape(128, 1).astype(np.int32)
        m["idxv"] = (pred * 896 + np.arange(7, dtype=np.int32)[None, :] * 128
                     + np.arange(128, dtype=np.int32)[:, None]).astype(np.int32)
        caches_k = (inp["cache_k_g0"], inp["cache_k_g1"], inp["cache_k_g2"])
        caches_v = (inp["cache_v_g0"], inp["cache_v_g1"], inp["cache_v_g2"])
        for g in range(3 if (NLAYERS >= 3 and _en('S')) else 0):
            m["ck%d" % g] = np.ascontiguousarray(f(caches_k[g])[0, s0:s0 + 16].reshape(16, -1, 512))
            m["cv%d" % g] = np.ascontiguousarray(f(caches_v[g])[0, s0:s0 + 16].reshape(16, -1, 512))
        m["sgla"] = np.ascontiguousarray(state_gla[0, s0:s0 + 16].transpose(2, 0, 1, 3))
        maps.append(m)
    return maps


NEG = -30000.0
_AC = {}


def _att_consts():
    if _AC:
        return _AC
    p = np.arange(128)
    _AC["mprev"] = np.where(p[:, None] >= p[None, :], 0.0, NEG).astype(np.float32)
    _AC["mcur"] = np.where(p[:, None] <= p[None, :], 0.0, NEG).astype(np.float32)
    m = np.arange(128)
    partner = np.where((m % 64) < 32, m + 32, m - 32)
    pm = np.zeros((128, 128), np.float32)
    pm[partner, m] = 1.0
    _AC["pm"] = pm
    n = np.arange(128)[:, None]
    i = np.arange(8)[None, :]
    sm = np.zeros((128, 13, 8), bool)
    sm[:, 0] = n >= i
    for r in range(4):
        sm[:, 1 + r] = ((i % 4) == r) & ~((i >= 4) & (n == 0))
    for r in range(8):
        sm[:, 5 + r] = (i == r) & (n >= 0)
    smf = np.where(sm, 0.0, NEG).astype(np.float32)
    _AC["smask"] = np.ascontiguousarray(np.tile(smf, (1, 1, 8)))
    kk = np.arange(128)[:, None]
    qq = np.arange(128)[None, :]
    same = (kk // 8) == (qq // 8)
    ki, qi = kk % 8, qq % 8
    sn = np.stack([same & (ki <= qi), same & ((ki == qi) | (ki == qi - 4)), same & (ki == qi)], axis=1)
    _AC["snew"] = np.ascontiguousarray(np.tile(np.where(sn, 0.0, NEG).astype(np.float32), (1, 1, 4)))
    return _AC


def _gla_masks():
    j = np.arange(128)
    same = (j[:, None] // 8) == (j[None, :] // 8)
    le = j[:, None] <= j[None, :]
    gt = j[:, None] > j[None, :]
    gm = np.zeros((128, 2, 3, 128), np.float32)
    gm[:, 0, 0] = np.where(le, -1.0 / 16, 0.0)
    gm[:, 0, 1] = np.where(gt, -1.0 / 16, 0.0)
    gm[:, 0, 2] = np.where(le, 1.0, 0.0)
    gm[:, 1, 0] = np.where(le & same, -1.0 / 16, 0.0)
    gm[:, 1, 1] = np.where(gt & same, -1.0 / 16, 0.0)
    gm[:, 1, 2] = np.where(le & same, 1.0, 0.0)
    seg = np.zeros((128, 2, 2, 16), np.float32)
    seg[:, 0, 0, 0] = -1.0 / 16
    seg[:, 0, 1, 0] = 1.0
    inseg = (j[:, None] // 8) == np.arange(16)[None, :]
    seg[:, 1, 0] = np.where(inseg, -1.0 / 16, 0.0)
    seg[:, 1, 1] = np.where(inseg, 1.0, 0.0)
    return gm, seg


def _tm(a):
    return np.ascontiguousarray(a.transpose(2, 1, 0).reshape(a.shape[2], -1))


_NC_CACHE = {}


def kernel(**inputs):
    if "nc" not in _NC_CACHE:
        _NC_CACHE["nc"] = build_program()
    nc = _NC_CACHE["nc"]
    maps = _prep_inputs(inputs)
    res = run_bass_kernel_spmd(nc, maps, core_ids=list(range(NCORES)))
    R = res.results
    B, DB, DS = 2, 128, 8
    y_prompt = np.zeros((B, SEQ, D), np.float32)
    y_sample = np.zeros((DB, DS, D), np.float32)
    conv_p = np.zeros((2, B, 30, D), np.float32)
    conv_s = np.zeros((2, DB, 30, D), np.float32)
    gla_p = np.zeros((1, B, 4, 128, 256), np.float32)
    gla_s = np.zeros((1, DB, 4, 128, 256), np.float32)
    kv_p = [np.zeros((1, B, w, 8, 64), np.float32) for w in (128, 128, 512, 512, 2048, 2048)]
    kv_s = [np.zeros((1, DB, DS, 8, 64), np.float32) for _ in range(6)]
    for c in range(NCORES):
        b, j = c // 4, c % 4
        s0 = 16 * c
        r = R[c]
        yt = _tm(r["yT"])
        y_prompt[b, LP * j:LP * (j + 1)] = yt[:LP]
        y_sample[s0:s0 + 16] = yt[LP:].reshape(16, 8, D)
        for jl in range(2):
            if j == 3:
                conv_p[jl, b] = _tm(r["conv_tail"][jl])[2:]
            conv_s[jl, s0:s0 + 16, :22] = r["conv_old"][jl]
            conv_s[jl, s0:s0 + 16, 22:] = _tm(r["conv_new"][jl]).reshape(16, 8, D)
        if j == 3:
            gla_p[0, b] = r["gla_p"].transpose(1, 0, 2)
        gla_s[0, s0:s0 + 16] = r["gla_s"].transpose(1, 2, 0, 3)
        if "kout" in r:
            for g, dd in enumerate((1, 4, 16)):
                kbase = (0, 512, 2560)[g]
                vbase = (0, 128, 640)[g]
                if j == 3:
                    blk = r["kout"][:, kbase:kbase + 4 * dd * 128].reshape(2, 64, 4, dd, 128)
                    kv_p[2 * g][0, b] = blk.transpose(4, 3, 2, 0, 1).reshape(128 * dd, 8, 64)
                    vblk = r["vout"][vbase:vbase + dd * 128].reshape(dd, 128, 512)
                    kv_p[2 * g + 1][0, b] = vblk.transpose(1, 0, 2).reshape(128 * dd, 8, 64)
                ks = r["ks_out"].reshape(2, 64, 3, 4, 16, 8)[:, :, g]
                kv_s[2 * g][0, s0:s0 + 16] = ks.transpose(3, 4, 2, 0, 1).reshape(16, 8, 8, 64)
                kv_s[2 * g + 1][0, s0:s0 + 16] = r["vs_out"].reshape(16, 8, 3, 8, 64)[:, :, g]
    return (y_prompt, y_sample, conv_p, conv_s, gla_p, gla_s, *kv_p, *kv_s)
```

```python
import numpy as np
from contextlib import ExitStack
import concourse.bass as bass
import concourse.mybir as mybir
from concourse.bass_utils import run_bass_kernel_spmd

F32 = mybir.dt.float32
BF16 = mybir.dt.bfloat16
I32 = mybir.dt.int32
AF = mybir.ActivationFunctionType
ALU = mybir.AluOpType
AX = mybir.AxisListType

NCORES = 8
D = 1024
KC = 8
LP = 2048
NS = 128
NT = LP + NS
SEQ = 8192
DEPTH = 4
EPS = 1e-6
NQ = 16
TN = 256
import os
NLAYERS = int(os.environ.get('NLAYERS', '4'))
ATT_EN = os.environ.get('ATT_EN', 'AXSHB')


def _en(x):
    return x in ATT_EN


A_PARTS = int(os.environ.get('A_PARTS', '31'))
QK_G = [int(c) for c in os.environ.get('QK_G', '012')]
QK_DMA = int(os.environ.get('QK_DMA', '7'))
QK_STEPS = int(os.environ.get('QK_STEPS', '9'))
B_STEPS = int(os.environ.get('B_STEPS', '9'))


class Tk:
    __slots__ = ("name", "w", "r", "multi", "wm", "psum")

    def __init__(self, name, multi=False, psum=False):
        self.name = name
        self.psum = psum
        self.w = None
        self.r = {}
        self.multi = multi
        self.wm = {}


class KB:
    def __init__(self, nc, es):
        self.nc = nc
        self.E = {"pe": nc.tensor, "act": nc.scalar, "dve": nc.vector, "pool": nc.gpsimd, "sp": nc.sync}
        self.sem = {k: es.enter_context(nc.semaphore("s_" + k)) for k in self.E}
        self.cnt = {k: 0 for k in self.E}
        self.seen = {k: {} for k in self.E}
        self.pend = {k: [] for k in self.E}
        self.dsem = {q: [es.enter_context(nc.semaphore("d_%s%d" % (q, i))) for i in range(NQ)]
                     for q in ("sp", "pool")}
        self.dcnt = {}
        for q in self.dsem:
            for s in self.dsem[q]:
                self.dcnt[s.name] = 0
        self.dnext = {q: 0 for q in self.dsem}
        self.ccsem = es.enter_context(nc.semaphore("cc_sem"))
        self.ccn = 0
        self.semobj = {self.ccsem.name: self.ccsem}
        for s in self.sem.values():
            self.semobj[s.name] = s
        for q in self.dsem:
            for s in self.dsem[q]:
                self.semobj[s.name] = s

    def _waits(self, e, reads, writes):
        need = {}

        def add(tok):
            if tok is None:
                return
            n, v = tok
            if need.get(n, 0) < v:
                need[n] = v
        own = self.sem[e].name if e in self.sem else None
        for t in reads:
            if t.multi:
                for n, v in t.wm.items():
                    add((n, v))
            else:
                add(t.w)
            if t.psum:
                for n, v in t.r.items():
                    if n != own:
                        add((n, v))
        for t in writes:
            if not t.multi:
                add(t.w)
            for n, v in t.r.items():
                add((n, v))
        for n, v in need.items():
            if self.seen[e].get(n, 0) >= v:
                continue
            if e == "pe" and n == self.sem["pe"].name:
                continue
            self.E[e].wait_ge(self.semobj[n], v)
            self.seen[e][n] = v

    def _record(self, tok, reads, writes):
        n, v = tok
        for t in reads:
            if t.r.get(n, 0) < v:
                t.r[n] = v
        for t in writes:
            if t.multi:
                if t.wm.get(n, 0) < v:
                    t.wm[n] = v
            else:
                t.w = tok
                t.r = {}

    def op(self, e, fn, reads=(), writes=(), inc=True):
        self._waits(e, reads, writes)
        ins = fn(self.E[e])
        if not inc:
            self.pend[e].append((tuple(reads), tuple(writes)))
            return
        self.cnt[e] += 1
        ins.then_inc(self.sem[e], 1)
        tok = (self.sem[e].name, self.cnt[e])
        for (r, w) in self.pend[e]:
            self._record(tok, r, w)
        self.pend[e] = []
        self._record(tok, reads, writes)

    def dma(self, q, out, in_, reads=(), writes=(), **kw):
        i = self.dnext[q]
        self.dnext[q] = (i + 1) % NQ
        s = self.dsem[q][i]
        prev = self.dcnt[s.name]
        if prev and self.seen[q].get(s.name, 0) < prev:
            self.E[q].wait_ge(s, prev)
            self.seen[q][s.name] = prev
        self._waits(q, reads, writes)
        self.E[q].dma_start(out=out, in_=in_, **kw).then_inc(s, 16)
        self.dcnt[s.name] = prev + 16
        self._record((s.name, prev + 16), reads, writes)

    def collective(self, fn, reads=(), writes=()):
        self._waits("pool", reads, writes)
        self.ccn += 1
        fn(self.E["pool"]).then_inc(self.ccsem, 1)
        self._record((self.ccsem.name, self.ccn), reads, writes)

    def dma_custom(self, q, fn, reads=(), writes=()):
        i = self.dnext[q]
        self.dnext[q] = (i + 1) % NQ
        s = self.dsem[q][i]
        prev = self.dcnt[s.name]
        if prev and self.seen[q].get(s.name, 0) < prev:
            self.E[q].wait_ge(s, prev)
            self.seen[q][s.name] = prev
        self._waits(q, reads, writes)
        fn(self.E[q]).then_inc(s, 16)
        self.dcnt[s.name] = prev + 16
        self._record((s.name, prev + 16), reads, writes)

    def barrier_on(self, k):
        n, v = self.sem[k].name, self.cnt[k]
        for e in self.E:
            if e != k and v and self.seen[e].get(n, 0) < v:
                self.E[e].wait_ge(self.sem[k], v)
                self.seen[e][n] = v

    def barrier(self):
        for e in self.E:
            for n, s in self.semobj.items():
                if n in self.dcnt:
                    v = self.dcnt[n]
                elif n == self.ccsem.name:
                    v = self.ccn
                else:
                    k = [kk for kk in self.sem if self.sem[kk].name == n][0]
                    if k == e:
                        continue
                    v = self.cnt[k]
                if v and self.seen[e].get(n, 0) < v:
                    self.E[e].wait_ge(s, v)
                    self.seen[e][n] = v

    def final_wait(self):
        e = "sp"
        for n, v in self.dcnt.items():
            if v and self.seen[e].get(n, 0) < v:
                self.E[e].wait_ge(self.semobj[n], v)
                self.seen[e][n] = v


def _vec_layout():
    lay = {}
    off = 0

    def add(name, n):
        nonlocal off
        lay[name] = (off, n)
        off += n
    add("g_pre", 4 * 8)
    add("g_post", 4 * 8)
    add("b_ada", 4 * 24)
    add("b_dw", 2 * 8)
    add("g_cln", 2 * 8)
    add("b_cln", 2 * 8)
    add("w_dw", 2 * 8 * 31)
    add("hflag", 1)
    add("eps", 1)
    add("one", 1)
    add("sel", 4)
    add("g_gn", 2)
    add("ident", 128)
    return lay, off


VLAY, NV = _vec_layout()


def _fm(v):
    v = np.asarray(v, np.float32)
    n = v.shape[-1] // 128
    r = v.reshape(v.shape[:-1] + (n, 128))
    return np.moveaxis(r, -1, 0)


def build_program():
    nc = bass.Bass("TRN2", target_bir_lowering=False)
    es = ExitStack()
    kb = KB(nc, es)

    def din(name, shape, dt=F32):
        return nc.dram_tensor(name, list(shape), dt, kind="ExternalInput").ap()

    def dout(name, shape, dt=F32):
        return nc.dram_tensor(name, list(shape), dt, kind="ExternalOutput").ap()

    xT_in = din("xT", [128, KC, NT])
    xh_in = din("xh", [128, KC, 32])
    cT_in = din("cT", [128, KC, 129])
    vecs_in = din("vecs", [128, NV])
    w_ada_in = din("w_ada", [DEPTH, D, 3 * D])
    w_conv_in_in = din("w_conv_in", [2, D, 3 * D])
    w_conv_out_in = din("w_conv_out", [2, D, D])
    sc_fm_in = din("sc_fm", [128, 2, KC, 16, 30])
    sc_old_in = din("sc_old", [2, 16, 22, D])
    w_gla_in_in = din("w_gla_in", [D, 3 * D])
    w_gla_out_in = din("w_gla_out", [D, D])
    w_a1_in = din("w_a1", [D, 16])
    w_a2_in = din("w_a2", [16, 512])
    b_a_in = din("b_a", [1, 512])
    gmask_in = din("gmask", [128, 2, 3, 128])
    gseg_in = din("gseg", [128, 2, 2, 16])
    sgla_in = din("sgla", [128, 16, 4, 256])
    w_att_in_in = din("w_att_in", [D, 5120])
    w_att_out_in = din("w_att_out", [512, D])
    rope_in = din("rope", [128, 2, NT])
    pm_in = din("pm", [128, 128])
    amask_in = din("amask", [128, 3, 512])
    smask_in = din("smask", [128, 13, 64])
    snew_in = din("snew", [128, 3, 512])
    idxk_in = din("idxk", [128, 1], I32)
    idxv_in = din("idxv", [128, 7], I32)
    if NLAYERS >= 3 and _en('S'):
        ck_in = [din("ck%d" % g, [16, w, 512]) for g, w in enumerate((128, 512, 2048))]
        cv_in = [din("cv%d" % g, [16, w, 512]) for g, w in enumerate((128, 512, 2048))]
    kout = dout("kout", [128, 10752])
    vout = dout("vout", [2688, 512])
    ks_out = dout("ks_out", [128, 12, NS])
    vs_out = dout("vs_out", [NS, 1536])
    gla_p_out = dout("gla_p", [128, 4, 256])
    gla_s_out = dout("gla_s", [128, 16, 4, 256])
    yT_out = dout("yT", [128, KC, NT])
    conv_tail_out = dout("conv_tail", [2, 128, KC, 32])
    conv_new_out = dout("conv_new", [2, 128, KC, NS])
    conv_old_out = dout("conv_old", [2, 16, 22, D])
    xs = nc.dram_tensor("xs", [128, KC, NT], F32)

    uid = [0]

    def S(name, shape, dt, stack=es):
        uid[0] += 1
        return stack.enter_context(nc.sbuf_tensor("sb_%s_%d" % (name, uid[0]), list(shape), dt))

    vecs = S("vecs_sb", [128, NV], F32)
    ones_bf = S("ones_bf", [128, 128], BF16)
    ident_bf = S("ident_bf", [128, 128], BF16)
    cT_bf = S("cT_bf", [128, KC, 129], BF16)
    modT = S("modT", [128, 24, 129], F32)
    modp = S("modp", [128, 3, 8], F32)
    mods = S("mods", [128, 2, 8, NS], F32)
    t_vecs, t_const, t_cT, t_modT, t_modd = Tk("vecs"), Tk("const"), Tk("cT"), Tk("modT"), Tk("modd")
    ps = [es.enter_context(nc.psum_tensor("ps%d" % i, [128, 512], F32)) for i in range(8)]
    t_ps = [Tk("ps%d" % i, psum=True) for i in range(8)]
    psn = [0]
    ps_reserved = set()

    def next_ps():
        while True:
            i = psn[0]
            psn[0] = (i + 1) % 8
            if i not in ps_reserved:
                return ps[i], t_ps[i]

    def V(name, i0=0, n=None):
        off, w = VLAY[name]
        if n is None:
            n = w - i0
        return vecs[:, off + i0: off + i0 + n]

    kb.dma("sp", vecs[:, :], vecs_in[:, :], writes=[t_vecs])
    kb.dma("pool", cT_bf[:, :, :], cT_in[:, :, :], writes=[t_cT])
    kb.op("dve", lambda e: e.memset(ones_bf[:, :], 1.0), writes=[t_const])
    kb.op("dve", lambda e: e.tensor_copy(out=ident_bf[:, :], in_=V("ident")), reads=[t_vecs], writes=[t_const])

    t_x = {}

    def xtk(key):
        if key not in t_x:
            t_x[key] = Tk("x%s" % (key,))
        return t_x[key]

    TILES = [(i * TN, TN) for i in range(LP // TN)] + [(LP, NS)]

    def compute_mod(l, ls, after_issue=None):
        wsrc = w_ada_in[l].rearrange("(kc p) n -> p kc n", p=128)
        ws = ExitStack()
        wb = [S("wada%d" % i, [128, KC, 512], BF16, ws) for i in range(6)]
        t_wb = [Tk("wada%d" % i) for i in range(6)]
        for blk in range(6):
            kb.dma("pool", wb[blk][:, :, :], wsrc[:, :, blk * 512:(blk + 1) * 512], writes=[t_wb[blk]])
        if after_issue is not None:
            after_issue()
        for blk in range(6):
            b = blk
            for mm in range(4):
                m = blk * 4 + mm
                p, tp = next_ps()
                for kc in range(KC):
                    kb.op("pe", lambda e, p=p, b=b, mm=mm, kc=kc: e.matmul(
                        p[:, 0:129], lhsT=wb[b][:, kc, mm * 128:(mm + 1) * 128], rhs=cT_bf[:, kc, :],
                        start=(kc == 0), stop=(kc == KC - 1)),
                        reads=[t_wb[b], t_cT], writes=[tp], inc=(kc == KC - 1))
                kb.op("act", lambda e, p=p, m=m: e.activation(
                    out=modT[:, m, :], in_=p[:, 0:129], func=AF.Identity,
                    bias=V("b_ada", l * 24 + m, 1), scale=1.0),
                    reads=[tp, t_vecs], writes=[t_modT])
        gpre = V("g_pre", l * 8, 8)
        gpost = V("g_post", l * 8, 8)
        kb.op("dve", lambda e: e.scalar_tensor_tensor(
            out=modp[:, 0, :], in0=modT[:, 8:16, 128], scalar=1.0, in1=gpre, op0=ALU.add, op1=ALU.mult),
            reads=[t_modT, t_vecs], writes=[t_modd])
        kb.op("dve", lambda e: e.tensor_copy(out=modp[:, 1, :], in_=modT[:, 0:8, 128]),
              reads=[t_modT], writes=[t_modd])
        kb.op("dve", lambda e: e.tensor_tensor(
            out=modp[:, 2, :], in0=modT[:, 16:24, 128], in1=gpost, op=ALU.mult),
            reads=[t_modT, t_vecs], writes=[t_modd])
        kb.op("dve", lambda e: e.scalar_tensor_tensor(
            out=mods[:, 0, :, :], in0=modT[:, 8:16, 0:NS], scalar=1.0,
            in1=gpre.unsqueeze(2).broadcast_to([128, 8, NS]), op0=ALU.add, op1=ALU.mult),
            reads=[t_modT, t_vecs], writes=[t_modd])
        kb.op("dve", lambda e: e.tensor_tensor(
            out=mods[:, 1, :, :], in0=modT[:, 16:24, 0:NS],
            in1=gpost.unsqueeze(2).broadcast_to([128, 8, NS]), op=ALU.mult),
            reads=[t_modT, t_vecs], writes=[t_modd])
        kb.barrier_on("pe")
        ws.close()

    class Bufs:
        pass

    def rstd_from_ps(p, tp, n, out, t_out, scale=1.0 / D):
        kb.op("act", lambda e: e.activation(out=out[:, 0:n], in_=p[:, 0:n], func=AF.Sqrt, bias=V("eps"), scale=scale),
              reads=[tp, t_vecs], writes=[t_out])
        kb.op("dve", lambda e: e.reciprocal(out=out[:, 0:n], in_=out[:, 0:n]), reads=[t_out], writes=[t_out])

    def prenorm(B, xt, t_xt, n, sample, hout=None, t_hout=None):
        if hout is None:
            hout, t_hout = B.hT[:, :, 0:n], B.t_hT
        kb.op("act", lambda e: e.activation(out=B.sq[:, :, 0:n], in_=xt[:, :, 0:n], func=AF.Square),
              reads=[t_xt], writes=[B.t_sq])
        p, tp = next_ps()
        for kc in range(KC):
            kb.op("pe", lambda e, kc=kc: e.matmul(p[:, 0:n], lhsT=ones_bf[:, :], rhs=B.sq[:, kc, 0:n],
                                                   start=(kc == 0), stop=(kc == KC - 1)),
                  reads=[B.t_sq, t_const], writes=[tp], inc=(kc == KC - 1))
        rstd_from_ps(p, tp, n, B.rstd, B.t_rstd)
        kb.op("dve", lambda e: e.tensor_tensor(
            out=B.t1[:, :, 0:n], in0=xt[:, :, 0:n],
            in1=B.rstd[:, 0:n].unsqueeze(1).broadcast_to([128, KC, n]), op=ALU.mult),
            reads=[t_xt, B.t_rstd], writes=[B.t_t1])
        if not sample:
            for kc in range(KC):
                kb.op("act", lambda e, kc=kc: e.activation(
                    out=hout[:, kc, :], in_=B.t1[:, kc, 0:n], func=AF.Identity,
                    bias=modp[:, 1, kc:kc + 1], scale=modp[:, 0, kc:kc + 1]),
                    reads=[B.t_t1, t_modd], writes=[t_hout])
        else:
            kb.op("pool", lambda e: e.tensor_tensor(out=B.t1[:, :, 0:n], in0=B.t1[:, :, 0:n],
                                                    in1=mods[:, 0, :, :], op=ALU.mult),
                  reads=[B.t_t1, t_modd], writes=[B.t_t1])
            kb.op("pool", lambda e: e.tensor_tensor(out=hout, in0=B.t1[:, :, 0:n],
                                                    in1=modT[:, 0:8, 0:NS], op=ALU.add),
                  reads=[B.t_t1, t_modT], writes=[t_hout])

    def postnorm_residual(B, xt, t_xt, n, sample, l):
        p, tp = next_ps()
        for kc in range(KC):
            kb.op("pe", lambda e, kc=kc: e.matmul(p[:, 0:n], lhsT=ones_bf[:, :], rhs=B.sq[:, kc, 0:n],
                                                   start=(kc == 0), stop=(kc == KC - 1)),
                  reads=[B.t_sq, t_const], writes=[tp], inc=(kc == KC - 1))
        rstd_from_ps(p, tp, n, B.rstd, B.t_rstd)
        kb.op("dve", lambda e: e.tensor_tensor(
            out=B.oT[:, :, 0:n], in0=B.oT[:, :, 0:n],
            in1=B.rstd[:, 0:n].unsqueeze(1).broadcast_to([128, KC, n]), op=ALU.mult),
            reads=[B.t_oT, B.t_rstd], writes=[B.t_oT])
        if not sample:
            for kc in range(KC):
                kb.op("dve", lambda e, kc=kc: e.scalar_tensor_tensor(
                    out=xt[:, kc, 0:n], in0=B.oT[:, kc, 0:n], scalar=modp[:, 2, kc:kc + 1],
                    in1=xt[:, kc, 0:n], op0=ALU.mult, op1=ALU.add),
                    reads=[B.t_oT, t_modd, t_xt], writes=[t_xt])
        else:
            kb.op("pool", lambda e: e.tensor_tensor(out=B.oT[:, :, 0:n], in0=B.oT[:, :, 0:n],
                                                    in1=mods[:, 1, :, :], op=ALU.mult),
                  reads=[B.t_oT, t_modd], writes=[B.t_oT])
            kb.op("pool", lambda e: e.tensor_tensor(out=xt[:, :, 0:n], in0=xt[:, :, 0:n],
                                                    in1=B.oT[:, :, 0:n], op=ALU.add),
                  reads=[B.t_oT, t_xt], writes=[t_xt])

    def common_bufs(ls, with_hT=True, nxt=1):
        B = Bufs()
        B.xt = [S("xt%d" % i, [128, KC, TN], F32, ls) for i in range(nxt)]
        B.t_xt = [Tk("xt%d" % i) for i in range(nxt)]
        B.sq = S("sq", [128, KC, TN], BF16, ls)
        B.t_sq = Tk("sq")
        B.rstd = S("rstd", [128, TN], F32, ls)
        B.t_rstd = Tk("rstd")
        B.oT = S("oT", [128, KC, TN], F32, ls)
        B.t_oT = Tk("oT")
        B.t1 = B.oT
        B.t_t1 = B.t_oT
        if with_hT:
            B.hT = S("hT", [128, KC, TN], BF16, ls)
            B.t_hT = Tk("hT")
        return B

    def load_w(ls, name, src2d, ncols, blk=512, issue=True):
        w = S(name, [128, KC, ncols], BF16, ls)
        src = src2d.rearrange("(kc p) n -> p kc n", p=128)
        tks = [Tk("%s_%d" % (name, b0)) for b0 in range(0, ncols, blk)]

        def do_issue():
            for i, b0 in enumerate(range(0, ncols, blk)):
                kb.dma("pool", w[:, :, b0:b0 + blk], src[:, :, b0:b0 + blk], writes=[tks[i]])
        if issue:
            do_issue()
            return w, tks, blk
        return w, tks, blk, do_issue

    def x_src(l):
        return xT_in if l == 0 else xs

    def x_dst(l):
        return yT_out if l == NLAYERS - 1 else xs

    def layer_conv(l, jl):
        ls = ExitStack()
        w_in, t_win, wblk, iss1 = load_w(ls, "w_in", w_conv_in_in[jl], 3 * D, issue=False)
        w_out, t_wout, _, iss2 = load_w(ls, "w_out", w_conv_out_in[jl], D, issue=False)
        compute_mod(l, ls, lambda: (iss1(), iss2()))
        B = common_bufs(ls, with_hT=False, nxt=2)
        SN = 512
        ub = [S("ub%d" % i, [128, KC, 32 + SN], BF16, ls) for i in range(2)]
        hT5 = S("hT5", [128, KC, SN], BF16, ls)
        t_hT5 = Tk("hT5")
        t_ub = [Tk("ub%d" % i) for i in range(2)]
        ues = S("ues", [128, KC, 16, 38], BF16, ls)
        t_ues = Tk("ues")
        sg = S("sg", [128, SN], F32, ls)
        t_sg = Tk("sg")
        sz = S("sz", [128, KC, SN], BF16, ls)
        t_sz = Tk("sz")
        NDG = 3
        Dg = [S("Dg%d" % i, [128, 31, 128], BF16, ls) for i in range(NDG)]
        t_Dg = [Tk("Dg%d" % i) for i in range(NDG)]
        t_DgB = [Tk("DgB%d" % i) for i in range(NDG)]
        yT = S("yT", [128, KC, TN], F32, ls)
        t_yT = Tk("yT")
        ybf = S("ybf", [128, KC, TN], BF16, ls)
        t_ybf = Tk("ybf")
        mean = S("mean", [128, TN], F32, ls)
        t_mean = Tk("mean")
        var, t_var = B.rstd, B.t_rstd
        yg, t_yg = ybf, t_ybf
        u32 = S("u32", [128, KC, NS], F32, ls)
        t_u32 = Tk("u32")
        xh = S("xh", [128, KC, 32], F32, ls)
        t_xh = Tk("xh")
        dcnt = [0]
        dg_carry = [None]

        def wtk(col0):
            return t_win[col0 // wblk]

        def inproj_u(n, utarget, t_ut, u32cols=None, r3=False, hsrc=None, t_hsrc=None):
            if hsrc is None:
                hsrc, t_hsrc = B.hT, B.t_hT
            def vw(ap):
                return ap.rearrange("p (s i) -> p s i", i=8) if r3 else ap
            for c in range(KC):
                pa, tpa = next_ps()
                pg, tpg = next_ps()
                for kc in range(KC):
                    kb.op("pe", lambda e, kc=kc, c=c, pa=pa: e.matmul(
                        pa[:, 0:n], lhsT=w_in[:, kc, c * 128:(c + 1) * 128], rhs=hsrc[:, kc, 0:n],
                        start=(kc == 0), stop=(kc == KC - 1)),
                        reads=[wtk(c * 128), t_hsrc], writes=[tpa], inc=(kc == KC - 1))
                for kc in range(KC):
                    kb.op("pe", lambda e, kc=kc, c=c, pg=pg: e.matmul(
                        pg[:, 0:n], lhsT=w_in[:, kc, D + c * 128:D + (c + 1) * 128], rhs=hsrc[:, kc, 0:n],
                        start=(kc == 0), stop=(kc == KC - 1)),
                        reads=[wtk(D + c * 128), t_hsrc], writes=[tpg], inc=(kc == KC - 1))
                kb.op("act", lambda e, pg=pg: e.activation(out=sg[:, 0:n], in_=pg[:, 0:n], func=AF.Sigmoid),
                      reads=[tpg], writes=[t_sg])
                kb.op("dve", lambda e, c=c, pa=pa: e.tensor_tensor(out=utarget(c), in0=vw(pa[:, 0:n]),
                                                                   in1=vw(sg[:, 0:n]), op=ALU.mult),
                      reads=[tpa, t_sg], writes=[t_ut])
                if u32cols is not None:
                    c0, nn = u32cols
                    kb.op("dve", lambda e, c=c, pa=pa: e.tensor_tensor(
                        out=u32[:, c, 0:nn], in0=pa[:, c0:c0 + nn], in1=sg[:, c0:c0 + nn], op=ALU.mult),
                        reads=[tpa, t_sg], writes=[t_u32])

        if l == 0:
            kb.dma("sp", xh[:, :, :], xh_in[:, :, :], writes=[t_xh])
            prenorm(B, xh, t_xh, 32, False, hout=hT5[:, :, 0:32], t_hout=t_hT5)
            inproj_u(32, lambda c: ub[0][:, c, 0:32], t_ub[0], hsrc=hT5, t_hsrc=t_hT5)
            kb.op("dve", lambda e: e.tensor_tensor(
                out=ub[0][:, :, 0:32], in0=ub[0][:, :, 0:32],
                in1=V("hflag").unsqueeze(1).broadcast_to([128, KC, 32]), op=ALU.mult),
                reads=[t_ub[0], t_vecs], writes=[t_ub[0]])
        else:
            utl = S("utl", [128, KC, 32], BF16, ls)
            t_utl = Tk("utl")
            gsl = S("gsl", [128, 4, KC * 32], BF16, ls)
            t_gsl = Tk("gsl")
            gxc = nc.dram_tensor("cv_gx%d" % l, [128, KC * 32], BF16)
            ggc = nc.dram_tensor("cv_gg%d" % l, [512, KC * 32], BF16)
            t_gxc, t_ggc = Tk("gxc"), Tk("ggc")
            kb.dma("sp", xh[:, :, :], x_src(l)[:, :, LP - 32:LP], reads=[xtk(LP // TN - 1)], writes=[t_xh])
            prenorm(B, xh, t_xh, 32, False, hout=hT5[:, :, 0:32], t_hout=t_hT5)
            inproj_u(32, lambda c: utl[:, c, :], t_utl, hsrc=hT5, t_hsrc=t_hT5)
            kb.dma("sp", gxc[:, :], utl[:, :, :].rearrange("p c t -> p (c t)"), reads=[t_utl], writes=[t_gxc])
            kb.collective(lambda e: e.collective_compute(
                "AllGather", ALU.bypass, replica_groups=[[0, 1, 2, 3], [4, 5, 6, 7]],
                ins=[gxc[:, :]], outs=[ggc[:, :]]), reads=[t_gxc], writes=[t_ggc])
            kb.dma("sp", gsl[:, :, :], ggc.ap().rearrange("(r p) n -> p r n", p=128), reads=[t_ggc], writes=[t_gsl])
            ubv = ub[0][:, :, 0:32]

            def slot(i):
                return gsl[:, i, :].rearrange("p (c t) -> p c t", t=32)
            kb.op("dve", lambda e: e.tensor_scalar(out=ubv, in0=slot(0), scalar1=V("sel", 1, 1), scalar2=None,
                                                   op0=ALU.mult), reads=[t_gsl, t_vecs], writes=[t_ub[0]])
            for i in (1, 2):
                kb.op("dve", lambda e, i=i: e.scalar_tensor_tensor(
                    out=ubv, in0=slot(i), scalar=V("sel", i + 1, 1), in1=ubv, op0=ALU.mult, op1=ALU.add),
                    reads=[t_gsl, t_vecs, t_ub[0]], writes=[t_ub[0]])

        for kc in range(KC):
            kb.dma("pool", ues[:, kc, :, 0:30], sc_fm_in[:, jl, kc, :, :], writes=[t_ues])
        kb.dma("sp", conv_old_out[jl], sc_old_in[jl])

        NST = LP // SN
        for st in range(NST + 1):
            sample = (st == NST)
            ui = st % 2
            if not sample:
                halves = [(SN * st + TN * hf, TN, 2 * st + hf) for hf in range(SN // TN)]
                nn5 = SN
            else:
                halves = [(LP, NS, len(TILES) - 1)]
                nn5 = NS
            for hi, (c0, n, ti) in enumerate(halves):
                xt, t_xt = B.xt[hi], B.t_xt[hi]
                kb.dma("sp", xt[:, :, 0:n], x_src(l)[:, :, c0:c0 + n], reads=[xtk(ti)], writes=[t_xt])
                prenorm(B, xt, t_xt, n, sample, hout=hT5[:, :, hi * TN:hi * TN + n], t_hout=t_hT5)
            n = nn5
            if not sample:
                last = (st == NST - 1)
                inproj_u(n, lambda c: ub[ui][:, c, 32:32 + n], t_ub[ui],
                         u32cols=((n - 32, 32) if last else None), hsrc=hT5, t_hsrc=t_hT5)
                if last:
                    kb.dma("sp", conv_tail_out[jl], u32[:, :, 0:32], reads=[t_u32])
                if st + 1 < NST:
                    kb.op("pool", lambda e: e.tensor_copy(out=ub[1 - ui][:, :, 0:32], in_=ub[ui][:, :, n:n + 32]),
                          reads=[t_ub[ui]], writes=[t_ub[1 - ui]])
            else:
                inproj_u(n, lambda c: ues[:, c, :, 30:38],
                         t_ues, u32cols=(0, NS), r3=True, hsrc=hT5, t_hsrc=t_hT5)
                kb.dma("sp", conv_new_out[jl], u32[:, :, :], reads=[t_u32])
            for c in range(KC):
                pz, tpz = next_ps()
                for kc in range(KC):
                    kb.op("pe", lambda e, kc=kc, c=c, pz=pz: e.matmul(
                        pz[:, 0:n], lhsT=w_in[:, kc, 2 * D + c * 128:2 * D + (c + 1) * 128], rhs=hT5[:, kc, 0:n],
                        start=(kc == 0), stop=(kc == KC - 1)),
                        reads=[wtk(2 * D + c * 128), t_hT5], writes=[tpz], inc=(kc == KC - 1))
                kb.op("act", lambda e, c=c, pz=pz: e.activation(out=sz[:, c, 0:n], in_=pz[:, 0:n], func=AF.Silu),
                      reads=[tpz], writes=[t_sz])
            for hi, (c0, n, ti) in enumerate(halves):
                h0 = hi * TN
                xt, t_xt = B.xt[hi], B.t_xt[hi]
                tx = xtk(ti)
                NDV = 20

                def build_dg(c_):
                    di_ = dcnt[0] % NDG
                    dcnt[0] += 1
                    wd_ = V("w_dw", (jl * 8 + c_) * 31, 31)
                    kb.op("dve", lambda e: e.tensor_tensor(
                        out=Dg[di_][:, 0:NDV, :], in0=ident_bf[:, :].unsqueeze(1).broadcast_to([128, NDV, 128]),
                        in1=wd_[:, 0:NDV].unsqueeze(2).broadcast_to([128, NDV, 128]), op=ALU.mult),
                        reads=[t_const, t_vecs], writes=[t_Dg[di_]])
                    kb.op("pool", lambda e: e.tensor_tensor(
                        out=Dg[di_][:, NDV:31, :], in0=ident_bf[:, :].unsqueeze(1).broadcast_to([128, 31 - NDV, 128]),
                        in1=wd_[:, NDV:31].unsqueeze(2).broadcast_to([128, 31 - NDV, 128]), op=ALU.mult),
                        reads=[t_const, t_vecs], writes=[t_DgB[di_]])
                    return di_
                di_next = dg_carry[0] if dg_carry[0] is not None else build_dg(0)
                for c in range(KC):
                    di = di_next
                    if c + 1 < KC:
                        di_next = build_dg(c + 1)
                    else:
                        dg_carry[0] = build_dg(0)
                    py, tpy = next_ps()
                    for k in range(31):
                        if not sample:
                            rhs = ub[ui][:, c, h0 + 2 + k:h0 + 2 + k + n]
                            rt = t_ub[ui]
                            outp = py[:, 0:n]
                        else:
                            rhs = ues[:, c, :, k:k + 8]
                            rt = t_ues
                            outp = py[:, 0:n].rearrange("p (s i) -> p s i", i=8)
                        kb.op("pe", lambda e, k=k, di=di, rhs=rhs, outp=outp: e.matmul(
                            outp, lhsT=Dg[di][:, k, :], rhs=rhs, start=(k == 0), stop=(k == 30)),
                            reads=[t_Dg[di] if k < 20 else t_DgB[di], rt], writes=[tpy], inc=(k == 30))
                    bdw = V("b_dw", jl * 8 + c, 1)
                    kb.op("act", lambda e, c=c, py=py, bdw=bdw: e.activation(
                        out=yT[:, c, 0:n], in_=py[:, 0:n], func=AF.Identity, bias=bdw, scale=1.0),
                        reads=[tpy, t_vecs], writes=[t_yT])
                    kb.op("act", lambda e, c=c, py=py, bdw=bdw: e.activation(
                        out=B.sq[:, c, 0:n], in_=py[:, 0:n], func=AF.Square, bias=bdw, scale=1.0),
                        reads=[tpy, t_vecs], writes=[B.t_sq])
                    kb.op("act", lambda e, c=c, py=py, bdw=bdw: e.activation(
                        out=ybf[:, c, 0:n], in_=py[:, 0:n], func=AF.Identity, bias=bdw, scale=1.0),
                        reads=[tpy, t_vecs], writes=[t_ybf])
                p1, tp1 = next_ps()
                p2, tp2 = next_ps()
                for c in range(KC):
                    kb.op("pe", lambda e, c=c: e.matmul(p1[:, 0:n], lhsT=ones_bf[:, :], rhs=ybf[:, c, 0:n],
                                                        start=(c == 0), stop=(c == KC - 1)),
                          reads=[t_ybf, t_const], writes=[tp1], inc=(c == KC - 1))
                for c in range(KC):
                    kb.op("pe", lambda e, c=c: e.matmul(p2[:, 0:n], lhsT=ones_bf[:, :], rhs=B.sq[:, c, 0:n],
                                                        start=(c == 0), stop=(c == KC - 1)),
                          reads=[B.t_sq, t_const], writes=[tp2], inc=(c == KC - 1))
                kb.op("dve", lambda e: e.tensor_scalar(out=mean[:, 0:n], in0=p1[:, 0:n], scalar1=1.0 / D, scalar2=None,
                                                       op0=ALU.mult), reads=[tp1], writes=[t_mean])
                kb.op("dve", lambda e: e.tensor_tensor(out=var[:, 0:n], in0=mean[:, 0:n], in1=mean[:, 0:n], op=ALU.mult),
                      reads=[t_mean], writes=[t_var])
                kb.op("dve", lambda e: e.scalar_tensor_tensor(
                    out=var[:, 0:n], in0=p2[:, 0:n], scalar=1.0 / D, in1=var[:, 0:n], op0=ALU.mult, op1=ALU.subtract),
                    reads=[tp2, t_var], writes=[t_var])
                kb.op("act", lambda e: e.activation(out=var[:, 0:n], in_=var[:, 0:n], func=AF.Sqrt, bias=V("eps"), scale=1.0),
                      reads=[t_var, t_vecs], writes=[t_var])
                kb.op("dve", lambda e: e.reciprocal(out=var[:, 0:n], in_=var[:, 0:n]), reads=[t_var], writes=[t_var])
                kb.op("dve", lambda e: e.tensor_tensor(
                    out=yT[:, :, 0:n], in0=yT[:, :, 0:n],
                    in1=mean[:, 0:n].unsqueeze(1).broadcast_to([128, KC, n]), op=ALU.subtract),
                    reads=[t_yT, t_mean], writes=[t_yT])
                kb.op("dve", lambda e: e.tensor_tensor(
                    out=yT[:, :, 0:n], in0=yT[:, :, 0:n],
                    in1=var[:, 0:n].unsqueeze(1).broadcast_to([128, KC, n]), op=ALU.mult),
                    reads=[t_yT, t_var], writes=[t_yT])
                for c in range(KC):
                    kb.op("act", lambda e, c=c: e.activation(
                        out=ybf[:, c, 0:n], in_=yT[:, c, 0:n], func=AF.Silu,
                        bias=V("b_cln", jl * 8 + c, 1), scale=V("g_cln", jl * 8 + c, 1)),
                        reads=[t_yT, t_vecs], writes=[t_ybf])
                kb.op("dve", lambda e: e.tensor_tensor(out=yg[:, :, 0:n], in0=ybf[:, :, 0:n], in1=sz[:, :, h0:h0 + n],
                                                        op=ALU.mult),
                      reads=[t_ybf, t_sz], writes=[t_yg])
                for m in range(KC):
                    po, tpo = next_ps()
                    for c in range(KC):
                        kb.op("pe", lambda e, c=c, m=m, po=po: e.matmul(
                            po[:, 0:n], lhsT=w_out[:, c, m * 128:(m + 1) * 128], rhs=yg[:, c, 0:n],
                            start=(c == 0), stop=(c == KC - 1)),
                            reads=[t_wout[m * 128 // 512], t_yg], writes=[tpo], inc=(c == KC - 1))
                    kb.op("act", lambda e, m=m, po=po: e.activation(out=B.oT[:, m, 0:n], in_=po[:, 0:n], func=AF.Identity),
                          reads=[tpo], writes=[B.t_oT])
                    kb.op("act", lambda e, m=m, po=po: e.activation(out=B.sq[:, m, 0:n], in_=po[:, 0:n], func=AF.Square),
                          reads=[tpo], writes=[B.t_sq])
                postnorm_residual(B, xt, t_xt, n, sample, l)
                kb.dma("sp", x_dst(l)[:, :, c0:c0 + n], xt[:, :, 0:n], reads=[t_xt], writes=[tx])
        kb.barrier()
        ls.close()


    def layer_gla(l, jl):
        ls = ExitStack()
        w_in, t_win, wblk, iss1 = load_w(ls, "wg_in", w_gla_in_in, 3 * D, issue=False)
        w_out, t_wout, _, iss2 = load_w(ls, "wg_out", w_gla_out_in, D, issue=False)
        compute_mod(l, ls, lambda: (iss1(), iss2()))
        B = common_bufs(ls)
        w_a1 = S("w_a1", [128, KC, 16], BF16, ls)
        t_wa = Tk("w_a")
        kb.dma("pool", w_a1[:, :, :], w_a1_in.rearrange("(kc p) n -> p kc n", p=128), writes=[t_wa])
        w_a2 = S("w_a2", [16, 512], BF16, ls)
        kb.dma("pool", w_a2[:, :], w_a2_in[:, :], writes=[t_wa])
        ba_row = S("ba_row", [1, 512], BF16, ls)
        kb.dma("pool", ba_row[:, :], b_a_in[:, :], writes=[t_wa])
        ones_row = S("ones_row", [1, 128], BF16, ls)
        kb.op("dve", lambda e: e.memset(ones_row[:, :], 1.0), writes=[t_wa])
        gm = S("gm", [128, 2, 3, 128], F32, ls)
        gseg = S("gseg", [128, 2, 2, 16], F32, ls)
        t_gm = Tk("gm")
        kb.dma("sp", gm[:, :, :, :], gmask_in[:, :, :, :], writes=[t_gm])
        kb.dma("sp", gseg[:, :, :, :], gseg_in[:, :, :, :], writes=[t_gm])

        def mk(name, shape, dt):
            return S(name, shape, dt, ls), Tk(name)
        qT, t_qT = mk("qT", [128, 4, TN], F32)
        kT, t_kT = mk("kT", [128, 4, TN], F32)
        rT, t_rT = mk("rT", [128, KC, TN], BF16)
        t1T, t_t1T = mk("t1T", [16, TN], BF16)
        cc2 = [dict(vtok=mk("vtok%d" % i, [128, 1024], BF16), la=mk("la%d" % i, [128, 512], F32),
                    ed=mk("ed%d" % i, [128, 512], F32), Kes=mk("Kes%d" % i, [128, 512], BF16),
                    ebt=mk("ebt%d" % i, [128, 4, 16], F32)) for i in range(2)]
        ccn = [0]
        vtok = t_vtok = la = t_la = ed = t_ed = Kes = t_Kes = ebt = t_ebt = None

        def use_cc():
            nonlocal vtok, t_vtok, la, t_la, ed, t_ed, Kes, t_Kes, ebt, t_ebt
            d = cc2[ccn[0] % 2]
            ccn[0] += 1
            (vtok, t_vtok), (la, t_la), (ed, t_ed), (Kes, t_Kes), (ebt, t_ebt) = (
                d["vtok"], d["la"], d["ed"], d["Kes"], d["ebt"])
        e1, t_e1 = mk("e1", [128, 4, 128], F32)
        e2, t_e2 = mk("e2", [128, 4, 128], F32)
        QeT, t_QeT = mk("QeT", [128, 4, 128], BF16)
        KeT, t_KeT = mk("KeT", [128, 4, 128], BF16)
        attm, t_attm = mk("attm", [128, 4, 128], BF16)
        go, t_go = mk("go", [128, KC, TN], F32)
        gsq, t_gsq = mk("gsq", [128, KC, TN], BF16)
        rsh, t_rsh = mk("rsh", [128, 4, 128], F32)
        Sst, t_S = mk("Sst", [128, 4, 256], F32)
        Sbf, t_Sbf = mk("Sbf", [128, 4, 256], BF16)
        Atot, t_Atot = mk("Atot", [128, 4], F32)
        big, t_big = mk("big16", [128, 4112], F32)
        accb, t_accb = mk("accb", [128, 4, 256], F32)
        S0bf, t_S0bf = mk("S0bf", [128, 4, 4, 256], BF16)
        Vblk, t_Vblk = mk("Vblk", [128, 4, 256], BF16)
        gx = nc.dram_tensor("gla_gx", [128, 1028], F32)
        gg = nc.dram_tensor("gla_gg", [512, 1028], F32)
        t_gx, t_gg = Tk("gx"), Tk("gg")
        DKS = 128.0 ** -0.5

        def wtk(col0):
            return t_win[col0 // wblk]

        def proj_fm(col0, n, evac):
            p, tp = next_ps()
            for kc in range(KC):
                kb.op("pe", lambda e, kc=kc: e.matmul(p[:, 0:n], lhsT=w_in[:, kc, col0:col0 + 128], rhs=B.hT[:, kc, 0:n],
                                                       start=(kc == 0), stop=(kc == KC - 1)),
                      reads=[wtk(col0), B.t_hT], writes=[tp], inc=(kc == KC - 1))
            evac(p, tp)

        def proj_tok(col0, c0):
            p, tp = next_ps()
            for kc in range(KC):
                kb.op("pe", lambda e, kc=kc: e.matmul(p[:, :], lhsT=B.hT[:, kc, c0:c0 + 128], rhs=w_in[:, kc, col0:col0 + 512],
                                                       start=(kc == 0), stop=(kc == KC - 1)),
                      reads=[wtk(col0), B.t_hT], writes=[tp], inc=(kc == KC - 1))
            return p, tp

        def tile_logarank(n):
            p, tp = next_ps()
            for kc in range(KC):
                kb.op("pe", lambda e, kc=kc: e.matmul(p[0:16, 0:n], lhsT=w_a1[:, kc, :], rhs=B.hT[:, kc, 0:n],
                                                       start=(kc == 0), stop=(kc == KC - 1)),
                      reads=[t_wa, B.t_hT], writes=[tp], inc=(kc == KC - 1))
            kb.op("act", lambda e: e.activation(out=t1T[:, 0:n], in_=p[0:16, 0:n], func=AF.Identity),
                  reads=[tp], writes=[t_t1T])

        def chunk_common(c0, mi, nseg):
            use_cc()
            pz, tpz = next_ps()
            kb.op("pe", lambda e: e.matmul(pz[:, :], lhsT=t1T[:, c0:c0 + 128], rhs=w_a2[:, :], start=True, stop=False),
                  reads=[t_t1T, t_wa], writes=[tpz], inc=False)
            kb.op("pe", lambda e: e.matmul(pz[:, :], lhsT=ones_row[:, :], rhs=ba_row[:, :], start=False, stop=True),
                  reads=[t_wa], writes=[tpz])
            kb.op("act", lambda e: e.activation(out=la[:, :], in_=pz[:, :], func=AF.Exp, scale=-1.0),
                  reads=[tpz], writes=[t_la])
            kb.op("act", lambda e: e.activation(out=la[:, :], in_=la[:, :], func=AF.Ln, bias=V("one"), scale=1.0),
                  reads=[t_la, t_vecs], writes=[t_la])
            pd, tpd = next_ps()
            kb.op("pe", lambda e: e.matmul(pd[:, :], lhsT=gm[:, mi, 1, :], rhs=la[:, :], start=True, stop=True),
                  reads=[t_gm, t_la], writes=[tpd])
            kb.op("act", lambda e: e.activation(out=ed[:, :], in_=pd[:, :], func=AF.Exp), reads=[tpd], writes=[t_ed])
            pk, tpk = proj_tok(512, c0)
            kb.op("dve", lambda e: e.tensor_tensor(out=Kes[:, :], in0=pk[:, :], in1=ed[:, :], op=ALU.mult),
                  reads=[tpk, t_ed], writes=[t_Kes])
            for half in range(2):
                pv, tpv = proj_tok(1024 + half * 512, c0)
                kb.op("act", lambda e, half=half, pv=pv: e.activation(out=vtok[:, half * 512:(half + 1) * 512], in_=pv[:, :],
                                                                    func=AF.Identity), reads=[tpv], writes=[t_vtok])
            pb, tpb = next_ps()
            for h in range(4):
                kb.op("pe", lambda e, h=h: e.matmul(pb[:, h * nseg:(h + 1) * nseg], lhsT=la[:, h * 128:(h + 1) * 128],
                                                     rhs=gseg[:, mi, 0, 0:nseg], start=True, stop=True),
                      reads=[t_la, t_gm], writes=[tpb], inc=(h == 3))
            kb.op("act", lambda e: e.activation(out=ebt[:, :, 0:nseg],
                                                in_=pb[:, 0:4 * nseg].rearrange("p (h s) -> p h s", s=nseg), func=AF.Exp),
                  reads=[tpb], writes=[t_ebt])

        def state_update_prompt(with_atot):
            for hp in range(2):
                p, tp = next_ps()
                for hh in range(2):
                    h = hp * 2 + hh
                    kb.op("pe", lambda e, h=h, hh=hh, p=p: e.matmul(
                        p[:, hh * 256:(hh + 1) * 256], lhsT=Kes[:, h * 128:(h + 1) * 128], rhs=vtok[:, h * 256:(h + 1) * 256],
                        start=True, stop=True), reads=[t_Kes, t_vtok], writes=[tp], inc=(hh == 1))
                for hh in range(2):
                    h = hp * 2 + hh
                    kb.op("dve", lambda e, h=h, hh=hh, p=p: e.scalar_tensor_tensor(
                        out=Sst[:, h, :], in0=Sst[:, h, :], scalar=ebt[:, h, 0:1], in1=p[:, hh * 256:(hh + 1) * 256],
                        op0=ALU.mult, op1=ALU.add), reads=[t_S, t_ebt, tp], writes=[t_S])
            if with_atot:
                kb.op("dve", lambda e: e.tensor_tensor(out=Atot[:, :], in0=Atot[:, :], in1=ebt[:, :, 0], op=ALU.mult),
                      reads=[t_Atot, t_ebt], writes=[t_Atot])

        def chunk_full(c0, mi, sample):
            pbc, tpbc = next_ps()
            for h in range(4):
                kb.op("pe", lambda e, h=h: e.matmul(pbc[:, h * 128:(h + 1) * 128], lhsT=la[:, h * 128:(h + 1) * 128],
                                                     rhs=gm[:, mi, 0, :], start=True, stop=True),
                      reads=[t_la, t_gm], writes=[tpbc], inc=(h == 3))
            kb.op("act", lambda e: e.activation(out=e1[:, :, :], in_=pbc[:, :].rearrange("p (h t) -> p h t", t=128),
                                                func=AF.Exp), reads=[tpbc], writes=[t_e1])
            kb.op("act", lambda e: e.activation(out=e2[:, :, :], in_=pbc[:, :].rearrange("p (h t) -> p h t", t=128),
                                                func=AF.Exp, scale=-1.0), reads=[tpbc], writes=[t_e2])
            kb.op("pool", lambda e: e.tensor_tensor(out=QeT[:, :, :], in0=qT[:, :, c0:c0 + 128], in1=e1[:, :, :], op=ALU.mult),
                  reads=[t_qT, t_e1], writes=[t_QeT])
            kb.op("pool", lambda e: e.tensor_tensor(out=KeT[:, :, :], in0=kT[:, :, c0:c0 + 128], in1=e2[:, :, :], op=ALU.mult),
                  reads=[t_kT, t_e2], writes=[t_KeT])
            pat, tpat = next_ps()
            for h in range(4):
                kb.op("pe", lambda e, h=h: e.matmul(pat[:, h * 128:(h + 1) * 128], lhsT=KeT[:, h, :], rhs=QeT[:, h, :],
                                                     start=True, stop=True),
                      reads=[t_KeT, t_QeT], writes=[tpat], inc=(h == 3))
            kb.op("dve", lambda e: e.tensor_tensor(
                out=attm[:, :, :], in0=pat[:, :].rearrange("p (h t) -> p h t", t=128),
                in1=gm[:, mi, 2, :].unsqueeze(1).broadcast_to([128, 4, 128]), op=ALU.mult),
                reads=[tpat, t_gm], writes=[t_attm])
            pos_ = [next_ps(), next_ps()]
            if not sample:
                kb.op("act", lambda e: e.activation(out=Sbf[:, :, :], in_=Sst[:, :, :], func=AF.Identity),
                      reads=[t_S], writes=[t_Sbf])
            for h in range(4):
                po, tpo = pos_[h // 2]
                for vc in range(2):
                    reg = po[:, ((h % 2) * 2 + vc) * 128:((h % 2) * 2 + vc + 1) * 128]
                    kb.op("pe", lambda e, h=h, vc=vc, reg=reg: e.matmul(
                        reg, lhsT=vtok[:, h * 256 + vc * 128:h * 256 + (vc + 1) * 128], rhs=attm[:, h, :],
                        start=(h % 2 == 0 and vc == 0), stop=False), reads=[t_vtok, t_attm], writes=[tpo], inc=False)
                    if not sample:
                        kb.op("pe", lambda e, h=h, vc=vc, reg=reg: e.matmul(
                            reg, lhsT=Sbf[:, h, vc * 128:(vc + 1) * 128], rhs=QeT[:, h, :], start=False, stop=True),
                            reads=[t_Sbf, t_QeT], writes=[tpo], inc=(h % 2 == 1 and vc == 1))
            if sample:
                for g in range(4):
                    kb.dma("pool", S0bf[:, :, :, :], sgla_in[:, 4 * g:4 * g + 4, :, :], writes=[t_S0bf])
                    for sl in range(4):
                        s_ = 4 * g + sl
                        for h in range(4):
                            po, tpo = pos_[h // 2]
                            for vc in range(2):
                                reg = po[:, ((h % 2) * 2 + vc) * 128 + 8 * s_:((h % 2) * 2 + vc) * 128 + 8 * s_ + 8]
                                lastm = (g == 3 and sl == 3 and vc == 1 and h % 2 == 1)
                                kb.op("pe", lambda e, h=h, vc=vc, reg=reg, sl=sl, s_=s_: e.matmul(
                                    reg, lhsT=S0bf[:, sl, h, vc * 128:(vc + 1) * 128], rhs=QeT[:, h, 8 * s_:8 * s_ + 8],
                                    start=False, stop=True), reads=[t_S0bf, t_QeT], writes=[tpo],
                                    inc=(lastm or (sl == 3 and vc == 1 and h == 3)))
            for hp in range(2):
                po, tpo = pos_[hp]
                kb.op("act", lambda e, hp=hp, po=po: e.activation(
                    out=go[:, hp * 4:(hp + 1) * 4, c0:c0 + 128], in_=po[:, :].rearrange("p (c t) -> p c t", t=128),
                    func=AF.Identity), reads=[tpo], writes=[t_go])
                kb.op("act", lambda e, hp=hp, po=po: e.activation(
                    out=gsq[:, hp * 4:(hp + 1) * 4, c0:c0 + 128], in_=po[:, :].rearrange("p (c t) -> p c t", t=128),
                    func=AF.Square), reads=[tpo], writes=[t_gsq])

        def tile_finish(n, xt, t_xt, sample):
            for c0 in range(0, n, 128):
                p, tp = next_ps()
                for h in range(4):
                    for vc in range(2):
                        kb.op("pe", lambda e, h=h, vc=vc: e.matmul(
                            p[:, h * 128:(h + 1) * 128], lhsT=ones_bf[:, :], rhs=gsq[:, h * 2 + vc, c0:c0 + 128],
                            start=(vc == 0), stop=(vc == 1)), reads=[t_gsq, t_const], writes=[tp],
                            inc=(h == 3 and vc == 1))
                kb.op("act", lambda e: e.activation(out=rsh[:, :, :], in_=p[:, :].rearrange("p (h t) -> p h t", t=128),
                                                    func=AF.Sqrt, bias=V("eps"), scale=1.0 / 256), reads=[tp, t_vecs],
                      writes=[t_rsh])
                kb.op("dve", lambda e: e.reciprocal(out=rsh[:, :, :], in_=rsh[:, :, :]), reads=[t_rsh], writes=[t_rsh])
                gv = go[:, :, c0:c0 + 128].rearrange("p (h v) t -> p h v t", v=2)
                kb.op("dve", lambda e, gv=gv: e.tensor_tensor(
                    out=gv, in0=gv, in1=rsh[:, :, :].unsqueeze(2).broadcast_to([128, 4, 2, 128]), op=ALU.mult),
                    reads=[t_go, t_rsh], writes=[t_go])
                for vc in range(2):
                    gvv = go[:, :, c0:c0 + 128].rearrange("p (h v) t -> p h v t", v=2)[:, :, vc, :]
                    rv = rT[:, :, c0:c0 + 128].rearrange("p (h v) t -> p h v t", v=2)[:, :, vc, :]
                    ov = gsq[:, :, c0:c0 + 128].rearrange("p (h v) t -> p h v t", v=2)[:, :, vc, :]
                    kb.op("dve", lambda e, gvv=gvv, rv=rv, ov=ov, vc=vc: e.scalar_tensor_tensor(
                        out=ov, in0=gvv, scalar=V("g_gn", vc, 1), in1=rv, op0=ALU.mult, op1=ALU.mult),
                        reads=[t_go, t_rT, t_vecs, t_gsq], writes=[t_gsq])
            for m in range(KC):
                po, tpo = next_ps()
                for c in range(KC):
                    kb.op("pe", lambda e, c=c, m=m, po=po: e.matmul(
                        po[:, 0:n], lhsT=w_out[:, c, m * 128:(m + 1) * 128], rhs=gsq[:, c, 0:n],
                        start=(c == 0), stop=(c == KC - 1)),
                        reads=[t_wout[m * 128 // 512], t_gsq], writes=[tpo], inc=(c == KC - 1))
                kb.op("act", lambda e, m=m, po=po: e.activation(out=B.oT[:, m, 0:n], in_=po[:, 0:n], func=AF.Identity),
                      reads=[tpo], writes=[B.t_oT])
                kb.op("act", lambda e, m=m, po=po: e.activation(out=B.sq[:, m, 0:n], in_=po[:, 0:n], func=AF.Square),
                      reads=[tpo], writes=[B.t_sq])
            postnorm_residual(B, xt, t_xt, n, sample, l)

        xt, t_xt = B.xt[0], B.t_xt[0]
        kb.op("dve", lambda e: e.memset(Sst[:, :, :], 0.0), writes=[t_S])
        kb.op("dve", lambda e: e.memset(Atot[:, :], 1.0), writes=[t_Atot])
        for ti, (c0t, n) in enumerate(TILES[:-1]):
            kb.dma("sp", xt[:, :, 0:n], x_src(l)[:, :, c0t:c0t + n], reads=[xtk(ti)], writes=[t_xt])
            prenorm(B, xt, t_xt, n, False)
            tile_logarank(n)
            for c0 in range(0, n, 128):
                chunk_common(c0, 0, 1)
                state_update_prompt(True)
        kb.dma("sp", gx[:, 0:1024], Sst[:, :, :].rearrange("p h v -> p (h v)"), reads=[t_S], writes=[t_gx])
        kb.dma("sp", gx[:, 1024:1028], Atot[:, :], reads=[t_Atot], writes=[t_gx])
        kb.collective(lambda e: e.collective_compute("AllGather", ALU.bypass, replica_groups=[[0, 1, 2, 3], [4, 5, 6, 7]],
                                                     ins=[gx[:, :]], outs=[gg[:, :]]), reads=[t_gx], writes=[t_gg])
        gsb = big[:, 0:4112].rearrange("p (r n) -> p r n", r=4)
        kb.dma("sp", gsb, gg.ap().rearrange("(r p) n -> p r n", p=128), reads=[t_gg], writes=[t_big])

        def Bs(i):
            return gsb[:, i, 0:1024].rearrange("p (h v) -> p h v", h=4)
        kb.op("dve", lambda e: e.tensor_copy(out=accb[:, :, :], in_=Bs(0)), reads=[t_big], writes=[t_accb])
        kb.op("dve", lambda e: e.tensor_scalar(out=Sst[:, :, :], in0=accb[:, :, :], scalar1=V("sel", 1, 1), scalar2=None,
                                               op0=ALU.mult), reads=[t_accb, t_vecs], writes=[t_S])
        for i in (1, 2):
            for h in range(4):
                kb.op("dve", lambda e, i=i, h=h: e.scalar_tensor_tensor(
                    out=accb[:, h, :], in0=accb[:, h, :], scalar=gsb[:, i, 1024 + h:1025 + h], in1=Bs(i)[:, h, :],
                    op0=ALU.mult, op1=ALU.add), reads=[t_accb, t_big], writes=[t_accb])
            kb.op("dve", lambda e, i=i: e.scalar_tensor_tensor(
                out=Sst[:, :, :], in0=accb[:, :, :], scalar=V("sel", i + 1, 1), in1=Sst[:, :, :],
                op0=ALU.mult, op1=ALU.add), reads=[t_accb, t_vecs, t_S], writes=[t_S])
        for ti, (c0t, n) in enumerate(TILES):
            sample = (c0t >= LP)
            mi = 1 if sample else 0
            tx = xtk(ti)
            kb.dma("sp", xt[:, :, 0:n], x_src(l)[:, :, c0t:c0t + n], reads=[tx], writes=[t_xt])
            prenorm(B, xt, t_xt, n, sample)
            tile_logarank(n)
            for h in range(4):
                proj_fm(h * 128, n, lambda p, tp, h=h: kb.op("act", lambda e: e.activation(
                    out=qT[:, h, 0:n], in_=p[:, 0:n], func=AF.Identity, scale=DKS), reads=[tp], writes=[t_qT]))
                proj_fm(512 + h * 128, n, lambda p, tp, h=h: kb.op("act", lambda e: e.activation(
                    out=kT[:, h, 0:n], in_=p[:, 0:n], func=AF.Identity), reads=[tp], writes=[t_kT]))
            for c in range(KC):
                proj_fm(2048 + c * 128, n, lambda p, tp, c=c: kb.op("act", lambda e: e.activation(
                    out=rT[:, c, 0:n], in_=p[:, 0:n], func=AF.Silu), reads=[tp], writes=[t_rT]))
            for c0 in range(0, n, 128):
                chunk_common(c0, mi, 16 if sample else 1)
                chunk_full(c0, mi, sample)
                if not sample:
                    state_update_prompt(False)
                else:
                    S0g = big[:, 0:4096].rearrange("p (s h v) -> p s h v", s=4, h=4)
                    for g in range(4):
                        kb.dma("sp", S0g, sgla_in[:, 4 * g:4 * g + 4, :, :], writes=[t_big])
                        for h in range(4):
                            kb.op("pool", lambda e, h=h, g=g: e.tensor_tensor(
                                out=Vblk[:, :, :], in0=vtok[:, h * 256:(h + 1) * 256].unsqueeze(1).broadcast_to([128, 4, 256]),
                                in1=gseg[:, 1, 1, 4 * g:4 * g + 4].unsqueeze(2).broadcast_to([128, 4, 256]), op=ALU.mult),
                                reads=[t_vtok, t_gm], writes=[t_Vblk])
                            for half in range(2):
                                p, tp = next_ps()
                                kb.op("pe", lambda e, h=h, half=half, p=p: e.matmul(
                                    p[:, :], lhsT=Kes[:, h * 128:(h + 1) * 128],
                                    rhs=Vblk[:, 2 * half:2 * half + 2, :].rearrange("p s v -> p (s v)"),
                                    start=True, stop=True), reads=[t_Kes, t_Vblk], writes=[tp])
                                for sl2 in range(2):
                                    sl = 2 * half + sl2
                                    s_ = 4 * g + sl
                                    kb.op("dve", lambda e, h=h, sl=sl, sl2=sl2, s_=s_, p=p: e.scalar_tensor_tensor(
                                        out=S0g[:, sl, h, :], in0=S0g[:, sl, h, :], scalar=ebt[:, h, s_:s_ + 1],
                                        in1=p[:, sl2 * 256:(sl2 + 1) * 256], op0=ALU.mult, op1=ALU.add),
                                        reads=[t_big, t_ebt, tp], writes=[t_big])
                        kb.dma("sp", gla_s_out[:, 4 * g:4 * g + 4, :, :], S0g, reads=[t_big])
            tile_finish(n, xt, t_xt, sample)
            kb.dma("sp", x_dst(l)[:, :, c0t:c0t + n], xt[:, :, 0:n], reads=[t_xt], writes=[tx])
            if ti == len(TILES) - 2:
                kb.dma("sp", gla_p_out[:, :, :], Sst[:, :, :], reads=[t_S])
        kb.barrier()
        ls.close()


    def layer_att(l):
        ls = ExitStack()
        compute_mod(l, ls)
        B = common_bufs(ls, with_hT=False)
        L = LP
        GD = [(128, 1), (512, 4), (2048, 16)]
        EXT = [dd * 128 + L for (_, dd) in GD]

        def sublen(g):
            return 128 + L // GD[g][1]
        kTd = [nc.dram_tensor("kTd%d" % g, [128, 4, EXT[g]], BF16) for g in range(3)]
        qTd = [nc.dram_tensor("qTd%d" % g, [128, 4, L], BF16) for g in range(3)]
        vd = [nc.dram_tensor("vd%d" % g, [EXT[g], 512], BF16) for g in range(3)]
        ktp = [nc.dram_tensor("ktp%d" % i, [128, 3584], BF16) for i in range(3)]
        vtp = [nc.dram_tensor("vtp%d" % i, [896, 512], BF16) for i in range(3)]
        ggK = [nc.dram_tensor("ggK%d" % i, [512, 3584], BF16) for i in range(3)]
        ggV = [nc.dram_tensor("ggV%d" % i, [3584, 512], BF16) for i in range(3)]
        t_kTd = [Tk("kTd%d" % g, multi=True) for g in range(3)]
        t_qTd = [Tk("qTd%d" % g, multi=True) for g in range(3)]
        t_vd = [Tk("vd%d" % g, multi=True) for g in range(3)]
        t_ktp = [Tk("ktp%d" % i, multi=True) for i in range(3)]
        t_vtp = [Tk("vtp%d" % i, multi=True) for i in range(3)]
        t_ggK = [Tk("ggK%d" % i) for i in range(3)]
        t_ggV = [Tk("ggV%d" % i) for i in range(3)]
        t_outs = Tk("att_outs", multi=True)

        def mk(name, shape, dt, st=None):
            return S(name, shape, dt, st if st is not None else ls), Tk(name)
        zT, t_zT = mk("zT", [128, 4, NT], BF16)
        kTs, t_kTs = mk("kTs", [128, 12, NS], BF16)
        qTs, t_qTs = mk("qTs", [128, 12, NS], BF16)
        vS, t_vS = mk("vS", [128, 1536], BF16)
        sA = ExitStack()
        hTa, t_hTa = mk("hTa", [128, KC, NT], BF16, sA)
        rope, t_rope = mk("rope", [128, 2, NT], F32, sA)
        pmf, t_pmf = mk("pmf", [128, 128], F32, sA)
        pmb, t_pmb = mk("pmb", [128, 128], BF16, sA)
        for ci in range(2):
            for hb_ in range(2):
                kb.dma("sp", rope[:, ci, hb_ * 1088:(hb_ + 1) * 1088], rope_in[:, ci, hb_ * 1088:(hb_ + 1) * 1088],
                       writes=[t_rope])
        kb.dma("sp", pmf[:, :], pm_in[:, :], writes=[t_pmf])
        kb.op("dve", lambda e: e.tensor_copy(out=pmb[:, :], in_=pmf[:, :]), reads=[t_pmf], writes=[t_pmb])

        sa = ExitStack()
        w_in, t_win, wblk = load_w(sa, "wa_qk", w_att_in_in[:, 0:3072], 3072)
        xb, t_xb = mk("xb", [128, 512], BF16, sa)
        t1, t_t1 = mk("ra1", [128, 512], F32, sa)
        t2, t_t2 = mk("ra2", [128, 512], F32, sa)
        kf, t_kf = mk("kf", [128, 512], F32, sa)
        kbf, t_kbf = mk("kbf", [128, 512], BF16, sa)
        xt, t_xt = B.xt[0], B.t_xt[0]

        def wtk(col0):
            return t_win[col0 // wblk]

        for ti, (c0t, n) in enumerate(TILES):
            sample = (c0t >= LP)
            kb.dma("sp", xt[:, :, 0:n], x_src(l)[:, :, c0t:c0t + n], reads=[xtk(ti)], writes=[t_xt])
            prenorm(B, xt, t_xt, n, sample, hout=hTa[:, :, c0t:c0t + n], t_hout=t_hTa)

        def dec2(ap2d, g, u):
            if g == 0:
                return ap2d[:, 512 * u:512 * u + 512]
            if g == 1:
                return ap2d.rearrange("p (n r) -> p r n", r=4)[:, u, :]
            return ap2d.rearrange("p (n r) -> p r n", r=16)[:, 4 * u:4 * u + 4, :]

        def qk_core(col0, rhs_fn, n, cos_ap, sin_ap, vwf):
            p, tp = next_ps()
            for kc in range(KC):
                kb.op("pe", lambda e, kc=kc: e.matmul(vwf(p[:, 0:n]), lhsT=w_in[:, kc, col0:col0 + 128], rhs=rhs_fn(kc),
                                                       start=(kc == 0), stop=(kc == KC - 1)),
                      reads=[wtk(col0), t_hTa], writes=[tp], inc=(kc == KC - 1))
            if QK_STEPS < 2:
                return
            kb.op("act", lambda e: e.activation(out=xb[:, 0:n], in_=p[:, 0:n], func=AF.Identity), reads=[tp], writes=[t_xb])
            if QK_STEPS < 3:
                return
            pr, tpr = next_ps()
            kb.op("pe", lambda e: e.matmul(pr[:, 0:n], lhsT=pmb[:, :], rhs=xb[:, 0:n], start=True, stop=True),
                  reads=[t_pmb, t_xb], writes=[tpr])
            if QK_STEPS < 4:
                return
            kb.op("dve", lambda e: e.tensor_tensor(out=vwf(t1[:, 0:n]), in0=vwf(p[:, 0:n]), in1=cos_ap, op=ALU.mult),
                  reads=[tp, t_rope], writes=[t_t1])
            if QK_STEPS < 5:
                return
            kb.op("dve", lambda e: e.tensor_tensor(out=vwf(t2[:, 0:n]), in0=vwf(pr[:, 0:n]), in1=sin_ap, op=ALU.mult),
                  reads=[tpr, t_rope], writes=[t_t2])
            if QK_STEPS < 6:
                return
            kb.op("pool", lambda e: e.tensor_tensor(out=kf[:, 0:n], in0=t1[:, 0:n], in1=t2[:, 0:n], op=ALU.add),
                  reads=[t_t1, t_t2], writes=[t_kf])
            if QK_STEPS < 7:
                return
            kb.op("act", lambda e: e.activation(out=kbf[:, 0:n], in_=kf[:, 0:n], func=AF.Identity),
                  reads=[t_kf], writes=[t_kbf])

        def tail_col(g, hc, r):
            if g == 0:
                return hc * 128
            if g == 1:
                return 512 + (hc * 4 + r) * 128
            return 2560 + (hc * 16 + r) * 128

        for g in (QK_G if (A_PARTS & 1) else []):
            W, dd = GD[g]
            vwf = (lambda a: a.rearrange("p (r n) -> p r n", r=4)) if g == 2 else (lambda a: a)
            for kind in range(2):
                for u in range(4):
                    for hc in range(4):
                        col0 = kind * 1536 + g * 512 + hc * 128
                        qk_core(col0, lambda kc, g=g, u=u: dec2(hTa[:, kc, 0:L], g, u), 512,
                                dec2(rope[:, 0, 0:L], g, u), dec2(rope[:, 1, 0:L], g, u), vwf)
                        if kind == 0:
                            if QK_DMA & 1:
                                kb.dma("sp", qTd[g][:, hc, 512 * u:512 * u + 512], kbf[:, :], reads=[t_kbf], writes=[t_qTd[g]])
                            continue
                        if not (QK_DMA & 2):
                            continue
                        if g == 0:
                            dst = kTd[g][:, hc, 128 + 512 * u:128 + 512 * u + 512]
                            src = kbf[:, :]
                        elif g == 1:
                            dst = kTd[g][:, hc, u * 640 + 128:u * 640 + 640]
                            src = kbf[:, :]
                        else:
                            dst = kTd[g][:, hc, :].rearrange("p (r e) -> p r e", e=256)[:, 4 * u:4 * u + 4, 128:256]
                            src = kbf[:, :].rearrange("p (r n) -> p r n", r=4)
                        kb.dma("sp", dst, src, reads=[t_kbf], writes=[t_kTd[g]])
                        if not (QK_DMA & 4):
                            continue
                        if g == 0 and u != 3:
                            continue
                        if g == 2:
                            col = tail_col(2, hc, 4 * u)
                            pc, off = col // 3584, col % 3584
                            kb.dma("sp", ktp[pc][:, off:off + 512], kbf[:, :], reads=[t_kbf], writes=[t_ktp[pc]])
                            kb.dma("sp", kout[:, col:col + 512], kf[:, :], reads=[t_kf], writes=[t_outs])
                        else:
                            col = tail_col(g, hc, u if g == 1 else 0)
                            pc, off = col // 3584, col % 3584
                            kb.dma("sp", ktp[pc][:, off:off + 128], kbf[:, 384:512], reads=[t_kbf], writes=[t_ktp[pc]])
                            kb.dma("sp", kout[:, col:col + 128], kf[:, 384:512], reads=[t_kf], writes=[t_outs])
        for g in range(3 if (A_PARTS & 2) else 0):
            for kind in range(2):
                for hc in range(4):
                    col0 = kind * 1536 + g * 512 + hc * 128
                    qk_core(col0, lambda kc: hTa[:, kc, LP:LP + NS], NS, rope[:, 0, LP:LP + NS], rope[:, 1, LP:LP + NS],
                            lambda a: a)
                    if kind == 0:
                        kb.op("pool", lambda e, g=g, hc=hc: e.tensor_copy(out=qTs[:, g * 4 + hc, :], in_=kbf[:, 0:NS]),
                              reads=[t_kbf], writes=[t_qTs])
                    else:
                        kb.op("pool", lambda e, g=g, hc=hc: e.tensor_copy(out=kTs[:, g * 4 + hc, :], in_=kbf[:, 0:NS]),
                              reads=[t_kbf], writes=[t_kTs])
                        kb.dma("sp", ks_out[:, g * 4 + hc, :], kf[:, 0:NS], reads=[t_kf], writes=[t_outs])
        kb.barrier()
        sa.close()
        sa = ExitStack()
        w_in, t_win, wblk = load_w(sa, "wa_vz", w_att_in_in[:, 3072:5120], 2048)
        vb, t_vb = mk("vb", [128, 512], BF16, sa)
        vf, t_vf = mk("vf", [128, 512], F32, sa)
        for ti, (c0t, n) in enumerate(TILES if (A_PARTS & 4) else []):
            for c in range(4):
                p, tp = next_ps()
                for kc in range(KC):
                    kb.op("pe", lambda e, kc=kc, c=c, p=p: e.matmul(
                        p[:, 0:n], lhsT=w_in[:, kc, 1536 + c * 128:1536 + (c + 1) * 128], rhs=hTa[:, kc, c0t:c0t + n],
                        start=(kc == 0), stop=(kc == KC - 1)), reads=[wtk(1536 + c * 128), t_hTa], writes=[tp],
                        inc=(kc == KC - 1))
                kb.op("act", lambda e, c=c, p=p: e.activation(out=zT[:, c, c0t:c0t + n], in_=p[:, 0:n], func=AF.Silu),
                      reads=[tp], writes=[t_zT])
        for g in range(3 if (A_PARTS & 8) else 0):
            W, dd = GD[g]
            nb = L // dd // 128
            for r in range(dd):
                for b in range(nb):
                    def lhs(kc, g=g, r=r, b=b):
                        a = hTa[:, kc, 0:L]
                        if g == 0:
                            return a[:, 128 * b:128 * b + 128]
                        return a.rearrange("p (n r) -> p r n", r=GD[g][1])[:, r, 128 * b:128 * b + 128]
                    p, tp = next_ps()
                    for kc in range(KC):
                        kb.op("pe", lambda e, kc=kc, p=p: e.matmul(
                            p[:, :], lhsT=lhs(kc), rhs=w_in[:, kc, g * 512:(g + 1) * 512],
                            start=(kc == 0), stop=(kc == KC - 1)), reads=[wtk(g * 512), t_hTa], writes=[tp],
                            inc=(kc == KC - 1))
                    kb.op("act", lambda e, p=p: e.activation(out=vb[:, :], in_=p[:, :], func=AF.Identity),
                          reads=[tp], writes=[t_vb])
                    e0 = r * sublen(g) + 128 + 128 * b
                    kb.dma("sp", vd[g][e0:e0 + 128, :], vb[:, :], reads=[t_vb], writes=[t_vd[g]])
                    if b == nb - 1:
                        row = (0, 128, 640)[g] + r * 128
                        pc, off = row // 896, row % 896
                        kb.dma("sp", vtp[pc][off:off + 128, :], vb[:, :], reads=[t_vb], writes=[t_vtp[pc]])
                        kb.op("dve", lambda e, p=p: e.tensor_copy(out=vf[:, :], in_=p[:, :]), reads=[tp], writes=[t_vf])
                        kb.dma("sp", vout[row:row + 128, :], vf[:, :], reads=[t_vf], writes=[t_outs])
            p, tp = next_ps()
            for kc in range(KC):
                kb.op("pe", lambda e, kc=kc, p=p: e.matmul(
                    p[:, :], lhsT=hTa[:, kc, LP:LP + NS], rhs=w_in[:, kc, g * 512:(g + 1) * 512],
                    start=(kc == 0), stop=(kc == KC - 1)), reads=[wtk(g * 512), t_hTa], writes=[tp],
                    inc=(kc == KC - 1))
            kb.op("act", lambda e, p=p, g=g: e.activation(out=vS[:, g * 512:(g + 1) * 512], in_=p[:, :], func=AF.Identity),
                  reads=[tp], writes=[t_vS])
            kb.op("dve", lambda e, p=p: e.tensor_copy(out=vf[:, :], in_=p[:, :]), reads=[tp], writes=[t_vf])
            kb.dma("sp", vs_out[:, g * 512:(g + 1) * 512], vf[:, :], reads=[t_vf], writes=[t_outs])
        kb.barrier()
        sa.close()
        sA.close()

        sb2 = ExitStack()
        w_out = S("wa_out", [128, 4, D], BF16, sb2)
        t_wo = Tk("wa_out")
        kb.dma("pool", w_out[:, :, :], w_att_out_in.rearrange("(c p) n -> p c n", p=128), writes=[t_wo])
        nacc, t_nacc = mk("nacc", [128, 4, L], F32, sb2)
        dacc, t_dacc = mk("dacc", [128, 4, L], F32, sb2)
        naccS, t_naccS = mk("naccS", [128, 4, NS], F32, sb2)
        daccS, t_daccS = mk("daccS", [128, 4, NS], F32, sb2)
        amask, t_am = mk("amask", [128, 3, 512], BF16, sb2)
        smask, t_sm = mk("smask", [128, 13, 64], BF16, sb2)
        snew, t_sn = mk("snew", [128, 3, 512], BF16, sb2)
        kb.dma("pool", amask[:, :, :], amask_in[:, :, :], writes=[t_am])
        kb.dma("pool", smask[:, :, :], smask_in[:, :, :], writes=[t_sm])
        kb.dma("pool", snew[:, :, :], snew_in[:, :, :], writes=[t_sn])
        PT, t_PT = mk("PT", [128, 2, 8, 128], BF16, sb2)
        kt, t_kt = mk("kt", [128, 4, 256], BF16, sb2)
        vt, t_vt = mk("vt", [128, 2, 512], BF16, sb2)
        quA, t_qu = mk("quA", [128, 4, 512], BF16, sb2)
        quB, _ = mk("quB", [128, 4, 512], BF16, sb2)
        qsA, t_qs = mk("qsA", [128, 12, NS], BF16, sb2)
        qsB, _ = mk("qsB", [128, 12, NS], BF16, sb2)
        for zt in (quA, quB, qsA, qsB):
            kb.op("pool", lambda e, zt=zt: e.memset(zt[:, :, :], 0.0), writes=[t_qu, t_qs])
        kb.op("pool", lambda e: e.tensor_copy(out=qsA[0:64, :, :], in_=qTs[0:64, :, :]), reads=[t_qTs], writes=[t_qs])
        kb.op("pool", lambda e: e.tensor_copy(out=qsB[64:128, :, :], in_=qTs[64:128, :, :]), reads=[t_qTs], writes=[t_qs])
        Kt2 = [mk("Kt%d" % i, [128, 512], BF16, sb2) for i in range(2)]
        Vt2 = [mk("Vt%d" % i, [128, 512], BF16, sb2) for i in range(2)]
        kTt2 = [mk("kTt%d" % i, [128, 4, 128], BF16, sb2) for i in range(2)]
        PTs2 = [mk("PTs%d" % i, [128, 64], BF16, sb2) for i in range(2)]
        Kf2 = [mk("Kf%d" % i, [128, 512], F32, sb2) for i in range(2)]
        Vf2 = [mk("Vf%d" % i, [128, 512], F32, sb2) for i in range(2)]
        stile = [0]
        hk, t_hk = mk("hk", [128, 3584], BF16, sb2)
        hv, _ = mk("hv", [128, 2, 512], BF16, sb2)
        t_hvb = [Tk("hv0"), Tk("hv1")]
        idxk, t_idx = mk("idxk", [128, 1], I32, sb2)
        idxv, _ = mk("idxv", [128, 7], I32, sb2)
        og, t_og = mk("og", [128, 4, TN], BF16, sb2)
        ps_reserved.add(7)
        ptb = ps[7][:, :].bitcast(BF16)[:, 0:512]
        t_ptb = t_ps[7]
        kb.dma("sp", idxk[:, :], idxk_in[:, :], writes=[t_idx])
        kb.dma("sp", idxv[:, :], idxv_in[:, :], writes=[t_idx])
        kb.op("dve", lambda e: e.memset(nacc[:, :, :], 0.0), writes=[t_nacc])
        kb.op("dve", lambda e: e.memset(dacc[:, :, :], 0.0), writes=[t_dacc])
        SC = 0.125

        for i in range(3 if _en('X') else 0):
            kb.collective(lambda e, i=i: e.collective_compute(
                "AllGather", ALU.bypass, replica_groups=[[0, 1, 2, 3], [4, 5, 6, 7]],
                ins=[ktp[i][:, :]], outs=[ggK[i][:, :]]), reads=[t_ktp[i]], writes=[t_ggK[i]])
            kb.collective(lambda e, i=i: e.collective_compute(
                "AllGather", ALU.bypass, replica_groups=[[0, 1, 2, 3], [4, 5, 6, 7]],
                ins=[vtp[i][:, :]], outs=[ggV[i][:, :]]), reads=[t_vtp[i]], writes=[t_ggV[i]])

        res = []
        for _ in range(4):
            i = psn[0]
            while i in ps_reserved:
                i = (i + 1) % 8
            ps_reserved.add(i)
            res.append(i)
        pnum = [(ps[res[0]], t_ps[res[0]]), (ps[res[1]], t_ps[res[1]])]
        pden = [(ps[res[2]], t_ps[res[2]]), (ps[res[3]], t_ps[res[3]])]
        started = set()

        def first(i):
            if i in started:
                return False
            started.add(i)
            return True

        def score_bank(maskrhs, t_mask, nn, per_head):
            p, tp = next_ps()
            kb.op("pe", lambda e: e.matmul(p[:, 0:nn], lhsT=ident_bf[:, :], rhs=maskrhs, start=True, stop=False),
                  reads=[t_const, t_mask], writes=[tp], inc=False)
            per_head(p, tp)
            return p, tp

        for g in range(3 if _en('S') else 0):
            for hb in range(2):
                def heads(p, tp, g=g, hb=hb):
                    for hh in range(4):
                        h = 4 * hb + hh
                        hc, pb = h // 2, (h % 2) * 64
                        kb.op("pe", lambda e, hh=hh, hc=hc, pb=pb: e.matmul(
                            p[:, hh * 128:(hh + 1) * 128], lhsT=kTs[:, g * 4 + hc, :],
                            rhs=(qsA if pb == 0 else qsB)[:, g * 4 + hc, :], start=False, stop=True),
                            reads=[t_kTs, t_qs], writes=[tp], inc=(hh == 3))
                p, tp = score_bank(snew[:, g, :], t_sn, 512, heads)
                kb.op("act", lambda e, p=p, hb=hb: e.activation(
                    out=PT[:, 0, 4 * hb:4 * hb + 4, :], in_=p[:, :].rearrange("p (h q) -> p h q", q=128),
                    func=AF.Exp, scale=SC), reads=[tp], writes=[t_PT])
            for hb in range(2):
                pd, tpd = pden[hb]
                kb.op("pe", lambda e, hb=hb, pd=pd: e.matmul(
                    pd[:, :], lhsT=ones_bf[:, :], rhs=PT[:, 0, 4 * hb:4 * hb + 4, :].rearrange("p h q -> p (h q)"),
                    start=first(res[2 + hb]), stop=False), reads=[t_const, t_PT], writes=[tpd])
            for hc in range(4):
                pn, tpn = pnum[hc // 2]
                for ab in range(2):
                    c0 = ((hc % 2) * 2 + ab) * 128
                    kb.op("pe", lambda e, hc=hc, ab=ab, c0=c0, pn=pn, g=g: e.matmul(
                        pn[:, c0:c0 + 128], lhsT=vS[:, g * 512 + hc * 128:g * 512 + (hc + 1) * 128],
                        rhs=PT[:, 0, 2 * hc + ab, :], start=first(res[hc // 2]), stop=False),
                        reads=[t_vS, t_PT], writes=[tpn])
        stiles = [(s_, g, r) for s_ in range(16 if _en('S') else 0) for g in range(3) for r in range((1, 4, 8)[g])]

        def s_load(i):
            s_, g, r = stiles[i]
            dd = GD[g][1]
            bi = i % 2
            (Kt, t_Kt), (Vt, t_Vt) = Kt2[bi], Vt2[bi]
            (Kf, t_Kf), (Vf, t_Vf) = Kf2[bi], Vf2[bi]
            kb.dma("sp", Kf[:, :], ck_in[g][s_, r::dd, :], writes=[t_Kf])
            kb.dma("sp", Vf[:, :], cv_in[g][s_, r::dd, :], writes=[t_Vf])
            kb.op("act", lambda e: e.activation(out=Kt[:, :], in_=Kf[:, :], func=AF.Identity),
                  reads=[t_Kf], writes=[t_Kt])
            kb.op("act", lambda e: e.activation(out=Vt[:, :], in_=Vf[:, :], func=AF.Identity),
                  reads=[t_Vf], writes=[t_Vt])

        def s_compute(i):
            s_, g, r = stiles[i]
            mt = (0, 1, 5)[g] + r
            bi = i % 2
            (Kt, t_Kt), (Vt, t_Vt), (kTt, t_kTt), (PTs, t_PTs) = Kt2[bi], Vt2[bi], kTt2[bi], PTs2[bi]
            for hc in range(4):
                kb.op("pe", lambda e, hc=hc: e.transpose(ptb[:, hc * 128:(hc + 1) * 128],
                                                          Kt[:, hc * 128:(hc + 1) * 128], ident_bf[:, :]),
                      reads=[t_Kt, t_const], writes=[t_ptb], inc=(hc == 3))
            kb.op("dve", lambda e: e.tensor_copy(out=kTt[:, :, :], in_=ptb[:, :].rearrange("p (c k) -> p c k", c=4)),
                  reads=[t_ptb], writes=[t_kTt])

            def heads(p, tp, g=g, s_=s_):
                for h in range(8):
                    hc, pb = h // 2, (h % 2) * 64
                    kb.op("pe", lambda e, h=h, hc=hc, pb=pb: e.matmul(
                        p[:, h * 8:(h + 1) * 8], lhsT=kTt[:, hc, :],
                        rhs=(qsA if pb == 0 else qsB)[:, g * 4 + hc, 8 * s_:8 * s_ + 8], start=False, stop=True),
                        reads=[t_kTt, t_qs], writes=[tp], inc=(h == 7))
            p, tp = score_bank(smask[:, mt, :], t_sm, 64, heads)
            kb.op("act", lambda e, p=p: e.activation(out=PTs[:, :], in_=p[:, 0:64], func=AF.Exp, scale=SC),
                  reads=[tp], writes=[t_PTs])
            for hb in range(2):
                pd, tpd = pden[hb]
                kb.op("pe", lambda e, hb=hb, pd=pd: e.matmul(
                    pd[:, :].rearrange("p (h q) -> p h q", q=128)[:, :, 8 * s_:8 * s_ + 8],
                    lhsT=ones_bf[:, :], rhs=PTs[:, 32 * hb:32 * hb + 32].rearrange("p (h i) -> p h i", i=8),
                    start=False, stop=False), reads=[t_const, t_PTs], writes=[tpd], inc=(hb == 1))
            for hc in range(4):
                pn, tpn = pnum[hc // 2]
                for ab in range(2):
                    c0 = ((hc % 2) * 2 + ab) * 128 + 8 * s_
                    kb.op("pe", lambda e, hc=hc, ab=ab, c0=c0, pn=pn: e.matmul(
                        pn[:, c0:c0 + 8], lhsT=Vt[:, hc * 128:(hc + 1) * 128],
                        rhs=PTs[:, (2 * hc + ab) * 8:(2 * hc + ab) * 8 + 8], start=False, stop=False),
                        reads=[t_Vt, t_PTs], writes=[tpn], inc=(hc == 3 and ab == 1))

        if stiles:
            s_load(0)
        for i in range(len(stiles)):
            if i + 1 < len(stiles):
                s_load(i + 1)
            s_compute(i)
        for bk in range(2 if _en('S') else 0):
            pn, tpn = pnum[bk]
            pd, tpd = pden[bk]
            pnv = pn[:, :].rearrange("p (c a q) -> p c a q", c=2, a=2)
            pdv = pd[:, :].rearrange("p (c a q) -> p c a q", c=2, a=2)
            for ab in range(2):
                rows = slice(64 * ab, 64 * ab + 64)
                kb.op("dve", lambda e, rows=rows, ab=ab, pnv=pnv, bk=bk: e.tensor_copy(
                    out=naccS[rows, 2 * bk:2 * bk + 2, :], in_=pnv[rows, :, ab, :]), reads=[tpn], writes=[t_naccS])
                kb.op("dve", lambda e, rows=rows, ab=ab, pdv=pdv, bk=bk: e.tensor_copy(
                    out=daccS[rows, 2 * bk:2 * bk + 2, :], in_=pdv[rows, :, ab, :]), reads=[tpd], writes=[t_daccS])
        for i in res:
            ps_reserved.discard(i)

        for i in range(3 if _en('H') else 0):
            kb.dma_custom("pool", lambda e, i=i: e.indirect_dma_start(
                out=hk[:, :], out_offset=None, in_=ggK[i][:, :],
                in_offset=bass.IndirectOffsetOnAxis(ap=idxk[:, 0:1], axis=0)), reads=[t_ggK[i], t_idx], writes=[t_hk])
            for c128 in range(28):
                col = i * 3584 + c128 * 128
                cb = col // 128
                if cb < 4:
                    g, hc, r = 0, cb, 0
                elif cb < 20:
                    g, hc, r = 1, (cb - 4) // 4, (cb - 4) % 4
                else:
                    g, hc, r = 2, (cb - 20) // 16, (cb - 20) % 16
                e0 = r * sublen(g)
                kb.dma("sp", kTd[g][:, hc, e0:e0 + 128], hk[:, c128 * 128:(c128 + 1) * 128], reads=[t_hk],
                       writes=[t_kTd[g]])
            for t in range(7):
                hb_ = t % 2
                kb.dma_custom("pool", lambda e, i=i, t=t, hb_=hb_: e.indirect_dma_start(
                    out=hv[:, hb_, :], out_offset=None, in_=ggV[i][:, :],
                    in_offset=bass.IndirectOffsetOnAxis(ap=idxv[:, t:t + 1], axis=0)), reads=[t_ggV[i], t_idx],
                    writes=[t_hvb[hb_]])
                T = i * 7 + t
                if T == 0:
                    g, r = 0, 0
                elif T < 5:
                    g, r = 1, T - 1
                else:
                    g, r = 2, T - 5
                e0 = r * sublen(g)
                kb.dma("sp", vd[g][e0:e0 + 128, :], hv[:, hb_, :], reads=[t_hvb[hb_]], writes=[t_vd[g]])

        for g in range(3 if _en('B') else 0):
            W, dd = GD[g]
            nb = L // dd // 128
            for u in range(4):
                kb.dma("sp", quA[0:64, :, :], qTd[g][0:64, :, 512 * u:512 * u + 512], reads=[t_qTd[g]], writes=[t_qu])
                kb.dma("sp", quB[64:128, :, :], qTd[g][64:128, :, 512 * u:512 * u + 512], reads=[t_qTd[g]], writes=[t_qu])
                for k in range(4):
                    if g == 0:
                        r, b = 0, 4 * u + k
                    elif g == 1:
                        r, b = u, k
                    else:
                        r, b = 4 * u + k, 0
                    e0 = r * sublen(g) + 128 * b
                    kb.dma("sp", kt[:, :, :], kTd[g][:, :, e0:e0 + 256], reads=[t_kTd[g]], writes=[t_kt])
                    kb.dma("sp", vt[:, :, :], vd[g][e0:e0 + 256, :].rearrange("(t p) c -> p t c", p=128),
                           reads=[t_vd[g]], writes=[t_vt])
                    if B_STEPS < 2:
                        continue
                    for half in range(2):
                        mi = 2 if half == 1 else (1 if b == 0 else 0)
                        for hb in range(2):
                            def heads(p, tp, hb=hb, half=half, k=k):
                                for hh in range(4):
                                    h = 4 * hb + hh
                                    hc, pb = h // 2, (h % 2) * 64
                                    kb.op("pe", lambda e, hh=hh, hc=hc, pb=pb: e.matmul(
                                        p[:, hh * 128:(hh + 1) * 128], lhsT=kt[:, hc, half * 128:(half + 1) * 128],
                                        rhs=(quA if pb == 0 else quB)[:, hc, k * 128:(k + 1) * 128], start=False, stop=True),
                                        reads=[t_kt, t_qu], writes=[tp], inc=(hh == 3))
                            p, tp = score_bank(amask[:, mi, :], t_am, 512, heads)
                            kb.op("act", lambda e, p=p, hb=hb, half=half: e.activation(
                                out=PT[:, half, 4 * hb:4 * hb + 4, :], in_=p[:, :].rearrange("p (h q) -> p h q", q=128),
                                func=AF.Exp, scale=SC), reads=[tp], writes=[t_PT])
                    if B_STEPS < 3:
                        continue
                    pdl = [next_ps(), next_ps()]
                    for hb in range(2):
                        pd, tpd = pdl[hb]
                        for half in range(2):
                            kb.op("pe", lambda e, hb=hb, half=half, pd=pd: e.matmul(
                                pd[:, :], lhsT=ones_bf[:, :],
                                rhs=PT[:, half, 4 * hb:4 * hb + 4, :].rearrange("p h q -> p (h q)"),
                                start=(half == 0), stop=(half == 1)), reads=[t_const, t_PT], writes=[tpd],
                                inc=(half == 1))
                    if B_STEPS < 4:
                        continue
                    pnl = [next_ps(), next_ps()]
                    for bk in range(2):
                        pn, tpn = pnl[bk]
                        fst = True
                        for hcl in range(2):
                            hc = 2 * bk + hcl
                            for ab in range(2):
                                c0 = (hcl * 2 + ab) * 128
                                for half in range(2):
                                    kb.op("pe", lambda e, hc=hc, ab=ab, c0=c0, half=half, pn=pn, fst=fst: e.matmul(
                                        pn[:, c0:c0 + 128], lhsT=vt[:, half, hc * 128:(hc + 1) * 128],
                                        rhs=PT[:, half, 2 * hc + ab, :], start=fst, stop=(half == 1)),
                                        reads=[t_vt, t_PT], writes=[tpn], inc=(hcl == 1 and ab == 1 and half == 1))
                                    fst = False
                    if B_STEPS < 5:
                        continue
                    def tokv(acc, rows, c2):
                        a = acc[:, c2:c2 + 2, 0:L]
                        if dd > 1:
                            a = a.rearrange("p c (n r) -> p c r n", r=dd)[:, :, r, :]
                        return a[rows, :, 128 * b:128 * b + 128]
                    for bk in range(2):
                        pn, tpn = pnl[bk]
                        pd, tpd = pdl[bk]
                        pnv = pn[:, :].rearrange("p (c a q) -> p c a q", c=2, a=2)
                        pdv = pd[:, :].rearrange("p (c a q) -> p c a q", c=2, a=2)
                        for ab in range(2):
                            rows = slice(64 * ab, 64 * ab + 64)
                            kb.op("dve", lambda e, rows=rows, ab=ab, pnv=pnv, bk=bk: e.tensor_tensor(
                                out=tokv(nacc, rows, 2 * bk), in0=tokv(nacc, rows, 2 * bk), in1=pnv[rows, :, ab, :],
                                op=ALU.add), reads=[tpn, t_nacc], writes=[t_nacc])
                            kb.op("dve", lambda e, rows=rows, ab=ab, pdv=pdv, bk=bk: e.tensor_tensor(
                                out=tokv(dacc, rows, 2 * bk), in0=tokv(dacc, rows, 2 * bk), in1=pdv[rows, :, ab, :],
                                op=ALU.add), reads=[tpd, t_dacc], writes=[t_dacc])

        for ti, (c0t, n) in enumerate(TILES):
            sample = (c0t >= LP)
            tx = xtk(ti)
            kb.dma("sp", xt[:, :, 0:n], x_src(l)[:, :, c0t:c0t + n], reads=[tx], writes=[t_xt])
            if not sample:
                na, da, tna, tda = nacc[:, :, c0t:c0t + n], dacc[:, :, c0t:c0t + n], t_nacc, t_dacc
            else:
                na, da, tna, tda = naccS[:, :, :], daccS[:, :, :], t_naccS, t_daccS
            kb.op("dve", lambda e, da=da: e.reciprocal(out=da, in_=da), reads=[tda], writes=[tda])
            kb.op("dve", lambda e, da=da, na=na: e.tensor_tensor(out=na, in0=na, in1=da, op=ALU.mult),
                  reads=[tda, tna], writes=[tna])
            kb.op("pool", lambda e, na=na: e.tensor_tensor(out=og[:, :, 0:n], in0=na, in1=zT[:, :, c0t:c0t + n], op=ALU.mult),
                  reads=[tna, t_zT], writes=[t_og])
            for m in range(KC):
                po, tpo = next_ps()
                for c in range(4):
                    kb.op("pe", lambda e, c=c, m=m, po=po: e.matmul(
                        po[:, 0:n], lhsT=w_out[:, c, m * 128:(m + 1) * 128], rhs=og[:, c, 0:n],
                        start=(c == 0), stop=(c == 3)), reads=[t_wo, t_og], writes=[tpo], inc=(c == 3))
                kb.op("act", lambda e, m=m, po=po: e.activation(out=B.oT[:, m, 0:n], in_=po[:, 0:n], func=AF.Identity),
                      reads=[tpo], writes=[B.t_oT])
                kb.op("act", lambda e, m=m, po=po: e.activation(out=B.sq[:, m, 0:n], in_=po[:, 0:n], func=AF.Square),
                      reads=[tpo], writes=[B.t_sq])
            postnorm_residual(B, xt, t_xt, n, sample, l)
            kb.dma("sp", x_dst(l)[:, :, c0t:c0t + n], xt[:, :, 0:n], reads=[t_xt], writes=[tx])
        kb.barrier()
        ps_reserved.discard(7)
        sb2.close()
        ls.close()

    layer_conv(0, 0)
    if NLAYERS >= 2:
        layer_gla(1, 0)
    if NLAYERS >= 3:
        layer_att(2)
    if NLAYERS >= 4:
        layer_conv(3, 1)
    kb.final_wait()
    global _KB_DEBUG
    _KB_DEBUG = (dict(kb.cnt), dict(kb.dcnt), kb.ccn)
    es.close()
    return nc


def _prep_inputs(inp):
    f = lambda a: np.ascontiguousarray(np.asarray(a, dtype=np.float32))
    x_prompt, x_sample = f(inp["x_prompt"]), f(inp["x_sample"])
    c_prompt, c_sample = f(inp["c_prompt"]), f(inp["c_sample"])
    state_conv = f(inp["state_conv"])
    shared = {
        "w_ada": f(inp["w_ada"]),
        "w_conv_in": f(inp["w_conv_in"]),
        "w_conv_out": f(inp["w_conv_out"]),
        "w_gla_in": f(inp["w_gla_in"][0]),
        "w_gla_out": f(inp["w_gla_out"][0]),
        "w_a1": f(inp["w_gla_a1"][0]),
        "w_a2": f(inp["w_gla_a2"][0]),
        "b_a": f(inp["b_gla_a"][0]).reshape(1, 512),
        "w_att_in": f(inp["w_att_in"][0]),
        "w_att_out": f(inp["w_att_out"][0]),
        "pm": _att_consts()["pm"],
        "smask": _att_consts()["smask"],
        "snew": _att_consts()["snew"],
        "gmask": _gla_masks()[0],
        "gseg": _gla_masks()[1],
    }
    state_gla = f(inp["state_gla"])
    maps = []
    for c in range(NCORES):
        b, j = c // 4, c % 4
        s0 = 16 * c
        m = dict(shared)
        xT = np.empty((128, KC, NT), np.float32)
        xT[:, :, :LP] = _fm(x_prompt[b, LP * j:LP * (j + 1)]).transpose(0, 2, 1)
        xT[:, :, LP:] = _fm(x_sample[s0:s0 + 16].reshape(NS, D)).transpose(0, 2, 1)
        m["xT"] = xT
        xh = np.zeros((128, KC, 32), np.float32)
        if j > 0:
            xh[:] = _fm(x_prompt[b, LP * j - 32:LP * j]).transpose(0, 2, 1)
        m["xh"] = xh
        cT = np.empty((128, KC, 129), np.float32)
        cT[:, :, :NS] = _fm(np.repeat(c_sample[s0:s0 + 16], 8, axis=0)).transpose(0, 2, 1)
        cT[:, :, NS] = _fm(c_prompt[b])
        m["cT"] = cT
        vecs = np.zeros((128, NV), np.float32)

        def put(name, arr):
            off, w = VLAY[name]
            vecs[:, off:off + w] = np.asarray(arr, np.float32).reshape(128, w)
        put("g_pre", _fm(inp["g_pre"]))
        put("g_post", _fm(inp["g_post"]))
        put("b_ada", _fm(inp["b_ada"]))
        put("b_dw", _fm(inp["b_dw"]))
        put("g_cln", _fm(inp["g_conv_ln"]))
        put("b_cln", _fm(inp["b_conv_ln"]))
        put("w_dw", _fm(inp["w_dw"]).transpose(0, 1, 3, 2))
        put("hflag", np.full((128, 1), 0.0 if j == 0 else 1.0))
        put("eps", np.full((128, 1), EPS))
        put("one", np.full((128, 1), 1.0))
        selv = np.zeros((128, 4), np.float32)
        selv[:, j] = 1.0
        put("sel", selv)
        put("g_gn", _fm(inp["g_gla_norm"][0]))
        put("ident", np.eye(128, dtype=np.float32))
        m["vecs"] = vecs
        sc = state_conv[:, s0:s0 + 16]
        m["sc_fm"] = np.ascontiguousarray(_fm(sc).transpose(0, 1, 4, 2, 3))
        m["sc_old"] = np.ascontiguousarray(sc[:, :, 8:, :])
        ac = _att_consts()
        pos = np.concatenate([LP * j + np.arange(LP), 2048 + (np.arange(NS) % 8)]).astype(np.float64)
        inv = 10000.0 ** (-np.arange(32, dtype=np.float64) / 32)
        dd_ = np.arange(128) % 64
        ang = pos[None, :] * inv[dd_ % 32][:, None]
        sgn = np.where(dd_ < 32, -1.0, 1.0)[:, None]
        m["rope"] = np.stack([np.cos(ang), sgn * np.sin(ang)], axis=1).astype(np.float32)
        am = np.stack([ac["mprev"], ac["mprev"] if j > 0 else np.full((128, 128), NEG, np.float32), ac["mcur"]], axis=1)
        m["amask"] = np.ascontiguousarray(np.tile(am, (1, 1, 4)))
        pred = max(j - 1, 0)
        m["idxk"] = (pred * 128 + np.arange(128, dtype=np.int32)).reshape(128, 1).astype(np.int32)
        m["idxv"] = (pred * 896 + np.arange(7, dtype=np.int32)[None, :] * 128
                     + np.arange(128, dtype=np.int32)[:, None]).astype(np.int32)
        caches_k = (inp["cache_k_g0"], inp["cache_k_g1"], inp["cache_k_g2"])
        caches_v = (inp["cache_v_g0"], inp["cache_v_g1"], inp["cache_v_g2"])
        for g in range(3 if (NLAYERS >= 3 and _en('S')) else 0):
            m["ck%d" % g] = np.ascontiguousarray(f(caches_k[g])[0, s0:s0 + 16].reshape(16, -1, 512))
            m["cv%d" % g] = np.ascontiguousarray(f(caches_v[g])[0, s0:s0 + 16].reshape(16, -1, 512))
        m["sgla"] = np.ascontiguousarray(state_gla[0, s0:s0 + 16].transpose(2, 0, 1, 3))
        maps.append(m)
    return maps


NEG = -30000.0
_AC = {}


def _att_consts():
    if _AC:
        return _AC
    p = np.arange(128)
    _AC["mprev"] = np.where(p[:, None] >= p[None, :], 0.0, NEG).astype(np.float32)
    _AC["mcur"] = np.where(p[:, None] <= p[None, :], 0.0, NEG).astype(np.float32)
    m = np.arange(128)
    partner = np.where((m % 64) < 32, m + 32, m - 32)
    pm = np.zeros((128, 128), np.float32)
    pm[partner, m] = 1.0
    _AC["pm"] = pm
    n = np.arange(128)[:, None]
    i = np.arange(8)[None, :]
    sm = np.zeros((128, 13, 8), bool)
    sm[:, 0] = n >= i
    for r in range(4):
        sm[:, 1 + r] = ((i % 4) == r) & ~((i >= 4) & (n == 0))
    for r in range(8):
        sm[:, 5 + r] = (i == r) & (n >= 0)
    smf = np.where(sm, 0.0, NEG).astype(np.float32)
    _AC["smask"] = np.ascontiguousarray(np.tile(smf, (1, 1, 8)))
    kk = np.arange(128)[:, None]
    qq = np.arange(128)[None, :]
    same = (kk // 8) == (qq // 8)
    ki, qi = kk % 8, qq % 8
    sn = np.stack([same & (ki <= qi), same & ((ki == qi) | (ki == qi - 4)), same & (ki == qi)], axis=1)
    _AC["snew"] = np.ascontiguousarray(np.tile(np.where(sn, 0.0, NEG).astype(np.float32), (1, 1, 4)))
    return _AC


def _gla_masks():
    j = np.arange(128)
    same = (j[:, None] // 8) == (j[None, :] // 8)
    le = j[:, None] <= j[None, :]
    gt = j[:, None] > j[None, :]
    gm = np.zeros((128, 2, 3, 128), np.float32)
    gm[:, 0, 0] = np.where(le, -1.0 / 16, 0.0)
    gm[:, 0, 1] = np.where(gt, -1.0 / 16, 0.0)
    gm[:, 0, 2] = np.where(le, 1.0, 0.0)
    gm[:, 1, 0] = np.where(le & same, -1.0 / 16, 0.0)
    gm[:, 1, 1] = np.where(gt & same, -1.0 / 16, 0.0)
    gm[:, 1, 2] = np.where(le & same, 1.0, 0.0)
    seg = np.zeros((128, 2, 2, 16), np.float32)
    seg[:, 0, 0, 0] = -1.0 / 16
    seg[:, 0, 1, 0] = 1.0
    inseg = (j[:, None] // 8) == np.arange(16)[None, :]
    seg[:, 1, 0] = np.where(inseg, -1.0 / 16, 0.0)
    seg[:, 1, 1] = np.where(inseg, 1.0, 0.0)
    return gm, seg


def _tm(a):
    return np.ascontiguousarray(a.transpose(2, 1, 0).reshape(a.shape[2], -1))


_NC_CACHE = {}


def kernel(**inputs):
    if "nc" not in _NC_CACHE:
        _NC_CACHE["nc"] = build_program()
    nc = _NC_CACHE["nc"]
    maps = _prep_inputs(inputs)
    res = run_bass_kernel_spmd(nc, maps, core_ids=list(range(NCORES)))
    R = res.results
    B, DB, DS = 2, 128, 8
    y_prompt = np.zeros((B, SEQ, D), np.float32)
    y_sample = np.zeros((DB, DS, D), np.float32)
    conv_p = np.zeros((2, B, 30, D), np.float32)
    conv_s = np.zeros((2, DB, 30, D), np.float32)
    gla_p = np.zeros((1, B, 4, 128, 256), np.float32)
    gla_s = np.zeros((1, DB, 4, 128, 256), np.float32)
    kv_p = [np.zeros((1, B, w, 8, 64), np.float32) for w in (128, 128, 512, 512, 2048, 2048)]
    kv_s = [np.zeros((1, DB, DS, 8, 64), np.float32) for _ in range(6)]
    for c in range(NCORES):
        b, j = c // 4, c % 4
        s0 = 16 * c
        r = R[c]
        yt = _tm(r["yT"])
        y_prompt[b, LP * j:LP * (j + 1)] = yt[:LP]
        y_sample[s0:s0 + 16] = yt[LP:].reshape(16, 8, D)
        for jl in range(2):
            if j == 3:
                conv_p[jl, b] = _tm(r["conv_tail"][jl])[2:]
            conv_s[jl, s0:s0 + 16, :22] = r["conv_old"][jl]
            conv_s[jl, s0:s0 + 16, 22:] = _tm(r["conv_new"][jl]).reshape(16, 8, D)
        if j == 3:
            gla_p[0, b] = r["gla_p"].transpose(1, 0, 2)
        gla_s[0, s0:s0 + 16] = r["gla_s"].transpose(1, 2, 0, 3)
        if "kout" in r:
            for g, dd in enumerate((1, 4, 16)):
                kbase = (0, 512, 2560)[g]
                vbase = (0, 128, 640)[g]
                if j == 3:
                    blk = r["kout"][:, kbase:kbase + 4 * dd * 128].reshape(2, 64, 4, dd, 128)
                    kv_p[2 * g][0, b] = blk.transpose(4, 3, 2, 0, 1).reshape(128 * dd, 8, 64)
                    vblk = r["vout"][vbase:vbase + dd * 128].reshape(dd, 128, 512)
                    kv_p[2 * g + 1][0, b] = vblk.transpose(1, 0, 2).reshape(128 * dd, 8, 64)
                ks = r["ks_out"].reshape(2, 64, 3, 4, 16, 8)[:, :, g]
                kv_s[2 * g][0, s0:s0 + 16] = ks.transpose(3, 4, 2, 0, 1).reshape(16, 8, 8, 64)
                kv_s[2 * g + 1][0, s0:s0 + 16] = r["vs_out"].reshape(16, 8, 3, 8, 64)[:, :, g]
    return (y_prompt, y_sample, conv_p, conv_s, gla_p, gla_s, *kv_p, *kv_s)
```
